# Optimizing a Trainium2 kernel written in Bass

```python
import math
import jax
import jax.numpy as jnp
from jax import lax
import numpy as np

D_MODEL = 1024
BATCH = 4
SEQ = 4096
DEPTH = 4

N_HEADS = 16
HEAD_DIM = D_MODEL // N_HEADS
D_FF = 4 * D_MODEL
N_MIXERS = 4
N_REPEATS = DEPTH // N_MIXERS
RMS_EPS = 1e-6
NEG_INF = -1e30
POS_BIG = 1e30
TINY = 1e-30

REL_BUCKETS = 32
REL_MAX_EXACT = REL_BUCKETS // 2
REL_MAX_DIST = 2048

BAND_BLOCK = 128
Q_BLOCK = 128
GATHER_CHUNK = 32

DILATED_PATTERNS = ((128, 1), (512, 4), (2048, 16))

NSA_KV_HEADS = 4
NSA_GROUP = N_HEADS // NSA_KV_HEADS
NSA_KV_DIM = NSA_KV_HEADS * HEAD_DIM
CMP_STRIDE = 16
CMP_BLOCK = 2 * CMP_STRIDE
CMP_HIDDEN = 256
SEL_BLOCK = 64
SEL_TOPK = 16
NSA_WINDOW = 512
NSA_IN_DIM = D_MODEL + 6 * NSA_KV_DIM + 3 * N_HEADS

FOX_IN_DIM = 3 * D_MODEL + N_HEADS

MOBA_BLOCK = 256
MOBA_TOPK = 3

kernel_name = "hybrid_dilated_nsa_fox_moba"


def rmsnorm(x, g):
    x32 = x.astype(jnp.float32)
    y = x32 * lax.rsqrt(jnp.mean(x32 * x32, axis=-1, keepdims=True) + RMS_EPS)
    return y.astype(x.dtype) * g


def rel_bucket(dist):
    d = jnp.maximum(dist, 0)
    df = jnp.maximum(d.astype(jnp.float32), 1.0)
    large = REL_MAX_EXACT + (jnp.log(df / REL_MAX_EXACT) / math.log(REL_MAX_DIST / REL_MAX_EXACT)
                             * (REL_BUCKETS - REL_MAX_EXACT)).astype(jnp.int32)
    large = jnp.minimum(large, REL_BUCKETS - 1)
    return jnp.where(d < REL_MAX_EXACT, d, large)


def masked_stats(logits, mask):
    l = jnp.where(mask, logits, NEG_INF)
    m = jnp.max(l, axis=-1, keepdims=True)
    p = jnp.where(mask, jnp.exp(l - m), 0.0)
    s = jnp.sum(p, axis=-1, keepdims=True)
    return p, m, s


def combine_by_denominators(parts):
    m_all = parts[0][1]
    for _, m, _ in parts[1:]:
        m_all = jnp.maximum(m_all, m)
    w = [s * jnp.exp(m - m_all) for _, m, s in parts]
    num = sum(wi * o for wi, (o, _, _) in zip(w, parts))
    return num / jnp.maximum(sum(w), TINY)


def split_chunks(t, axis, size):
    shp = t.shape
    t = t.reshape(shp[:axis] + (shp[axis] // size, size) + shp[axis + 1:])
    return jnp.moveaxis(t, axis, 0)


def merge_chunks(t, axis):
    t = jnp.moveaxis(t, 0, axis)
    shp = t.shape
    return t.reshape(shp[:axis] + (shp[axis] * shp[axis + 1],) + shp[axis + 2:])


def banded_attention(q, k, v, bias_tab, max_dist, dilation):
    b, kh, g, L, dh = q.shape
    blk = BAND_BLOCK
    n_prev = -(-max_dist // blk)
    nb = -(-L // blk)
    pad = nb * blk - L
    qb = jnp.pad(q, ((0, 0), (0, 0), (0, 0), (0, pad), (0, 0))).reshape(b, kh, g, nb, blk, dh)

    def windows(t):
        tp = jnp.pad(t, ((0, 0), (0, 0), (n_prev * blk, pad), (0, 0))).reshape(b, kh, n_prev + nb, blk, dh)
        return jnp.concatenate([tp[:, :, i:i + nb] for i in range(n_prev + 1)], axis=3)

    kw, vw = windows(k), windows(v)
    nk = (n_prev + 1) * blk
    qi = jnp.arange(blk)[:, None]
    kj = jnp.arange(nk)[None, :]
    rel = qi + n_prev * blk - kj
    kpos = (jnp.arange(nb) * blk)[:, None, None] - n_prev * blk + kj[None]
    mask = (rel >= 0) & (rel <= max_dist) & (kpos >= 0)
    bias = jnp.transpose(bias_tab[rel_bucket(rel * dilation)], (2, 3, 0, 1))[None, :, :, None]
    logits = jnp.einsum("bhgnqd,bhnkd->bhgnqk", qb, kw, preferred_element_type=jnp.float32) * (dh ** -0.5) + bias
    p, m, s = masked_stats(logits, mask)
    o = jnp.einsum("bhgnqk,bhnkd->bhgnqd", p.astype(v.dtype), vw,
                   preferred_element_type=jnp.float32) / jnp.maximum(s, TINY)

    def unblock(t):
        return t.reshape(b, kh, g, nb * blk, t.shape[-1])[:, :, :, :L]

    return unblock(o), unblock(m), unblock(s)


def dilated_mixer(h, w_in, q_gain, k_gain, rel_table):
    b, S, _ = h.shape
    qkv = (h @ w_in).reshape(b, S, 3, N_HEADS, HEAD_DIM)
    q = rmsnorm(qkv[:, :, 0], q_gain)
    k = rmsnorm(qkv[:, :, 1], k_gain)
    v = qkv[:, :, 2]
    parts = []
    for window, dil in DILATED_PATTERNS:
        n = S // dil

        def by_residue(t):
            return t.reshape(b, n, dil, N_HEADS, HEAD_DIM).transpose(0, 2, 3, 1, 4).reshape(b * dil, N_HEADS, n, HEAD_DIM)

        def back(t):
            c = t.shape[-1]
            return t.reshape(b, dil, N_HEADS, n, c).transpose(0, 3, 1, 2, 4).reshape(b, S, N_HEADS, c)

        o, m, s = banded_attention(by_residue(q)[:, :, None], by_residue(k), by_residue(v),
                                   rel_table[:, :, None], window // dil, dil)
        parts.append((back(o), back(m), back(s)))
    return combine_by_denominators(parts).reshape(b, S, D_MODEL)


def nsa_mixer(h, w_in, cmp_pos, cmp_w1, cmp_w2, q_gain, k_gain, rel_table):
    b, S, _ = h.shape
    KH, G, dh = NSA_KV_HEADS, NSA_GROUP, HEAD_DIM
    scale = dh ** -0.5
    proj = h @ w_in
    splits = np.cumsum([D_MODEL] + [NSA_KV_DIM] * 6).tolist()
    q, k_c, v_c, k_s, v_s, k_w, v_w, gate_logits = jnp.split(proj, splits, axis=-1)
    q = rmsnorm(q.reshape(b, S, KH, G, dh), q_gain).transpose(0, 2, 3, 1, 4)

    def heads(t):
        return t.reshape(b, S, KH, dh).transpose(0, 2, 1, 3)

    k_c, v_c, k_s, v_s, k_w, v_w = map(heads, (k_c, v_c, k_s, v_s, k_w, v_w))
    k_s, k_w = rmsnorm(k_s, k_gain), rmsnorm(k_w, k_gain)
    gates = jax.nn.sigmoid(gate_logits.astype(jnp.float32)).reshape(b, S, KH, G, 3).transpose(0, 2, 3, 1, 4)
    bias_tab = rel_table.reshape(REL_BUCKETS, KH, G)
    qpos = jnp.arange(S)

    n_c = S // CMP_STRIDE - 1

    def compress(t, pos, w1, w2):
        chunks = t.reshape(b, KH, S // CMP_STRIDE, CMP_STRIDE, dh)
        blocks = jnp.concatenate([chunks[:, :, :-1], chunks[:, :, 1:]], axis=3) + pos
        hid = jax.nn.gelu(blocks.reshape(b, KH, n_c, CMP_BLOCK * dh) @ w1)
        return hid @ w2

    k_cmp = rmsnorm(compress(k_c, cmp_pos[0], cmp_w1[0], cmp_w2[0]), k_gain)
    v_cmp = compress(v_c, cmp_pos[1], cmp_w1[1], cmp_w2[1])
    c_start = jnp.arange(n_c) * CMP_STRIDE
    dist_c = qpos[:, None] - (c_start + CMP_BLOCK - 1)[None, :]
    bias_c = jnp.transpose(bias_tab[rel_bucket(dist_c)], (2, 3, 0, 1))
    logits_c = jnp.einsum("bhgsd,bhnd->bhgsn", q, k_cmp, preferred_element_type=jnp.float32) * scale + bias_c
    p_c, _, s_c = masked_stats(logits_c, dist_c >= 0)
    probs_c = p_c / jnp.maximum(s_c, TINY)
    o_cmp = jnp.einsum("bhgsn,bhnd->bhgsd", probs_c.astype(v_cmp.dtype), v_cmp, preferred_element_type=jnp.float32)

    n_sel = S // SEL_BLOCK
    s_start = jnp.arange(n_sel) * SEL_BLOCK
    overlap = ((c_start[:, None] < s_start[None, :] + SEL_BLOCK)
               & (c_start[:, None] + CMP_BLOCK > s_start[None, :])).astype(jnp.float32)
    imp = jnp.einsum("bhgsn,nj->bhsj", probs_c, overlap)
    own = (qpos // SEL_BLOCK)[:, None]
    jj = jnp.arange(n_sel)[None, :]
    imp = jnp.where(jj == own, POS_BIG, jnp.where(jj < own, imp, NEG_INF))
    k_top = min(SEL_TOPK, n_sel)
    top_val, top_idx = lax.top_k(imp, k_top)
    top_valid = top_val > 0.5 * NEG_INF
    kb = k_s.reshape(b, KH, n_sel, SEL_BLOCK, dh)
    vb = v_s.reshape(b, KH, n_sel, SEL_BLOCK, dh)
    b_i = jnp.arange(b)[:, None, None, None]
    h_i = jnp.arange(KH)[None, :, None, None]
    g_i = jnp.arange(G)[None, None, :, None, None]
    tab = jnp.transpose(bias_tab, (1, 2, 0))
    C = GATHER_CHUNK

    def sel_chunk(args):
        qc, idx, valid, pos = args
        kg = kb[b_i, h_i, idx].reshape(b, KH, C, k_top * SEL_BLOCK, dh)
        vg = vb[b_i, h_i, idx].reshape(b, KH, C, k_top * SEL_BLOCK, dh)
        kpos = (idx[..., None] * SEL_BLOCK + jnp.arange(SEL_BLOCK)).reshape(b, KH, C, k_top * SEL_BLOCK)
        dist = pos[:, None] - kpos
        mask = (jnp.repeat(valid, SEL_BLOCK, axis=-1) & (dist >= 0))[:, :, None]
        bias = tab[h_i[:, :, None], g_i, rel_bucket(dist)[:, :, None]]
        logits = jnp.einsum("bhgqd,bhqkd->bhgqk", qc, kg, preferred_element_type=jnp.float32) * scale + bias
        p, _, s = masked_stats(logits, mask)
        return jnp.einsum("bhgqk,bhqkd->bhgqd", p.astype(vg.dtype), vg,
                          preferred_element_type=jnp.float32) / jnp.maximum(s, TINY)

    o_sel = merge_chunks(lax.map(sel_chunk, (split_chunks(q, 3, C), split_chunks(top_idx, 2, C),
                                             split_chunks(top_valid, 2, C), split_chunks(qpos, 0, C))), 3)

    o_win, _, _ = banded_attention(q, k_w, v_w, bias_tab, NSA_WINDOW - 1, 1)

    o = gates[..., 0:1] * o_cmp + gates[..., 1:2] * o_sel + gates[..., 2:3] * o_win
    return o.transpose(0, 3, 1, 2, 4).reshape(b, S, D_MODEL)


def fox_mixer(h, w_in, b_f, q_gain, k_gain):
    b, S, _ = h.shape
    scale = HEAD_DIM ** -0.5
    proj = h @ w_in
    qkv = proj[..., :3 * D_MODEL].reshape(b, S, 3, N_HEADS, HEAD_DIM)
    q = rmsnorm(qkv[:, :, 0], q_gain).transpose(0, 2, 1, 3)
    k = rmsnorm(qkv[:, :, 1], k_gain).transpose(0, 2, 1, 3)
    v = qkv[:, :, 2].transpose(0, 2, 1, 3)
    log_f = jax.nn.log_sigmoid((proj[..., 3 * D_MODEL:] + b_f).astype(jnp.float32))
    c = jnp.cumsum(log_f, axis=1).transpose(0, 2, 1)
    kpos = jnp.arange(S)

    def q_block(args):
        qb, cb, pos = args
        logits = (jnp.einsum("bhqd,bhkd->bhqk", qb, k, preferred_element_type=jnp.float32) * scale
                  + (cb[..., :, None] - c[:, :, None, :]))
        p, _, s = masked_stats(logits, pos[:, None] >= kpos[None, :])
        return jnp.einsum("bhqk,bhkd->bhqd", p.astype(v.dtype), v,
                          preferred_element_type=jnp.float32) / jnp.maximum(s, TINY)

    o = merge_chunks(lax.map(q_block, (split_chunks(q, 2, Q_BLOCK), split_chunks(c, 2, Q_BLOCK),
                                       split_chunks(kpos, 0, Q_BLOCK))), 2)
    return o.transpose(0, 2, 1, 3).reshape(b, S, D_MODEL)


def moba_mixer(h, w_in, q_gain, k_gain, rel_table):
    b, S, _ = h.shape
    H, dh = N_HEADS, HEAD_DIM
    scale = dh ** -0.5
    qkv = (h @ w_in).reshape(b, S, 3, H, dh)
    q = rmsnorm(qkv[:, :, 0], q_gain).transpose(0, 2, 1, 3)
    k = rmsnorm(qkv[:, :, 1], k_gain).transpose(0, 2, 1, 3)
    v = qkv[:, :, 2].transpose(0, 2, 1, 3)
    nblk = -(-S // MOBA_BLOCK)
    pad = nblk * MOBA_BLOCK - S

    def to_blocks(t):
        return jnp.pad(t, ((0, 0), (0, 0), (0, pad), (0, 0))).reshape(b, H, nblk, MOBA_BLOCK, dh)

    qb, kb, vb = to_blocks(q), to_blocks(k), to_blocks(v)

    i = jnp.arange(MOBA_BLOCK)
    rel = i[:, None] - i[None, :]
    bias_own = jnp.transpose(rel_table[rel_bucket(rel)], (2, 0, 1))
    logits = jnp.einsum("bhnqd,bhnkd->bhnqk", qb, kb, preferred_element_type=jnp.float32) * scale + bias_own[:, None]
    p, m, s = masked_stats(logits, rel >= 0)
    o_own = jnp.einsum("bhnqk,bhnkd->bhnqd", p.astype(vb.dtype), vb,
                       preferred_element_type=jnp.float32) / jnp.maximum(s, TINY)

    def unblock(t):
        return t.reshape(b, H, nblk * MOBA_BLOCK, t.shape[-1])[:, :, :S]

    parts = [(unblock(o_own), unblock(m), unblock(s))]

    k_top = min(MOBA_TOPK, nblk - 1)
    if k_top > 0:
        k_mean = jnp.mean(kb, axis=3)
        gate = jnp.einsum("bhsd,bhnd->bhsn", q, k_mean, preferred_element_type=jnp.float32)
        qpos = jnp.arange(S)
        past = jnp.arange(nblk)[None, :] < (qpos // MOBA_BLOCK)[:, None]
        top_val, top_idx = lax.top_k(jnp.where(past, gate, NEG_INF), k_top)
        top_valid = top_val > 0.5 * NEG_INF
        b_i = jnp.arange(b)[:, None, None, None]
        h_i = jnp.arange(H)[None, :, None, None]
        tab = rel_table.T
        C = GATHER_CHUNK

        def sel_chunk(args):
            qc, idx, valid, pos = args
            kg = kb[b_i, h_i, idx].reshape(b, H, C, k_top * MOBA_BLOCK, dh)
            vg = vb[b_i, h_i, idx].reshape(b, H, C, k_top * MOBA_BLOCK, dh)
            kpos = (idx[..., None] * MOBA_BLOCK + jnp.arange(MOBA_BLOCK)).reshape(b, H, C, k_top * MOBA_BLOCK)
            bias = tab[h_i, rel_bucket(pos[:, None] - kpos)]
            mask = jnp.repeat(valid, MOBA_BLOCK, axis=-1)
            lg = jnp.einsum("bhqd,bhqkd->bhqk", qc, kg, preferred_element_type=jnp.float32) * scale + bias
            pp, mm, ss = masked_stats(lg, mask)
            oo = jnp.einsum("bhqk,bhqkd->bhqd", pp.astype(vg.dtype), vg,
                            preferred_element_type=jnp.float32) / jnp.maximum(ss, TINY)
            return oo, mm, ss

        o_sel, m_sel, s_sel = lax.map(sel_chunk, (split_chunks(q, 2, C), split_chunks(top_idx, 2, C),
                                                  split_chunks(top_valid, 2, C), split_chunks(qpos, 0, C)))
        parts.append((merge_chunks(o_sel, 2), merge_chunks(m_sel, 2), merge_chunks(s_sel, 2)))

    o = combine_by_denominators(parts)
    return o.transpose(0, 2, 1, 3).reshape(b, S, D_MODEL)


def setup_inputs(seed: int = 0) -> dict:
    key = jax.random.key(seed)
    ks = jax.random.split(key, 17)
    f32 = jnp.float32
    nrm = lambda k, shape, scale: jax.random.normal(k, shape, f32) * scale
    return {
        "x": jax.random.normal(ks[0], (BATCH, SEQ, D_MODEL), f32),
        "rel_table": nrm(ks[1], (REL_BUCKETS, N_HEADS), 0.5),
        "attn_norm": 1.0 + nrm(ks[2], (DEPTH, D_MODEL), 0.02),
        "mlp_norm": 1.0 + nrm(ks[3], (DEPTH, D_MODEL), 0.02),
        "q_gain": 1.0 + nrm(ks[4], (DEPTH, HEAD_DIM), 0.02),
        "k_gain": 1.0 + nrm(ks[5], (DEPTH, HEAD_DIM), 0.02),
        "w_out": nrm(ks[6], (DEPTH, D_MODEL, D_MODEL), D_MODEL ** -0.5),
        "mlp_w_up": nrm(ks[7], (DEPTH, D_MODEL, D_FF), D_MODEL ** -0.5),
        "mlp_w_down": nrm(ks[8], (DEPTH, D_FF, D_MODEL), D_FF ** -0.5),
        "dsa_w_in": nrm(ks[9], (N_REPEATS, D_MODEL, 3 * D_MODEL), D_MODEL ** -0.5),
        "nsa_w_in": nrm(ks[10], (N_REPEATS, D_MODEL, NSA_IN_DIM), D_MODEL ** -0.5),
        "nsa_cmp_pos": nrm(ks[11], (N_REPEATS, 2, CMP_BLOCK, HEAD_DIM), 0.1),
        "nsa_cmp_w1": nrm(ks[12], (N_REPEATS, 2, CMP_BLOCK * HEAD_DIM, CMP_HIDDEN), (CMP_BLOCK * HEAD_DIM) ** -0.5),
        "nsa_cmp_w2": nrm(ks[13], (N_REPEATS, 2, CMP_HIDDEN, HEAD_DIM), CMP_HIDDEN ** -0.5),
        "fox_w_in": nrm(ks[14], (N_REPEATS, D_MODEL, FOX_IN_DIM), D_MODEL ** -0.5),
        "fox_b_f": jax.random.uniform(ks[15], (N_REPEATS, N_HEADS), f32, 1.0, 6.0),
        "moba_w_in": nrm(ks[16], (N_REPEATS, D_MODEL, 3 * D_MODEL), D_MODEL ** -0.5),
    }


def reference(x, rel_table, attn_norm, mlp_norm, q_gain, k_gain, w_out, mlp_w_up, mlp_w_down,
              dsa_w_in, nsa_w_in, nsa_cmp_pos, nsa_cmp_w1, nsa_cmp_w2, fox_w_in, fox_b_f, moba_w_in):
    for layer in range(DEPTH):
        kind = layer % N_MIXERS
        r = layer // N_MIXERS
        h = rmsnorm(x, attn_norm[layer])
        if kind == 0:
            mixed = dilated_mixer(h, dsa_w_in[r], q_gain[layer], k_gain[layer], rel_table)
        elif kind == 1:
            mixed = nsa_mixer(h, nsa_w_in[r], nsa_cmp_pos[r], nsa_cmp_w1[r], nsa_cmp_w2[r],
                              q_gain[layer], k_gain[layer], rel_table)
        elif kind == 2:
            mixed = fox_mixer(h, fox_w_in[r], fox_b_f[r], q_gain[layer], k_gain[layer])
        else:
            mixed = moba_mixer(h, moba_w_in[r], q_gain[layer], k_gain[layer], rel_table)
        x = x + mixed.astype(x.dtype) @ w_out[layer]
        h = rmsnorm(x, mlp_norm[layer])
        x = x + jnp.square(jax.nn.relu(h @ mlp_w_up[layer])) @ mlp_w_down[layer]
    return x
```

```python
import math
import os
import contextlib
import numpy as np
import ml_dtypes
import concourse.bass as bass
import concourse.mybir as mybir
from concourse.bass_utils import run_bass_kernel_spmd

F32 = mybir.dt.float32
BF16 = mybir.dt.bfloat16
ALU = mybir.AluOpType
AF = mybir.ActivationFunctionType
AX = mybir.AxisListType

S = 4096
D = 1024
H = 16
DH = 64
NT = 32
DFF = 4096
NEG = -1.0e30
BIG = 30000.0
WU = 4480
WOFF = 384
GLEN = WU + 128
GCLEN = 4096 + 2032 + 16
NCOLS = {0: 3072, 1: 2608, 2: 3088, 3: 3072}

ENGS = ["pe", "act", "dve", "pool", "sp"]


class Prog:
    def __init__(self, nc, ndma=12):
        self.nc = nc
        self.streams = {e: [] for e in ENGS}
        self.cnt = {e: 0 for e in ENGS}
        self.sems = {}
        self.stack = contextlib.ExitStack()
        for e in ENGS:
            self.sems[e] = self.stack.enter_context(nc.semaphore("s_" + e))
        self.dsem = []
        for i in range(2 * ndma):
            self.dsem.append([self.stack.enter_context(nc.semaphore("d_%d" % i)), 0])
        self.ndma = ndma
        self.dnext = {"sp": 0, "pool": 0, "act": 0}
        self.waited = {e: {} for e in ENGS}
        self.lastw = {}
        self.readers = {}

    def _sem(self, key):
        return self.sems[key] if isinstance(key, str) else self.dsem[key[1]][0]

    def _wait(self, eng, tok):
        key, val, src = tok
        if src == "pe" and eng == "pe":
            return
        w = self.waited[eng]
        if w.get(key, 0) >= val:
            return
        w[key] = val
        self.streams[eng].append(("w", self._sem(key), val))

    def _deps(self, eng, reads, writes):
        for r in reads:
            t = self.lastw.get(r)
            if t is not None:
                self._wait(eng, t)
        for w in writes:
            t = self.lastw.get(w)
            if t is not None:
                self._wait(eng, t)
            for t in self.readers.get(w, ()):
                self._wait(eng, t)

    def _commit(self, tok, reads, writes):
        for w in writes:
            self.lastw[w] = tok
            self.readers[w] = []
        for r in reads:
            lst = self.readers.setdefault(r, [])
            lst.append(tok)
            if len(lst) > 24:
                best = {}
                for t in lst:
                    k = t[0]
                    if k not in best or best[k][1] < t[1]:
                        best[k] = t
                self.readers[r] = list(best.values())

    def op(self, eng, fn, reads=(), writes=(), crit=None):
        self._deps(eng, reads, writes)
        self.cnt[eng] += 1
        tok = (eng, self.cnt[eng], eng)
        cw = None
        if crit is not None:
            ct = self.lastw.get(crit)
            if ct is not None and ct[2] != eng:
                cw = (self._sem(ct[0]), ct[1])
        self.streams[eng].append(("o", fn, self.sems[eng], 1, cw))
        self._commit(tok, reads, writes)
        return tok

    def dma(self, q, out, in_, reads=(), writes=()):
        self._deps(q, reads, writes)
        base = self.ndma if q == "pool" else 0
        i = base + self.dnext[q]
        self.dnext[q] = (self.dnext[q] + 1) % self.ndma
        if self.dsem[i][1] > 0:
            self._wait(q, (("d", i), self.dsem[i][1], "dma"))
        self.dsem[i][1] += 16
        tok = (("d", i), self.dsem[i][1], "dma")
        self.streams[q].append(("o", lambda e, o=out, s=in_: e.dma_start(out=o, in_=s), self.dsem[i][0], 16))
        self._commit(tok, reads, writes)
        return tok

    def barrier(self):
        for e in ENGS:
            for e2 in ENGS:
                if e2 != e and self.cnt[e2] > 0:
                    self._wait(e, (e2, self.cnt[e2], e2))
            for i, (s, v) in enumerate(self.dsem):
                if v > 0:
                    self._wait(e, (("d", i), v, "dma"))
        self.lastw = {}
        self.readers = {}

    def emit(self):
        nc = self.nc
        streams = self.streams

        def replay(eng, items):
            for it in items:
                if it[0] == "w":
                    eng.wait_ge(it[1], it[2])
                else:
                    ins = it[1](eng)
                    if len(it) > 4 and it[4] is not None:
                        ins = ins._wait_ge(it[4][0], it[4][1])
                    ins.then_inc(it[2], it[3])

        with nc.Block() as block:
            @block.tensor
            def _(e):
                replay(e, streams["pe"])

            @block.scalar
            def _(e):
                replay(e, streams["act"])

            @block.vector
            def _(e):
                replay(e, streams["dve"])

            @block.gpsimd
            def _(e):
                replay(e, streams["pool"])

            @block.sync
            def _(e):
                replay(e, streams["sp"])


def bcast_rows(t, off, n):
    return bass.AP(t, off, [[0, 128], [1, n]])


def build_program(layers, stop=None):
    nc = bass.Bass("TRN2", target_bir_lowering=False)
    es = contextlib.ExitStack()

    def din(name, shape, dt=F32):
        return nc.dram_tensor(name, list(shape), dt, kind="ExternalInput")

    t_x = din("x", [S, D])
    t_an = din("attn_norm", [4, D]); t_mn = din("mlp_norm", [4, D])
    t_qg = din("q_gain", [4, DH]); t_kg = din("k_gain", [4, DH])
    t_wout = din("w_out", [4, D, D]); t_wup = din("mlp_w_up", [4, D, DFF]); t_wdn = din("mlp_w_down", [4, DFF, D])
    t_win = {0: din("dsa_w_in", [D, 3072]), 1: din("nsa_w_in", [D, 2608]),
             2: din("fox_w_in", [D, 3088]), 3: din("moba_w_in", [D, 3072])}
    t_posT = din("nsa_posT", [2, DH, 32])
    t_w1 = din("nsa_cmp_w1", [2, 2048, 256]); t_w2 = din("nsa_cmp_w2", [2, 256, DH])
    t_bf = din("fox_b_f", [1, H])
    t_r31 = din("rel31", [1, H])
    t_gB = din("gB", [H, GLEN]); t_gBw = din("gBw", [H, GLEN]); t_gBd = din("gBd", [H, GLEN])
    t_gBc = din("gBc", [H, GCLEN]); t_g0 = din("g0", [1, GLEN]); t_lnc = din("lnc", [1, GLEN])
    t_cf = din("cf32", [128, 4, 128]); t_idb = din("identb", [128, 128], BF16)
    t_E16 = din("E16", [16, S], BF16); t_E64 = din("E64", [64, S], BF16)
    t_OV = din("OV", [128, 2, 64], BF16); t_ST = din("ST", [128, 128])
    t_y = nc.dram_tensor("y", [S, D], F32, kind="ExternalOutput")
    t_X = [nc.dram_tensor("xs%d" % i, [S, D], F32, kind="Internal") for i in range(2)]
    t_QK = nc.dram_tensor("qk", [32, 128, S], BF16, kind="Internal")
    t_V = nc.dram_tensor("vv", [16, S, DH], BF16, kind="Internal")
    t_MIX = nc.dram_tensor("mix", [S, D], BF16, kind="Internal")
    t_AUG = nc.dram_tensor("aug", [192, S], BF16, kind="Internal")

    def SB(name, shape, dt):
        return es.enter_context(nc.sbuf_tensor("sb_" + name, list(shape), dt))

    banks = [es.enter_context(nc.psum_tensor("bank%d" % i, [128, 512], F32)) for i in range(8)]
    AA = SB("arenaA", [128, 32768], BF16)
    AB = SB("arenaB", [128, 32768], BF16)
    WO = SB("wout", [128, 8, 1024], BF16)
    identb = SB("identb", [128, 128], BF16)
    cf = SB("cf", [128, 4, 128], F32)
    ones_b = SB("ones_b", [128, 128], BF16)
    gbc_a = SB("gbc_a", [128, D], F32); gbc_m = SB("gbc_m", [128, D], F32)
    gq = SB("gq", [128, DH], F32); gk = SB("gk", [128, DH], F32); gqk = SB("gqk", [128, DH], F32)
    small = SB("small", [128, 64], F32)
    b31 = SB("b31", [128, H], F32)
    GA = SB("genA", [128, 21760], BF16)

    P = Prog(nc)
    gates = AB[:, 0:2 * NT * 48].bitcast(F32).rearrange("p (t c) -> p t c", c=48)

    def bf(ap):
        return ap

    def f32v(arena, off_b, n):
        return arena[:, off_b // 2: off_b // 2 + 2 * n].bitcast(F32)

    def b16v(arena, off_b, n):
        return arena[:, off_b // 2: off_b // 2 + n]

    P.dma("sp", identb[:], t_idb.ap()[:, :], writes=["identb"])
    P.dma("sp", cf[:], t_cf.ap()[:, :, :], writes=["cf"])
    P.op("dve", lambda e: e.memset(ones_b[:], 1.0), writes=["ones_b"])
    P.dma("sp", b31[:], bcast_rows(t_r31, 0, H), writes=["b31"])
    P.op("dve", lambda e: e.memset(small[:], 0.0), writes=["ssq", "rstd", "ssq8", "rs", "ssq4"])

    def phase1(L, kind, t_src):
        ncol = NCOLS[kind]
        wv = AA[:, 0:8 * ncol].rearrange("p (k n) -> p k n", n=ncol)
        win = t_win[kind].ap()
        for kc in range(8):
            P.dma("pool", wv[:, kc, :], win[kc * 128:(kc + 1) * 128, :], writes=[("win", kc)])
        P.dma("pool", WO[:, :, :], t_wout.ap()[L].rearrange("(k p) n -> p k n", p=128), writes=["wo"])
        P.dma("sp", gbc_a[:], bcast_rows(t_an, L * D, D), writes=["gbc_a"])
        P.dma("sp", gbc_m[:], bcast_rows(t_mn, L * D, D), writes=["gbc_m"])
        P.dma("sp", gq[:], bcast_rows(t_qg, L * DH, DH), writes=["gq"])
        P.dma("sp", gk[:], bcast_rows(t_kg, L * DH, DH), writes=["gk"])
        P.op("dve", lambda e: e.tensor_tensor(out=gqk[:], in0=gq[:], in1=gk[:], op=ALU.mult),
             reads=["gq", "gk"], writes=["gqk"])

        xt = [f32v(GA, 0, 1024), f32v(GA, 4096, 1024)]
        junk = b16v(GA, 8192, 1024)
        hb = b16v(GA, 10240, 1024)
        hT = [b16v(GA, 12288, 1024), b16v(GA, 14336, 1024)]
        sq = f32v(GA, 16384, 512)
        tm = b16v(GA, 18432, 18 * 128)
        vm = b16v(GA, 23040, 1024)
        tst = b16v(GA, 25088, 18 * 512).rearrange("p (b n) -> p b n", n=512)
        ssq = small[:, 0:1]; rstd = small[:, 1:2]; ssq8 = small[:, 8:16]
        if kind == 2:
            bfb = SBv["bfb"]
            caq = SBv["caq"]; cak = SBv["cak"]; carry = SBv["carry"]; fz = SBv["fz"]
            P.dma("sp", bfb[:], bcast_rows(t_bf, 0, H), writes=["bfb"])
            P.op("dve", lambda e: e.memset(carry[:], 0.0), writes=["carry"])
            P.op("dve", lambda e: e.memset(caq[:], 1.0), writes=["caq"])
            P.op("dve", lambda e: e.memset(cak[:], 1.0), writes=["cak"])
            P.op("dve", lambda e: e.memset(tm[:, 2048:2304], 0.0), writes=["tm"])

        if kind == 1:
            chunks = [(0, 512, [(0, 512, "q", 0)]), (512, 512, [(0, 512, "q", 512)]),
                      (1024, 512, [(0, 256, "c", 1536), (256, 256, "c", 1792)]),
                      (1536, 512, [(0, 256, "k", 1024), (256, 256, "v", 0)]),
                      (2048, 512, [(0, 256, "k", 1280), (256, 256, "v", 256)]),
                      (2560, 48, [(0, 48, "g", 0)])]
            nv = 8
        else:
            chunks = [(0, 512, [(0, 512, "q", 0)]), (512, 512, [(0, 512, "q", 512)]),
                      (1024, 512, [(0, 512, "k", 1024)]), (1536, 512, [(0, 512, "k", 1536)]),
                      (2048, 512, [(0, 512, "v", 0)]), (2560, 512, [(0, 512, "v", 512)])]
            if kind == 2:
                chunks.append((3072, 16, [(0, 16, "f", 0)]))
            nv = 16
        nblk = 18 if kind == 2 else 16
        xsrc = t_src.ap()
        cb = 0
        for t in range(NT):
            tt = t % 4
            g = t // 4
            s = t % 2
            P.dma("sp", xt[s], xsrc[t * 128:(t + 1) * 128, :], writes=[("xt", s)])
            P.op("act", lambda e, s=s: e.activation(out=junk, in_=xt[s], func=AF.Square, accum_out=ssq),
                 reads=[("xt", s)], writes=["junk", "ssq"])
            P.op("dve", lambda e: e.tensor_scalar(out=rstd, in0=ssq, scalar1=1.0 / D, scalar2=1e-6,
                                                   op0=ALU.mult, op1=ALU.add), reads=["ssq"], writes=["rstd"])
            P.op("act", lambda e: e.activation(out=rstd, in_=rstd, func=AF.Sqrt), reads=["rstd"], writes=["rstd"])
            P.op("dve", lambda e: e.reciprocal(out=rstd, in_=rstd), reads=["rstd"], writes=["rstd"])
            P.op("dve", lambda e, s=s: e.scalar_tensor_tensor(out=hb, in0=xt[s], scalar=rstd, in1=gbc_a[:],
                                                              op0=ALU.mult, op1=ALU.mult),
                 reads=[("xt", s), "rstd", "gbc_a"], writes=["hb"])
            pT = banks[3][:].bitcast(BF16)
            for kc in range(8):
                P.op("pe", lambda e, kc=kc: e.transpose(out=pT[:, kc * 128:(kc + 1) * 128],
                                                         in_=hb[:, kc * 128:(kc + 1) * 128], identity=identb[:]),
                     reads=["hb", "identb"], writes=["b3"])
            P.op("act", lambda e, s=s: e.copy(out=hT[s], in_=pT), reads=["b3"], writes=[("hT", s)])
            hTs = hT[s].rearrange("p (k n) -> p k n", n=128)
            for (c0, n, segs) in chunks:
                bk = banks[cb % 3]; bkey = ("pb", cb % 3); cb += 1
                for kc in range(8):
                    P.op("pe", lambda e, kc=kc, bk=bk, c0=c0, n=n, hTs=hTs: e.matmul(
                        out=bk[:, 0:n], lhsT=hTs[:, kc, :], rhs=wv[:, kc, c0:c0 + n], start=(kc == 0), stop=(kc == 7)),
                        reads=[("hT", s), ("win", kc)], writes=[bkey], crit=("hT", s))
                need_norm = any(sg[2] in ("q", "k") for sg in segs)
                if need_norm:
                    nh = n // 64
                    P.op("act", lambda e, bk=bk, n=n: e.activation(out=sq[:, 0:n], in_=bk[:, 0:n], func=AF.Square),
                         reads=[bkey], writes=["sq"])
                    P.op("dve", lambda e, n=n, nh=nh: e.tensor_reduce(
                        out=ssq8[:, 0:nh], in_=sq[:, 0:n].rearrange("p (a b) -> p a b", b=64), axis=AX.X, op=ALU.add),
                        reads=["sq"], writes=["ssq8"])
                    P.op("dve", lambda e, nh=nh: e.tensor_scalar(out=ssq8[:, 0:nh], in0=ssq8[:, 0:nh], scalar1=1.0 / DH,
                                                                  scalar2=1e-6, op0=ALU.mult, op1=ALU.add),
                         reads=["ssq8"], writes=["ssq8"])
                    P.op("act", lambda e, nh=nh: e.activation(out=ssq8[:, 0:nh], in_=ssq8[:, 0:nh], func=AF.Sqrt),
                         reads=["ssq8"], writes=["ssq8"])
                    P.op("dve", lambda e, nh=nh: e.reciprocal(out=ssq8[:, 0:nh], in_=ssq8[:, 0:nh]),
                         reads=["ssq8"], writes=["ssq8"])
                for (so, sn, typ, dc) in segs:
                    if typ in ("q", "k"):
                        h0 = so // 64; nh = sn // 64
                        dst = tm[:, dc:dc + sn].rearrange("p (a b) -> p a b", b=64)
                        if typ == "k":
                            P.op("dve", lambda e, bk=bk, so=so, sn=sn, h0=h0, nh=nh, dst=dst: e.tensor_tensor(
                                out=dst, in0=bk[:, so:so + sn].rearrange("p (a b) -> p a b", b=64),
                                in1=ssq8[:, h0:h0 + nh].unsqueeze(2).to_broadcast([128, nh, 64]), op=ALU.mult),
                                reads=[bkey, "ssq8"], writes=["tm"])
                        else:
                            sqv = sq[:, so:so + sn].rearrange("p (a b) -> p a b", b=64)
                            P.op("dve", lambda e, bk=bk, so=so, sn=sn, h0=h0, nh=nh, sqv=sqv: e.tensor_tensor(
                                out=sqv, in0=bk[:, so:so + sn].rearrange("p (a b) -> p a b", b=64),
                                in1=ssq8[:, h0:h0 + nh].unsqueeze(2).to_broadcast([128, nh, 64]), op=ALU.mult),
                                reads=[bkey, "ssq8", "sq"], writes=["sq"])
                            P.op("dve", lambda e, nh=nh, sqv=sqv, dst=dst: e.tensor_tensor(
                                out=dst, in0=sqv, in1=gqk[:].unsqueeze(1).to_broadcast([128, nh, 64]), op=ALU.mult),
                                reads=["sq", "gqk"], writes=["tm"])
                    elif typ == "c":
                        P.op("act", lambda e, bk=bk, so=so, sn=sn, dc=dc: e.copy(out=tm[:, dc:dc + sn], in_=bk[:, so:so + sn]),
                             reads=[bkey], writes=["tm"])
                    elif typ == "v":
                        P.op("act", lambda e, bk=bk, so=so, sn=sn, dc=dc: e.copy(out=vm[:, dc:dc + sn], in_=bk[:, so:so + sn]),
                             reads=[bkey], writes=["vm"])
                    elif typ == "g":
                        P.op("act", lambda e, bk=bk, t=t: e.activation(out=gates[:, t, :], in_=bk[:, 0:48], func=AF.Sigmoid),
                             reads=[bkey], writes=["gates"])
                    elif typ == "f" and os.environ.get("KF_SKIP") == "1":
                        pass
                    elif typ == "f":
                        P.op("dve", lambda e, bk=bk: e.tensor_tensor(out=fz[:, 0:16], in0=bk[:, 0:16], in1=bfb[:], op=ALU.add),
                             reads=[bkey, "bfb"], writes=["fz"])
                        P.op("act", lambda e: e.activation(out=fz[:, 0:16], in_=fz[:, 0:16], func=AF.Exp, scale=-1.0),
                             reads=["fz"], writes=["fz"])
                        P.op("act", lambda e: e.activation(out=fz[:, 0:16], in_=fz[:, 0:16], func=AF.Ln, bias=1.0),
                             reads=["fz"], writes=["fz"])
                        b7 = banks[7]
                        P.op("pe", lambda e: e.matmul(out=b7[:, 0:16], lhsT=cf[:, 2, :], rhs=fz[:, 0:16], start=True, stop=True),
                             reads=["fz", "cf"], writes=["b7"])
                        P.op("pe", lambda e: e.matmul(out=b7[:, 16:32], lhsT=cf[:, 3, :], rhs=fz[:, 0:16], start=True, stop=True),
                             reads=["fz", "cf"], writes=["b7"])
                        P.op("dve", lambda e: e.tensor_tensor(out=fz[:, 16:32], in0=b7[:, 0:16], in1=carry[:], op=ALU.add),
                             reads=["b7", "carry", "fz"], writes=["fz2"])
                        P.op("dve", lambda e: e.tensor_tensor(out=carry[:], in0=b7[:, 16:32], in1=carry[:], op=ALU.add),
                             reads=["b7", "carry", "fz2"], writes=["carry"])
                        P.op("dve", lambda e: e.tensor_scalar(out=fz[:, 16:32], in0=fz[:, 16:32], scalar1=-8.0, scalar2=None,
                                                               op0=ALU.mult), reads=["fz2"], writes=["fz2"])
                        caqv = caq[:].rearrange("p (r h) -> p r h", h=16)
                        cakv = cak[:].rearrange("p (r h) -> p r h", h=16)
                        cur = fz[:, 16:32]; nxt = fz[:, 32:48]
                        for r in range(3):
                            P.op("dve", lambda e, r=r, cur=cur: e.tensor_copy(out=caqv[:, r, :], in_=cur),
                                 reads=["fz2"], writes=["caq"])
                            P.op("dve", lambda e, r=r: e.tensor_scalar(out=cakv[:, 3 + r, :], in0=caqv[:, r, :], scalar1=-1.0,
                                                                        scalar2=None, op0=ALU.mult),
                                 reads=["caq"], writes=["cak"])
                            if r < 2:
                                P.op("dve", lambda e, r=r, cur=cur, nxt=nxt: e.tensor_tensor(out=nxt, in0=cur, in1=caqv[:, r, :],
                                                                                             op=ALU.subtract),
                                     reads=["fz2", "caq"], writes=["fz2"])
                                cur, nxt = nxt, cur
                        P.op("dve", lambda e: e.tensor_copy(out=tm[:, 2048:2144], in_=caq[:]), reads=["caq"], writes=["tm"])
                        P.op("dve", lambda e: e.tensor_copy(out=tm[:, 2176:2272], in_=cak[:]), reads=["cak"], writes=["tm"])
            for half in range((nblk + 7) // 8):
                pb = banks[4 + half][:].bitcast(BF16)
                nb = min(8, nblk - half * 8)
                for b in range(nb):
                    blk = half * 8 + b
                    P.op("pe", lambda e, pb=pb, b=b, blk=blk: e.transpose(out=pb[:, b * 128:(b + 1) * 128],
                                                                           in_=tm[:, blk * 128:(blk + 1) * 128], identity=identb[:]),
                         reads=["tm", "identb"], writes=[("b4", half)])
                dstv = tst[:, half * 8:half * 8 + nb, tt * 128:(tt + 1) * 128]
                srcv = pb[:, 0:nb * 128].rearrange("p (b n) -> p b n", n=128)
                eng = "act" if half == 0 else "dve"
                if eng == "act":
                    P.op("act", lambda e, dstv=dstv, srcv=srcv: e.copy(out=dstv, in_=srcv),
                         reads=[("b4", half)], writes=[("tst", half)])
                else:
                    P.op("dve", lambda e, dstv=dstv, srcv=srcv: e.tensor_copy(out=dstv, in_=srcv),
                         reads=[("b4", half)], writes=[("tst", half)])
            vdst = bass.AP(t_V, t * 128 * DH, [[DH, 128], [S * DH, nv], [1, DH]])
            P.dma("pool", vdst, vm[:, 0:nv * 64].rearrange("p (u d) -> p u d", d=64), reads=["vm"])
            if tt == 3:
                for half in range(2):
                    qdst = bass.AP(t_QK, half * 128 * S + g * 512, [[S, 64], [2 * 128 * S, 16], [1, 512]])
                    P.dma("pool", qdst, tst[half * 64:(half + 1) * 64, 0:16, :], reads=[("tst", 0), ("tst", 1)])
                if kind == 2:
                    for qk in range(2):
                        adst = bass.AP(t_AUG, qk * 96 * S + g * 512, [[S, 96], [1, 512]])
                        P.dma("pool", adst, tst[0:96, 16 + qk, :], reads=[("tst", 2)])
        P.barrier()

    def phase3(L, t_src, t_dst):
        wu = AA[:, :].rearrange("p (k n) -> p k n", n=DFF)
        wup = t_wup.ap()[L].rearrange("(k p) n -> p k n", p=128)
        for kc in range(8):
            P.dma("pool", wu[:, kc, :], wup[:, kc, :], writes=[("wup", kc)])
        wd = AB[:, :].rearrange("p (k n) -> p k n", n=1024)
        wdn = t_wdn.ap()[L].rearrange("(k p) n -> p k n", p=128)
        for q4 in range(4):
            P.dma("pool", wd[:, q4 * 8:(q4 + 1) * 8, :], wdn[:, q4 * 8:(q4 + 1) * 8, :], writes=[("wdn", q4)])
        ot = [b16v(GA, 0, 1024), b16v(GA, 2048, 1024)]
        xr = [f32v(GA, 4096, 1024), f32v(GA, 8192, 1024)]
        oT = b16v(GA, 12288, 1024)
        h2 = b16v(GA, 14336, 1024)
        h2T = b16v(GA, 16384, 1024)
        sq = f32v(GA, 18432, 512)
        uT = b16v(GA, 20480, 32 * 128).rearrange("p (f n) -> p f n", n=128)
        xo = [f32v(GA, 28672, 1024), f32v(GA, 32768, 1024)]
        junk = b16v(GA, 36864, 1024)
        ssq = small[:, 0:1]; rstd = small[:, 1:2]
        src = t_src.ap(); dst = t_dst.ap(); mix = t_MIX.ap()
        cb = 0
        for t in range(NT):
            s = t % 2
            P.dma("sp", ot[s], mix[t * 128:(t + 1) * 128, :], writes=[("ot", s)])
            P.dma("sp", xr[s], src[t * 128:(t + 1) * 128, :], writes=[("xr", s)])
            pT = banks[3][:].bitcast(BF16)
            for kc in range(8):
                P.op("pe", lambda e, kc=kc, s=s: e.transpose(out=pT[:, kc * 128:(kc + 1) * 128],
                                                              in_=ot[s][:, kc * 128:(kc + 1) * 128], identity=identb[:]),
                     reads=[("ot", s), "identb"], writes=["b3"])
            P.op("act", lambda e: e.copy(out=oT, in_=pT), reads=["b3"], writes=["oT"])
            oTv = oT.rearrange("p (k n) -> p k n", n=128)
            for c in range(2):
                bk = banks[cb % 3]; bkey = ("pb", cb % 3); cb += 1
                for kc in range(8):
                    P.op("pe", lambda e, kc=kc, bk=bk, c=c: e.matmul(out=bk[:, :], lhsT=oTv[:, kc, :], rhs=WO[:, kc, c * 512:(c + 1) * 512],
                                                                    start=(kc == 0), stop=(kc == 7)),
                         reads=["oT", "wo"], writes=[bkey], crit="oT")
                P.op("dve", lambda e, bk=bk, c=c, s=s: e.tensor_tensor(out=xr[s][:, c * 512:(c + 1) * 512], in0=bk[:, :],
                                                                      in1=xr[s][:, c * 512:(c + 1) * 512], op=ALU.add),
                     reads=[bkey, ("xr", s)], writes=[("xr", s)])
            P.op("act", lambda e, s=s: e.activation(out=junk, in_=xr[s], func=AF.Square, accum_out=ssq),
                 reads=[("xr", s)], writes=["junk", "ssq"])
            P.op("dve", lambda e: e.tensor_scalar(out=rstd, in0=ssq, scalar1=1.0 / D, scalar2=1e-6,
                                                   op0=ALU.mult, op1=ALU.add), reads=["ssq"], writes=["rstd"])
            P.op("act", lambda e: e.activation(out=rstd, in_=rstd, func=AF.Sqrt), reads=["rstd"], writes=["rstd"])
            P.op("dve", lambda e: e.reciprocal(out=rstd, in_=rstd), reads=["rstd"], writes=["rstd"])
            P.op("dve", lambda e, s=s: e.scalar_tensor_tensor(out=h2, in0=xr[s], scalar=rstd, in1=gbc_m[:],
                                                              op0=ALU.mult, op1=ALU.mult),
                 reads=[("xr", s), "rstd", "gbc_m"], writes=["h2"])
            pT2 = banks[4][:].bitcast(BF16)
            for kc in range(8):
                P.op("pe", lambda e, kc=kc: e.transpose(out=pT2[:, kc * 128:(kc + 1) * 128],
                                                         in_=h2[:, kc * 128:(kc + 1) * 128], identity=identb[:]),
                     reads=["h2", "identb"], writes=["b4"])
            P.op("act", lambda e: e.copy(out=h2T, in_=pT2), reads=["b4"], writes=["h2T"])
            h2Tv = h2T.rearrange("p (k n) -> p k n", n=128)
            for fq in range(8):
                bk = banks[cb % 3]; bkey = ("pb", cb % 3); cb += 1
                for fi in range(4):
                    fc = fq * 4 + fi
                    for kc in range(8):
                        P.op("pe", lambda e, kc=kc, bk=bk, fc=fc, fi=fi: e.matmul(
                            out=bk[:, fi * 128:(fi + 1) * 128], lhsT=wu[:, kc, fc * 128:(fc + 1) * 128], rhs=h2Tv[:, kc, :],
                            start=(kc == 0), stop=(kc == 7)), reads=["h2T", ("wup", kc)], writes=[bkey])
                P.op("act", lambda e, bk=bk: e.activation(out=sq, in_=bk[:, :], func=AF.Square), reads=[bkey], writes=["sq"])
                P.op("dve", lambda e, bk=bk, fq=fq: e.scalar_tensor_tensor(
                    out=uT[:, fq * 4:(fq + 1) * 4, :], in0=bk[:, :].rearrange("p (f n) -> p f n", n=128), scalar=0.0,
                    in1=sq.rearrange("p (f n) -> p f n", n=128), op0=ALU.is_gt, op1=ALU.mult),
                    reads=[bkey, "sq"], writes=[("uT", fq)])
            for c in range(2):
                bk = banks[cb % 3]; bkey = ("pb", cb % 3); cb += 1
                for fc in range(32):
                    P.op("pe", lambda e, fc=fc, bk=bk, c=c: e.matmul(out=bk[:, :], lhsT=uT[:, fc, :], rhs=wd[:, fc, c * 512:(c + 1) * 512],
                                                                    start=(fc == 0), stop=(fc == 31)),
                         reads=[("uT", fc // 4), ("wdn", fc // 8)], writes=[bkey], crit=("uT", fc // 4))
                P.op("dve", lambda e, bk=bk, c=c, s=s: e.tensor_tensor(out=xo[s][:, c * 512:(c + 1) * 512], in0=bk[:, :],
                                                                      in1=xr[s][:, c * 512:(c + 1) * 512], op=ALU.add),
                     reads=[bkey, ("xr", s)], writes=[("xo", s)])
            P.dma("sp", dst[t * 128:(t + 1) * 128, :], xo[s], reads=[("xo", s)])
        P.barrier()

    ka = b16v(AA, 0, S)
    qa = b16v(AA, 8192, S)
    Hk = f32v(AA, 16384, WU)
    va = b16v(GA, 0, NT * 66).rearrange("p (t c) -> p t c", c=66)
    Wst = f32v(GA, 4352, WU)
    lg = [f32v(GA, 22528 + i * 2048, 512) for i in range(3)] + [f32v(AB, 20480, 512)]
    pTb = [b16v(GA, 28672 + i * 1024, 512) for i in range(3)] + [b16v(AB, 22528, 512)]
    NR = 4
    lbank_idx = [0, 1, 2, 6]
    lbanks = [banks[i] for i in lbank_idx]
    ost = f32v(GA, 31744, NT * 64).rearrange("p (t d) -> p t d", d=64)
    rs_t = small[:, 2:3]

    def build_strip(gsrcs, step, width, key="W"):
        nchunk = (width + 511) // 512
        for gi, (tg, off) in enumerate(gsrcs):
            P.dma("sp", Hk[:, 0:width], bass.AP(tg, off, [[step, 128], [1, width]]), writes=["Hk"])
            for c in range(nchunk):
                n = min(512, width - c * 512)
                bk = banks[7]
                bkey = ("bk", 7)
                P.op("pe", lambda e, bk=bk, c=c, n=n: e.matmul(out=bk[:, 0:n], lhsT=cf[:, 1, :], rhs=Hk[:, c * 512:c * 512 + n],
                                                             start=True, stop=True), reads=["Hk", "cf"], writes=[bkey])
                if gi == 0:
                    P.op("act", lambda e, bk=bk, c=c, n=n: e.copy(out=Wst[:, c * 512:c * 512 + n], in_=bk[:, 0:n]),
                         reads=[bkey], writes=[key])
                else:
                    P.op("dve", lambda e, bk=bk, c=c, n=n: e.tensor_tensor(out=Wst[:, c * 512:c * 512 + n], in0=bk[:, 0:n],
                                                                          in1=Wst[:, c * 512:c * 512 + n], op=ALU.add),
                         reads=[bkey, key], writes=[key])

    oTs = [f32v(AB, 16384, 512), f32v(AB, 18432, 512)]
    gctr = [0]

    def attn_core(K, steps, nvc, ka_fn, va_fn, epilogue, kreads, vreads, qreads):
        gfirst = {}; glast = {}; gpar = {}
        for n, st in enumerate(steps):
            G = st[0]
            if G not in gfirst:
                gfirst[G] = n
                gpar[G] = gctr[0] % 2
                gctr[0] += 1
            glast[G] = n

        def emit_pv(n):
            G, j, woff, act = steps[n][0:4]
            r = n % NR
            par = gpar[G]
            otb = banks[3 + par]; okey = ("bk", 3 + par)
            P.op("pe", lambda e, otb=otb, r=r, j=j, n=n, G=G: e.matmul(
                out=otb[0:nvc, :], lhsT=va_fn(j), rhs=pTb[r][:, :], start=(gfirst[G] == n), stop=(glast[G] == n)),
                reads=[("pT", r)] + vreads, writes=[okey])
            if glast[G] == n:
                ots = oTs[par]
                P.op("act", lambda e, ots=ots, otb=otb: e.copy(out=ots[0:nvc, :], in_=otb[0:nvc, :]),
                     reads=[okey], writes=[("oTs", par)])
                tb = banks[5]; tkey = ("bk", 5)
                for t in range(4):
                    P.op("pe", lambda e, tb=tb, ots=ots, t=t: e.transpose(
                        out=tb[:, t * nvc:(t + 1) * nvc], in_=ots[0:nvc, t * 128:(t + 1) * 128], identity=cf[0:nvc, 0, 0:nvc]),
                        reads=[("oTs", par), "cf"], writes=[tkey], crit=("oTs", par))
                for t in range(4):
                    epilogue(4 * G + t, tb[:, t * nvc:(t + 1) * nvc], tkey)

        LAG = 2
        for n, st in enumerate(steps):
            G, j, woff, act = st[0:4]
            mode = st[4] if len(st) > 4 else "W"
            r = n % NR
            bk = lbanks[r]; bkey = ("bk", lbank_idx[r])
            P.op("pe", lambda e, bk=bk, j=j, G=G: e.matmul(out=bk[:, :], lhsT=ka_fn(j, K), rhs=qa[0:K, G * 512:(G + 1) * 512],
                                                          start=True, stop=True),
                 reads=kreads + qreads, writes=[bkey])
            if mode == "W":
                P.op("dve", lambda e, bk=bk, r=r, woff=woff: e.scalar_tensor_tensor(
                    out=lg[r], in0=bk[:, :], scalar=0.125, in1=Wst[:, woff:woff + 512], op0=ALU.mult, op1=ALU.add),
                    reads=[bkey, "W"], writes=[("lg", r)])
                P.op("act", lambda e, r=r: e.activation(out=pTb[r], in_=lg[r], func=AF.Exp),
                     reads=[("lg", r)], writes=[("pT", r)])
            elif mode == "plain":
                P.op("act", lambda e, r=r, bk=bk: e.activation(out=pTb[r], in_=bk[:, :], func=AF.Exp, scale=0.125),
                     reads=[bkey], writes=[("pT", r)])
            else:
                bap = mode[1]
                P.op("act", lambda e, r=r, bk=bk, bap=bap: e.activation(out=pTb[r], in_=bk[:, :], func=AF.Exp, bias=bap, scale=0.125),
                     reads=[bkey, "b31"], writes=[("pT", r)])
            if n >= LAG:
                emit_pv(n - LAG)
        for n in range(max(0, len(steps) - LAG), len(steps)):
            emit_pv(n)

    def std_steps(maxspan=None, plain_past=False, const_far=None):
        steps = []
        for G in range(8):
            j0 = 0 if maxspan is None else max(0, 4 * G - maxspan)
            for j in range(j0, 4 * G + 4):
                act = [i for i in range(4 * G, 4 * G + 4) if i >= j and (maxspan is None or i - j <= maxspan)]
                if act:
                    Dd = 512 * G - 128 * j
                    mode = "W"
                    if plain_past and Dd >= 128:
                        mode = "plain"
                    elif const_far is not None and Dd >= 1664:
                        mode = ("const", const_far)
                    steps.append((G, j, Dd + WOFF, act, mode))
        return steps

    def ep_plain(gate_col=None, accumulate=False):
        def ep(i, oa, okey):
            P.op("dve", lambda e, oa=oa: e.tensor_scalar(out=rs_t, in0=oa[:, 64:65], scalar1=1e-30, scalar2=None, op0=ALU.max),
                 reads=[okey], writes=["rs"])
            P.op("dve", lambda e: e.reciprocal(out=rs_t, in_=rs_t), reads=["rs"], writes=["rs"])
            if gate_col is not None:
                P.op("dve", lambda e, i=i: e.tensor_tensor(out=rs_t, in0=rs_t, in1=gates[:, i, gate_col:gate_col + 1], op=ALU.mult),
                     reads=["rs", "gates"], writes=["rs"])
            if accumulate:
                P.op("dve", lambda e, oa=oa, i=i: e.scalar_tensor_tensor(out=ost[:, i, :], in0=oa[:, 0:64], scalar=rs_t, in1=ost[:, i, :],
                                                                       op0=ALU.mult, op1=ALU.add),
                     reads=[okey, "rs", ("ost", i)], writes=[("ost", i)])
            else:
                P.op("dve", lambda e, oa=oa, i=i: e.tensor_scalar(out=ost[:, i, :], in0=oa[:, 0:64], scalar1=rs_t, scalar2=None, op0=ALU.mult),
                     reads=[okey, "rs"], writes=[("ost", i)])
        return ep

    def load_k(unit, K, extra=None):
        P.dma("sp", ka[0:64, :], t_QK.ap()[unit, 0:64, :], writes=["ka"])
        if extra is not None:
            tg, r0, nr = extra
            P.dma("sp", ka[64:64 + nr, :], tg.ap()[r0:r0 + nr, :], writes=["ka_hi"])

    def load_q(unit, extra=None):
        P.dma("sp", qa[0:64, :], t_QK.ap()[unit, 0:64, :], writes=["qa"])
        if extra is not None:
            tg, r0, nr = extra
            P.dma("sp", qa[64:64 + nr, :], tg.ap()[r0:r0 + nr, :], writes=["qa_hi"])

    def load_v(unit):
        P.dma("sp", va[:, :, 0:64], t_V.ap()[unit].rearrange("(t p) d -> p t d", p=128), writes=["va"])

    def store_o(h):
        dst = bass.AP(t_MIX, h * DH, [[D, 128], [128 * D, NT], [1, DH]])
        P.dma("pool", dst, ost[:, :, :], reads=[("ost", i) for i in range(NT)])

    ka_std = lambda j, K: ka[0:K, j * 128:(j + 1) * 128]
    va_std = lambda j: va[:, j, :]

    def phase2_common_init():
        P.op("dve", lambda e: e.memset(va[:, :, 64:66], 1.0), writes=["va_ones"])

    def phase2_fox():
        phase2_common_init()
        build_strip([(t_g0, 0)], 1, WU)
        steps = std_steps(plain_past=True)
        for h in range(H):
            load_k(16 + h, 70)
            P.dma("sp", ka[64:70, :], bass.AP(t_AUG, (96 + h) * S, [[16 * S, 6], [1, S]]), writes=["ka_hi"])
            load_q(h)
            P.dma("sp", qa[64:70, :], bass.AP(t_AUG, h * S, [[16 * S, 6], [1, S]]), writes=["qa_hi"])
            load_v(h)
            attn_core(70, steps, 66, ka_std, va_std, ep_plain(), ["ka", "ka_hi"], ["va", "va_ones"], ["qa", "qa_hi"])
            store_o(h)
        P.barrier()

    def phase2_dil():
        phase2_common_init()
        steps = std_steps(maxspan=16)
        for h in range(H):
            build_strip([(t_gBd, h * GLEN), (t_lnc, 0)], 1, 2048 + WOFF + 512 + 128)
            load_k(16 + h, 64); load_q(h); load_v(h)
            attn_core(64, steps, 66, ka_std, va_std, ep_plain(), ["ka"], ["va", "va_ones"], ["qa"])
            store_o(h)
        P.barrier()

    def phase2_moba():
        phase2_common_init()
        steps = std_steps()
        km32 = f32v(GA, 39936, 16)
        kmb = b16v(GA, 40064, 16)
        gm = f32v(GA, 40128, 16)
        mx8 = f32v(GA, 40192, 8)
        mb80 = f32v(GA, 40256, 80)
        P.op("dve", lambda e: e.memset(mb80, 0.0), writes=["mb80"])
        for h in range(H):
            steps = std_steps(const_far=b31[:, h:h + 1])
            build_strip([(t_gB, h * GLEN)], 1, WU)
            load_k(16 + h, 80, (t_E16, 0, 16)); load_q(h); load_v(h)
            P.op("dve", lambda e: e.tensor_reduce(out=km32[0:64, :], in_=ka[0:64, :].rearrange("p (n b) -> p n b", b=256),
                                                  axis=AX.X, op=ALU.add), reads=["ka"], writes=["km32"])
            P.op("dve", lambda e: e.tensor_scalar(out=kmb[0:64, :], in0=km32[0:64, :], scalar1=1.0 / 256, scalar2=None, op0=ALU.mult),
                 reads=["km32"], writes=["kmb"])
            for i in range(NT):
                own = i // 2
                b6 = banks[2]; b7 = banks[7]
                P.op("dve", lambda e: e.memset(gm, NEG), writes=["gm"])
                if own > 0:
                    P.op("pe", lambda e, i=i: e.matmul(out=b6[:, 0:16], lhsT=qa[0:64, i * 128:(i + 1) * 128], rhs=kmb[0:64, :],
                                                       start=True, stop=True), reads=["qa", "kmb"], writes=[("bk", 2)])
                    P.op("dve", lambda e, own=own: e.tensor_copy(out=gm[:, 0:own], in_=b6[:, 0:own]), reads=[("bk", 2), "gm"], writes=["gm"])
                if own > 3:
                    P.op("dve", lambda e: e.max(out=mx8, in_=gm), reads=["gm"], writes=["mx8"])
                    P.op("dve", lambda e: e.tensor_scalar(out=mb80[:, 64:80], in0=gm, scalar1=mx8[:, 2:3], scalar2=None, op0=ALU.is_ge),
                         reads=["gm", "mx8"], writes=["mb80"])
                else:
                    P.op("dve", lambda e: e.tensor_scalar(out=mb80[:, 64:80], in0=gm, scalar1=-1.0e29, scalar2=None, op0=ALU.is_ge),
                         reads=["gm"], writes=["mb80"])
                P.op("dve", lambda e: e.tensor_scalar(out=mb80[:, 64:80], in0=mb80[:, 64:80], scalar1=-1.0, scalar2=BIG,
                                                       op0=ALU.add, op1=ALU.mult), reads=["mb80"], writes=["mb80"])
                P.op("dve", lambda e, own=own: e.memset(mb80[:, 64 + own:65 + own], 0.0), reads=["mb80"], writes=["mb80"])
                P.op("pe", lambda e: e.transpose(out=b7[0:80, 0:128], in_=mb80, identity=cf[:, 0, :]), reads=["mb80", "cf"], writes=[("bk", 7)])
                P.op("act", lambda e, i=i: e.copy(out=qa[64:80, i * 128:(i + 1) * 128], in_=b7[64:80, 0:128]),
                     reads=[("bk", 7)], writes=["qa_hi"])
            attn_core(80, steps, 66, ka_std, va_std, ep_plain(), ["ka", "ka_hi"], ["va", "va_ones"], ["qa", "qa_hi"])
            store_o(h)
        P.barrier()

    def phase2_nsa():
        phase2_common_init()
        kcR = b16v(AA, 16384, S)
        kcA = b16v(AA, 24576, S)
        kcB = b16v(AA, 34304, S)
        w1 = b16v(AA, 42496, 32 * 256).rearrange("p (t c) -> p t c", c=256)
        w2 = b16v(AA, 58880, 128).rearrange("p (m d) -> p m d", d=64)
        hid = b16v(AA, 59392, 512).rearrange("p (m n) -> p m n", n=256)
        kcn = b16v(AA, 60416, 256).rearrange("p (t d) -> p t d", d=128)
        kcmpT = b16v(AA, 60928, 256)
        vcmp = b16v(AA, 61440, 2 * 66).rearrange("p (t c) -> p t c", c=66)
        vcmpA = b16v(AA, 61952, 2 * 66).rearrange("p (t c) -> p t c", c=66)
        posT = f32v(AA, 62464, 32)
        impb = SBv["imp"]
        STt = SBv["ST"]
        OVt = SBv["OV"]
        impm = f32v(GA, 39936, 64)
        mx16 = f32v(GA, 40192, 16)
        mb128 = f32v(GA, 40256, 128)
        x2 = f32v(GA, 40768, 256)
        tg = f32v(GA, 41792, 256)
        P.dma("sp", STt[:], t_ST.ap()[:, :], writes=["ST"])
        P.dma("sp", OVt[:], t_OV.ap()[:, :, :], writes=["OV"])
        P.op("dve", lambda e: e.memset(mb128, 0.0), writes=["mb128"])
        stepsel = std_steps()
        stepwin = std_steps(maxspan=4)
        stepcmp = []
        for G in range(8):
            for ntl in range(2):
                if G >= 4 * ntl:
                    stepcmp.append((G, ntl, 512 * G - 2048 * ntl, list(range(4 * G, 4 * G + 4))))
        kc_fn = lambda j, K: kcmpT[0:64, j * 128:(j + 1) * 128]
        vc_fn = lambda j: vcmp[:, j, :]
        vcA_fn = lambda j: vcmpA[:, j, :]
        for kh in range(4):
            P.op("dve", lambda e: e.memset(kcn, 0.0), writes=["kcn"])
            P.op("dve", lambda e: e.memset(vcmp, 0.0), writes=["vcmp"])
            P.op("dve", lambda e: e.memset(vcmp[:, :, 64:66], 1.0), reads=["vcmp"], writes=["vcmp"])
            P.op("dve", lambda e: e.memset(vcmpA[:, :, 64:66], 1.0), writes=["vcmpA"])
            P.op("dve", lambda e: e.tensor_copy(out=vcmpA[:, :, 0:64], in_=OVt[:]), reads=["vcmpA", "OV"], writes=["vcmpA"])
            for kv in range(2):
                P.dma("sp", kcR[0:64, :], t_QK.ap()[24 + 4 * kv + kh, 0:64, :], writes=["Hk"])
                P.dma("sp", posT[0:64, :], t_posT.ap()[kv], writes=["posT"])
                P.dma("pool", w1[0:64, :, :], t_w1.ap()[kv].rearrange("(t d) c -> d t c", d=64), writes=["w1"])
                P.dma("pool", w2[:, :, :], t_w2.ap()[kv].rearrange("(m p) d -> p m d", p=128), writes=["w2"])
                for ab, dstb in ((0, kcA), (1, kcB)):
                    P.op("dve", lambda e, ab=ab, dstb=dstb: e.tensor_tensor(
                        out=dstb[0:64, :].rearrange("p (n s) -> p n s", s=16), in0=kcR[0:64, :].rearrange("p (n s) -> p n s", s=16),
                        in1=posT[0:64, ab * 16:(ab + 1) * 16].unsqueeze(1).to_broadcast([64, 256, 16]), op=ALU.add),
                        reads=["Hk", "posT"], writes=[("kcAB", ab)] if ab == 1 else ["Hk"])
                kcAv = kcA[0:64, :].rearrange("p (n s) -> p n s", s=16)
                kcBv = kcB[0:64, :].rearrange("p (n s) -> p n s", s=16)
                for mc in range(2):
                    bk = banks[7]; bkey = ("bk", 7)
                    for t in range(32):
                        rhs = kcAv[:, 0:255, t] if t < 16 else kcBv[:, 1:256, t - 16]
                        P.op("pe", lambda e, bk=bk, t=t, mc=mc, rhs=rhs: e.matmul(out=bk[:, 0:255], lhsT=w1[0:64, t, mc * 128:(mc + 1) * 128],
                                                                              rhs=rhs, start=(t == 0), stop=(t == 31)),
                             reads=["Hk", ("kcAB", 1), "w1"], writes=[bkey])
                    P.op("act", lambda e, bk=bk: e.activation(out=x2[:, 0:255], in_=bk[:, 0:255], func=AF.Square), reads=[bkey], writes=["x2"])
                    P.op("dve", lambda e: e.tensor_scalar(out=x2[:, 0:255], in0=x2[:, 0:255], scalar1=0.044715, scalar2=1.0,
                                                           op0=ALU.mult, op1=ALU.add), reads=["x2"], writes=["x2"])
                    P.op("dve", lambda e, bk=bk: e.tensor_tensor(out=tg[:, 0:255], in0=bk[:, 0:255], in1=x2[:, 0:255], op=ALU.mult),
                         reads=[bkey, "x2"], writes=["tg"])
                    P.op("act", lambda e: e.activation(out=tg[:, 0:255], in_=tg[:, 0:255], func=AF.Sigmoid, scale=1.5957691216),
                         reads=["tg"], writes=["tg"])
                    P.op("dve", lambda e, bk=bk, mc=mc: e.tensor_tensor(out=hid[:, mc, 0:255], in0=bk[:, 0:255], in1=tg[:, 0:255], op=ALU.mult),
                         reads=[bkey, "tg"], writes=["hid"])
                for ntl in range(2):
                    nn = 128 if ntl == 0 else 127
                    b6 = banks[2]
                    for mc in range(2):
                        P.op("pe", lambda e, ntl=ntl, nn=nn, mc=mc: e.matmul(out=b6[0:nn, 0:64], lhsT=hid[:, mc, ntl * 128:ntl * 128 + nn],
                                                                         rhs=w2[:, mc, :], start=(mc == 0), stop=(mc == 1)),
                             reads=["hid", "w2"], writes=[("bk", 2)])
                    if kv == 0:
                        ssq = small[:, 4:5]
                        P.op("act", lambda e, nn=nn: e.activation(out=x2[0:nn, 0:64], in_=b6[0:nn, 0:64], func=AF.Square, accum_out=ssq[0:nn, :]),
                             reads=[("bk", 2)], writes=["x2", "ssq4"])
                        P.op("dve", lambda e: e.tensor_scalar(out=ssq, in0=ssq, scalar1=1.0 / DH, scalar2=1e-6, op0=ALU.mult, op1=ALU.add),
                             reads=["ssq4"], writes=["ssq4"])
                        P.op("act", lambda e: e.activation(out=ssq, in_=ssq, func=AF.Sqrt), reads=["ssq4"], writes=["ssq4"])
                        P.op("dve", lambda e: e.reciprocal(out=ssq, in_=ssq), reads=["ssq4"], writes=["ssq4"])
                        P.op("dve", lambda e, nn=nn, ntl=ntl: e.tensor_scalar(out=kcn[0:nn, ntl, 0:64], in0=b6[0:nn, 0:64], scalar1=ssq[0:nn, :],
                                                                           scalar2=None, op0=ALU.mult), reads=[("bk", 2), "ssq4"], writes=["kcn"])
                        b7b = banks[7][:].bitcast(BF16)
                        P.op("pe", lambda e, ntl=ntl: e.transpose(out=b7b[:, 0:128], in_=kcn[:, ntl, :], identity=identb[:]),
                             reads=["kcn", "identb"], writes=[("bk", 7)])
                        P.op("act", lambda e, ntl=ntl: e.copy(out=kcmpT[0:64, ntl * 128:(ntl + 1) * 128], in_=b7b[0:64, 0:128]),
                             reads=[("bk", 7)], writes=["kcmpT"])
                    else:
                        P.op("act", lambda e, nn=nn, ntl=ntl: e.copy(out=vcmp[0:nn, ntl, 0:64], in_=b6[0:nn, 0:64]),
                             reads=[("bk", 2)], writes=["vcmp"])
            if os.environ.get("KNSA") == "1":
                break
            P.op("dve", lambda e: e.memset(impb, 0.0), writes=["imp"])

            def epA(i, oa, okey):
                P.op("dve", lambda e, oa=oa: e.tensor_scalar(out=rs_t, in0=oa[:, 64:65], scalar1=1e-30, scalar2=None, op0=ALU.max),
                     reads=[okey], writes=["rs"])
                P.op("dve", lambda e: e.reciprocal(out=rs_t, in_=rs_t), reads=["rs"], writes=["rs"])
                P.op("dve", lambda e, oa=oa, i=i: e.scalar_tensor_tensor(out=impb[:, i, :], in0=oa[:, 0:64], scalar=rs_t, in1=impb[:, i, :],
                                                                       op0=ALU.mult, op1=ALU.add), reads=[okey, "rs", "imp"], writes=["imp"])
            for g in range(4):
                u = kh * 4 + g
                build_strip([(t_gBc, u * GCLEN)], 1 if os.environ.get("KSTEP1") == "1" else 16, 4096)
                load_q(u)
                attn_core(64, stepcmp, 66, kc_fn, vcA_fn, epA, ["kcmpT"], ["vcmpA"], ["qa"])
            if os.environ.get("KNSA") == "2":
                break
            for i in range(NT):
                b7 = banks[7]
                P.op("dve", lambda e, i=i: e.tensor_tensor(out=impm, in0=impb[:, i, :], in1=STt[:, 64 - 2 * i:128 - 2 * i], op=ALU.add),
                     reads=["imp", "ST"], writes=["impm"])
                P.op("dve", lambda e: e.max(out=mx16[:, 0:8], in_=impm), reads=["impm"], writes=["mx16"])
                P.op("dve", lambda e: e.match_replace(out=mb128[:, 0:64], in_to_replace=mx16[:, 0:8], in_values=impm, imm_value=NEG),
                     reads=["impm", "mx16"], writes=["mb128"])
                P.op("dve", lambda e: e.max(out=mx16[:, 8:16], in_=mb128[:, 0:64]), reads=["mb128"], writes=["mx16"])
                P.op("dve", lambda e: e.tensor_scalar(out=mx16[:, 14:15], in0=mx16[:, 14:15], scalar1=-1.0e29, scalar2=None, op0=ALU.max),
                     reads=["mx16"], writes=["mx16"])
                P.op("dve", lambda e: e.tensor_scalar(out=mb128[:, 64:128], in0=impm, scalar1=mx16[:, 14:15], scalar2=None, op0=ALU.is_ge),
                     reads=["impm", "mx16", "mb128"], writes=["mb128"])
                P.op("dve", lambda e: e.tensor_scalar(out=mb128[:, 64:128], in0=mb128[:, 64:128], scalar1=-1.0, scalar2=BIG,
                                                       op0=ALU.add, op1=ALU.mult), reads=["mb128"], writes=["mb128"])
                P.op("dve", lambda e, i=i: e.memset(mb128[0:64, 64 + 2 * i:65 + 2 * i], 0.0), reads=["mb128"], writes=["mb128"])
                P.op("dve", lambda e, i=i: e.memset(mb128[64:128, 65 + 2 * i:66 + 2 * i], 0.0), reads=["mb128"], writes=["mb128"])
                P.op("pe", lambda e: e.transpose(out=b7[:, 0:128], in_=mb128, identity=cf[:, 0, :]), reads=["mb128", "cf"], writes=[("bk", 7)])
                P.op("act", lambda e, i=i: e.copy(out=qa[64:128, i * 128:(i + 1) * 128], in_=b7[64:128, 0:128]),
                     reads=[("bk", 7)], writes=["qa_hi"])
            if os.environ.get("KNSA") == "3":
                break
            for g in range(4):
                u = kh * 4 + g
                load_q(u)
                build_strip([(t_gBc, u * GCLEN)], 1 if os.environ.get("KSTEP1") == "1" else 16, 4096)
                attn_core(64, stepcmp, 66, kc_fn, vc_fn, ep_plain(gate_col=u * 3 + 0), ["kcmpT"], ["vcmp"], ["qa"])
                build_strip([(t_gB, u * GLEN)], 1, WU)
                load_k(16 + kh, 128, (t_E64, 0, 64)); load_v(kh)
                attn_core(128, std_steps(const_far=b31[:, u:u + 1]), 66, ka_std, va_std, ep_plain(gate_col=u * 3 + 1, accumulate=True),
                          ["ka", "ka_hi"], ["va", "va_ones"], ["qa", "qa_hi"])
                build_strip([(t_gBw, u * GLEN)], 1, 512 + WOFF + 512 + 128)
                load_k(20 + kh, 64); load_v(4 + kh)
                attn_core(64, stepwin, 66, ka_std, va_std, ep_plain(gate_col=u * 3 + 2, accumulate=True),
                          ["ka"], ["va", "va_ones"], ["qa"])
                store_o(u)
        P.barrier()

    SBv = {}
    SBv["bfb"] = SB("bfb", [128, H], F32)
    SBv["caq"] = SB("caq", [128, 96], BF16)
    SBv["cak"] = SB("cak", [128, 96], BF16)
    SBv["carry"] = SB("carry", [128, H], F32)
    SBv["fz"] = SB("fz", [128, 48], F32)
    SBv["imp"] = AB[:, 4096:4096 + 2 * NT * 64].bitcast(F32).rearrange("p (t d) -> p t d", d=64)
    SBv["ST"] = SB("STt", [128, 128], F32)
    SBv["OV"] = SB("OVt", [128, 2, 64], BF16)

    cur = t_x
    if stop == "strip":
        for rep, src in enumerate([[(t_g0, 0)], [(t_gB, 3 * GLEN)], [(t_gBd, 2 * GLEN), (t_lnc, 0)]]):
            build_strip(src, 1, WU)
            for c in range(4):
                P.dma("sp", t_y.ap()[(rep * 4 + c) * 128:(rep * 4 + c + 1) * 128, :], Wst[:, c * 1024:(c + 1) * 1024], reads=["W"])
        layers = []
    for idx, L in enumerate(layers):
        kind = L % 4
        lastl = (idx == len(layers) - 1)
        dstt = t_y if lastl else t_X[idx % 2]
        phase1(L, kind, cur)
        if stop == "p1":
            break
        if kind == 0:
            phase2_dil()
        elif kind == 1:
            phase2_nsa()
        elif kind == 2:
            phase2_fox()
        else:
            phase2_moba()
        if stop == "p2":
            for t in range(NT):
                tmpb = b16v(GA, (t % 2) * 2048, 1024)
                P.dma("sp", tmpb, t_MIX.ap()[t * 128:(t + 1) * 128, :], writes=[("dbg", t % 2)])
                tmpf = f32v(GA, 8192 + (t % 2) * 4096, 1024)
                P.op("dve", lambda e, tmpb=tmpb, tmpf=tmpf: e.tensor_copy(out=tmpf, in_=tmpb), reads=[("dbg", t % 2)], writes=[("dbgf", t % 2)])
                P.dma("sp", t_y.ap()[t * 128:(t + 1) * 128, :], tmpf, reads=[("dbgf", t % 2)])
            break
        phase3(L, cur, dstt)
        cur = dstt
    P.barrier()
    P.emit()
    return nc


def _rel_bucket_np(d):
    d = np.maximum(d, 0)
    df = np.maximum(d.astype(np.float32), np.float32(1.0))
    large = 16 + (np.log(df / np.float32(16)) / np.float32(math.log(2048 / 16)) * np.float32(16)).astype(np.int32)
    large = np.minimum(large, 31)
    return np.where(d < 16, d, large)


def _host_tables(rel_table):
    rel_table = np.asarray(rel_table, dtype=np.float32)
    m = np.arange(GLEN)
    d = m - 511
    bk = _rel_bucket_np(d)
    gat = rel_table[bk, :].T.copy()
    valid = d >= 0
    gB = np.where(valid[None, :], gat, np.float32(NEG)).astype(np.float32)
    gBw = np.where((valid & (d <= 511))[None, :], gat, np.float32(NEG)).astype(np.float32)
    cnt = ((d <= 128) & valid).astype(np.int32) + ((d % 4 == 0) & (d <= 512) & valid) + ((d % 16 == 0) & (d <= 2048) & valid)
    gBd = np.where((cnt > 0)[None, :], gat, np.float32(NEG)).astype(np.float32)
    lnc = np.where(cnt > 0, np.log(np.maximum(cnt, 1)), 0.0).astype(np.float32)[None, :]
    g0 = np.where(valid, 0.0, NEG).astype(np.float32)[None, :]
    mc = np.arange(GCLEN)
    dc = mc - 2063
    gatc = rel_table[_rel_bucket_np(dc), :].T.copy()
    gBc = np.where((dc >= 0)[None, :], gatc, np.float32(NEG)).astype(np.float32)
    cf = np.zeros((128, 4, 128), np.float32)
    cf[:, 0, :] = np.eye(128)
    cf[:, 1, :] = np.eye(128)[::-1]
    cf[:, 2, :] = np.triu(np.ones((128, 128)))
    cf[:, 3, :] = 1.0
    identb = np.eye(128).astype(ml_dtypes.bfloat16)
    tok = np.arange(S)
    E16 = (tok[None, :] // 256 == np.arange(16)[:, None]).astype(ml_dtypes.bfloat16)
    E64 = (tok[None, :] // 64 == np.arange(64)[:, None]).astype(ml_dtypes.bfloat16)
    n = np.arange(256)
    j = np.arange(64)
    ov = ((16 * n[:, None] < 64 * j[None, :] + 64) & (16 * n[:, None] + 32 > 64 * j[None, :]) & (n[:, None] < 255))
    OV = ov.reshape(2, 128, 64).transpose(1, 0, 2).astype(ml_dtypes.bfloat16).copy()
    r = (np.arange(128) >= 64).astype(np.int32)
    c = np.arange(128)
    ST = np.where(c[None, :] < 64 + r[:, None], 0.0, NEG).astype(np.float32)
    assert (_rel_bucket_np(np.arange(1537, 8192)) == 31).all()
    return dict(rel31=np.ascontiguousarray(rel_table[31:32, :]), gB=gB, gBw=gBw, gBd=gBd, gBc=gBc, g0=g0, lnc=lnc, cf32=cf, identb=identb, E16=E16, E64=E64, OV=OV, ST=ST)


_PROG_CACHE = {}


def _get_prog(layers):
    key = tuple(layers)
    if key not in _PROG_CACHE:
        import os
        _PROG_CACHE[key] = build_program(list(layers), stop=os.environ.get("KSTOP"))
    return _PROG_CACHE[key]


def _common_inputs(rel_table, attn_norm, mlp_norm, q_gain, k_gain, w_out, mlp_w_up, mlp_w_down, dsa_w_in, nsa_w_in,
                   nsa_cmp_pos, nsa_cmp_w1, nsa_cmp_w2, fox_w_in, fox_b_f, moba_w_in):
    f = lambda a: np.ascontiguousarray(np.asarray(a, dtype=np.float32))
    m = dict(attn_norm=f(attn_norm), mlp_norm=f(mlp_norm), q_gain=f(q_gain), k_gain=f(k_gain), w_out=f(w_out),
             mlp_w_up=f(mlp_w_up), mlp_w_down=f(mlp_w_down), dsa_w_in=f(dsa_w_in)[0], nsa_w_in=f(nsa_w_in)[0],
             fox_w_in=f(fox_w_in)[0], moba_w_in=f(moba_w_in)[0],
             nsa_posT=np.ascontiguousarray(f(nsa_cmp_pos)[0].transpose(0, 2, 1)),
             nsa_cmp_w1=f(nsa_cmp_w1)[0], nsa_cmp_w2=f(nsa_cmp_w2)[0], fox_b_f=f(fox_b_f))
    m.update(_host_tables(rel_table))
    return m


def run_layers(layers, x, n_cores=8, **params):
    nc = _get_prog(layers)
    common = _common_inputs(**params)
    x = np.asarray(x, dtype=np.float32)
    in_maps = []
    for c in range(n_cores):
        mm = dict(common)
        mm["x"] = np.ascontiguousarray(x[c % x.shape[0]])
        in_maps.append(mm)
    res = run_bass_kernel_spmd(nc, in_maps, core_ids=list(range(n_cores)))
    return res


def kernel(x, rel_table, attn_norm, mlp_norm, q_gain, k_gain, w_out, mlp_w_up, mlp_w_down,
           dsa_w_in, nsa_w_in, nsa_cmp_pos, nsa_cmp_w1, nsa_cmp_w2, fox_w_in, fox_b_f, moba_w_in):
    params = dict(rel_table=rel_table, attn_norm=attn_norm, mlp_norm=mlp_norm, q_gain=q_gain, k_gain=k_gain,
                  w_out=w_out, mlp_w_up=mlp_w_up, mlp_w_down=mlp_w_down, dsa_w_in=dsa_w_in, nsa_w_in=nsa_w_in,
                  nsa_cmp_pos=nsa_cmp_pos, nsa_cmp_w1=nsa_cmp_w1, nsa_cmp_w2=nsa_cmp_w2, fox_w_in=fox_w_in,
                  fox_b_f=fox_b_f, moba_w_in=moba_w_in)
    res = run_layers([0, 1, 2, 3], x, n_cores=8, **params)
    out = np.stack([np.asarray(res.results[b]["y"], dtype=np.float32) for b in range(4)], axis=0)
    return out
```

```python
import math
import os
import contextlib
import numpy as np
import ml_dtypes
import concourse.bass as bass
import concourse.mybir as mybir
from concourse.bass_utils import run_bass_kernel_spmd

F32 = mybir.dt.float32
BF16 = mybir.dt.bfloat16
ALU = mybir.AluOpType
AF = mybir.ActivationFunctionType
AX = mybir.AxisListType

S = 4096
D = 1024
H = 16
DH = 64
NT = 32
DFF = 4096
NEG = -1.0e30
BIG = 30000.0
WU = 4480
WOFF = 384
GLEN = WU + 128
GCLEN = 4096 + 2032 + 16
NCOLS = {0: 3072, 1: 2608, 2: 3088, 3: 3072}

ENGS = ["pe", "act", "dve", "pool", "sp"]


class Prog:
    def __init__(self, nc, ndma=12):
        self.nc = nc
        self.streams = {e: [] for e in ENGS}
        self.cnt = {e: 0 for e in ENGS}
        self.sems = {}
        self.stack = contextlib.ExitStack()
        for e in ENGS:
            self.sems[e] = self.stack.enter_context(nc.semaphore("s_" + e))
        self.dsem = []
        for i in range(2 * ndma):
            self.dsem.append([self.stack.enter_context(nc.semaphore("d_%d" % i)), 0])
        self.ndma = ndma
        self.dnext = {"sp": 0, "pool": 0, "act": 0}
        self.waited = {e: {} for e in ENGS}
        self.lastw = {}
        self.readers = {}

    def _sem(self, key):
        return self.sems[key] if isinstance(key, str) else self.dsem[key[1]][0]

    def _wait(self, eng, tok):
        key, val, src = tok
        if src == "pe" and eng == "pe":
            return
        w = self.waited[eng]
        if w.get(key, 0) >= val:
            return
        w[key] = val
        self.streams[eng].append(("w", self._sem(key), val))

    def _deps(self, eng, reads, writes):
        for r in reads:
            t = self.lastw.get(r)
            if t is not None:
                self._wait(eng, t)
        for w in writes:
            t = self.lastw.get(w)
            if t is not None:
                self._wait(eng, t)
            for t in self.readers.get(w, ()):
                self._wait(eng, t)

    def _commit(self, tok, reads, writes):
        for w in writes:
            self.lastw[w] = tok
            self.readers[w] = []
        for r in reads:
            lst = self.readers.setdefault(r, [])
            lst.append(tok)
            if len(lst) > 24:
                best = {}
                for t in lst:
                    k = t[0]
                    if k not in best or best[k][1] < t[1]:
                        best[k] = t
                self.readers[r] = list(best.values())

    def op(self, eng, fn, reads=(), writes=(), crit=None):
        self._deps(eng, reads, writes)
        self.cnt[eng] += 1
        tok = (eng, self.cnt[eng], eng)
        cw = None
        if crit is not None:
            ct = self.lastw.get(crit)
            if ct is not None and ct[2] != eng:
                cw = (self._sem(ct[0]), ct[1])
        self.streams[eng].append(("o", fn, self.sems[eng], 1, cw))
        self._commit(tok, reads, writes)
        return tok

    def dma(self, q, out, in_, reads=(), writes=()):
        self._deps(q, reads, writes)
        base = self.ndma if q == "pool" else 0
        i = base + self.dnext[q]
        self.dnext[q] = (self.dnext[q] + 1) % self.ndma
        if self.dsem[i][1] > 0:
            self._wait(q, (("d", i), self.dsem[i][1], "dma"))
        self.dsem[i][1] += 16
        tok = (("d", i), self.dsem[i][1], "dma")
        self.streams[q].append(("o", lambda e, o=out, s=in_: e.dma_start(out=o, in_=s), self.dsem[i][0], 16))
        self._commit(tok, reads, writes)
        return tok

    def barrier(self):
        for e in ENGS:
            for e2 in ENGS:
                if e2 != e and self.cnt[e2] > 0:
                    self._wait(e, (e2, self.cnt[e2], e2))
            for i, (s, v) in enumerate(self.dsem):
                if v > 0:
                    self._wait(e, (("d", i), v, "dma"))
        self.lastw = {}
        self.readers = {}

    def emit(self):
        nc = self.nc
        streams = self.streams

        def replay(eng, items):
            for it in items:
                if it[0] == "w":
                    eng.wait_ge(it[1], it[2])
                else:
                    ins = it[1](eng)
                    if len(it) > 4 and it[4] is not None:
                        ins = ins._wait_ge(it[4][0], it[4][1])
                    ins.then_inc(it[2], it[3])

        with nc.Block() as block:
            @block.tensor
            def _(e):
                replay(e, streams["pe"])

            @block.scalar
            def _(e):
                replay(e, streams["act"])

            @block.vector
            def _(e):
                replay(e, streams["dve"])

            @block.gpsimd
            def _(e):
                replay(e, streams["pool"])

            @block.sync
            def _(e):
                replay(e, streams["sp"])


def bcast_rows(t, off, n):
    return bass.AP(t, off, [[0, 128], [1, n]])


def build_program(layers, stop=None):
    nc = bass.Bass("TRN2", target_bir_lowering=False)
    es = contextlib.ExitStack()

    def din(name, shape, dt=F32):
        return nc.dram_tensor(name, list(shape), dt, kind="ExternalInput")

    t_x = din("x", [S, D])
    t_an = din("attn_norm", [4, D]); t_mn = din("mlp_norm", [4, D])
    t_qg = din("q_gain", [4, DH]); t_kg = din("k_gain", [4, DH])
    t_wout = din("w_out", [4, D, D]); t_wup = din("mlp_w_up", [4, D, DFF]); t_wdn = din("mlp_w_down", [4, DFF, D])
    t_win = {0: din("dsa_w_in", [D, 3072]), 1: din("nsa_w_in", [D, 2608]),
             2: din("fox_w_in", [D, 3088]), 3: din("moba_w_in", [D, 3072])}
    t_posT = din("nsa_posT", [2, DH, 32])
    t_w1 = din("nsa_cmp_w1", [2, 2048, 256]); t_w2 = din("nsa_cmp_w2", [2, 256, DH])
    t_bf = din("fox_b_f", [1, H])
    t_r31 = din("rel31", [1, H])
    t_gB = din("gB", [H, GLEN]); t_gBw = din("gBw", [H, GLEN]); t_gBd = din("gBd", [H, GLEN])
    t_gBc = din("gBc", [H, GCLEN]); t_g0 = din("g0", [1, GLEN]); t_lnc = din("lnc", [1, GLEN])
    t_cf = din("cf32", [128, 4, 128]); t_idb = din("identb", [128, 128], BF16)
    t_E16 = din("E16", [16, S], BF16); t_E64 = din("E64", [64, S], BF16)
    t_OV = din("OV", [128, 2, 64], BF16); t_ST = din("ST", [128, 128])
    t_y = nc.dram_tensor("y", [S, D], F32, kind="ExternalOutput")
    t_X = [nc.dram_tensor("xs%d" % i, [S, D], F32, kind="Internal") for i in range(2)]
    t_QK = nc.dram_tensor("qk", [32, 128, S], BF16, kind="Internal")
    t_V = nc.dram_tensor("vv", [16, S, DH], BF16, kind="Internal")
    t_MIX = nc.dram_tensor("mix", [S, D], BF16, kind="Internal")
    t_AUG = nc.dram_tensor("aug", [192, S], BF16, kind="Internal")
    t_gD = nc.dram_tensor("gD", [H, GLEN], F32, kind="Internal")

    def SB(name, shape, dt):
        return es.enter_context(nc.sbuf_tensor("sb_" + name, list(shape), dt))

    banks = [es.enter_context(nc.psum_tensor("bank%d" % i, [128, 512], F32)) for i in range(8)]
    AA = SB("arenaA", [128, 32768], BF16)
    AB = SB("arenaB", [128, 32768], BF16)
    WO = SB("wout", [128, 8, 1024], BF16)
    identb = SB("identb", [128, 128], BF16)
    cf = SB("cf", [128, 4, 128], F32)
    ones_b = SB("ones_b", [128, 128], BF16)
    gbc_a = SB("gbc_a", [128, D], F32); gbc_m = SB("gbc_m", [128, D], F32)
    gq = SB("gq", [128, DH], F32); gk = SB("gk", [128, DH], F32); gqk = SB("gqk", [128, DH], F32)
    small = SB("small", [128, 64], F32)
    b31 = SB("b31", [128, H], F32)
    GA = SB("genA", [128, 21760], BF16)

    P = Prog(nc)
    gates = AB[:, 0:2 * NT * 48].bitcast(F32).rearrange("p (t c) -> p t c", c=48)

    def bf(ap):
        return ap

    def f32v(arena, off_b, n):
        return arena[:, off_b // 2: off_b // 2 + 2 * n].bitcast(F32)

    def b16v(arena, off_b, n):
        return arena[:, off_b // 2: off_b // 2 + n]

    P.dma("sp", identb[:], t_idb.ap()[:, :], writes=["identb"])
    P.dma("sp", cf[:], t_cf.ap()[:, :, :], writes=["cf"])
    P.op("dve", lambda e: e.memset(ones_b[:], 1.0), writes=["ones_b"])
    P.dma("sp", b31[:], bcast_rows(t_r31, 0, H), writes=["b31"])
    P.op("dve", lambda e: e.memset(small[:], 0.0), writes=["ssq", "rstd", "ssq8", "rs", "ssq4"])

    def phase1(L, kind, t_src):
        ncol = NCOLS[kind]
        wv = AA[:, 0:8 * ncol].rearrange("p (k n) -> p k n", n=ncol)
        win = t_win[kind].ap()
        for kc in range(8):
            P.dma("pool", wv[:, kc, :], win[kc * 128:(kc + 1) * 128, :], writes=[("win", kc)])
        P.dma("pool", WO[:, :, :], t_wout.ap()[L].rearrange("(k p) n -> p k n", p=128), writes=["wo"])
        P.dma("sp", gbc_a[:], bcast_rows(t_an, L * D, D), writes=["gbc_a"])
        P.dma("sp", gbc_m[:], bcast_rows(t_mn, L * D, D), writes=["gbc_m"])
        P.dma("sp", gq[:], bcast_rows(t_qg, L * DH, DH), writes=["gq"])
        P.dma("sp", gk[:], bcast_rows(t_kg, L * DH, DH), writes=["gk"])
        P.op("dve", lambda e: e.tensor_tensor(out=gqk[:], in0=gq[:], in1=gk[:], op=ALU.mult),
             reads=["gq", "gk"], writes=["gqk"])

        xt = [f32v(GA, 0, 1024), f32v(GA, 4096, 1024)]
        junk = b16v(GA, 8192, 1024)
        hb = b16v(GA, 10240, 1024)
        hT = [b16v(GA, 12288, 1024), b16v(GA, 14336, 1024)]
        sq = f32v(GA, 16384, 512)
        tm = b16v(GA, 18432, 18 * 128)
        vm = b16v(GA, 23040, 1024)
        tst = b16v(GA, 25088, 18 * 512).rearrange("p (b n) -> p b n", n=512)
        ssq = small[:, 0:1]; rstd = small[:, 1:2]; ssq8 = small[:, 8:16]
        if kind == 2:
            bfb = SBv["bfb"]
            caq = SBv["caq"]; cak = SBv["cak"]; carry = SBv["carry"]; fz = SBv["fz"]
            P.dma("sp", bfb[:], bcast_rows(t_bf, 0, H), writes=["bfb"])
            P.op("dve", lambda e: e.memset(carry[:], 0.0), writes=["carry"])
            P.op("dve", lambda e: e.memset(caq[:], 1.0), writes=["caq"])
            P.op("dve", lambda e: e.memset(cak[:], 1.0), writes=["cak"])
            P.op("dve", lambda e: e.memset(tm[:, 2048:2304], 0.0), writes=["tm"])

        if kind == 1:
            chunks = [(0, 512, [(0, 512, "q", 0)]), (512, 512, [(0, 512, "q", 512)]),
                      (1024, 512, [(0, 256, "c", 1536), (256, 256, "c", 1792)]),
                      (1536, 512, [(0, 256, "k", 1024), (256, 256, "v", 0)]),
                      (2048, 512, [(0, 256, "k", 1280), (256, 256, "v", 256)]),
                      (2560, 48, [(0, 48, "g", 0)])]
            nv = 8
        else:
            chunks = [(0, 512, [(0, 512, "q", 0)]), (512, 512, [(0, 512, "q", 512)]),
                      (1024, 512, [(0, 512, "k", 1024)]), (1536, 512, [(0, 512, "k", 1536)]),
                      (2048, 512, [(0, 512, "v", 0)]), (2560, 512, [(0, 512, "v", 512)])]
            if kind == 2:
                chunks.append((3072, 16, [(0, 16, "f", 0)]))
            nv = 16
        nblk = 18 if kind == 2 else 16
        xsrc = t_src.ap()
        cb = 0
        for t in range(NT):
            tt = t % 4
            g = t // 4
            s = t % 2
            P.dma("sp", xt[s], xsrc[t * 128:(t + 1) * 128, :], writes=[("xt", s)])
            P.op("act", lambda e, s=s: e.activation(out=junk, in_=xt[s], func=AF.Square, accum_out=ssq),
                 reads=[("xt", s)], writes=["junk", "ssq"])
            P.op("dve", lambda e: e.tensor_scalar(out=rstd, in0=ssq, scalar1=1.0 / D, scalar2=1e-6,
                                                   op0=ALU.mult, op1=ALU.add), reads=["ssq"], writes=["rstd"])
            P.op("act", lambda e: e.activation(out=rstd, in_=rstd, func=AF.Sqrt), reads=["rstd"], writes=["rstd"])
            P.op("dve", lambda e: e.reciprocal(out=rstd, in_=rstd), reads=["rstd"], writes=["rstd"])
            P.op("dve", lambda e, s=s: e.scalar_tensor_tensor(out=hb, in0=xt[s], scalar=rstd, in1=gbc_a[:],
                                                              op0=ALU.mult, op1=ALU.mult),
                 reads=[("xt", s), "rstd", "gbc_a"], writes=["hb"])
            pT = banks[3][:].bitcast(BF16)
            for kc in range(8):
                P.op("pe", lambda e, kc=kc: e.transpose(out=pT[:, kc * 128:(kc + 1) * 128],
                                                         in_=hb[:, kc * 128:(kc + 1) * 128], identity=identb[:]),
                     reads=["hb", "identb"], writes=["b3"])
            P.op("act", lambda e, s=s: e.copy(out=hT[s], in_=pT), reads=["b3"], writes=[("hT", s)])
            hTs = hT[s].rearrange("p (k n) -> p k n", n=128)
            for (c0, n, segs) in chunks:
                bk = banks[cb % 3]; bkey = ("pb", cb % 3); cb += 1
                for kc in range(8):
                    P.op("pe", lambda e, kc=kc, bk=bk, c0=c0, n=n, hTs=hTs: e.matmul(
                        out=bk[:, 0:n], lhsT=hTs[:, kc, :], rhs=wv[:, kc, c0:c0 + n], start=(kc == 0), stop=(kc == 7)),
                        reads=[("hT", s), ("win", kc)], writes=[bkey], crit=("hT", s))
                need_norm = any(sg[2] in ("q", "k") for sg in segs)
                if need_norm:
                    nh = n // 64
                    P.op("act", lambda e, bk=bk, n=n: e.activation(out=sq[:, 0:n], in_=bk[:, 0:n], func=AF.Square),
                         reads=[bkey], writes=["sq"])
                    P.op("dve", lambda e, n=n, nh=nh: e.tensor_reduce(
                        out=ssq8[:, 0:nh], in_=sq[:, 0:n].rearrange("p (a b) -> p a b", b=64), axis=AX.X, op=ALU.add),
                        reads=["sq"], writes=["ssq8"])
                    P.op("dve", lambda e, nh=nh: e.tensor_scalar(out=ssq8[:, 0:nh], in0=ssq8[:, 0:nh], scalar1=1.0 / DH,
                                                                  scalar2=1e-6, op0=ALU.mult, op1=ALU.add),
                         reads=["ssq8"], writes=["ssq8"])
                    P.op("act", lambda e, nh=nh: e.activation(out=ssq8[:, 0:nh], in_=ssq8[:, 0:nh], func=AF.Sqrt),
                         reads=["ssq8"], writes=["ssq8"])
                    P.op("dve", lambda e, nh=nh: e.reciprocal(out=ssq8[:, 0:nh], in_=ssq8[:, 0:nh]),
                         reads=["ssq8"], writes=["ssq8"])
                for (so, sn, typ, dc) in segs:
                    if typ in ("q", "k"):
                        h0 = so // 64; nh = sn // 64
                        dst = tm[:, dc:dc + sn].rearrange("p (a b) -> p a b", b=64)
                        if typ == "k":
                            P.op("dve", lambda e, bk=bk, so=so, sn=sn, h0=h0, nh=nh, dst=dst: e.tensor_tensor(
                                out=dst, in0=bk[:, so:so + sn].rearrange("p (a b) -> p a b", b=64),
                                in1=ssq8[:, h0:h0 + nh].unsqueeze(2).to_broadcast([128, nh, 64]), op=ALU.mult),
                                reads=[bkey, "ssq8"], writes=["tm"])
                        else:
                            sqv = sq[:, so:so + sn].rearrange("p (a b) -> p a b", b=64)
                            P.op("dve", lambda e, bk=bk, so=so, sn=sn, h0=h0, nh=nh, sqv=sqv: e.tensor_tensor(
                                out=sqv, in0=bk[:, so:so + sn].rearrange("p (a b) -> p a b", b=64),
                                in1=ssq8[:, h0:h0 + nh].unsqueeze(2).to_broadcast([128, nh, 64]), op=ALU.mult),
                                reads=[bkey, "ssq8", "sq"], writes=["sq"])
                            P.op("dve", lambda e, nh=nh, sqv=sqv, dst=dst: e.tensor_tensor(
                                out=dst, in0=sqv, in1=gqk[:].unsqueeze(1).to_broadcast([128, nh, 64]), op=ALU.mult),
                                reads=["sq", "gqk"], writes=["tm"])
                    elif typ == "c":
                        P.op("act", lambda e, bk=bk, so=so, sn=sn, dc=dc: e.copy(out=tm[:, dc:dc + sn], in_=bk[:, so:so + sn]),
                             reads=[bkey], writes=["tm"])
                    elif typ == "v":
                        P.op("act", lambda e, bk=bk, so=so, sn=sn, dc=dc: e.copy(out=vm[:, dc:dc + sn], in_=bk[:, so:so + sn]),
                             reads=[bkey], writes=["vm"])
                    elif typ == "g":
                        P.op("act", lambda e, bk=bk, t=t: e.activation(out=gates[:, t, :], in_=bk[:, 0:48], func=AF.Sigmoid),
                             reads=[bkey], writes=["gates"])
                    elif typ == "f" and os.environ.get("KF_SKIP") == "1":
                        pass
                    elif typ == "f":
                        P.op("dve", lambda e, bk=bk: e.tensor_tensor(out=fz[:, 0:16], in0=bk[:, 0:16], in1=bfb[:], op=ALU.add),
                             reads=[bkey, "bfb"], writes=["fz"])
                        P.op("act", lambda e: e.activation(out=fz[:, 0:16], in_=fz[:, 0:16], func=AF.Exp, scale=-1.0),
                             reads=["fz"], writes=["fz"])
                        P.op("act", lambda e: e.activation(out=fz[:, 0:16], in_=fz[:, 0:16], func=AF.Ln, bias=1.0),
                             reads=["fz"], writes=["fz"])
                        b7 = banks[7]
                        P.op("pe", lambda e: e.matmul(out=b7[:, 0:16], lhsT=cf[:, 2, :], rhs=fz[:, 0:16], start=True, stop=True),
                             reads=["fz", "cf"], writes=["b7"])
                        P.op("pe", lambda e: e.matmul(out=b7[:, 16:32], lhsT=cf[:, 3, :], rhs=fz[:, 0:16], start=True, stop=True),
                             reads=["fz", "cf"], writes=["b7"])
                        P.op("dve", lambda e: e.tensor_tensor(out=fz[:, 16:32], in0=b7[:, 0:16], in1=carry[:], op=ALU.add),
                             reads=["b7", "carry", "fz"], writes=["fz2"])
                        P.op("dve", lambda e: e.tensor_tensor(out=carry[:], in0=b7[:, 16:32], in1=carry[:], op=ALU.add),
                             reads=["b7", "carry", "fz2"], writes=["carry"])
                        P.op("dve", lambda e: e.tensor_scalar(out=fz[:, 16:32], in0=fz[:, 16:32], scalar1=-8.0, scalar2=None,
                                                               op0=ALU.mult), reads=["fz2"], writes=["fz2"])
                        caqv = caq[:].rearrange("p (r h) -> p r h", h=16)
                        cakv = cak[:].rearrange("p (r h) -> p r h", h=16)
                        cur = fz[:, 16:32]; nxt = fz[:, 32:48]
                        for r in range(3):
                            P.op("dve", lambda e, r=r, cur=cur: e.tensor_copy(out=caqv[:, r, :], in_=cur),
                                 reads=["fz2"], writes=["caq"])
                            P.op("dve", lambda e, r=r: e.tensor_scalar(out=cakv[:, 3 + r, :], in0=caqv[:, r, :], scalar1=-1.0,
                                                                        scalar2=None, op0=ALU.mult),
                                 reads=["caq"], writes=["cak"])
                            if r < 2:
                                P.op("dve", lambda e, r=r, cur=cur, nxt=nxt: e.tensor_tensor(out=nxt, in0=cur, in1=caqv[:, r, :],
                                                                                             op=ALU.subtract),
                                     reads=["fz2", "caq"], writes=["fz2"])
                                cur, nxt = nxt, cur
                        P.op("dve", lambda e: e.tensor_copy(out=tm[:, 2048:2144], in_=caq[:]), reads=["caq"], writes=["tm"])
                        P.op("dve", lambda e: e.tensor_copy(out=tm[:, 2176:2272], in_=cak[:]), reads=["cak"], writes=["tm"])
            for half in range((nblk + 7) // 8):
                pb = banks[4 + half][:].bitcast(BF16)
                nb = min(8, nblk - half * 8)
                for b in range(nb):
                    blk = half * 8 + b
                    P.op("pe", lambda e, pb=pb, b=b, blk=blk: e.transpose(out=pb[:, b * 128:(b + 1) * 128],
                                                                           in_=tm[:, blk * 128:(blk + 1) * 128], identity=identb[:]),
                         reads=["tm", "identb"], writes=[("b4", half)])
                dstv = tst[:, half * 8:half * 8 + nb, tt * 128:(tt + 1) * 128]
                srcv = pb[:, 0:nb * 128].rearrange("p (b n) -> p b n", n=128)
                eng = "act" if half == 0 else "dve"
                if eng == "act":
                    P.op("act", lambda e, dstv=dstv, srcv=srcv: e.copy(out=dstv, in_=srcv),
                         reads=[("b4", half)], writes=[("tst", half)])
                else:
                    P.op("dve", lambda e, dstv=dstv, srcv=srcv: e.tensor_copy(out=dstv, in_=srcv),
                         reads=[("b4", half)], writes=[("tst", half)])
            vdst = bass.AP(t_V, t * 128 * DH, [[DH, 128], [S * DH, nv], [1, DH]])
            P.dma("pool", vdst, vm[:, 0:nv * 64].rearrange("p (u d) -> p u d", d=64), reads=["vm"])
            if tt == 3:
                for half in range(2):
                    qdst = bass.AP(t_QK, half * 128 * S + g * 512, [[S, 64], [2 * 128 * S, 16], [1, 512]])
                    P.dma("pool", qdst, tst[half * 64:(half + 1) * 64, 0:16, :], reads=[("tst", 0), ("tst", 1)])
                if kind == 2:
                    for qk in range(2):
                        adst = bass.AP(t_AUG, qk * 96 * S + g * 512, [[S, 96], [1, 512]])
                        P.dma("pool", adst, tst[0:96, 16 + qk, :], reads=[("tst", 2)])
        P.barrier()

    def phase3(L, t_src, t_dst):
        wu = AA[:, :].rearrange("p (k n) -> p k n", n=DFF)
        wup = t_wup.ap()[L].rearrange("(k p) n -> p k n", p=128)
        for kc in range(8):
            P.dma("pool", wu[:, kc, :], wup[:, kc, :], writes=[("wup", kc)])
        wd = AB[:, :].rearrange("p (k n) -> p k n", n=1024)
        wdn = t_wdn.ap()[L].rearrange("(k p) n -> p k n", p=128)
        for q4 in range(4):
            P.dma("pool", wd[:, q4 * 8:(q4 + 1) * 8, :], wdn[:, q4 * 8:(q4 + 1) * 8, :], writes=[("wdn", q4)])
        ot = [b16v(GA, 0, 1024), b16v(GA, 2048, 1024)]
        xr = [f32v(GA, 4096, 1024), f32v(GA, 8192, 1024)]
        oT = b16v(GA, 12288, 1024)
        h2 = b16v(GA, 14336, 1024)
        h2T = b16v(GA, 16384, 1024)
        sq = f32v(GA, 18432, 512)
        uT = b16v(GA, 20480, 32 * 128).rearrange("p (f n) -> p f n", n=128)
        xo = [f32v(GA, 28672, 1024), f32v(GA, 32768, 1024)]
        junk = b16v(GA, 36864, 1024)
        ssq = small[:, 0:1]; rstd = small[:, 1:2]
        src = t_src.ap(); dst = t_dst.ap(); mix = t_MIX.ap()
        cb = 0
        for t in range(NT):
            s = t % 2
            P.dma("sp", ot[s], mix[t * 128:(t + 1) * 128, :], writes=[("ot", s)])
            P.dma("sp", xr[s], src[t * 128:(t + 1) * 128, :], writes=[("xr", s)])
            pT = banks[3][:].bitcast(BF16)
            for kc in range(8):
                P.op("pe", lambda e, kc=kc, s=s: e.transpose(out=pT[:, kc * 128:(kc + 1) * 128],
                                                              in_=ot[s][:, kc * 128:(kc + 1) * 128], identity=identb[:]),
                     reads=[("ot", s), "identb"], writes=["b3"])
            P.op("act", lambda e: e.copy(out=oT, in_=pT), reads=["b3"], writes=["oT"])
            oTv = oT.rearrange("p (k n) -> p k n", n=128)
            for c in range(2):
                bk = banks[cb % 3]; bkey = ("pb", cb % 3); cb += 1
                for kc in range(8):
                    P.op("pe", lambda e, kc=kc, bk=bk, c=c: e.matmul(out=bk[:, :], lhsT=oTv[:, kc, :], rhs=WO[:, kc, c * 512:(c + 1) * 512],
                                                                    start=(kc == 0), stop=(kc == 7)),
                         reads=["oT", "wo"], writes=[bkey], crit="oT")
                P.op("dve", lambda e, bk=bk, c=c, s=s: e.tensor_tensor(out=xr[s][:, c * 512:(c + 1) * 512], in0=bk[:, :],
                                                                      in1=xr[s][:, c * 512:(c + 1) * 512], op=ALU.add),
                     reads=[bkey, ("xr", s)], writes=[("xr", s)])
            P.op("act", lambda e, s=s: e.activation(out=junk, in_=xr[s], func=AF.Square, accum_out=ssq),
                 reads=[("xr", s)], writes=["junk", "ssq"])
            P.op("dve", lambda e: e.tensor_scalar(out=rstd, in0=ssq, scalar1=1.0 / D, scalar2=1e-6,
                                                   op0=ALU.mult, op1=ALU.add), reads=["ssq"], writes=["rstd"])
            P.op("act", lambda e: e.activation(out=rstd, in_=rstd, func=AF.Sqrt), reads=["rstd"], writes=["rstd"])
            P.op("dve", lambda e: e.reciprocal(out=rstd, in_=rstd), reads=["rstd"], writes=["rstd"])
            P.op("dve", lambda e, s=s: e.scalar_tensor_tensor(out=h2, in0=xr[s], scalar=rstd, in1=gbc_m[:],
                                                              op0=ALU.mult, op1=ALU.mult),
                 reads=[("xr", s), "rstd", "gbc_m"], writes=["h2"])
            pT2 = banks[4][:].bitcast(BF16)
            for kc in range(8):
                P.op("pe", lambda e, kc=kc: e.transpose(out=pT2[:, kc * 128:(kc + 1) * 128],
                                                         in_=h2[:, kc * 128:(kc + 1) * 128], identity=identb[:]),
                     reads=["h2", "identb"], writes=["b4"])
            P.op("act", lambda e: e.copy(out=h2T, in_=pT2), reads=["b4"], writes=["h2T"])
            h2Tv = h2T.rearrange("p (k n) -> p k n", n=128)
            for fq in range(8):
                bk = banks[cb % 3]; bkey = ("pb", cb % 3); cb += 1
                for fi in range(4):
                    fc = fq * 4 + fi
                    for kc in range(8):
                        P.op("pe", lambda e, kc=kc, bk=bk, fc=fc, fi=fi: e.matmul(
                            out=bk[:, fi * 128:(fi + 1) * 128], lhsT=wu[:, kc, fc * 128:(fc + 1) * 128], rhs=h2Tv[:, kc, :],
                            start=(kc == 0), stop=(kc == 7)), reads=["h2T", ("wup", kc)], writes=[bkey])
                P.op("act", lambda e, bk=bk: e.activation(out=sq, in_=bk[:, :], func=AF.Square), reads=[bkey], writes=["sq"])
                P.op("dve", lambda e, bk=bk, fq=fq: e.scalar_tensor_tensor(
                    out=uT[:, fq * 4:(fq + 1) * 4, :], in0=bk[:, :].rearrange("p (f n) -> p f n", n=128), scalar=0.0,
                    in1=sq.rearrange("p (f n) -> p f n", n=128), op0=ALU.is_gt, op1=ALU.mult),
                    reads=[bkey, "sq"], writes=[("uT", fq)])
            for c in range(2):
                bk = banks[cb % 3]; bkey = ("pb", cb % 3); cb += 1
                for fc in range(32):
                    P.op("pe", lambda e, fc=fc, bk=bk, c=c: e.matmul(out=bk[:, :], lhsT=uT[:, fc, :], rhs=wd[:, fc, c * 512:(c + 1) * 512],
                                                                    start=(fc == 0), stop=(fc == 31)),
                         reads=[("uT", fc // 4), ("wdn", fc // 8)], writes=[bkey], crit=("uT", fc // 4))
                P.op("dve", lambda e, bk=bk, c=c, s=s: e.tensor_tensor(out=xo[s][:, c * 512:(c + 1) * 512], in0=bk[:, :],
                                                                      in1=xr[s][:, c * 512:(c + 1) * 512], op=ALU.add),
                     reads=[bkey, ("xr", s)], writes=[("xo", s)])
            P.dma("sp", dst[t * 128:(t + 1) * 128, :], xo[s], reads=[("xo", s)])
        P.barrier()

    KA = [b16v(AA, 0, S), b16v(AB, 24576, S)]
    QA = [b16v(AA, 8192, S), b16v(AB, 32768, S)]
    VA = [b16v(GA, 0, NT * 66).rearrange("p (t c) -> p t c", c=66),
          b16v(AB, 40960, NT * 66).rearrange("p (t c) -> p t c", c=66)]
    WS = [f32v(GA, 4352, WU), f32v(AB, 45184, WU)]
    Hk = f32v(AA, 16384, WU)
    lg = [f32v(GA, 22528 + i * 2048, 512) for i in range(3)] + [f32v(AB, 20480, 512)]
    pTb = [b16v(GA, 28672 + i * 1024, 512) for i in range(3)] + [b16v(AB, 22528, 512)]
    NR = 4
    LAG = 3
    lbank_idx = [0, 1, 2, 6]
    lbanks = [banks[i] for i in lbank_idx]
    ost = f32v(GA, 31744, NT * 64).rearrange("p (t d) -> p t d", d=64)
    rs_t = small[:, 2:3]
    oTs = [f32v(AB, 16384, 512), f32v(AB, 18432, 512)]
    gctr = [0]
    rctr = [0]
    b7 = banks[7]
    B7 = ("bk", 7)

    def strip_dma(spec):
        tg, off, step, width = spec[0:4]
        rd = list(spec[4]) if len(spec) > 4 else []
        P.dma("sp", Hk[:, 0:width], bass.AP(tg, off, [[step, 128], [1, width]]), reads=rd, writes=["Hk"])

    def strip_compute(spec, s):
        width = spec[3]
        W = WS[s]
        for c in range((width + 511) // 512):
            n = min(512, width - c * 512)
            P.op("pe", lambda e, c=c, n=n: e.matmul(out=b7[:, 0:n], lhsT=cf[:, 1, :], rhs=Hk[:, c * 512:c * 512 + n],
                                                   start=True, stop=True), reads=["Hk", "cf"], writes=[B7])
            P.op("act", lambda e, c=c, n=n, W=W: e.copy(out=W[:, c * 512:c * 512 + n], in_=b7[:, 0:n]),
                 reads=[B7], writes=[("W", s)])

    def attn_core(run, s, hook=None):
        K = run["K"]; steps = run["steps"]; nvc = 66
        ka_fn = run["ka_fn"]; va_fn = run["va_fn"]; epilogue = run["ep"]
        kreads = run["kreads"](s); vreads = run["vreads"](s); qreads = run["qreads"](s)
        qa = QA[s]; W = WS[s]
        gfirst = {}; glast = {}; gpar = {}
        for n, st in enumerate(steps):
            G = st[0]
            if G not in gfirst:
                gfirst[G] = n
                gpar[G] = gctr[0] % 2
                gctr[0] += 1
            glast[G] = n

        def emit_pv(n):
            G, j = steps[n][0:2]
            r = n % NR
            par = gpar[G]
            otb = banks[3 + par]; okey = ("bk", 3 + par)
            P.op("pe", lambda e, otb=otb, r=r, j=j, n=n, G=G: e.matmul(
                out=otb[0:nvc, :], lhsT=va_fn(j, s), rhs=pTb[r][:, :], start=(gfirst[G] == n), stop=(glast[G] == n)),
                reads=[("pT", r)] + vreads, writes=[okey])
            if glast[G] == n:
                ots = oTs[par]
                P.op("act", lambda e, ots=ots, otb=otb: e.copy(out=ots[0:nvc, :], in_=otb[0:nvc, :]),
                     reads=[okey], writes=[("oTs", par)])
                tb = banks[5]; tkey = ("bk", 5)
                for t in range(4):
                    P.op("pe", lambda e, tb=tb, ots=ots, t=t: e.transpose(
                        out=tb[:, t * nvc:(t + 1) * nvc], in_=ots[0:nvc, t * 128:(t + 1) * 128], identity=cf[0:nvc, 0, 0:nvc]),
                        reads=[("oTs", par), "cf"], writes=[tkey], crit=("oTs", par))
                for t in range(4):
                    epilogue(4 * G + t, tb[:, t * nvc:(t + 1) * nvc], tkey)

        hook_at = min(5, len(steps) - 1)
        for n, st in enumerate(steps):
            G, j, woff = st[0:3]
            mode = st[4] if len(st) > 4 else "W"
            r = n % NR
            bk = lbanks[r]; bkey = ("bk", lbank_idx[r])
            P.op("pe", lambda e, bk=bk, j=j, G=G: e.matmul(out=bk[:, :], lhsT=ka_fn(j, K, s), rhs=qa[0:K, G * 512:(G + 1) * 512],
                                                          start=True, stop=True),
                 reads=kreads + qreads, writes=[bkey])
            if mode == "W":
                P.op("dve", lambda e, bk=bk, r=r, woff=woff: e.scalar_tensor_tensor(
                    out=lg[r], in0=bk[:, :], scalar=0.125, in1=W[:, woff:woff + 512], op0=ALU.mult, op1=ALU.add),
                    reads=[bkey, ("W", s)], writes=[("lg", r)])
                P.op("act", lambda e, r=r: e.activation(out=pTb[r], in_=lg[r], func=AF.Exp),
                     reads=[("lg", r)], writes=[("pT", r)])
            elif mode == "plain":
                P.op("act", lambda e, r=r, bk=bk: e.activation(out=pTb[r], in_=bk[:, :], func=AF.Exp, scale=0.125),
                     reads=[bkey], writes=[("pT", r)])
            else:
                bap = mode[1]
                P.op("act", lambda e, r=r, bk=bk, bap=bap: e.activation(out=pTb[r], in_=bk[:, :], func=AF.Exp, bias=bap, scale=0.125),
                     reads=[bkey, "b31"], writes=[("pT", r)])
            if n >= LAG:
                emit_pv(n - LAG)
            if n == hook_at and hook is not None:
                hook()
        for n in range(max(0, len(steps) - LAG), len(steps)):
            emit_pv(n)

    def execute(runs):
        def sset(i):
            return (rctr[0] + i) % 2

        def pf_dma(i):
            s = sset(i)
            runs[i]["loads"](s)
            if runs[i].get("strip") is not None:
                strip_dma(runs[i]["strip"])

        def pf_compute(i):
            s = sset(i)
            if runs[i].get("strip") is not None:
                strip_compute(runs[i]["strip"], s)
            if runs[i].get("pre") is not None:
                runs[i]["pre"](s)

        pf_dma(0)
        pf_compute(0)
        for i in range(len(runs)):
            hook = None
            if i + 1 < len(runs):
                pf_dma(i + 1)
                hook = (lambda i=i: pf_compute(i + 1))
            attn_core(runs[i], sset(i), hook)
            if runs[i].get("post") is not None:
                runs[i]["post"]()
        rctr[0] += len(runs)

    def std_steps(maxspan=None, plain_past=False, const_far=None):
        steps = []
        for G in range(8):
            j0 = 0 if maxspan is None else max(0, 4 * G - maxspan)
            for j in range(j0, 4 * G + 4):
                act = [i for i in range(4 * G, 4 * G + 4) if i >= j and (maxspan is None or i - j <= maxspan)]
                if act:
                    Dd = 512 * G - 128 * j
                    mode = "W"
                    if plain_past and Dd >= 128:
                        mode = "plain"
                    elif const_far is not None and Dd >= 1664:
                        mode = ("const", const_far)
                    steps.append((G, j, Dd + WOFF, act, mode))
        return steps

    def ep_plain(gate_col=None, accumulate=False):
        def ep(i, oa, okey):
            P.op("dve", lambda e, oa=oa: e.tensor_scalar(out=rs_t, in0=oa[:, 64:65], scalar1=1e-30, scalar2=None, op0=ALU.max),
                 reads=[okey], writes=["rs"])
            P.op("dve", lambda e: e.reciprocal(out=rs_t, in_=rs_t), reads=["rs"], writes=["rs"])
            if gate_col is not None:
                P.op("dve", lambda e, i=i: e.tensor_tensor(out=rs_t, in0=rs_t, in1=gates[:, i, gate_col:gate_col + 1], op=ALU.mult),
                     reads=["rs", "gates"], writes=["rs"])
            if accumulate:
                P.op("dve", lambda e, oa=oa, i=i: e.scalar_tensor_tensor(out=ost[:, i, :], in0=oa[:, 0:64], scalar=rs_t, in1=ost[:, i, :],
                                                                       op0=ALU.mult, op1=ALU.add),
                     reads=[okey, "rs", ("ost", i)], writes=[("ost", i)])
            else:
                P.op("dve", lambda e, oa=oa, i=i: e.tensor_scalar(out=ost[:, i, :], in0=oa[:, 0:64], scalar1=rs_t, scalar2=None, op0=ALU.mult),
                     reads=[okey, "rs"], writes=[("ost", i)])
        return ep

    def ld_k(s, unit, extra=None):
        P.dma("sp", KA[s][0:64, :], t_QK.ap()[unit, 0:64, :], writes=[("ka", s)])
        if extra is not None:
            tg, r0, nr = extra
            P.dma("sp", KA[s][64:64 + nr, :], tg.ap()[r0:r0 + nr, :], writes=[("ka_hi", s)])

    def ld_q(s, unit):
        P.dma("sp", QA[s][0:64, :], t_QK.ap()[unit, 0:64, :], writes=[("qa", s)])

    def ld_v(s, unit):
        P.dma("sp", VA[s][:, :, 0:64], t_V.ap()[unit].rearrange("(t p) d -> p t d", p=128), writes=[("va", s)])

    def store_o(h):
        dst = bass.AP(t_MIX, h * DH, [[D, 128], [128 * D, NT], [1, DH]])
        P.dma("pool", dst, ost[:, :, :], reads=[("ost", i) for i in range(NT)])

    ka_std = lambda j, K, s: KA[s][0:K, j * 128:(j + 1) * 128]
    va_std = lambda j, s: VA[s][:, j, :]

    def phase2_common_init():
        for s in range(2):
            P.op("dve", lambda e, s=s: e.memset(VA[s][:, :, 64:66], 1.0), writes=[("va_ones", s)])

    def base_run(K, steps, ep, hi_k=False, hi_q=False):
        return dict(K=K, steps=steps, ka_fn=ka_std, va_fn=va_std, ep=ep,
                    kreads=(lambda s: [("ka", s)] + ([("ka_hi", s)] if hi_k else [])),
                    vreads=(lambda s: [("va", s), ("va_ones", s)]),
                    qreads=(lambda s: [("qa", s)] + ([("qa_hi", s)] if hi_q else [])))

    def phase2_fox():
        phase2_common_init()
        spec = (t_g0, 0, 1, WU)
        for s in range(2):
            strip_dma(spec)
            strip_compute(spec, s)
        steps = std_steps(plain_past=True)
        runs = []
        for h in range(H):
            r = base_run(70, steps, ep_plain(), hi_k=True, hi_q=True)

            def loads(s, h=h):
                ld_k(s, 16 + h)
                P.dma("sp", KA[s][64:70, :], bass.AP(t_AUG, (96 + h) * S, [[16 * S, 6], [1, S]]), writes=[("ka_hi", s)])
                ld_q(s, h)
                P.dma("sp", QA[s][64:70, :], bass.AP(t_AUG, h * S, [[16 * S, 6], [1, S]]), writes=[("qa_hi", s)])
                ld_v(s, h)
            r["loads"] = loads
            r["post"] = (lambda h=h: store_o(h))
            runs.append(r)
        execute(runs)
        P.barrier()

    def phase2_dil():
        phase2_common_init()
        gsb = f32v(AA, 16384, GLEN)[0:16, :]
        lsb = f32v(AB, 45184, GLEN)[0:16, :]
        P.dma("sp", gsb, t_gBd.ap()[:, :], writes=["Hk"])
        P.dma("sp", lsb, bass.AP(t_lnc, 0, [[0, 16], [1, GLEN]]), writes=[("W", 1)])
        P.op("dve", lambda e: e.tensor_tensor(out=gsb, in0=gsb, in1=lsb, op=ALU.add), reads=["Hk", ("W", 1)], writes=["Hk"])
        P.dma("sp", t_gD.ap()[:, :], gsb, reads=["Hk"], writes=["gD"])
        steps = std_steps(maxspan=16)
        runs = []
        for h in range(H):
            r = base_run(64, steps, ep_plain())
            r["strip"] = (t_gD, h * GLEN, 1, 2048 + WOFF + 512 + 128, ["gD"])

            def loads(s, h=h):
                ld_k(s, 16 + h); ld_q(s, h); ld_v(s, h)
            r["loads"] = loads
            r["post"] = (lambda h=h: store_o(h))
            runs.append(r)
        execute(runs)
        P.barrier()

    def phase2_moba():
        phase2_common_init()
        km32 = f32v(GA, 39936, 16)
        kmb = b16v(GA, 40064, 16)
        gm = f32v(GA, 40128, 16)
        mx8 = f32v(GA, 40192, 8)
        mb80 = f32v(GA, 40256, 80)
        P.op("dve", lambda e: e.memset(mb80, 0.0), writes=["mb80"])

        def pre(s):
            ka = KA[s]; qa = QA[s]
            P.op("dve", lambda e: e.tensor_reduce(out=km32[0:64, :], in_=ka[0:64, :].rearrange("p (n b) -> p n b", b=256),
                                                  axis=AX.X, op=ALU.add), reads=[("ka", s)], writes=["km32"])
            P.op("dve", lambda e: e.tensor_scalar(out=kmb[0:64, :], in0=km32[0:64, :], scalar1=1.0 / 256, scalar2=None, op0=ALU.mult),
                 reads=["km32"], writes=["kmb"])
            for i in range(NT):
                own = i // 2
                P.op("dve", lambda e: e.memset(gm, NEG), writes=["gm"])
                if own > 0:
                    P.op("pe", lambda e, i=i: e.matmul(out=b7[:, 0:16], lhsT=qa[0:64, i * 128:(i + 1) * 128], rhs=kmb[0:64, :],
                                                       start=True, stop=True), reads=[("qa", s), "kmb"], writes=[B7])
                    P.op("dve", lambda e, own=own: e.tensor_copy(out=gm[:, 0:own], in_=b7[:, 0:own]), reads=[B7, "gm"], writes=["gm"])
                if own > 3:
                    P.op("dve", lambda e: e.max(out=mx8, in_=gm), reads=["gm"], writes=["mx8"])
                    P.op("dve", lambda e: e.tensor_scalar(out=mb80[:, 64:80], in0=gm, scalar1=mx8[:, 2:3], scalar2=None, op0=ALU.is_ge),
                         reads=["gm", "mx8"], writes=["mb80"])
                else:
                    P.op("dve", lambda e: e.tensor_scalar(out=mb80[:, 64:80], in0=gm, scalar1=-1.0e29, scalar2=None, op0=ALU.is_ge),
                         reads=["gm"], writes=["mb80"])
                P.op("dve", lambda e: e.tensor_scalar(out=mb80[:, 64:80], in0=mb80[:, 64:80], scalar1=-1.0, scalar2=BIG,
                                                       op0=ALU.add, op1=ALU.mult), reads=["mb80"], writes=["mb80"])
                P.op("dve", lambda e, own=own: e.memset(mb80[:, 64 + own:65 + own], 0.0), reads=["mb80"], writes=["mb80"])
                P.op("pe", lambda e: e.transpose(out=b7[0:80, 128:256], in_=mb80, identity=cf[:, 0, :]), reads=["mb80", "cf"], writes=[B7])
                P.op("act", lambda e, i=i: e.copy(out=qa[64:80, i * 128:(i + 1) * 128], in_=b7[64:80, 128:256]),
                     reads=[B7], writes=[("qa_hi", s)])

        runs = []
        for h in range(H):
            r = base_run(80, std_steps(const_far=b31[:, h:h + 1]), ep_plain(), hi_k=True, hi_q=True)
            r["strip"] = (t_gB, h * GLEN, 1, 1664 + WOFF + 512 + 128)

            def loads(s, h=h):
                ld_k(s, 16 + h, (t_E16, 0, 16)); ld_q(s, h); ld_v(s, h)
            r["loads"] = loads
            r["pre"] = pre
            r["post"] = (lambda h=h: store_o(h))
            runs.append(r)
        execute(runs)
        P.barrier()

    def phase2_nsa():
        phase2_common_init()
        kcR = b16v(AA, 16384, S)
        kcA = b16v(AA, 24576, S)
        kcB = b16v(AA, 34304, S)
        w1 = b16v(AA, 42496, 32 * 256).rearrange("p (t c) -> p t c", c=256)
        w2 = b16v(AA, 58880, 128).rearrange("p (m d) -> p m d", d=64)
        hid = b16v(AA, 59392, 512).rearrange("p (m n) -> p m n", n=256)
        kcn = b16v(AA, 60416, 256).rearrange("p (t d) -> p t d", d=128)
        kcmpT = b16v(AA, 60928, 256)
        vcmp = b16v(AA, 61440, 2 * 66).rearrange("p (t c) -> p t c", c=66)
        vcmpA = b16v(AA, 61952, 2 * 66).rearrange("p (t c) -> p t c", c=66)
        posT = f32v(AA, 62464, 32)
        impb = SBv["imp"]
        STt = SBv["ST"]
        OVt = SBv["OV"]
        impm = f32v(GA, 39936, 64)
        mx16 = f32v(GA, 40192, 16)
        mb128 = f32v(GA, 40256, 128)
        x2 = f32v(GA, 40768, 256)
        tg = f32v(GA, 41792, 256)
        P.dma("sp", STt[:], t_ST.ap()[:, :], writes=["ST"])
        P.dma("sp", OVt[:], t_OV.ap()[:, :, :], writes=["OV"])
        P.op("dve", lambda e: e.memset(mb128, 0.0), writes=["mb128"])
        stepwin = std_steps(maxspan=4)
        stepcmp = []
        for G in range(8):
            for ntl in range(2):
                if G >= 4 * ntl:
                    stepcmp.append((G, ntl, 512 * G - 2048 * ntl, list(range(4 * G, 4 * G + 4))))
        kc_fn = lambda j, K, s: kcmpT[0:64, j * 128:(j + 1) * 128]
        vc_fn = lambda j, s: vcmp[:, j, :]
        vcA_fn = lambda j, s: vcmpA[:, j, :]
        b7b = b7[:].bitcast(BF16)

        def compress(kh):
            P.op("dve", lambda e: e.memset(kcn, 0.0), writes=["kcn"])
            P.op("dve", lambda e: e.memset(vcmp, 0.0), writes=["vcmp"])
            P.op("dve", lambda e: e.memset(vcmp[:, :, 64:66], 1.0), reads=["vcmp"], writes=["vcmp"])
            P.op("dve", lambda e: e.memset(vcmpA[:, :, 64:66], 1.0), writes=["vcmpA"])
            P.op("dve", lambda e: e.tensor_copy(out=vcmpA[:, :, 0:64], in_=OVt[:]), reads=["vcmpA", "OV"], writes=["vcmpA"])
            for kv in range(2):
                P.dma("sp", kcR[0:64, :], t_QK.ap()[24 + 4 * kv + kh, 0:64, :], writes=["Hk"])
                P.dma("sp", posT[0:64, :], t_posT.ap()[kv], writes=["posT"])
                P.dma("pool", w1[0:64, :, :], t_w1.ap()[kv].rearrange("(t d) c -> d t c", d=64), writes=["w1"])
                P.dma("pool", w2[:, :, :], t_w2.ap()[kv].rearrange("(m p) d -> p m d", p=128), writes=["w2"])
                for ab, dstb in ((0, kcA), (1, kcB)):
                    P.op("dve", lambda e, ab=ab, dstb=dstb: e.tensor_tensor(
                        out=dstb[0:64, :].rearrange("p (n s) -> p n s", s=16), in0=kcR[0:64, :].rearrange("p (n s) -> p n s", s=16),
                        in1=posT[0:64, ab * 16:(ab + 1) * 16].unsqueeze(1).to_broadcast([64, 256, 16]), op=ALU.add),
                        reads=["Hk", "posT"], writes=[("kcAB", ab)] if ab == 1 else ["Hk"])
                kcAv = kcA[0:64, :].rearrange("p (n s) -> p n s", s=16)
                kcBv = kcB[0:64, :].rearrange("p (n s) -> p n s", s=16)
                for mc in range(2):
                    for t in range(32):
                        rhs = kcAv[:, 0:255, t] if t < 16 else kcBv[:, 1:256, t - 16]
                        P.op("pe", lambda e, t=t, mc=mc, rhs=rhs: e.matmul(out=b7[:, 0:255], lhsT=w1[0:64, t, mc * 128:(mc + 1) * 128],
                                                                       rhs=rhs, start=(t == 0), stop=(t == 31)),
                             reads=["Hk", ("kcAB", 1), "w1"], writes=[B7])
                    P.op("act", lambda e: e.activation(out=x2[:, 0:255], in_=b7[:, 0:255], func=AF.Square), reads=[B7], writes=["x2"])
                    P.op("dve", lambda e: e.tensor_scalar(out=x2[:, 0:255], in0=x2[:, 0:255], scalar1=0.044715, scalar2=1.0,
                                                           op0=ALU.mult, op1=ALU.add), reads=["x2"], writes=["x2"])
                    P.op("dve", lambda e: e.tensor_tensor(out=tg[:, 0:255], in0=b7[:, 0:255], in1=x2[:, 0:255], op=ALU.mult),
                         reads=[B7, "x2"], writes=["tg"])
                    P.op("act", lambda e: e.activation(out=tg[:, 0:255], in_=tg[:, 0:255], func=AF.Sigmoid, scale=1.5957691216),
                         reads=["tg"], writes=["tg"])
                    P.op("dve", lambda e, mc=mc: e.tensor_tensor(out=hid[:, mc, 0:255], in0=b7[:, 0:255], in1=tg[:, 0:255], op=ALU.mult),
                         reads=[B7, "tg"], writes=["hid"])
                for ntl in range(2):
                    nn = 128 if ntl == 0 else 127
                    for mc in range(2):
                        P.op("pe", lambda e, ntl=ntl, nn=nn, mc=mc: e.matmul(out=b7[0:nn, 256:320], lhsT=hid[:, mc, ntl * 128:ntl * 128 + nn],
                                                                         rhs=w2[:, mc, :], start=(mc == 0), stop=(mc == 1)),
                             reads=["hid", "w2"], writes=[B7])
                    if kv == 0:
                        ssq = small[:, 4:5]
                        P.op("act", lambda e, nn=nn: e.activation(out=x2[0:nn, 0:64], in_=b7[0:nn, 256:320], func=AF.Square, accum_out=ssq[0:nn, :]),
                             reads=[B7], writes=["x2", "ssq4"])
                        P.op("dve", lambda e: e.tensor_scalar(out=ssq, in0=ssq, scalar1=1.0 / DH, scalar2=1e-6, op0=ALU.mult, op1=ALU.add),
                             reads=["ssq4"], writes=["ssq4"])
                        P.op("act", lambda e: e.activation(out=ssq, in_=ssq, func=AF.Sqrt), reads=["ssq4"], writes=["ssq4"])
                        P.op("dve", lambda e: e.reciprocal(out=ssq, in_=ssq), reads=["ssq4"], writes=["ssq4"])
                        P.op("dve", lambda e, nn=nn, ntl=ntl: e.tensor_scalar(out=kcn[0:nn, ntl, 0:64], in0=b7[0:nn, 256:320], scalar1=ssq[0:nn, :],
                                                                           scalar2=None, op0=ALU.mult), reads=[B7, "ssq4"], writes=["kcn"])
                        P.op("pe", lambda e, ntl=ntl: e.transpose(out=b7b[:, 768:896], in_=kcn[:, ntl, :], identity=identb[:]),
                             reads=["kcn", "identb"], writes=[B7])
                        P.op("act", lambda e, ntl=ntl: e.copy(out=kcmpT[0:64, ntl * 128:(ntl + 1) * 128], in_=b7b[0:64, 768:896]),
                             reads=[B7], writes=["kcmpT"])
                    else:
                        P.op("act", lambda e, nn=nn, ntl=ntl: e.copy(out=vcmp[0:nn, ntl, 0:64], in_=b7[0:nn, 256:320]),
                             reads=[B7], writes=["vcmp"])
            P.op("dve", lambda e: e.memset(impb, 0.0), writes=["imp"])

        def epA(i, oa, okey):
            P.op("dve", lambda e, oa=oa: e.tensor_scalar(out=rs_t, in0=oa[:, 64:65], scalar1=1e-30, scalar2=None, op0=ALU.max),
                 reads=[okey], writes=["rs"])
            P.op("dve", lambda e: e.reciprocal(out=rs_t, in_=rs_t), reads=["rs"], writes=["rs"])
            P.op("dve", lambda e, oa=oa, i=i: e.scalar_tensor_tensor(out=impb[:, i, :], in0=oa[:, 0:64], scalar=rs_t, in1=impb[:, i, :],
                                                                   op0=ALU.mult, op1=ALU.add), reads=[okey, "rs", "imp"], writes=["imp"])

        def selection():
            for i in range(NT):
                P.op("dve", lambda e, i=i: e.tensor_tensor(out=impm, in0=impb[:, i, :], in1=STt[:, 64 - 2 * i:128 - 2 * i], op=ALU.add),
                     reads=["imp", "ST"], writes=["impm"])
                P.op("dve", lambda e: e.max(out=mx16[:, 0:8], in_=impm), reads=["impm"], writes=["mx16"])
                P.op("dve", lambda e: e.match_replace(out=mb128[:, 0:64], in_to_replace=mx16[:, 0:8], in_values=impm, imm_value=NEG),
                     reads=["impm", "mx16"], writes=["mb128"])
                P.op("dve", lambda e: e.max(out=mx16[:, 8:16], in_=mb128[:, 0:64]), reads=["mb128"], writes=["mx16"])
                P.op("dve", lambda e: e.tensor_scalar(out=mx16[:, 14:15], in0=mx16[:, 14:15], scalar1=-1.0e29, scalar2=None, op0=ALU.max),
                     reads=["mx16"], writes=["mx16"])
                P.op("dve", lambda e: e.tensor_scalar(out=mb128[:, 64:128], in0=impm, scalar1=mx16[:, 14:15], scalar2=None, op0=ALU.is_ge),
                     reads=["impm", "mx16", "mb128"], writes=["mb128"])
                P.op("dve", lambda e: e.tensor_scalar(out=mb128[:, 64:128], in0=mb128[:, 64:128], scalar1=-1.0, scalar2=BIG,
                                                       op0=ALU.add, op1=ALU.mult), reads=["mb128"], writes=["mb128"])
                P.op("dve", lambda e, i=i: e.memset(mb128[0:64, 64 + 2 * i:65 + 2 * i], 0.0), reads=["mb128"], writes=["mb128"])
                P.op("dve", lambda e, i=i: e.memset(mb128[64:128, 65 + 2 * i:66 + 2 * i], 0.0), reads=["mb128"], writes=["mb128"])
                P.op("pe", lambda e: e.transpose(out=b7[:, 0:128], in_=mb128, identity=cf[:, 0, :]), reads=["mb128", "cf"], writes=[B7])
                for s in range(2):
                    P.op("act", lambda e, i=i, s=s: e.copy(out=QA[s][64:128, i * 128:(i + 1) * 128], in_=b7[64:128, 0:128]),
                         reads=[B7], writes=[("qa_hi", s)])

        def cmp_run(u, ep, vfn, vkey):
            return dict(K=64, steps=stepcmp, ka_fn=kc_fn, va_fn=vfn, ep=ep,
                        kreads=(lambda s: ["kcmpT"]), vreads=(lambda s: [vkey]), qreads=(lambda s: [("qa", s)]),
                        strip=(t_gBc, u * GCLEN, 16, 4096), loads=(lambda s, u=u: ld_q(s, u)))

        for kh in range(4):
            compress(kh)
            runs = []
            for g in range(4):
                u = kh * 4 + g
                runs.append(cmp_run(u, epA, vcA_fn, "vcmpA"))
            runs[-1]["post"] = selection
            for g in range(4):
                u = kh * 4 + g
                runs.append(cmp_run(u, ep_plain(gate_col=u * 3 + 0), vc_fn, "vcmp"))
                r = base_run(128, std_steps(const_far=b31[:, u:u + 1]), ep_plain(gate_col=u * 3 + 1, accumulate=True), hi_k=True, hi_q=True)
                r["strip"] = (t_gB, u * GLEN, 1, 1664 + WOFF + 512 + 128)
                r["loads"] = (lambda s, u=u, kh=kh: (ld_k(s, 16 + kh, (t_E64, 0, 64)), ld_q(s, u), ld_v(s, kh)))
                runs.append(r)
                r = base_run(64, stepwin, ep_plain(gate_col=u * 3 + 2, accumulate=True))
                r["strip"] = (t_gBw, u * GLEN, 1, 512 + WOFF + 512 + 128)
                r["loads"] = (lambda s, u=u, kh=kh: (ld_k(s, 20 + kh), ld_q(s, u), ld_v(s, 4 + kh)))
                r["post"] = (lambda u=u: store_o(u))
                runs.append(r)
            execute(runs)
        P.barrier()

    SBv = {}
    SBv["bfb"] = SB("bfb", [128, H], F32)
    SBv["caq"] = SB("caq", [128, 96], BF16)
    SBv["cak"] = SB("cak", [128, 96], BF16)
    SBv["carry"] = SB("carry", [128, H], F32)
    SBv["fz"] = SB("fz", [128, 48], F32)
    SBv["imp"] = AB[:, 4096:4096 + 2 * NT * 64].bitcast(F32).rearrange("p (t d) -> p t d", d=64)
    SBv["ST"] = SB("STt", [128, 128], F32)
    SBv["OV"] = SB("OVt", [128, 2, 64], BF16)

    cur = t_x
    for idx, L in enumerate(layers):
        kind = L % 4
        lastl = (idx == len(layers) - 1)
        dstt = t_y if lastl else t_X[idx % 2]
        phase1(L, kind, cur)
        if stop == "p1":
            break
        if kind == 0:
            phase2_dil()
        elif kind == 1:
            phase2_nsa()
        elif kind == 2:
            phase2_fox()
        else:
            phase2_moba()
        if stop == "p2":
            for t in range(NT):
                tmpb = b16v(GA, (t % 2) * 2048, 1024)
                P.dma("sp", tmpb, t_MIX.ap()[t * 128:(t + 1) * 128, :], writes=[("dbg", t % 2)])
                tmpf = f32v(GA, 8192 + (t % 2) * 4096, 1024)
                P.op("dve", lambda e, tmpb=tmpb, tmpf=tmpf: e.tensor_copy(out=tmpf, in_=tmpb), reads=[("dbg", t % 2)], writes=[("dbgf", t % 2)])
                P.dma("sp", t_y.ap()[t * 128:(t + 1) * 128, :], tmpf, reads=[("dbgf", t % 2)])
            break
        phase3(L, cur, dstt)
        cur = dstt
    P.barrier()
    P.emit()
    return nc


def _rel_bucket_np(d):
    d = np.maximum(d, 0)
    df = np.maximum(d.astype(np.float32), np.float32(1.0))
    large = 16 + (np.log(df / np.float32(16)) / np.float32(math.log(2048 / 16)) * np.float32(16)).astype(np.int32)
    large = np.minimum(large, 31)
    return np.where(d < 16, d, large)


def _host_tables(rel_table):
    rel_table = np.asarray(rel_table, dtype=np.float32)
    m = np.arange(GLEN)
    d = m - 511
    bk = _rel_bucket_np(d)
    gat = rel_table[bk, :].T.copy()
    valid = d >= 0
    gB = np.where(valid[None, :], gat, np.float32(NEG)).astype(np.float32)
    gBw = np.where((valid & (d <= 511))[None, :], gat, np.float32(NEG)).astype(np.float32)
    cnt = ((d <= 128) & valid).astype(np.int32) + ((d % 4 == 0) & (d <= 512) & valid) + ((d % 16 == 0) & (d <= 2048) & valid)
    gBd = np.where((cnt > 0)[None, :], gat, np.float32(NEG)).astype(np.float32)
    lnc = np.where(cnt > 0, np.log(np.maximum(cnt, 1)), 0.0).astype(np.float32)[None, :]
    g0 = np.where(valid, 0.0, NEG).astype(np.float32)[None, :]
    mc = np.arange(GCLEN)
    dc = mc - 2063
    gatc = rel_table[_rel_bucket_np(dc), :].T.copy()
    gBc = np.where((dc >= 0)[None, :], gatc, np.float32(NEG)).astype(np.float32)
    cf = np.zeros((128, 4, 128), np.float32)
    cf[:, 0, :] = np.eye(128)
    cf[:, 1, :] = np.eye(128)[::-1]
    cf[:, 2, :] = np.triu(np.ones((128, 128)))
    cf[:, 3, :] = 1.0
    identb = np.eye(128).astype(ml_dtypes.bfloat16)
    tok = np.arange(S)
    E16 = (tok[None, :] // 256 == np.arange(16)[:, None]).astype(ml_dtypes.bfloat16)
    E64 = (tok[None, :] // 64 == np.arange(64)[:, None]).astype(ml_dtypes.bfloat16)
    n = np.arange(256)
    j = np.arange(64)
    ov = ((16 * n[:, None] < 64 * j[None, :] + 64) & (16 * n[:, None] + 32 > 64 * j[None, :]) & (n[:, None] < 255))
    OV = ov.reshape(2, 128, 64).transpose(1, 0, 2).astype(ml_dtypes.bfloat16).copy()
    r = (np.arange(128) >= 64).astype(np.int32)
    c = np.arange(128)
    ST = np.where(c[None, :] < 64 + r[:, None], 0.0, NEG).astype(np.float32)
    assert (_rel_bucket_np(np.arange(1537, 8192)) == 31).all()
    return dict(rel31=np.ascontiguousarray(rel_table[31:32, :]), gB=gB, gBw=gBw, gBd=gBd, gBc=gBc, g0=g0, lnc=lnc, cf32=cf, identb=identb, E16=E16, E64=E64, OV=OV, ST=ST)


_PROG_CACHE = {}


def _get_prog(layers):
    key = tuple(layers)
    if key not in _PROG_CACHE:
        import os
        _PROG_CACHE[key] = build_program(list(layers), stop=os.environ.get("KSTOP"))
    return _PROG_CACHE[key]


def _common_inputs(rel_table, attn_norm, mlp_norm, q_gain, k_gain, w_out, mlp_w_up, mlp_w_down, dsa_w_in, nsa_w_in,
                   nsa_cmp_pos, nsa_cmp_w1, nsa_cmp_w2, fox_w_in, fox_b_f, moba_w_in):
    f = lambda a: np.ascontiguousarray(np.asarray(a, dtype=np.float32))
    m = dict(attn_norm=f(attn_norm), mlp_norm=f(mlp_norm), q_gain=f(q_gain), k_gain=f(k_gain), w_out=f(w_out),
             mlp_w_up=f(mlp_w_up), mlp_w_down=f(mlp_w_down), dsa_w_in=f(dsa_w_in)[0], nsa_w_in=f(nsa_w_in)[0],
             fox_w_in=f(fox_w_in)[0], moba_w_in=f(moba_w_in)[0],
             nsa_posT=np.ascontiguousarray(f(nsa_cmp_pos)[0].transpose(0, 2, 1)),
             nsa_cmp_w1=f(nsa_cmp_w1)[0], nsa_cmp_w2=f(nsa_cmp_w2)[0], fox_b_f=f(fox_b_f))
    m.update(_host_tables(rel_table))
    return m


def run_layers(layers, x, n_cores=8, **params):
    nc = _get_prog(layers)
    common = _common_inputs(**params)
    x = np.asarray(x, dtype=np.float32)
    in_maps = []
    for c in range(n_cores):
        mm = dict(common)
        mm["x"] = np.ascontiguousarray(x[c % x.shape[0]])
        in_maps.append(mm)
    res = run_bass_kernel_spmd(nc, in_maps, core_ids=list(range(n_cores)))
    return res


def kernel(x, rel_table, attn_norm, mlp_norm, q_gain, k_gain, w_out, mlp_w_up, mlp_w_down,
           dsa_w_in, nsa_w_in, nsa_cmp_pos, nsa_cmp_w1, nsa_cmp_w2, fox_w_in, fox_b_f, moba_w_in):
    params = dict(rel_table=rel_table, attn_norm=attn_norm, mlp_norm=mlp_norm, q_gain=q_gain, k_gain=k_gain,
                  w_out=w_out, mlp_w_up=mlp_w_up, mlp_w_down=mlp_w_down, dsa_w_in=dsa_w_in, nsa_w_in=nsa_w_in,
                  nsa_cmp_pos=nsa_cmp_pos, nsa_cmp_w1=nsa_cmp_w1, nsa_cmp_w2=nsa_cmp_w2, fox_w_in=fox_w_in,
                  fox_b_f=fox_b_f, moba_w_in=moba_w_in)
    res = run_layers([0, 1, 2, 3], x, n_cores=8, **params)
    out = np.stack([np.asarray(res.results[b]["y"], dtype=np.float32) for b in range(4)], axis=0)
    return out
```

```python
import math
import os
import contextlib
import numpy as np
import ml_dtypes
import concourse.bass as bass
import concourse.mybir as mybir
from concourse.bass_utils import run_bass_kernel_spmd

F32 = mybir.dt.float32
BF16 = mybir.dt.bfloat16
ALU = mybir.AluOpType
AF = mybir.ActivationFunctionType
AX = mybir.AxisListType

S = 4096
D = 1024
H = 16
DH = 64
NT = 32
DFF = 4096
NEG = -1.0e30
BIG = 30000.0
WU = 4480
WOFF = 384
GLEN = WU + 128
GCLEN = 4096 + 2032 + 16
NCOLS = {0: 3072, 1: 2608, 2: 3088, 3: 3072}

ENGS = ["pe", "act", "dve", "pool", "sp"]


class Prog:
    def __init__(self, nc, ndma=12):
        self.nc = nc
        self.streams = {e: [] for e in ENGS}
        self.cnt = {e: 0 for e in ENGS}
        self.sems = {}
        self.stack = contextlib.ExitStack()
        for e in ENGS:
            self.sems[e] = self.stack.enter_context(nc.semaphore("s_" + e))
        self.dsem = []
        for i in range(2 * ndma):
            self.dsem.append([self.stack.enter_context(nc.semaphore("d_%d" % i)), 0])
        self.ndma = ndma
        self.dnext = {"sp": 0, "pool": 0, "act": 0}
        self.waited = {e: {} for e in ENGS}
        self.lastw = {}
        self.readers = {}

    def _sem(self, key):
        return self.sems[key] if isinstance(key, str) else self.dsem[key[1]][0]

    def _wait(self, eng, tok):
        key, val, src = tok
        if src == "pe" and eng == "pe":
            return
        w = self.waited[eng]
        if w.get(key, 0) >= val:
            return
        w[key] = val
        self.streams[eng].append(("w", self._sem(key), val))

    def _deps(self, eng, reads, writes):
        for r in reads:
            t = self.lastw.get(r)
            if t is not None:
                self._wait(eng, t)
        for w in writes:
            t = self.lastw.get(w)
            if t is not None:
                self._wait(eng, t)
            for t in self.readers.get(w, ()):
                self._wait(eng, t)

    def _commit(self, tok, reads, writes):
        for w in writes:
            self.lastw[w] = tok
            self.readers[w] = []
        for r in reads:
            lst = self.readers.setdefault(r, [])
            lst.append(tok)
            if len(lst) > 24:
                best = {}
                for t in lst:
                    k = t[0]
                    if k not in best or best[k][1] < t[1]:
                        best[k] = t
                self.readers[r] = list(best.values())

    def op(self, eng, fn, reads=(), writes=(), crit=None):
        self._deps(eng, reads, writes)
        self.cnt[eng] += 1
        tok = (eng, self.cnt[eng], eng)
        cw = None
        if crit is not None:
            ct = self.lastw.get(crit)
            if ct is not None and ct[2] != eng:
                cw = (self._sem(ct[0]), ct[1])
        self.streams[eng].append(("o", fn, self.sems[eng], 1, cw))
        self._commit(tok, reads, writes)
        return tok

    def dma(self, q, out, in_, reads=(), writes=()):
        self._deps(q, reads, writes)
        base = self.ndma if q == "pool" else 0
        i = base + self.dnext[q]
        self.dnext[q] = (self.dnext[q] + 1) % self.ndma
        if self.dsem[i][1] > 0:
            self._wait(q, (("d", i), self.dsem[i][1], "dma"))
        self.dsem[i][1] += 16
        tok = (("d", i), self.dsem[i][1], "dma")
        self.streams[q].append(("o", lambda e, o=out, s=in_: e.dma_start(out=o, in_=s), self.dsem[i][0], 16))
        self._commit(tok, reads, writes)
        return tok

    def barrier(self):
        for e in ENGS:
            for e2 in ENGS:
                if e2 != e and self.cnt[e2] > 0:
                    self._wait(e, (e2, self.cnt[e2], e2))
            for i, (s, v) in enumerate(self.dsem):
                if v > 0:
                    self._wait(e, (("d", i), v, "dma"))
        self.lastw = {}
        self.readers = {}

    def emit(self):
        nc = self.nc
        streams = self.streams

        def replay(eng, items):
            for it in items:
                if it[0] == "w":
                    eng.wait_ge(it[1], it[2])
                else:
                    ins = it[1](eng)
                    if len(it) > 4 and it[4] is not None:
                        ins = ins._wait_ge(it[4][0], it[4][1])
                    ins.then_inc(it[2], it[3])

        with nc.Block() as block:
            @block.tensor
            def _(e):
                replay(e, streams["pe"])

            @block.scalar
            def _(e):
                replay(e, streams["act"])

            @block.vector
            def _(e):
                replay(e, streams["dve"])

            @block.gpsimd
            def _(e):
                replay(e, streams["pool"])

            @block.sync
            def _(e):
                replay(e, streams["sp"])


def bcast_rows(t, off, n):
    return bass.AP(t, off, [[0, 128], [1, n]])


def build_program(layers, stop=None):
    nc = bass.Bass("TRN2", target_bir_lowering=False)
    es = contextlib.ExitStack()

    def din(name, shape, dt=F32):
        return nc.dram_tensor(name, list(shape), dt, kind="ExternalInput")

    t_x = din("x", [S, D])
    t_an = din("attn_norm", [4, D]); t_mn = din("mlp_norm", [4, D])
    t_qg = din("q_gain", [4, DH]); t_kg = din("k_gain", [4, DH])
    t_wout = din("w_out", [4, D, D]); t_wup = din("mlp_w_up", [4, D, DFF]); t_wdn = din("mlp_w_down", [4, DFF, D])
    t_win = {0: din("dsa_w_in", [D, 3072]), 1: din("nsa_w_in", [D, 2608]),
             2: din("fox_w_in", [D, 3088]), 3: din("moba_w_in", [D, 3072])}
    t_posT = din("nsa_posT", [2, DH, 32])
    t_w1 = din("nsa_cmp_w1", [2, 2048, 256]); t_w2 = din("nsa_cmp_w2", [2, 256, DH])
    t_bf = din("fox_b_f", [1, H])
    t_r31 = din("rel31", [1, H])
    t_gB = din("gB", [H, GLEN]); t_gBw = din("gBw", [H, GLEN]); t_gBd = din("gBd", [H, GLEN])
    t_gBc = din("gBc", [H, GCLEN]); t_g0 = din("g0", [1, GLEN]); t_lnc = din("lnc", [1, GLEN])
    t_cf = din("cf32", [128, 4, 128]); t_idb = din("identb", [128, 128], BF16)
    t_E16 = din("E16", [16, S], BF16); t_E64 = din("E64", [64, S], BF16)
    t_OV = din("OV", [128, 2, 64], BF16); t_ST = din("ST", [128, 128])
    t_y = nc.dram_tensor("y", [S, D], F32, kind="ExternalOutput")
    t_X = [nc.dram_tensor("xs%d" % i, [S, D], F32, kind="Internal") for i in range(2)]
    t_QK = nc.dram_tensor("qk", [32, 128, S], BF16, kind="Internal")
    t_V = nc.dram_tensor("vv", [16, S, DH], BF16, kind="Internal")
    t_MIX = nc.dram_tensor("mix", [S, D], BF16, kind="Internal")
    t_AUG = nc.dram_tensor("aug", [192, S], BF16, kind="Internal")
    t_gD = nc.dram_tensor("gD", [H, GLEN], F32, kind="Internal")

    def SB(name, shape, dt):
        return es.enter_context(nc.sbuf_tensor("sb_" + name, list(shape), dt))

    banks = [es.enter_context(nc.psum_tensor("bank%d" % i, [128, 512], F32)) for i in range(8)]
    AA = SB("arenaA", [128, 32768], BF16)
    AB = SB("arenaB", [128, 32768], BF16)
    WO = SB("wout", [128, 8, 1024], BF16)
    identb = SB("identb", [128, 128], BF16)
    cf = SB("cf", [128, 4, 128], F32)
    ones_b = SB("ones_b", [128, 128], BF16)
    gbc_a = SB("gbc_a", [128, D], F32); gbc_m = SB("gbc_m", [128, D], F32)
    gq = SB("gq", [128, DH], F32); gk = SB("gk", [128, DH], F32); gqk = SB("gqk", [128, DH], F32)
    small = SB("small", [128, 64], F32)
    b31 = SB("b31", [128, H], F32)
    GA = SB("genA", [128, 21760], BF16)

    P = Prog(nc)
    gates = AB[:, 0:2 * NT * 48].bitcast(F32).rearrange("p (t c) -> p t c", c=48)

    def bf(ap):
        return ap

    def f32v(arena, off_b, n):
        return arena[:, off_b // 2: off_b // 2 + 2 * n].bitcast(F32)

    def b16v(arena, off_b, n):
        return arena[:, off_b // 2: off_b // 2 + n]

    P.dma("sp", identb[:], t_idb.ap()[:, :], writes=["identb"])
    P.dma("sp", cf[:], t_cf.ap()[:, :, :], writes=["cf"])
    P.op("dve", lambda e: e.memset(ones_b[:], 1.0), writes=["ones_b"])
    P.dma("sp", b31[:], bcast_rows(t_r31, 0, H), writes=["b31"])
    P.op("dve", lambda e: e.memset(small[:], 0.0), writes=["ssq", "rstd", "ssq8", "rs", "ssq4"])

    def phase1(L, kind, t_src):
        ncol = NCOLS[kind]
        wv = AA[:, 0:8 * ncol].rearrange("p (k n) -> p k n", n=ncol)
        win = t_win[kind].ap()
        for kc in range(8):
            P.dma("pool", wv[:, kc, :], win[kc * 128:(kc + 1) * 128, :], writes=[("win", kc)])
        P.dma("pool", WO[:, :, :], t_wout.ap()[L].rearrange("(k p) n -> p k n", p=128), writes=["wo"])
        P.dma("sp", gbc_a[:], bcast_rows(t_an, L * D, D), writes=["gbc_a"])
        P.dma("sp", gbc_m[:], bcast_rows(t_mn, L * D, D), writes=["gbc_m"])
        P.dma("sp", gq[:], bcast_rows(t_qg, L * DH, DH), writes=["gq"])
        P.dma("sp", gk[:], bcast_rows(t_kg, L * DH, DH), writes=["gk"])
        P.op("dve", lambda e: e.tensor_tensor(out=gqk[:], in0=gq[:], in1=gk[:], op=ALU.mult),
             reads=["gq", "gk"], writes=["gqk"])

        xt = [f32v(GA, 0, 1024), f32v(GA, 4096, 1024)]
        junk = b16v(GA, 8192, 1024)
        hb = b16v(GA, 10240, 1024)
        hT = [b16v(GA, 12288, 1024), b16v(GA, 14336, 1024)]
        sq = f32v(GA, 16384, 512)
        tm = b16v(GA, 18432, 18 * 128)
        vm = b16v(GA, 23040, 1024)
        tst = b16v(GA, 25088, 18 * 512).rearrange("p (b n) -> p b n", n=512)
        ssq = small[:, 0:1]; rstd = small[:, 1:2]; ssq8 = small[:, 8:16]
        if kind == 2:
            bfb = SBv["bfb"]
            caq = SBv["caq"]; cak = SBv["cak"]; carry = SBv["carry"]; fz = SBv["fz"]
            P.dma("sp", bfb[:], bcast_rows(t_bf, 0, H), writes=["bfb"])
            P.op("dve", lambda e: e.memset(carry[:], 0.0), writes=["carry"])
            P.op("dve", lambda e: e.memset(caq[:], 1.0), writes=["caq"])
            P.op("dve", lambda e: e.memset(cak[:], 1.0), writes=["cak"])
            P.op("dve", lambda e: e.memset(tm[:, 2048:2304], 0.0), writes=["tm"])

        if kind == 1:
            chunks = [(0, 512, [(0, 512, "q", 0)]), (512, 512, [(0, 512, "q", 512)]),
                      (1024, 512, [(0, 256, "c", 1536), (256, 256, "c", 1792)]),
                      (1536, 512, [(0, 256, "k", 1024), (256, 256, "v", 0)]),
                      (2048, 512, [(0, 256, "k", 1280), (256, 256, "v", 256)]),
                      (2560, 48, [(0, 48, "g", 0)])]
            nv = 8
        else:
            chunks = [(0, 512, [(0, 512, "q", 0)]), (512, 512, [(0, 512, "q", 512)]),
                      (1024, 512, [(0, 512, "k", 1024)]), (1536, 512, [(0, 512, "k", 1536)]),
                      (2048, 512, [(0, 512, "v", 0)]), (2560, 512, [(0, 512, "v", 512)])]
            if kind == 2:
                chunks.append((3072, 16, [(0, 16, "f", 0)]))
            nv = 16
        nblk = 18 if kind == 2 else 16
        xsrc = t_src.ap()
        cb = 0
        for t in range(NT):
            tt = t % 4
            g = t // 4
            s = t % 2
            P.dma("sp", xt[s], xsrc[t * 128:(t + 1) * 128, :], writes=[("xt", s)])
            P.op("act", lambda e, s=s: e.activation(out=junk, in_=xt[s], func=AF.Square, accum_out=ssq),
                 reads=[("xt", s)], writes=["junk", "ssq"])
            P.op("dve", lambda e: e.tensor_scalar(out=rstd, in0=ssq, scalar1=1.0 / D, scalar2=1e-6,
                                                   op0=ALU.mult, op1=ALU.add), reads=["ssq"], writes=["rstd"])
            P.op("act", lambda e: e.activation(out=rstd, in_=rstd, func=AF.Sqrt), reads=["rstd"], writes=["rstd"])
            P.op("dve", lambda e: e.reciprocal(out=rstd, in_=rstd), reads=["rstd"], writes=["rstd"])
            P.op("dve", lambda e, s=s: e.scalar_tensor_tensor(out=hb, in0=xt[s], scalar=rstd, in1=gbc_a[:],
                                                              op0=ALU.mult, op1=ALU.mult),
                 reads=[("xt", s), "rstd", "gbc_a"], writes=["hb"])
            pT = banks[3][:].bitcast(BF16)
            for kc in range(8):
                P.op("pe", lambda e, kc=kc: e.transpose(out=pT[:, kc * 128:(kc + 1) * 128],
                                                         in_=hb[:, kc * 128:(kc + 1) * 128], identity=identb[:]),
                     reads=["hb", "identb"], writes=["b3"])
            P.op("act", lambda e, s=s: e.copy(out=hT[s], in_=pT), reads=["b3"], writes=[("hT", s)])
            hTs = hT[s].rearrange("p (k n) -> p k n", n=128)
            for (c0, n, segs) in chunks:
                bk = banks[cb % 3]; bkey = ("pb", cb % 3); cb += 1
                for kc in range(8):
                    P.op("pe", lambda e, kc=kc, bk=bk, c0=c0, n=n, hTs=hTs: e.matmul(
                        out=bk[:, 0:n], lhsT=hTs[:, kc, :], rhs=wv[:, kc, c0:c0 + n], start=(kc == 0), stop=(kc == 7)),
                        reads=[("hT", s), ("win", kc)], writes=[bkey], crit=("hT", s))
                need_norm = any(sg[2] in ("q", "k") for sg in segs)
                if need_norm:
                    nh = n // 64
                    P.op("act", lambda e, bk=bk, n=n: e.activation(out=sq[:, 0:n], in_=bk[:, 0:n], func=AF.Square),
                         reads=[bkey], writes=["sq"])
                    P.op("dve", lambda e, n=n, nh=nh: e.tensor_reduce(
                        out=ssq8[:, 0:nh], in_=sq[:, 0:n].rearrange("p (a b) -> p a b", b=64), axis=AX.X, op=ALU.add),
                        reads=["sq"], writes=["ssq8"])
                    P.op("dve", lambda e, nh=nh: e.tensor_scalar(out=ssq8[:, 0:nh], in0=ssq8[:, 0:nh], scalar1=1.0 / DH,
                                                                  scalar2=1e-6, op0=ALU.mult, op1=ALU.add),
                         reads=["ssq8"], writes=["ssq8"])
                    P.op("act", lambda e, nh=nh: e.activation(out=ssq8[:, 0:nh], in_=ssq8[:, 0:nh], func=AF.Sqrt),
                         reads=["ssq8"], writes=["ssq8"])
                    P.op("dve", lambda e, nh=nh: e.reciprocal(out=ssq8[:, 0:nh], in_=ssq8[:, 0:nh]),
                         reads=["ssq8"], writes=["ssq8"])
                for (so, sn, typ, dc) in segs:
                    if typ in ("q", "k"):
                        h0 = so // 64; nh = sn // 64
                        dst = tm[:, dc:dc + sn].rearrange("p (a b) -> p a b", b=64)
                        if typ == "k":
                            P.op("dve", lambda e, bk=bk, so=so, sn=sn, h0=h0, nh=nh, dst=dst: e.tensor_tensor(
                                out=dst, in0=bk[:, so:so + sn].rearrange("p (a b) -> p a b", b=64),
                                in1=ssq8[:, h0:h0 + nh].unsqueeze(2).to_broadcast([128, nh, 64]), op=ALU.mult),
                                reads=[bkey, "ssq8"], writes=["tm"])
                        else:
                            sqv = sq[:, so:so + sn].rearrange("p (a b) -> p a b", b=64)
                            P.op("dve", lambda e, bk=bk, so=so, sn=sn, h0=h0, nh=nh, sqv=sqv: e.tensor_tensor(
                                out=sqv, in0=bk[:, so:so + sn].rearrange("p (a b) -> p a b", b=64),
                                in1=ssq8[:, h0:h0 + nh].unsqueeze(2).to_broadcast([128, nh, 64]), op=ALU.mult),
                                reads=[bkey, "ssq8", "sq"], writes=["sq"])
                            P.op("dve", lambda e, nh=nh, sqv=sqv, dst=dst: e.tensor_tensor(
                                out=dst, in0=sqv, in1=gqk[:].unsqueeze(1).to_broadcast([128, nh, 64]), op=ALU.mult),
                                reads=["sq", "gqk"], writes=["tm"])
                    elif typ == "c":
                        P.op("act", lambda e, bk=bk, so=so, sn=sn, dc=dc: e.copy(out=tm[:, dc:dc + sn], in_=bk[:, so:so + sn]),
                             reads=[bkey], writes=["tm"])
                    elif typ == "v":
                        P.op("act", lambda e, bk=bk, so=so, sn=sn, dc=dc: e.copy(out=vm[:, dc:dc + sn], in_=bk[:, so:so + sn]),
                             reads=[bkey], writes=["vm"])
                    elif typ == "g":
                        P.op("act", lambda e, bk=bk, t=t: e.activation(out=gates[:, t, :], in_=bk[:, 0:48], func=AF.Sigmoid),
                             reads=[bkey], writes=["gates"])
                    elif typ == "f" and os.environ.get("KF_SKIP") == "1":
                        pass
                    elif typ == "f":
                        P.op("dve", lambda e, bk=bk: e.tensor_tensor(out=fz[:, 0:16], in0=bk[:, 0:16], in1=bfb[:], op=ALU.add),
                             reads=[bkey, "bfb"], writes=["fz"])
                        P.op("act", lambda e: e.activation(out=fz[:, 0:16], in_=fz[:, 0:16], func=AF.Exp, scale=-1.0),
                             reads=["fz"], writes=["fz"])
                        P.op("act", lambda e: e.activation(out=fz[:, 0:16], in_=fz[:, 0:16], func=AF.Ln, bias=1.0),
                             reads=["fz"], writes=["fz"])
                        b7 = banks[7]
                        P.op("pe", lambda e: e.matmul(out=b7[:, 0:16], lhsT=cf[:, 2, :], rhs=fz[:, 0:16], start=True, stop=True),
                             reads=["fz", "cf"], writes=["b7"])
                        P.op("pe", lambda e: e.matmul(out=b7[:, 16:32], lhsT=cf[:, 3, :], rhs=fz[:, 0:16], start=True, stop=True),
                             reads=["fz", "cf"], writes=["b7"])
                        P.op("dve", lambda e: e.tensor_tensor(out=fz[:, 16:32], in0=b7[:, 0:16], in1=carry[:], op=ALU.add),
                             reads=["b7", "carry", "fz"], writes=["fz2"])
                        P.op("dve", lambda e: e.tensor_tensor(out=carry[:], in0=b7[:, 16:32], in1=carry[:], op=ALU.add),
                             reads=["b7", "carry", "fz2"], writes=["carry"])
                        P.op("dve", lambda e: e.tensor_scalar(out=fz[:, 16:32], in0=fz[:, 16:32], scalar1=-8.0, scalar2=None,
                                                               op0=ALU.mult), reads=["fz2"], writes=["fz2"])
                        caqv = caq[:].rearrange("p (r h) -> p r h", h=16)
                        cakv = cak[:].rearrange("p (r h) -> p r h", h=16)
                        cur = fz[:, 16:32]; nxt = fz[:, 32:48]
                        for r in range(3):
                            P.op("dve", lambda e, r=r, cur=cur: e.tensor_copy(out=caqv[:, r, :], in_=cur),
                                 reads=["fz2"], writes=["caq"])
                            P.op("dve", lambda e, r=r: e.tensor_scalar(out=cakv[:, 3 + r, :], in0=caqv[:, r, :], scalar1=-1.0,
                                                                        scalar2=None, op0=ALU.mult),
                                 reads=["caq"], writes=["cak"])
                            if r < 2:
                                P.op("dve", lambda e, r=r, cur=cur, nxt=nxt: e.tensor_tensor(out=nxt, in0=cur, in1=caqv[:, r, :],
                                                                                             op=ALU.subtract),
                                     reads=["fz2", "caq"], writes=["fz2"])
                                cur, nxt = nxt, cur
                        P.op("dve", lambda e: e.tensor_copy(out=tm[:, 2048:2144], in_=caq[:]), reads=["caq"], writes=["tm"])
                        P.op("dve", lambda e: e.tensor_copy(out=tm[:, 2176:2272], in_=cak[:]), reads=["cak"], writes=["tm"])
            for half in range((nblk + 7) // 8):
                pb = banks[4 + half][:].bitcast(BF16)
                nb = min(8, nblk - half * 8)
                for b in range(nb):
                    blk = half * 8 + b
                    P.op("pe", lambda e, pb=pb, b=b, blk=blk: e.transpose(out=pb[:, b * 128:(b + 1) * 128],
                                                                           in_=tm[:, blk * 128:(blk + 1) * 128], identity=identb[:]),
                         reads=["tm", "identb"], writes=[("b4", half)])
                dstv = tst[:, half * 8:half * 8 + nb, tt * 128:(tt + 1) * 128]
                srcv = pb[:, 0:nb * 128].rearrange("p (b n) -> p b n", n=128)
                eng = "act" if half == 0 else "dve"
                if eng == "act":
                    P.op("act", lambda e, dstv=dstv, srcv=srcv: e.copy(out=dstv, in_=srcv),
                         reads=[("b4", half)], writes=[("tst", half)])
                else:
                    P.op("dve", lambda e, dstv=dstv, srcv=srcv: e.tensor_copy(out=dstv, in_=srcv),
                         reads=[("b4", half)], writes=[("tst", half)])
            vdst = bass.AP(t_V, t * 128 * DH, [[DH, 128], [S * DH, nv], [1, DH]])
            P.dma("pool", vdst, vm[:, 0:nv * 64].rearrange("p (u d) -> p u d", d=64), reads=["vm"])
            if tt == 3:
                for half in range(2):
                    qdst = bass.AP(t_QK, half * 128 * S + g * 512, [[S, 64], [2 * 128 * S, 16], [1, 512]])
                    P.dma("pool", qdst, tst[half * 64:(half + 1) * 64, 0:16, :], reads=[("tst", 0), ("tst", 1)])
                if kind == 2:
                    for qk in range(2):
                        adst = bass.AP(t_AUG, qk * 96 * S + g * 512, [[S, 96], [1, 512]])
                        P.dma("pool", adst, tst[0:96, 16 + qk, :], reads=[("tst", 2)])
        P.barrier()

    def phase3(L, t_src, t_dst):
        wu = AA[:, :].rearrange("p (k n) -> p k n", n=DFF)
        wup = t_wup.ap()[L].rearrange("(k p) n -> p k n", p=128)
        for kc in range(8):
            P.dma("pool", wu[:, kc, :], wup[:, kc, :], writes=[("wup", kc)])
        wd = AB[:, :].rearrange("p (k n) -> p k n", n=1024)
        wdn = t_wdn.ap()[L].rearrange("(k p) n -> p k n", p=128)
        for q4 in range(4):
            P.dma("pool", wd[:, q4 * 8:(q4 + 1) * 8, :], wdn[:, q4 * 8:(q4 + 1) * 8, :], writes=[("wdn", q4)])
        ot = [b16v(GA, 0, 1024), b16v(GA, 2048, 1024)]
        xr = [f32v(GA, 4096, 1024), f32v(GA, 8192, 1024)]
        oT = b16v(GA, 12288, 1024)
        h2 = b16v(GA, 14336, 1024)
        h2T = b16v(GA, 16384, 1024)
        sq = f32v(GA, 18432, 512)
        uT = b16v(GA, 20480, 32 * 128).rearrange("p (f n) -> p f n", n=128)
        xo = [f32v(GA, 28672, 1024), f32v(GA, 32768, 1024)]
        junk = b16v(GA, 36864, 1024)
        ssq = small[:, 0:1]; rstd = small[:, 1:2]
        src = t_src.ap(); dst = t_dst.ap(); mix = t_MIX.ap()
        cb = 0
        for t in range(NT):
            s = t % 2
            P.dma("sp", ot[s], mix[t * 128:(t + 1) * 128, :], writes=[("ot", s)])
            P.dma("sp", xr[s], src[t * 128:(t + 1) * 128, :], writes=[("xr", s)])
            pT = banks[3][:].bitcast(BF16)
            for kc in range(8):
                P.op("pe", lambda e, kc=kc, s=s: e.transpose(out=pT[:, kc * 128:(kc + 1) * 128],
                                                              in_=ot[s][:, kc * 128:(kc + 1) * 128], identity=identb[:]),
                     reads=[("ot", s), "identb"], writes=["b3"])
            P.op("act", lambda e: e.copy(out=oT, in_=pT), reads=["b3"], writes=["oT"])
            oTv = oT.rearrange("p (k n) -> p k n", n=128)
            for c in range(2):
                bk = banks[cb % 3]; bkey = ("pb", cb % 3); cb += 1
                for kc in range(8):
                    P.op("pe", lambda e, kc=kc, bk=bk, c=c: e.matmul(out=bk[:, :], lhsT=oTv[:, kc, :], rhs=WO[:, kc, c * 512:(c + 1) * 512],
                                                                    start=(kc == 0), stop=(kc == 7)),
                         reads=["oT", "wo"], writes=[bkey], crit="oT")
                P.op("dve", lambda e, bk=bk, c=c, s=s: e.tensor_tensor(out=xr[s][:, c * 512:(c + 1) * 512], in0=bk[:, :],
                                                                      in1=xr[s][:, c * 512:(c + 1) * 512], op=ALU.add),
                     reads=[bkey, ("xr", s)], writes=[("xr", s)])
            P.op("act", lambda e, s=s: e.activation(out=junk, in_=xr[s], func=AF.Square, accum_out=ssq),
                 reads=[("xr", s)], writes=["junk", "ssq"])
            P.op("dve", lambda e: e.tensor_scalar(out=rstd, in0=ssq, scalar1=1.0 / D, scalar2=1e-6,
                                                   op0=ALU.mult, op1=ALU.add), reads=["ssq"], writes=["rstd"])
            P.op("act", lambda e: e.activation(out=rstd, in_=rstd, func=AF.Sqrt), reads=["rstd"], writes=["rstd"])
            P.op("dve", lambda e: e.reciprocal(out=rstd, in_=rstd), reads=["rstd"], writes=["rstd"])
            P.op("dve", lambda e, s=s: e.scalar_tensor_tensor(out=h2, in0=xr[s], scalar=rstd, in1=gbc_m[:],
                                                              op0=ALU.mult, op1=ALU.mult),
                 reads=[("xr", s), "rstd", "gbc_m"], writes=["h2"])
            pT2 = banks[4][:].bitcast(BF16)
            for kc in range(8):
                P.op("pe", lambda e, kc=kc: e.transpose(out=pT2[:, kc * 128:(kc + 1) * 128],
                                                         in_=h2[:, kc * 128:(kc + 1) * 128], identity=identb[:]),
                     reads=["h2", "identb"], writes=["b4"])
            P.op("act", lambda e: e.copy(out=h2T, in_=pT2), reads=["b4"], writes=["h2T"])
            h2Tv = h2T.rearrange("p (k n) -> p k n", n=128)
            for fq in range(8):
                bk = banks[cb % 3]; bkey = ("pb", cb % 3); cb += 1
                for fi in range(4):
                    fc = fq * 4 + fi
                    for kc in range(8):
                        P.op("pe", lambda e, kc=kc, bk=bk, fc=fc, fi=fi: e.matmul(
                            out=bk[:, fi * 128:(fi + 1) * 128], lhsT=wu[:, kc, fc * 128:(fc + 1) * 128], rhs=h2Tv[:, kc, :],
                            start=(kc == 0), stop=(kc == 7)), reads=["h2T", ("wup", kc)], writes=[bkey])
                P.op("act", lambda e, bk=bk: e.activation(out=sq, in_=bk[:, :], func=AF.Square), reads=[bkey], writes=["sq"])
                P.op("dve", lambda e, bk=bk, fq=fq: e.scalar_tensor_tensor(
                    out=uT[:, fq * 4:(fq + 1) * 4, :], in0=bk[:, :].rearrange("p (f n) -> p f n", n=128), scalar=0.0,
                    in1=sq.rearrange("p (f n) -> p f n", n=128), op0=ALU.is_gt, op1=ALU.mult),
                    reads=[bkey, "sq"], writes=[("uT", fq)])
            for c in range(2):
                bk = banks[cb % 3]; bkey = ("pb", cb % 3); cb += 1
                for fc in range(32):
                    P.op("pe", lambda e, fc=fc, bk=bk, c=c: e.matmul(out=bk[:, :], lhsT=uT[:, fc, :], rhs=wd[:, fc, c * 512:(c + 1) * 512],
                                                                    start=(fc == 0), stop=(fc == 31)),
                         reads=[("uT", fc // 4), ("wdn", fc // 8)], writes=[bkey], crit=("uT", fc // 4))
                P.op("dve", lambda e, bk=bk, c=c, s=s: e.tensor_tensor(out=xo[s][:, c * 512:(c + 1) * 512], in0=bk[:, :],
                                                                      in1=xr[s][:, c * 512:(c + 1) * 512], op=ALU.add),
                     reads=[bkey, ("xr", s)], writes=[("xo", s)])
            P.dma("sp", dst[t * 128:(t + 1) * 128, :], xo[s], reads=[("xo", s)])
        P.barrier()

    KA = [b16v(AA, 0, S), b16v(AB, 24576, S)]
    QA = [b16v(AA, 8192, S), b16v(AB, 32768, S)]
    VA = [b16v(GA, 0, NT * 66).rearrange("p (t c) -> p t c", c=66),
          b16v(AB, 40960, NT * 66).rearrange("p (t c) -> p t c", c=66)]
    WS = [f32v(GA, 4352, WU), f32v(AB, 45184, WU)]
    Hk = f32v(AA, 16384, WU)
    lg = [f32v(GA, 22528 + i * 2048, 512) for i in range(3)] + [f32v(AB, 20480, 512)]
    pTb = [b16v(GA, 28672 + i * 1024, 512) for i in range(3)] + [b16v(AB, 22528, 512)]
    NR = 4
    LAG = 3
    lbank_idx = [0, 1, 2, 6]
    lbanks = [banks[i] for i in lbank_idx]
    ost = f32v(GA, 31744, NT * 64).rearrange("p (t d) -> p t d", d=64)
    rs_t = small[:, 2:3]
    oTs = [f32v(AB, 16384, 512), f32v(AB, 18432, 512)]
    gctr = [0]
    rctr = [0]
    b7 = banks[7]
    B7 = ("bk", 7)

    deferred = []

    def tick():
        fire = []
        for it in deferred:
            it[0] -= 1
        while deferred and deferred[0][0] <= 0:
            fire.append(deferred.pop(0)[1])
        for fn in fire:
            fn()

    def force_until(tag):
        idx = -1
        for k, it in enumerate(deferred):
            if it[2] == tag:
                idx = k
        for _ in range(idx + 1):
            deferred.pop(0)[1]()

    def flush_deferred():
        while deferred:
            deferred.pop(0)[1]()

    rs4 = small[:, 16:20].unsqueeze(2)
    eptmp = f32v(AB, 63488, 256).rearrange("p (t d) -> p t d", d=64)

    def strip_dma(spec):
        tg, off, step, width = spec[0:4]
        rd = list(spec[4]) if len(spec) > 4 else []
        P.dma("sp", Hk[:, 0:width], bass.AP(tg, off, [[step, 128], [1, width]]), reads=rd, writes=["Hk"])

    def strip_compute(spec, s):
        width = spec[3]
        W = WS[s]
        for c in range((width + 511) // 512):
            n = min(512, width - c * 512)
            P.op("pe", lambda e, c=c, n=n: e.matmul(out=b7[:, 0:n], lhsT=cf[:, 1, :], rhs=Hk[:, c * 512:c * 512 + n],
                                                   start=True, stop=True), reads=["Hk", "cf"], writes=[B7])
            P.op("act", lambda e, c=c, n=n, W=W: e.copy(out=W[:, c * 512:c * 512 + n], in_=b7[:, 0:n]),
                 reads=[B7], writes=[("W", s)])

    def attn_core(run, s, hook=None):
        K = run["K"]; steps = run["steps"]; nvc = 66
        ka_fn = run["ka_fn"]; va_fn = run["va_fn"]; epilogue = run["ep"]
        kreads = run["kreads"](s); vreads = run["vreads"](s); qreads = run["qreads"](s)
        qa = QA[s]; W = WS[s]
        gfirst = {}; glast = {}; gpar = {}
        for n, st in enumerate(steps):
            G = st[0]
            if G not in gfirst:
                gfirst[G] = n
                gpar[G] = gctr[0] % 2
                gctr[0] += 1
            glast[G] = n

        def emit_pv(n):
            G, j = steps[n][0:2]
            r = n % NR
            par = gpar[G]
            otb = banks[3 + par]; okey = ("bk", 3 + par)
            P.op("pe", lambda e, otb=otb, r=r, j=j, n=n, G=G: e.matmul(
                out=otb[0:nvc, :], lhsT=va_fn(j, s), rhs=pTb[r][:, :], start=(gfirst[G] == n), stop=(glast[G] == n)),
                reads=[("pT", r)] + vreads, writes=[okey])
            if glast[G] == n:
                ots = oTs[par]
                force_until(("fin", par))
                P.op("act", lambda e, ots=ots, otb=otb: e.copy(out=ots[0:nvc, :], in_=otb[0:nvc, :]),
                     reads=[okey], writes=[("oTs", par)])

                def finish(G=G, ots=ots, par=par):
                    tb = banks[5]; tkey = ("bk", 5)
                    for t in range(4):
                        P.op("pe", lambda e, tb=tb, ots=ots, t=t: e.transpose(
                            out=tb[:, t * nvc:(t + 1) * nvc], in_=ots[0:nvc, t * 128:(t + 1) * 128], identity=cf[0:nvc, 0, 0:nvc]),
                            reads=[("oTs", par), "cf"], writes=[tkey], crit=("oTs", par))
                    epilogue(G, tb[:, 0:4 * nvc].rearrange("p (t c) -> p t c", c=nvc), tkey)
                deferred.append([4, finish, ("fin", par)])

        hook_at = min(5, len(steps) - 1)
        for n, st in enumerate(steps):
            G, j, woff = st[0:3]
            mode = st[4] if len(st) > 4 else "W"
            r = n % NR
            bk = lbanks[r]; bkey = ("bk", lbank_idx[r])
            P.op("pe", lambda e, bk=bk, j=j, G=G: e.matmul(out=bk[:, :], lhsT=ka_fn(j, K, s), rhs=qa[0:K, G * 512:(G + 1) * 512],
                                                          start=True, stop=True),
                 reads=kreads + qreads, writes=[bkey])
            if mode == "W":
                P.op("dve", lambda e, bk=bk, r=r, woff=woff: e.scalar_tensor_tensor(
                    out=lg[r], in0=bk[:, :], scalar=0.125, in1=W[:, woff:woff + 512], op0=ALU.mult, op1=ALU.add),
                    reads=[bkey, ("W", s)], writes=[("lg", r)])
                P.op("act", lambda e, r=r: e.activation(out=pTb[r], in_=lg[r], func=AF.Exp),
                     reads=[("lg", r)], writes=[("pT", r)])
            elif mode == "plain":
                P.op("act", lambda e, r=r, bk=bk: e.activation(out=pTb[r], in_=bk[:, :], func=AF.Exp, scale=0.125),
                     reads=[bkey], writes=[("pT", r)])
            else:
                bap = mode[1]
                P.op("act", lambda e, r=r, bk=bk, bap=bap: e.activation(out=pTb[r], in_=bk[:, :], func=AF.Exp, bias=bap, scale=0.125),
                     reads=[bkey, "b31"], writes=[("pT", r)])
            if n >= LAG:
                emit_pv(n - LAG)
            tick()
            if n == hook_at and hook is not None:
                hook()
        for n in range(max(0, len(steps) - LAG), len(steps)):
            emit_pv(n)
        if run.get("post") is not None:
            deferred.append([4, run["post"], None])

    def execute(runs):
        def sset(i):
            return (rctr[0] + i) % 2

        def pf_dma(i):
            s = sset(i)
            runs[i]["loads"](s)
            if runs[i].get("strip") is not None:
                strip_dma(runs[i]["strip"])

        def pf_compute(i):
            s = sset(i)
            if runs[i].get("strip") is not None:
                strip_compute(runs[i]["strip"], s)
            if runs[i].get("pre") is not None:
                runs[i]["pre"](s)

        pf_dma(0)
        pf_compute(0)
        for i in range(len(runs)):
            hook = None
            if i + 1 < len(runs):
                pf_dma(i + 1)
                hook = (lambda i=i: pf_compute(i + 1))
            attn_core(runs[i], sset(i), hook)
        rctr[0] += len(runs)

    def std_steps(maxspan=None, plain_past=False, const_far=None):
        steps = []
        for G in range(8):
            j0 = 0 if maxspan is None else max(0, 4 * G - maxspan)
            for j in range(j0, 4 * G + 4):
                act = [i for i in range(4 * G, 4 * G + 4) if i >= j and (maxspan is None or i - j <= maxspan)]
                if act:
                    Dd = 512 * G - 128 * j
                    mode = "W"
                    if plain_past and Dd >= 128:
                        mode = "plain"
                    elif const_far is not None and Dd >= 1664:
                        mode = ("const", const_far)
                    steps.append((G, j, Dd + WOFF, act, mode))
        return steps

    def ep_common(G, tbv, okey, gate_col):
        P.op("dve", lambda e: e.tensor_scalar(out=rs4, in0=tbv[:, :, 64:65], scalar1=1e-30, scalar2=None, op0=ALU.max),
             reads=[okey], writes=["rs"])
        P.op("dve", lambda e: e.reciprocal(out=rs4, in_=rs4), reads=["rs"], writes=["rs"])
        if gate_col is not None:
            P.op("dve", lambda e: e.tensor_tensor(out=rs4, in0=rs4, in1=gates[:, 4 * G:4 * G + 4, gate_col:gate_col + 1], op=ALU.mult),
                 reads=["rs", "gates"], writes=["rs"])

    def ep_plain(gate_col=None, accumulate=False):
        def ep(G, tbv, okey):
            ep_common(G, tbv, okey, gate_col)
            okeys = [("ost", i) for i in range(4 * G, 4 * G + 4)]
            if accumulate:
                P.op("dve", lambda e: e.tensor_tensor(out=eptmp, in0=tbv[:, :, 0:64], in1=rs4.to_broadcast([128, 4, 64]), op=ALU.mult),
                     reads=[okey, "rs"], writes=["eptmp"])
                P.op("dve", lambda e: e.tensor_tensor(out=ost[:, 4 * G:4 * G + 4, :], in0=eptmp, in1=ost[:, 4 * G:4 * G + 4, :], op=ALU.add),
                     reads=["eptmp"] + okeys, writes=okeys)
            else:
                P.op("dve", lambda e: e.tensor_tensor(out=ost[:, 4 * G:4 * G + 4, :], in0=tbv[:, :, 0:64], in1=rs4.to_broadcast([128, 4, 64]), op=ALU.mult),
                     reads=[okey, "rs"], writes=okeys)
        return ep

    def ld_k(s, unit, extra=None):
        P.dma("sp", KA[s][0:64, :], t_QK.ap()[unit, 0:64, :], writes=[("ka", s)])
        if extra is not None:
            tg, r0, nr = extra
            P.dma("sp", KA[s][64:64 + nr, :], tg.ap()[r0:r0 + nr, :], writes=[("ka_hi", s)])

    def ld_q(s, unit):
        P.dma("sp", QA[s][0:64, :], t_QK.ap()[unit, 0:64, :], writes=[("qa", s)])

    def ld_v(s, unit):
        P.dma("sp", VA[s][:, :, 0:64], t_V.ap()[unit].rearrange("(t p) d -> p t d", p=128), writes=[("va", s)])

    def store_o(h):
        dst = bass.AP(t_MIX, h * DH, [[D, 128], [128 * D, NT], [1, DH]])
        P.dma("pool", dst, ost[:, :, :], reads=[("ost", i) for i in range(NT)])

    ka_std = lambda j, K, s: KA[s][0:K, j * 128:(j + 1) * 128]
    va_std = lambda j, s: VA[s][:, j, :]

    def phase2_common_init():
        for s in range(2):
            P.op("dve", lambda e, s=s: e.memset(VA[s][:, :, 64:66], 1.0), writes=[("va_ones", s)])

    def base_run(K, steps, ep, hi_k=False, hi_q=False):
        return dict(K=K, steps=steps, ka_fn=ka_std, va_fn=va_std, ep=ep,
                    kreads=(lambda s: [("ka", s)] + ([("ka_hi", s)] if hi_k else [])),
                    vreads=(lambda s: [("va", s), ("va_ones", s)]),
                    qreads=(lambda s: [("qa", s)] + ([("qa_hi", s)] if hi_q else [])))

    def phase2_fox():
        phase2_common_init()
        spec = (t_g0, 0, 1, WU)
        for s in range(2):
            strip_dma(spec)
            strip_compute(spec, s)
        steps = std_steps(plain_past=True)
        runs = []
        for h in range(H):
            r = base_run(70, steps, ep_plain(), hi_k=True, hi_q=True)

            def loads(s, h=h):
                ld_k(s, 16 + h)
                P.dma("sp", KA[s][64:70, :], bass.AP(t_AUG, (96 + h) * S, [[16 * S, 6], [1, S]]), writes=[("ka_hi", s)])
                ld_q(s, h)
                P.dma("sp", QA[s][64:70, :], bass.AP(t_AUG, h * S, [[16 * S, 6], [1, S]]), writes=[("qa_hi", s)])
                ld_v(s, h)
            r["loads"] = loads
            r["post"] = (lambda h=h: store_o(h))
            runs.append(r)
        execute(runs)
        flush_deferred()
        P.barrier()

    def phase2_dil():
        phase2_common_init()
        gsb = f32v(AA, 16384, GLEN)[0:16, :]
        lsb = f32v(AB, 45184, GLEN)[0:16, :]
        P.dma("sp", gsb, t_gBd.ap()[:, :], writes=["Hk"])
        P.dma("sp", lsb, bass.AP(t_lnc, 0, [[0, 16], [1, GLEN]]), writes=[("W", 1)])
        P.op("dve", lambda e: e.tensor_tensor(out=gsb, in0=gsb, in1=lsb, op=ALU.add), reads=["Hk", ("W", 1)], writes=["Hk"])
        P.dma("sp", t_gD.ap()[:, :], gsb, reads=["Hk"], writes=["gD"])
        steps = std_steps(maxspan=16)
        runs = []
        for h in range(H):
            r = base_run(64, steps, ep_plain())
            r["strip"] = (t_gD, h * GLEN, 1, 2048 + WOFF + 512 + 128, ["gD"])

            def loads(s, h=h):
                ld_k(s, 16 + h); ld_q(s, h); ld_v(s, h)
            r["loads"] = loads
            r["post"] = (lambda h=h: store_o(h))
            runs.append(r)
        execute(runs)
        flush_deferred()
        P.barrier()

    def phase2_moba():
        phase2_common_init()
        km32 = f32v(GA, 39936, 16)
        kmb = b16v(GA, 40064, 16)
        gm = f32v(GA, 40128, 16)
        mx8 = f32v(GA, 40192, 8)
        mb80 = f32v(GA, 40256, 80)
        P.op("dve", lambda e: e.memset(mb80, 0.0), writes=["mb80"])

        def pre(s):
            ka = KA[s]; qa = QA[s]
            P.op("dve", lambda e: e.tensor_reduce(out=km32[0:64, :], in_=ka[0:64, :].rearrange("p (n b) -> p n b", b=256),
                                                  axis=AX.X, op=ALU.add), reads=[("ka", s)], writes=["km32"])
            P.op("dve", lambda e: e.tensor_scalar(out=kmb[0:64, :], in0=km32[0:64, :], scalar1=1.0 / 256, scalar2=None, op0=ALU.mult),
                 reads=["km32"], writes=["kmb"])
            for i in range(NT):
                own = i // 2
                P.op("dve", lambda e: e.memset(gm, NEG), writes=["gm"])
                if own > 0:
                    P.op("pe", lambda e, i=i: e.matmul(out=b7[:, 0:16], lhsT=qa[0:64, i * 128:(i + 1) * 128], rhs=kmb[0:64, :],
                                                       start=True, stop=True), reads=[("qa", s), "kmb"], writes=[B7])
                    P.op("dve", lambda e, own=own: e.tensor_copy(out=gm[:, 0:own], in_=b7[:, 0:own]), reads=[B7, "gm"], writes=["gm"])
                if own > 3:
                    P.op("dve", lambda e: e.max(out=mx8, in_=gm), reads=["gm"], writes=["mx8"])
                    P.op("dve", lambda e: e.tensor_scalar(out=mb80[:, 64:80], in0=gm, scalar1=mx8[:, 2:3], scalar2=None, op0=ALU.is_ge),
                         reads=["gm", "mx8"], writes=["mb80"])
                else:
                    P.op("dve", lambda e: e.tensor_scalar(out=mb80[:, 64:80], in0=gm, scalar1=-1.0e29, scalar2=None, op0=ALU.is_ge),
                         reads=["gm"], writes=["mb80"])
                P.op("dve", lambda e: e.tensor_scalar(out=mb80[:, 64:80], in0=mb80[:, 64:80], scalar1=-1.0, scalar2=BIG,
                                                       op0=ALU.add, op1=ALU.mult), reads=["mb80"], writes=["mb80"])
                P.op("dve", lambda e, own=own: e.memset(mb80[:, 64 + own:65 + own], 0.0), reads=["mb80"], writes=["mb80"])
                P.op("pe", lambda e: e.transpose(out=b7[0:80, 128:256], in_=mb80, identity=cf[:, 0, :]), reads=["mb80", "cf"], writes=[B7])
                P.op("act", lambda e, i=i: e.copy(out=qa[64:80, i * 128:(i + 1) * 128], in_=b7[64:80, 128:256]),
                     reads=[B7], writes=[("qa_hi", s)])

        runs = []
        for h in range(H):
            r = base_run(80, std_steps(const_far=b31[:, h:h + 1]), ep_plain(), hi_k=True, hi_q=True)
            r["strip"] = (t_gB, h * GLEN, 1, 1664 + WOFF + 512 + 128)

            def loads(s, h=h):
                ld_k(s, 16 + h, (t_E16, 0, 16)); ld_q(s, h); ld_v(s, h)
            r["loads"] = loads
            r["pre"] = pre
            r["post"] = (lambda h=h: store_o(h))
            runs.append(r)
        execute(runs)
        flush_deferred()
        P.barrier()

    def phase2_nsa():
        phase2_common_init()
        kcR = b16v(AA, 16384, S)
        kcA = b16v(AA, 24576, S)
        kcB = b16v(AA, 34304, S)
        w1 = b16v(AA, 42496, 32 * 256).rearrange("p (t c) -> p t c", c=256)
        w2 = b16v(AA, 58880, 128).rearrange("p (m d) -> p m d", d=64)
        hid = b16v(AA, 59392, 512).rearrange("p (m n) -> p m n", n=256)
        kcn = b16v(AA, 60416, 256).rearrange("p (t d) -> p t d", d=128)
        kcmpT = b16v(AA, 60928, 256)
        vcmp = b16v(AA, 61440, 2 * 66).rearrange("p (t c) -> p t c", c=66)
        vcmpA = b16v(AA, 61952, 2 * 66).rearrange("p (t c) -> p t c", c=66)
        posT = f32v(AA, 62464, 32)
        impb = SBv["imp"]
        STt = SBv["ST"]
        OVt = SBv["OV"]
        impm = f32v(GA, 39936, 64)
        mx16 = f32v(GA, 40192, 16)
        mb128 = f32v(GA, 40256, 128)
        x2 = f32v(GA, 40768, 256)
        tg = f32v(GA, 41792, 256)
        P.dma("sp", STt[:], t_ST.ap()[:, :], writes=["ST"])
        P.dma("sp", OVt[:], t_OV.ap()[:, :, :], writes=["OV"])
        P.op("dve", lambda e: e.memset(mb128, 0.0), writes=["mb128"])
        stepwin = std_steps(maxspan=4)
        stepcmp = []
        for G in range(8):
            for ntl in range(2):
                if G >= 4 * ntl:
                    stepcmp.append((G, ntl, 512 * G - 2048 * ntl, list(range(4 * G, 4 * G + 4))))
        kc_fn = lambda j, K, s: kcmpT[0:64, j * 128:(j + 1) * 128]
        vc_fn = lambda j, s: vcmp[:, j, :]
        vcA_fn = lambda j, s: vcmpA[:, j, :]
        b7b = b7[:].bitcast(BF16)

        def compress(kh):
            P.op("dve", lambda e: e.memset(kcn, 0.0), writes=["kcn"])
            P.op("dve", lambda e: e.memset(vcmp, 0.0), writes=["vcmp"])
            P.op("dve", lambda e: e.memset(vcmp[:, :, 64:66], 1.0), reads=["vcmp"], writes=["vcmp"])
            P.op("dve", lambda e: e.memset(vcmpA[:, :, 64:66], 1.0), writes=["vcmpA"])
            P.op("dve", lambda e: e.tensor_copy(out=vcmpA[:, :, 0:64], in_=OVt[:]), reads=["vcmpA", "OV"], writes=["vcmpA"])
            for kv in range(2):
                P.dma("sp", kcR[0:64, :], t_QK.ap()[24 + 4 * kv + kh, 0:64, :], writes=["Hk"])
                P.dma("sp", posT[0:64, :], t_posT.ap()[kv], writes=["posT"])
                P.dma("pool", w1[0:64, :, :], t_w1.ap()[kv].rearrange("(t d) c -> d t c", d=64), writes=["w1"])
                P.dma("pool", w2[:, :, :], t_w2.ap()[kv].rearrange("(m p) d -> p m d", p=128), writes=["w2"])
                for ab, dstb in ((0, kcA), (1, kcB)):
                    P.op("dve", lambda e, ab=ab, dstb=dstb: e.tensor_tensor(
                        out=dstb[0:64, :].rearrange("p (n s) -> p n s", s=16), in0=kcR[0:64, :].rearrange("p (n s) -> p n s", s=16),
                        in1=posT[0:64, ab * 16:(ab + 1) * 16].unsqueeze(1).to_broadcast([64, 256, 16]), op=ALU.add),
                        reads=["Hk", "posT"], writes=[("kcAB", ab)] if ab == 1 else ["Hk"])
                kcAv = kcA[0:64, :].rearrange("p (n s) -> p n s", s=16)
                kcBv = kcB[0:64, :].rearrange("p (n s) -> p n s", s=16)
                for mc in range(2):
                    for t in range(32):
                        rhs = kcAv[:, 0:255, t] if t < 16 else kcBv[:, 1:256, t - 16]
                        P.op("pe", lambda e, t=t, mc=mc, rhs=rhs: e.matmul(out=b7[:, 0:255], lhsT=w1[0:64, t, mc * 128:(mc + 1) * 128],
                                                                       rhs=rhs, start=(t == 0), stop=(t == 31)),
                             reads=["Hk", ("kcAB", 1), "w1"], writes=[B7])
                    P.op("act", lambda e: e.activation(out=x2[:, 0:255], in_=b7[:, 0:255], func=AF.Square), reads=[B7], writes=["x2"])
                    P.op("dve", lambda e: e.tensor_scalar(out=x2[:, 0:255], in0=x2[:, 0:255], scalar1=0.044715, scalar2=1.0,
                                                           op0=ALU.mult, op1=ALU.add), reads=["x2"], writes=["x2"])
                    P.op("dve", lambda e: e.tensor_tensor(out=tg[:, 0:255], in0=b7[:, 0:255], in1=x2[:, 0:255], op=ALU.mult),
                         reads=[B7, "x2"], writes=["tg"])
                    P.op("act", lambda e: e.activation(out=tg[:, 0:255], in_=tg[:, 0:255], func=AF.Sigmoid, scale=1.5957691216),
                         reads=["tg"], writes=["tg"])
                    P.op("dve", lambda e, mc=mc: e.tensor_tensor(out=hid[:, mc, 0:255], in0=b7[:, 0:255], in1=tg[:, 0:255], op=ALU.mult),
                         reads=[B7, "tg"], writes=["hid"])
                for ntl in range(2):
                    nn = 128 if ntl == 0 else 127
                    for mc in range(2):
                        P.op("pe", lambda e, ntl=ntl, nn=nn, mc=mc: e.matmul(out=b7[0:nn, 256:320], lhsT=hid[:, mc, ntl * 128:ntl * 128 + nn],
                                                                         rhs=w2[:, mc, :], start=(mc == 0), stop=(mc == 1)),
                             reads=["hid", "w2"], writes=[B7])
                    if kv == 0:
                        ssq = small[:, 4:5]
                        P.op("act", lambda e, nn=nn: e.activation(out=x2[0:nn, 0:64], in_=b7[0:nn, 256:320], func=AF.Square, accum_out=ssq[0:nn, :]),
                             reads=[B7], writes=["x2", "ssq4"])
                        P.op("dve", lambda e: e.tensor_scalar(out=ssq, in0=ssq, scalar1=1.0 / DH, scalar2=1e-6, op0=ALU.mult, op1=ALU.add),
                             reads=["ssq4"], writes=["ssq4"])
                        P.op("act", lambda e: e.activation(out=ssq, in_=ssq, func=AF.Sqrt), reads=["ssq4"], writes=["ssq4"])
                        P.op("dve", lambda e: e.reciprocal(out=ssq, in_=ssq), reads=["ssq4"], writes=["ssq4"])
                        P.op("dve", lambda e, nn=nn, ntl=ntl: e.tensor_scalar(out=kcn[0:nn, ntl, 0:64], in0=b7[0:nn, 256:320], scalar1=ssq[0:nn, :],
                                                                           scalar2=None, op0=ALU.mult), reads=[B7, "ssq4"], writes=["kcn"])
                        P.op("pe", lambda e, ntl=ntl: e.transpose(out=b7b[:, 768:896], in_=kcn[:, ntl, :], identity=identb[:]),
                             reads=["kcn", "identb"], writes=[B7])
                        P.op("act", lambda e, ntl=ntl: e.copy(out=kcmpT[0:64, ntl * 128:(ntl + 1) * 128], in_=b7b[0:64, 768:896]),
                             reads=[B7], writes=["kcmpT"])
                    else:
                        P.op("act", lambda e, nn=nn, ntl=ntl: e.copy(out=vcmp[0:nn, ntl, 0:64], in_=b7[0:nn, 256:320]),
                             reads=[B7], writes=["vcmp"])
            P.op("dve", lambda e: e.memset(impb, 0.0), writes=["imp"])

        def epA(G, tbv, okey):
            ep_common(G, tbv, okey, None)
            P.op("dve", lambda e: e.tensor_tensor(out=eptmp, in0=tbv[:, :, 0:64], in1=rs4.to_broadcast([128, 4, 64]), op=ALU.mult),
                 reads=[okey, "rs"], writes=["eptmp"])
            P.op("dve", lambda e: e.tensor_tensor(out=impb[:, 4 * G:4 * G + 4, :], in0=eptmp, in1=impb[:, 4 * G:4 * G + 4, :], op=ALU.add),
                 reads=["eptmp", "imp"], writes=["imp"])

        def selection():
            for i in range(NT):
                P.op("dve", lambda e, i=i: e.tensor_tensor(out=impm, in0=impb[:, i, :], in1=STt[:, 64 - 2 * i:128 - 2 * i], op=ALU.add),
                     reads=["imp", "ST"], writes=["impm"])
                P.op("dve", lambda e: e.max(out=mx16[:, 0:8], in_=impm), reads=["impm"], writes=["mx16"])
                P.op("dve", lambda e: e.match_replace(out=mb128[:, 0:64], in_to_replace=mx16[:, 0:8], in_values=impm, imm_value=NEG),
                     reads=["impm", "mx16"], writes=["mb128"])
                P.op("dve", lambda e: e.max(out=mx16[:, 8:16], in_=mb128[:, 0:64]), reads=["mb128"], writes=["mx16"])
                P.op("dve", lambda e: e.tensor_scalar(out=mx16[:, 14:15], in0=mx16[:, 14:15], scalar1=-1.0e29, scalar2=None, op0=ALU.max),
                     reads=["mx16"], writes=["mx16"])
                P.op("dve", lambda e: e.tensor_scalar(out=mb128[:, 64:128], in0=impm, scalar1=mx16[:, 14:15], scalar2=None, op0=ALU.is_ge),
                     reads=["impm", "mx16", "mb128"], writes=["mb128"])
                P.op("dve", lambda e: e.tensor_scalar(out=mb128[:, 64:128], in0=mb128[:, 64:128], scalar1=-1.0, scalar2=BIG,
                                                       op0=ALU.add, op1=ALU.mult), reads=["mb128"], writes=["mb128"])
                P.op("dve", lambda e, i=i: e.memset(mb128[0:64, 64 + 2 * i:65 + 2 * i], 0.0), reads=["mb128"], writes=["mb128"])
                P.op("dve", lambda e, i=i: e.memset(mb128[64:128, 65 + 2 * i:66 + 2 * i], 0.0), reads=["mb128"], writes=["mb128"])
                P.op("pe", lambda e: e.transpose(out=b7[:, 0:128], in_=mb128, identity=cf[:, 0, :]), reads=["mb128", "cf"], writes=[B7])
                for s in range(2):
                    P.op("act", lambda e, i=i, s=s: e.copy(out=QA[s][64:128, i * 128:(i + 1) * 128], in_=b7[64:128, 0:128]),
                         reads=[B7], writes=[("qa_hi", s)])

        def cmp_run(u, ep, vfn, vkey):
            return dict(K=64, steps=stepcmp, ka_fn=kc_fn, va_fn=vfn, ep=ep,
                        kreads=(lambda s: ["kcmpT"]), vreads=(lambda s: [vkey]), qreads=(lambda s: [("qa", s)]),
                        strip=(t_gBc, u * GCLEN, 16, 4096), loads=(lambda s, u=u: ld_q(s, u)))

        for kh in range(4):
            compress(kh)
            runs = []
            for g in range(4):
                u = kh * 4 + g
                runs.append(cmp_run(u, epA, vcA_fn, "vcmpA"))
            runs[-1]["post"] = selection
            for g in range(4):
                u = kh * 4 + g
                runs.append(cmp_run(u, ep_plain(gate_col=u * 3 + 0), vc_fn, "vcmp"))
                r = base_run(128, std_steps(const_far=b31[:, u:u + 1]), ep_plain(gate_col=u * 3 + 1, accumulate=True), hi_k=True, hi_q=True)
                r["strip"] = (t_gB, u * GLEN, 1, 1664 + WOFF + 512 + 128)
                r["loads"] = (lambda s, u=u, kh=kh: (ld_k(s, 16 + kh, (t_E64, 0, 64)), ld_q(s, u), ld_v(s, kh)))
                runs.append(r)
                r = base_run(64, stepwin, ep_plain(gate_col=u * 3 + 2, accumulate=True))
                r["strip"] = (t_gBw, u * GLEN, 1, 512 + WOFF + 512 + 128)
                r["loads"] = (lambda s, u=u, kh=kh: (ld_k(s, 20 + kh), ld_q(s, u), ld_v(s, 4 + kh)))
                r["post"] = (lambda u=u: store_o(u))
                runs.append(r)
            execute(runs)
        flush_deferred()
        P.barrier()

    SBv = {}
    SBv["bfb"] = SB("bfb", [128, H], F32)
    SBv["caq"] = SB("caq", [128, 96], BF16)
    SBv["cak"] = SB("cak", [128, 96], BF16)
    SBv["carry"] = SB("carry", [128, H], F32)
    SBv["fz"] = SB("fz", [128, 48], F32)
    SBv["imp"] = AB[:, 4096:4096 + 2 * NT * 64].bitcast(F32).rearrange("p (t d) -> p t d", d=64)
    SBv["ST"] = SB("STt", [128, 128], F32)
    SBv["OV"] = SB("OVt", [128, 2, 64], BF16)

    cur = t_x
    for idx, L in enumerate(layers):
        kind = L % 4
        lastl = (idx == len(layers) - 1)
        dstt = t_y if lastl else t_X[idx % 2]
        phase1(L, kind, cur)
        if stop == "p1":
            break
        if kind == 0:
            phase2_dil()
        elif kind == 1:
            phase2_nsa()
        elif kind == 2:
            phase2_fox()
        else:
            phase2_moba()
        if stop == "p2":
            for t in range(NT):
                tmpb = b16v(GA, (t % 2) * 2048, 1024)
                P.dma("sp", tmpb, t_MIX.ap()[t * 128:(t + 1) * 128, :], writes=[("dbg", t % 2)])
                tmpf = f32v(GA, 8192 + (t % 2) * 4096, 1024)
                P.op("dve", lambda e, tmpb=tmpb, tmpf=tmpf: e.tensor_copy(out=tmpf, in_=tmpb), reads=[("dbg", t % 2)], writes=[("dbgf", t % 2)])
                P.dma("sp", t_y.ap()[t * 128:(t + 1) * 128, :], tmpf, reads=[("dbgf", t % 2)])
            break
        phase3(L, cur, dstt)
        cur = dstt
    P.barrier()
    P.emit()
    return nc


def _rel_bucket_np(d):
    d = np.maximum(d, 0)
    df = np.maximum(d.astype(np.float32), np.float32(1.0))
    large = 16 + (np.log(df / np.float32(16)) / np.float32(math.log(2048 / 16)) * np.float32(16)).astype(np.int32)
    large = np.minimum(large, 31)
    return np.where(d < 16, d, large)


def _host_tables(rel_table):
    rel_table = np.asarray(rel_table, dtype=np.float32)
    m = np.arange(GLEN)
    d = m - 511
    bk = _rel_bucket_np(d)
    gat = rel_table[bk, :].T.copy()
    valid = d >= 0
    gB = np.where(valid[None, :], gat, np.float32(NEG)).astype(np.float32)
    gBw = np.where((valid & (d <= 511))[None, :], gat, np.float32(NEG)).astype(np.float32)
    cnt = ((d <= 128) & valid).astype(np.int32) + ((d % 4 == 0) & (d <= 512) & valid) + ((d % 16 == 0) & (d <= 2048) & valid)
    gBd = np.where((cnt > 0)[None, :], gat, np.float32(NEG)).astype(np.float32)
    lnc = np.where(cnt > 0, np.log(np.maximum(cnt, 1)), 0.0).astype(np.float32)[None, :]
    g0 = np.where(valid, 0.0, NEG).astype(np.float32)[None, :]
    mc = np.arange(GCLEN)
    dc = mc - 2063
    gatc = rel_table[_rel_bucket_np(dc), :].T.copy()
    gBc = np.where((dc >= 0)[None, :], gatc, np.float32(NEG)).astype(np.float32)
    cf = np.zeros((128, 4, 128), np.float32)
    cf[:, 0, :] = np.eye(128)
    cf[:, 1, :] = np.eye(128)[::-1]
    cf[:, 2, :] = np.triu(np.ones((128, 128)))
    cf[:, 3, :] = 1.0
    identb = np.eye(128).astype(ml_dtypes.bfloat16)
    tok = np.arange(S)
    E16 = (tok[None, :] // 256 == np.arange(16)[:, None]).astype(ml_dtypes.bfloat16)
    E64 = (tok[None, :] // 64 == np.arange(64)[:, None]).astype(ml_dtypes.bfloat16)
    n = np.arange(256)
    j = np.arange(64)
    ov = ((16 * n[:, None] < 64 * j[None, :] + 64) & (16 * n[:, None] + 32 > 64 * j[None, :]) & (n[:, None] < 255))
    OV = ov.reshape(2, 128, 64).transpose(1, 0, 2).astype(ml_dtypes.bfloat16).copy()
    r = (np.arange(128) >= 64).astype(np.int32)
    c = np.arange(128)
    ST = np.where(c[None, :] < 64 + r[:, None], 0.0, NEG).astype(np.float32)
    assert (_rel_bucket_np(np.arange(1537, 8192)) == 31).all()
    return dict(rel31=np.ascontiguousarray(rel_table[31:32, :]), gB=gB, gBw=gBw, gBd=gBd, gBc=gBc, g0=g0, lnc=lnc, cf32=cf, identb=identb, E16=E16, E64=E64, OV=OV, ST=ST)


_PROG_CACHE = {}


def _get_prog(layers):
    key = tuple(layers)
    if key not in _PROG_CACHE:
        import os
        _PROG_CACHE[key] = build_program(list(layers), stop=os.environ.get("KSTOP"))
    return _PROG_CACHE[key]


def _common_inputs(rel_table, attn_norm, mlp_norm, q_gain, k_gain, w_out, mlp_w_up, mlp_w_down, dsa_w_in, nsa_w_in,
                   nsa_cmp_pos, nsa_cmp_w1, nsa_cmp_w2, fox_w_in, fox_b_f, moba_w_in):
    f = lambda a: np.ascontiguousarray(np.asarray(a, dtype=np.float32))
    m = dict(attn_norm=f(attn_norm), mlp_norm=f(mlp_norm), q_gain=f(q_gain), k_gain=f(k_gain), w_out=f(w_out),
             mlp_w_up=f(mlp_w_up), mlp_w_down=f(mlp_w_down), dsa_w_in=f(dsa_w_in)[0], nsa_w_in=f(nsa_w_in)[0],
             fox_w_in=f(fox_w_in)[0], moba_w_in=f(moba_w_in)[0],
             nsa_posT=np.ascontiguousarray(f(nsa_cmp_pos)[0].transpose(0, 2, 1)),
             nsa_cmp_w1=f(nsa_cmp_w1)[0], nsa_cmp_w2=f(nsa_cmp_w2)[0], fox_b_f=f(fox_b_f))
    m.update(_host_tables(rel_table))
    return m


def run_layers(layers, x, n_cores=8, **params):
    nc = _get_prog(layers)
    common = _common_inputs(**params)
    x = np.asarray(x, dtype=np.float32)
    in_maps = []
    for c in range(n_cores):
        mm = dict(common)
        mm["x"] = np.ascontiguousarray(x[c % x.shape[0]])
        in_maps.append(mm)
    res = run_bass_kernel_spmd(nc, in_maps, core_ids=list(range(n_cores)))
    return res


def kernel(x, rel_table, attn_norm, mlp_norm, q_gain, k_gain, w_out, mlp_w_up, mlp_w_down,
           dsa_w_in, nsa_w_in, nsa_cmp_pos, nsa_cmp_w1, nsa_cmp_w2, fox_w_in, fox_b_f, moba_w_in):
    params = dict(rel_table=rel_table, attn_norm=attn_norm, mlp_norm=mlp_norm, q_gain=q_gain, k_gain=k_gain,
                  w_out=w_out, mlp_w_up=mlp_w_up, mlp_w_down=mlp_w_down, dsa_w_in=dsa_w_in, nsa_w_in=nsa_w_in,
                  nsa_cmp_pos=nsa_cmp_pos, nsa_cmp_w1=nsa_cmp_w1, nsa_cmp_w2=nsa_cmp_w2, fox_w_in=fox_w_in,
                  fox_b_f=fox_b_f, moba_w_in=moba_w_in)
    res = run_layers([0, 1, 2, 3], x, n_cores=8, **params)
    out = np.stack([np.asarray(res.results[b]["y"], dtype=np.float32) for b in range(4)], axis=0)
    return out
```

```python
import math
import os
import contextlib
import numpy as np
import ml_dtypes
import concourse.bass as bass
import concourse.mybir as mybir
from concourse.bass_utils import run_bass_kernel_spmd

F32 = mybir.dt.float32
BF16 = mybir.dt.bfloat16
ALU = mybir.AluOpType
AF = mybir.ActivationFunctionType
AX = mybir.AxisListType

S = 4096
D = 1024
H = 16
DH = 64
NT = 32
DFF = 4096
NEG = -1.0e30
BIG = 30000.0
WU = 4480
WOFF = 384
GLEN = WU + 128
GCLEN = 4096 + 2032 + 16
NCOLS = {0: 3072, 1: 2608, 2: 3088, 3: 3072}

ENGS = ["pe", "act", "dve", "pool", "sp"]


class Prog:
    def __init__(self, nc, ndma=12):
        self.nc = nc
        self.streams = {e: [] for e in ENGS}
        self.cnt = {e: 0 for e in ENGS}
        self.sems = {}
        self.stack = contextlib.ExitStack()
        for e in ENGS:
            self.sems[e] = self.stack.enter_context(nc.semaphore("s_" + e))
        self.dsem = []
        for i in range(2 * ndma):
            self.dsem.append([self.stack.enter_context(nc.semaphore("d_%d" % i)), 0])
        self.ndma = ndma
        self.dnext = {"sp": 0, "pool": 0, "act": 0}
        self.waited = {e: {} for e in ENGS}
        self.lastw = {}
        self.readers = {}

    def _sem(self, key):
        return self.sems[key] if isinstance(key, str) else self.dsem[key[1]][0]

    def _wait(self, eng, tok):
        key, val, src = tok
        if src == "pe" and eng == "pe":
            return
        w = self.waited[eng]
        if w.get(key, 0) >= val:
            return
        w[key] = val
        self.streams[eng].append(("w", self._sem(key), val))

    def _deps(self, eng, reads, writes):
        for r in reads:
            t = self.lastw.get(r)
            if t is not None:
                self._wait(eng, t)
        for w in writes:
            t = self.lastw.get(w)
            if t is not None:
                self._wait(eng, t)
            for t in self.readers.get(w, ()):
                self._wait(eng, t)

    def _commit(self, tok, reads, writes):
        for w in writes:
            self.lastw[w] = tok
            self.readers[w] = []
        for r in reads:
            lst = self.readers.setdefault(r, [])
            lst.append(tok)
            if len(lst) > 24:
                best = {}
                for t in lst:
                    k = t[0]
                    if k not in best or best[k][1] < t[1]:
                        best[k] = t
                self.readers[r] = list(best.values())

    def op(self, eng, fn, reads=(), writes=(), crit=None):
        self._deps(eng, reads, writes)
        self.cnt[eng] += 1
        tok = (eng, self.cnt[eng], eng)
        cw = None
        if crit is not None:
            ct = self.lastw.get(crit)
            if ct is not None and ct[2] != eng:
                cw = (self._sem(ct[0]), ct[1])
        self.streams[eng].append(("o", fn, self.sems[eng], 1, cw))
        self._commit(tok, reads, writes)
        return tok

    def dma(self, q, out, in_, reads=(), writes=()):
        self._deps(q, reads, writes)
        base = self.ndma if q == "pool" else 0
        i = base + self.dnext[q]
        self.dnext[q] = (self.dnext[q] + 1) % self.ndma
        if self.dsem[i][1] > 0:
            self._wait(q, (("d", i), self.dsem[i][1], "dma"))
        self.dsem[i][1] += 16
        tok = (("d", i), self.dsem[i][1], "dma")
        self.streams[q].append(("o", lambda e, o=out, s=in_: e.dma_start(out=o, in_=s), self.dsem[i][0], 16))
        self._commit(tok, reads, writes)
        return tok

    def barrier(self):
        for e in ENGS:
            for e2 in ENGS:
                if e2 != e and self.cnt[e2] > 0:
                    self._wait(e, (e2, self.cnt[e2], e2))
            for i, (s, v) in enumerate(self.dsem):
                if v > 0:
                    self._wait(e, (("d", i), v, "dma"))
        self.lastw = {}
        self.readers = {}

    def emit(self):
        nc = self.nc
        streams = self.streams

        def replay(eng, items):
            for it in items:
                if it[0] == "w":
                    eng.wait_ge(it[1], it[2])
                else:
                    ins = it[1](eng)
                    if len(it) > 4 and it[4] is not None:
                        ins = ins._wait_ge(it[4][0], it[4][1])
                    ins.then_inc(it[2], it[3])

        with nc.Block() as block:
            @block.tensor
            def _(e):
                replay(e, streams["pe"])

            @block.scalar
            def _(e):
                replay(e, streams["act"])

            @block.vector
            def _(e):
                replay(e, streams["dve"])

            @block.gpsimd
            def _(e):
                replay(e, streams["pool"])

            @block.sync
            def _(e):
                replay(e, streams["sp"])


def bcast_rows(t, off, n):
    return bass.AP(t, off, [[0, 128], [1, n]])


def build_program(layers, stop=None):
    nc = bass.Bass("TRN2", target_bir_lowering=False)
    es = contextlib.ExitStack()

    def din(name, shape, dt=F32):
        return nc.dram_tensor(name, list(shape), dt, kind="ExternalInput")

    t_x = din("x", [S, D])
    t_an = din("attn_norm", [4, D]); t_mn = din("mlp_norm", [4, D])
    t_qg = din("q_gain", [4, DH]); t_kg = din("k_gain", [4, DH])
    t_wout = din("w_out", [4, D, D]); t_wup = din("mlp_w_up", [4, D, DFF]); t_wdn = din("mlp_w_down", [4, DFF, D])
    t_win = {0: din("dsa_w_in", [D, 3072]), 1: din("nsa_w_in", [D, 2608]),
             2: din("fox_w_in", [D, 3088]), 3: din("moba_w_in", [D, 3072])}
    t_posT = din("nsa_posT", [2, DH, 32])
    t_w1 = din("nsa_cmp_w1", [2, 2048, 256]); t_w2 = din("nsa_cmp_w2", [2, 256, DH])
    t_bf = din("fox_b_f", [1, H])
    t_r31 = din("rel31", [1, H])
    t_gB = din("gB", [H, GLEN]); t_gBw = din("gBw", [H, GLEN]); t_gBd = din("gBd", [H, GLEN])
    t_gBc = din("gBc", [H, GCLEN]); t_g0 = din("g0", [1, GLEN]); t_lnc = din("lnc", [1, GLEN])
    t_cf = din("cf32", [128, 4, 128]); t_idb = din("identb", [128, 128], BF16)
    t_E16 = din("E16", [16, S], BF16); t_E64 = din("E64", [64, S], BF16)
    t_OV = din("OV", [128, 2, 64], BF16); t_ST = din("ST", [128, 128])
    t_y = nc.dram_tensor("y", [S, D], F32, kind="ExternalOutput")
    t_X = [nc.dram_tensor("xs%d" % i, [S, D], F32, kind="Internal") for i in range(2)]
    t_QK = nc.dram_tensor("qk", [32, 128, S], BF16, kind="Internal")
    t_V = nc.dram_tensor("vv", [16, S, DH], BF16, kind="Internal")
    t_MIX = nc.dram_tensor("mix", [S, D], BF16, kind="Internal")
    t_AUG = nc.dram_tensor("aug", [192, S], BF16, kind="Internal")
    t_gD = nc.dram_tensor("gD", [H, GLEN], F32, kind="Internal")

    def SB(name, shape, dt):
        return es.enter_context(nc.sbuf_tensor("sb_" + name, list(shape), dt))

    banks = [es.enter_context(nc.psum_tensor("bank%d" % i, [128, 512], F32)) for i in range(8)]
    AA = SB("arenaA", [128, 32768], BF16)
    AB = SB("arenaB", [128, 32768], BF16)
    WO = SB("wout", [128, 8, 1024], BF16)
    identb = SB("identb", [128, 128], BF16)
    cf = SB("cf", [128, 4, 128], F32)
    ones_b = SB("ones_b", [128, 128], BF16)
    gbc_a = SB("gbc_a", [128, D], F32); gbc_m = SB("gbc_m", [128, D], F32)
    gq = SB("gq", [128, DH], F32); gk = SB("gk", [128, DH], F32); gqk = SB("gqk", [128, DH], F32)
    small = SB("small", [128, 64], F32)
    b31 = SB("b31", [128, H], F32)
    GA = SB("genA", [128, 21760], BF16)

    P = Prog(nc)
    gates = AB[:, 0:2 * NT * 48].bitcast(F32).rearrange("p (t c) -> p t c", c=48)

    def bf(ap):
        return ap

    def f32v(arena, off_b, n):
        return arena[:, off_b // 2: off_b // 2 + 2 * n].bitcast(F32)

    def b16v(arena, off_b, n):
        return arena[:, off_b // 2: off_b // 2 + n]

    P.dma("sp", identb[:], t_idb.ap()[:, :], writes=["identb"])
    P.dma("sp", cf[:], t_cf.ap()[:, :, :], writes=["cf"])
    P.op("dve", lambda e: e.memset(ones_b[:], 1.0), writes=["ones_b"])
    P.dma("sp", b31[:], bcast_rows(t_r31, 0, H), writes=["b31"])
    P.op("dve", lambda e: e.memset(small[:], 0.0), writes=["ssq", "rstd", "ssq8", "rs", "ssq4"])

    def phase1(L, kind, t_src):
        ncol = NCOLS[kind]
        wv = AA[:, 0:8 * ncol].rearrange("p (k n) -> p k n", n=ncol)
        win = t_win[kind].ap()
        for kc in range(8):
            P.dma("pool", wv[:, kc, :], win[kc * 128:(kc + 1) * 128, :], writes=[("win", kc)])
        P.dma("pool", WO[:, :, :], t_wout.ap()[L].rearrange("(k p) n -> p k n", p=128), writes=["wo"])
        P.dma("sp", gbc_a[:], bcast_rows(t_an, L * D, D), writes=["gbc_a"])
        P.dma("sp", gbc_m[:], bcast_rows(t_mn, L * D, D), writes=["gbc_m"])
        P.dma("sp", gq[:], bcast_rows(t_qg, L * DH, DH), writes=["gq"])
        P.dma("sp", gk[:], bcast_rows(t_kg, L * DH, DH), writes=["gk"])
        P.op("dve", lambda e: e.tensor_tensor(out=gqk[:], in0=gq[:], in1=gk[:], op=ALU.mult),
             reads=["gq", "gk"], writes=["gqk"])

        xt = [f32v(GA, 0, 1024), f32v(GA, 4096, 1024)]
        junk = b16v(GA, 8192, 1024)
        hb = b16v(GA, 10240, 1024)
        hT = [b16v(GA, 12288, 1024), b16v(GA, 14336, 1024)]
        sq = f32v(GA, 16384, 512)
        tm = b16v(GA, 18432, 18 * 128)
        vm = b16v(GA, 23040, 1024)
        tst = b16v(GA, 25088, 18 * 512).rearrange("p (b n) -> p b n", n=512)
        ssq = small[:, 0:1]; rstd = small[:, 1:2]; ssq8 = small[:, 8:16]
        if kind == 2:
            bfb = SBv["bfb"]
            caq = SBv["caq"]; cak = SBv["cak"]; carry = SBv["carry"]; fz = SBv["fz"]
            P.dma("sp", bfb[:], bcast_rows(t_bf, 0, H), writes=["bfb"])
            P.op("dve", lambda e: e.memset(carry[:], 0.0), writes=["carry"])
            P.op("dve", lambda e: e.memset(caq[:], 1.0), writes=["caq"])
            P.op("dve", lambda e: e.memset(cak[:], 1.0), writes=["cak"])
            P.op("dve", lambda e: e.memset(tm[:, 2048:2304], 0.0), writes=["tm"])

        if kind == 1:
            chunks = [(0, 512, [(0, 512, "q", 0)]), (512, 512, [(0, 512, "q", 512)]),
                      (1024, 512, [(0, 256, "c", 1536), (256, 256, "c", 1792)]),
                      (1536, 512, [(0, 256, "k", 1024), (256, 256, "v", 0)]),
                      (2048, 512, [(0, 256, "k", 1280), (256, 256, "v", 256)]),
                      (2560, 48, [(0, 48, "g", 0)])]
            nv = 8
        else:
            chunks = [(0, 512, [(0, 512, "q", 0)]), (512, 512, [(0, 512, "q", 512)]),
                      (1024, 512, [(0, 512, "k", 1024)]), (1536, 512, [(0, 512, "k", 1536)]),
                      (2048, 512, [(0, 512, "v", 0)]), (2560, 512, [(0, 512, "v", 512)])]
            if kind == 2:
                chunks.append((3072, 16, [(0, 16, "f", 0)]))
            nv = 16
        nblk = 18 if kind == 2 else 16
        xsrc = t_src.ap()
        cb = 0
        for t in range(NT):
            tt = t % 4
            g = t // 4
            s = t % 2
            P.dma("sp", xt[s], xsrc[t * 128:(t + 1) * 128, :], writes=[("xt", s)])
            P.op("act", lambda e, s=s: e.activation(out=junk, in_=xt[s], func=AF.Square, accum_out=ssq),
                 reads=[("xt", s)], writes=["junk", "ssq"])
            P.op("dve", lambda e: e.tensor_scalar(out=rstd, in0=ssq, scalar1=1.0 / D, scalar2=1e-6,
                                                   op0=ALU.mult, op1=ALU.add), reads=["ssq"], writes=["rstd"])
            P.op("act", lambda e: e.activation(out=rstd, in_=rstd, func=AF.Sqrt), reads=["rstd"], writes=["rstd"])
            P.op("dve", lambda e: e.reciprocal(out=rstd, in_=rstd), reads=["rstd"], writes=["rstd"])
            P.op("dve", lambda e, s=s: e.scalar_tensor_tensor(out=hb, in0=xt[s], scalar=rstd, in1=gbc_a[:],
                                                              op0=ALU.mult, op1=ALU.mult),
                 reads=[("xt", s), "rstd", "gbc_a"], writes=["hb"])
            pT = banks[3][:].bitcast(BF16)
            for kc in range(8):
                P.op("pe", lambda e, kc=kc: e.transpose(out=pT[:, kc * 128:(kc + 1) * 128],
                                                         in_=hb[:, kc * 128:(kc + 1) * 128], identity=identb[:]),
                     reads=["hb", "identb"], writes=["b3"])
            P.op("act", lambda e, s=s: e.copy(out=hT[s], in_=pT), reads=["b3"], writes=[("hT", s)])
            hTs = hT[s].rearrange("p (k n) -> p k n", n=128)
            for (c0, n, segs) in chunks:
                bk = banks[cb % 3]; bkey = ("pb", cb % 3); cb += 1
                for kc in range(8):
                    P.op("pe", lambda e, kc=kc, bk=bk, c0=c0, n=n, hTs=hTs: e.matmul(
                        out=bk[:, 0:n], lhsT=hTs[:, kc, :], rhs=wv[:, kc, c0:c0 + n], start=(kc == 0), stop=(kc == 7)),
                        reads=[("hT", s), ("win", kc)], writes=[bkey], crit=("hT", s))
                need_norm = any(sg[2] in ("q", "k") for sg in segs)
                if need_norm:
                    nh = n // 64
                    P.op("act", lambda e, bk=bk, n=n: e.activation(out=sq[:, 0:n], in_=bk[:, 0:n], func=AF.Square),
                         reads=[bkey], writes=["sq"])
                    P.op("dve", lambda e, n=n, nh=nh: e.tensor_reduce(
                        out=ssq8[:, 0:nh], in_=sq[:, 0:n].rearrange("p (a b) -> p a b", b=64), axis=AX.X, op=ALU.add),
                        reads=["sq"], writes=["ssq8"])
                    P.op("dve", lambda e, nh=nh: e.tensor_scalar(out=ssq8[:, 0:nh], in0=ssq8[:, 0:nh], scalar1=1.0 / DH,
                                                                  scalar2=1e-6, op0=ALU.mult, op1=ALU.add),
                         reads=["ssq8"], writes=["ssq8"])
                    P.op("act", lambda e, nh=nh: e.activation(out=ssq8[:, 0:nh], in_=ssq8[:, 0:nh], func=AF.Sqrt),
                         reads=["ssq8"], writes=["ssq8"])
                    P.op("dve", lambda e, nh=nh: e.reciprocal(out=ssq8[:, 0:nh], in_=ssq8[:, 0:nh]),
                         reads=["ssq8"], writes=["ssq8"])
                for (so, sn, typ, dc) in segs:
                    if typ in ("q", "k"):
                        h0 = so // 64; nh = sn // 64
                        dst = tm[:, dc:dc + sn].rearrange("p (a b) -> p a b", b=64)
                        if typ == "k":
                            P.op("dve", lambda e, bk=bk, so=so, sn=sn, h0=h0, nh=nh, dst=dst: e.tensor_tensor(
                                out=dst, in0=bk[:, so:so + sn].rearrange("p (a b) -> p a b", b=64),
                                in1=ssq8[:, h0:h0 + nh].unsqueeze(2).to_broadcast([128, nh, 64]), op=ALU.mult),
                                reads=[bkey, "ssq8"], writes=["tm"])
                        else:
                            sqv = sq[:, so:so + sn].rearrange("p (a b) -> p a b", b=64)
                            P.op("dve", lambda e, bk=bk, so=so, sn=sn, h0=h0, nh=nh, sqv=sqv: e.tensor_tensor(
                                out=sqv, in0=bk[:, so:so + sn].rearrange("p (a b) -> p a b", b=64),
                                in1=ssq8[:, h0:h0 + nh].unsqueeze(2).to_broadcast([128, nh, 64]), op=ALU.mult),
                                reads=[bkey, "ssq8", "sq"], writes=["sq"])
                            P.op("dve", lambda e, nh=nh, sqv=sqv, dst=dst: e.tensor_tensor(
                                out=dst, in0=sqv, in1=gqk[:].unsqueeze(1).to_broadcast([128, nh, 64]), op=ALU.mult),
                                reads=["sq", "gqk"], writes=["tm"])
                    elif typ == "c":
                        P.op("act", lambda e, bk=bk, so=so, sn=sn, dc=dc: e.copy(out=tm[:, dc:dc + sn], in_=bk[:, so:so + sn]),
                             reads=[bkey], writes=["tm"])
                    elif typ == "v":
                        P.op("act", lambda e, bk=bk, so=so, sn=sn, dc=dc: e.copy(out=vm[:, dc:dc + sn], in_=bk[:, so:so + sn]),
                             reads=[bkey], writes=["vm"])
                    elif typ == "g":
                        P.op("act", lambda e, bk=bk, t=t: e.activation(out=gates[:, t, :], in_=bk[:, 0:48], func=AF.Sigmoid),
                             reads=[bkey], writes=["gates"])
                    elif typ == "f" and os.environ.get("KF_SKIP") == "1":
                        pass
                    elif typ == "f":
                        P.op("dve", lambda e, bk=bk: e.tensor_tensor(out=fz[:, 0:16], in0=bk[:, 0:16], in1=bfb[:], op=ALU.add),
                             reads=[bkey, "bfb"], writes=["fz"])
                        P.op("act", lambda e: e.activation(out=fz[:, 0:16], in_=fz[:, 0:16], func=AF.Exp, scale=-1.0),
                             reads=["fz"], writes=["fz"])
                        P.op("act", lambda e: e.activation(out=fz[:, 0:16], in_=fz[:, 0:16], func=AF.Ln, bias=1.0),
                             reads=["fz"], writes=["fz"])
                        b7 = banks[7]
                        P.op("pe", lambda e: e.matmul(out=b7[:, 0:16], lhsT=cf[:, 2, :], rhs=fz[:, 0:16], start=True, stop=True),
                             reads=["fz", "cf"], writes=["b7"])
                        P.op("pe", lambda e: e.matmul(out=b7[:, 16:32], lhsT=cf[:, 3, :], rhs=fz[:, 0:16], start=True, stop=True),
                             reads=["fz", "cf"], writes=["b7"])
                        P.op("dve", lambda e: e.tensor_tensor(out=fz[:, 16:32], in0=b7[:, 0:16], in1=carry[:], op=ALU.add),
                             reads=["b7", "carry", "fz"], writes=["fz2"])
                        P.op("dve", lambda e: e.tensor_tensor(out=carry[:], in0=b7[:, 16:32], in1=carry[:], op=ALU.add),
                             reads=["b7", "carry", "fz2"], writes=["carry"])
                        P.op("dve", lambda e: e.tensor_scalar(out=fz[:, 16:32], in0=fz[:, 16:32], scalar1=-8.0, scalar2=None,
                                                               op0=ALU.mult), reads=["fz2"], writes=["fz2"])
                        caqv = caq[:].rearrange("p (r h) -> p r h", h=16)
                        cakv = cak[:].rearrange("p (r h) -> p r h", h=16)
                        cur = fz[:, 16:32]; nxt = fz[:, 32:48]
                        for r in range(3):
                            P.op("dve", lambda e, r=r, cur=cur: e.tensor_copy(out=caqv[:, r, :], in_=cur),
                                 reads=["fz2"], writes=["caq"])
                            P.op("dve", lambda e, r=r: e.tensor_scalar(out=cakv[:, 3 + r, :], in0=caqv[:, r, :], scalar1=-1.0,
                                                                        scalar2=None, op0=ALU.mult),
                                 reads=["caq"], writes=["cak"])
                            if r < 2:
                                P.op("dve", lambda e, r=r, cur=cur, nxt=nxt: e.tensor_tensor(out=nxt, in0=cur, in1=caqv[:, r, :],
                                                                                             op=ALU.subtract),
                                     reads=["fz2", "caq"], writes=["fz2"])
                                cur, nxt = nxt, cur
                        P.op("dve", lambda e: e.tensor_copy(out=tm[:, 2048:2144], in_=caq[:]), reads=["caq"], writes=["tm"])
                        P.op("dve", lambda e: e.tensor_copy(out=tm[:, 2176:2272], in_=cak[:]), reads=["cak"], writes=["tm"])
            for half in range((nblk + 7) // 8):
                pb = banks[4 + half][:].bitcast(BF16)
                nb = min(8, nblk - half * 8)
                for b in range(nb):
                    blk = half * 8 + b
                    P.op("pe", lambda e, pb=pb, b=b, blk=blk: e.transpose(out=pb[:, b * 128:(b + 1) * 128],
                                                                           in_=tm[:, blk * 128:(blk + 1) * 128], identity=identb[:]),
                         reads=["tm", "identb"], writes=[("b4", half)])
                dstv = tst[:, half * 8:half * 8 + nb, tt * 128:(tt + 1) * 128]
                srcv = pb[:, 0:nb * 128].rearrange("p (b n) -> p b n", n=128)
                eng = "act" if half == 0 else "dve"
                if eng == "act":
                    P.op("act", lambda e, dstv=dstv, srcv=srcv: e.copy(out=dstv, in_=srcv),
                         reads=[("b4", half)], writes=[("tst", half)])
                else:
                    P.op("dve", lambda e, dstv=dstv, srcv=srcv: e.tensor_copy(out=dstv, in_=srcv),
                         reads=[("b4", half)], writes=[("tst", half)])
            vdst = bass.AP(t_V, t * 128 * DH, [[DH, 128], [S * DH, nv], [1, DH]])
            P.dma("pool", vdst, vm[:, 0:nv * 64].rearrange("p (u d) -> p u d", d=64), reads=["vm"])
            if tt == 3:
                for half in range(2):
                    qdst = bass.AP(t_QK, half * 128 * S + g * 512, [[S, 64], [2 * 128 * S, 16], [1, 512]])
                    P.dma("pool", qdst, tst[half * 64:(half + 1) * 64, 0:16, :], reads=[("tst", 0), ("tst", 1)])
                if kind == 2:
                    for qk in range(2):
                        adst = bass.AP(t_AUG, qk * 96 * S + g * 512, [[S, 96], [1, 512]])
                        P.dma("pool", adst, tst[0:96, 16 + qk, :], reads=[("tst", 2)])
        P.barrier()

    def phase3(L, t_src, t_dst):
        wu = AA[:, :].rearrange("p (k n) -> p k n", n=DFF)
        wup = t_wup.ap()[L].rearrange("(k p) n -> p k n", p=128)
        for kc in range(8):
            P.dma("pool", wu[:, kc, :], wup[:, kc, :], writes=[("wup", kc)])
        wd = AB[:, :].rearrange("p (k n) -> p k n", n=1024)
        wdn = t_wdn.ap()[L].rearrange("(k p) n -> p k n", p=128)
        for q4 in range(4):
            P.dma("pool", wd[:, q4 * 8:(q4 + 1) * 8, :], wdn[:, q4 * 8:(q4 + 1) * 8, :], writes=[("wdn", q4)])
        ot = [b16v(GA, 0, 1024), b16v(GA, 2048, 1024)]
        xr = [f32v(GA, 4096, 1024), f32v(GA, 8192, 1024)]
        oT = b16v(GA, 12288, 1024)
        h2 = b16v(GA, 14336, 1024)
        h2T = b16v(GA, 16384, 1024)
        sq = f32v(GA, 18432, 512)
        uT = b16v(GA, 20480, 32 * 128).rearrange("p (f n) -> p f n", n=128)
        xo = [f32v(GA, 28672, 1024), f32v(GA, 32768, 1024)]
        junk = b16v(GA, 36864, 1024)
        ssq = small[:, 0:1]; rstd = small[:, 1:2]
        src = t_src.ap(); dst = t_dst.ap(); mix = t_MIX.ap()
        cbc = [0]

        def nextbank():
            bk = banks[cbc[0] % 3]; bkey = ("pb", cbc[0] % 3); cbc[0] += 1
            return bk, bkey

        pT = banks[3][:].bitcast(BF16)
        pT2 = banks[4][:].bitcast(BF16)
        oTv = oT.rearrange("p (k n) -> p k n", n=128)
        h2Tv = h2T.rearrange("p (k n) -> p k n", n=128)

        def A1(t):
            s = t % 2
            P.dma("sp", ot[s], mix[t * 128:(t + 1) * 128, :], writes=[("ot", s)])
            P.dma("sp", xr[s], src[t * 128:(t + 1) * 128, :], writes=[("xr", s)])
            for kc in range(8):
                P.op("pe", lambda e, kc=kc, s=s: e.transpose(out=pT[:, kc * 128:(kc + 1) * 128],
                                                              in_=ot[s][:, kc * 128:(kc + 1) * 128], identity=identb[:]),
                     reads=[("ot", s), "identb"], writes=["b3"])
            P.op("act", lambda e: e.copy(out=oT, in_=pT), reads=["b3"], writes=["oT"])
            for c in range(2):
                bk, bkey = nextbank()
                for kc in range(8):
                    P.op("pe", lambda e, kc=kc, bk=bk, c=c: e.matmul(out=bk[:, :], lhsT=oTv[:, kc, :], rhs=WO[:, kc, c * 512:(c + 1) * 512],
                                                                    start=(kc == 0), stop=(kc == 7)),
                         reads=["oT", "wo"], writes=[bkey], crit="oT")
                P.op("dve", lambda e, bk=bk, c=c, s=s: e.tensor_tensor(out=xr[s][:, c * 512:(c + 1) * 512], in0=bk[:, :],
                                                                      in1=xr[s][:, c * 512:(c + 1) * 512], op=ALU.add),
                     reads=[bkey, ("xr", s)], writes=[("xr", s)])
            P.op("act", lambda e, s=s: e.activation(out=junk, in_=xr[s], func=AF.Square, accum_out=ssq),
                 reads=[("xr", s)], writes=["junk", "ssq"])
            P.op("dve", lambda e: e.tensor_scalar(out=rstd, in0=ssq, scalar1=1.0 / D, scalar2=1e-6,
                                                   op0=ALU.mult, op1=ALU.add), reads=["ssq"], writes=["rstd"])
            P.op("act", lambda e: e.activation(out=rstd, in_=rstd, func=AF.Sqrt), reads=["rstd"], writes=["rstd"])
            P.op("dve", lambda e: e.reciprocal(out=rstd, in_=rstd), reads=["rstd"], writes=["rstd"])
            P.op("dve", lambda e, s=s: e.scalar_tensor_tensor(out=h2, in0=xr[s], scalar=rstd, in1=gbc_m[:],
                                                              op0=ALU.mult, op1=ALU.mult),
                 reads=[("xr", s), "rstd", "gbc_m"], writes=["h2"])

        def A2(t):
            for kc in range(8):
                P.op("pe", lambda e, kc=kc: e.transpose(out=pT2[:, kc * 128:(kc + 1) * 128],
                                                         in_=h2[:, kc * 128:(kc + 1) * 128], identity=identb[:]),
                     reads=["h2", "identb"], writes=["b4"])
            P.op("act", lambda e: e.copy(out=h2T, in_=pT2), reads=["b4"], writes=["h2T"])

        def Bup(t):
            for fq in range(8):
                bk, bkey = nextbank()
                for fi in range(4):
                    fc = fq * 4 + fi
                    for kc in range(8):
                        P.op("pe", lambda e, kc=kc, bk=bk, fc=fc, fi=fi: e.matmul(
                            out=bk[:, fi * 128:(fi + 1) * 128], lhsT=wu[:, kc, fc * 128:(fc + 1) * 128], rhs=h2Tv[:, kc, :],
                            start=(kc == 0), stop=(kc == 7)), reads=["h2T", ("wup", kc)], writes=[bkey])
                P.op("act", lambda e, bk=bk: e.activation(out=sq, in_=bk[:, :], func=AF.Square), reads=[bkey], writes=["sq"])
                P.op("dve", lambda e, bk=bk, fq=fq: e.scalar_tensor_tensor(
                    out=uT[:, fq * 4:(fq + 1) * 4, :], in0=bk[:, :].rearrange("p (f n) -> p f n", n=128), scalar=0.0,
                    in1=sq.rearrange("p (f n) -> p f n", n=128), op0=ALU.is_gt, op1=ALU.mult),
                    reads=[bkey, "sq"], writes=[("uT", fq)])

        def Cdown(t):
            s = t % 2
            for c in range(2):
                bk, bkey = nextbank()
                for fc in range(32):
                    P.op("pe", lambda e, fc=fc, bk=bk, c=c: e.matmul(out=bk[:, :], lhsT=uT[:, fc, :], rhs=wd[:, fc, c * 512:(c + 1) * 512],
                                                                    start=(fc == 0), stop=(fc == 31)),
                         reads=[("uT", fc // 4), ("wdn", fc // 8)], writes=[bkey], crit=("uT", fc // 4))
                P.op("dve", lambda e, bk=bk, c=c, s=s: e.tensor_tensor(out=xo[s][:, c * 512:(c + 1) * 512], in0=bk[:, :],
                                                                      in1=xr[s][:, c * 512:(c + 1) * 512], op=ALU.add),
                     reads=[bkey, ("xr", s)], writes=[("xo", s)])
            P.dma("sp", dst[t * 128:(t + 1) * 128, :], xo[s], reads=[("xo", s)])

        A1(0); A2(0); Bup(0)
        for t in range(NT):
            if t + 1 < NT:
                A1(t + 1)
            Cdown(t)
            if t + 1 < NT:
                A2(t + 1); Bup(t + 1)
        P.barrier()

    KA = [b16v(AA, 0, S), b16v(AB, 24576, S)]
    QA = [b16v(AA, 8192, S), b16v(AB, 32768, S)]
    VA = [b16v(GA, 0, NT * 66).rearrange("p (t c) -> p t c", c=66),
          b16v(AB, 40960, NT * 66).rearrange("p (t c) -> p t c", c=66)]
    WS = [f32v(GA, 4352, WU), f32v(AB, 45184, WU)]
    Hk = f32v(AA, 16384, WU)
    lg = [f32v(GA, 22528 + i * 2048, 512) for i in range(3)] + [f32v(AB, 20480, 512)]
    pTb = [b16v(GA, 28672 + i * 1024, 512) for i in range(3)] + [b16v(AB, 22528, 512)]
    NR = 4
    LAG = 3
    lbank_idx = [0, 1, 2, 6]
    lbanks = [banks[i] for i in lbank_idx]
    ost = f32v(GA, 31744, NT * 64).rearrange("p (t d) -> p t d", d=64)
    rs_t = small[:, 2:3]
    oTs = [f32v(AB, 16384, 512), f32v(AB, 18432, 512)]
    gctr = [0]
    rctr = [0]
    b7 = banks[7]
    B7 = ("bk", 7)

    deferred = []

    def tick():
        fire = []
        for it in deferred:
            it[0] -= 1
        while deferred and deferred[0][0] <= 0:
            fire.append(deferred.pop(0)[1])
        for fn in fire:
            fn()

    def force_until(tag):
        idx = -1
        for k, it in enumerate(deferred):
            if it[2] == tag:
                idx = k
        for _ in range(idx + 1):
            deferred.pop(0)[1]()

    def flush_deferred():
        while deferred:
            deferred.pop(0)[1]()

    rs4 = small[:, 16:20].unsqueeze(2)
    eptmp = f32v(AB, 63488, 256).rearrange("p (t d) -> p t d", d=64)

    def strip_dma(spec):
        tg, off, step, width = spec[0:4]
        rd = list(spec[4]) if len(spec) > 4 else []
        P.dma("sp", Hk[:, 0:width], bass.AP(tg, off, [[step, 128], [1, width]]), reads=rd, writes=["Hk"])

    def strip_compute(spec, s):
        width = spec[3]
        W = WS[s]
        for c in range((width + 511) // 512):
            n = min(512, width - c * 512)
            P.op("pe", lambda e, c=c, n=n: e.matmul(out=b7[:, 0:n], lhsT=cf[:, 1, :], rhs=Hk[:, c * 512:c * 512 + n],
                                                   start=True, stop=True), reads=["Hk", "cf"], writes=[B7])
            P.op("act", lambda e, c=c, n=n, W=W: e.copy(out=W[:, c * 512:c * 512 + n], in_=b7[:, 0:n]),
                 reads=[B7], writes=[("W", s)])

    def attn_core(run, s, hook=None):
        K = run["K"]; steps = run["steps"]; nvc = 66
        ka_fn = run["ka_fn"]; va_fn = run["va_fn"]; epilogue = run["ep"]
        kreads = run["kreads"](s); vreads = run["vreads"](s); qreads = run["qreads"](s)
        qa = QA[s]; W = WS[s]
        gfirst = {}; glast = {}; gpar = {}
        for n, st in enumerate(steps):
            G = st[0]
            if G not in gfirst:
                gfirst[G] = n
                gpar[G] = gctr[0] % 2
                gctr[0] += 1
            glast[G] = n

        def emit_pv(n):
            G, j = steps[n][0:2]
            r = n % NR
            par = gpar[G]
            otb = banks[3 + par]; okey = ("bk", 3 + par)
            P.op("pe", lambda e, otb=otb, r=r, j=j, n=n, G=G: e.matmul(
                out=otb[0:nvc, :], lhsT=va_fn(j, s), rhs=pTb[r][:, :], start=(gfirst[G] == n), stop=(glast[G] == n)),
                reads=[("pT", r)] + vreads, writes=[okey])
            if glast[G] == n:
                ots = oTs[par]
                force_until(("fin", par))
                P.op("act", lambda e, ots=ots, otb=otb: e.copy(out=ots[0:nvc, :], in_=otb[0:nvc, :]),
                     reads=[okey], writes=[("oTs", par)])

                def finish(G=G, ots=ots, par=par):
                    tb = banks[5]; tkey = ("bk", 5)
                    for t in range(4):
                        P.op("pe", lambda e, tb=tb, ots=ots, t=t: e.transpose(
                            out=tb[:, t * nvc:(t + 1) * nvc], in_=ots[0:nvc, t * 128:(t + 1) * 128], identity=cf[0:nvc, 0, 0:nvc]),
                            reads=[("oTs", par), "cf"], writes=[tkey], crit=("oTs", par))
                    epilogue(G, tb[:, 0:4 * nvc].rearrange("p (t c) -> p t c", c=nvc), tkey)
                deferred.append([4, finish, ("fin", par)])

        hook_at = min(5, len(steps) - 1)
        for n, st in enumerate(steps):
            G, j, woff = st[0:3]
            mode = st[4] if len(st) > 4 else "W"
            r = n % NR
            bk = lbanks[r]; bkey = ("bk", lbank_idx[r])
            P.op("pe", lambda e, bk=bk, j=j, G=G: e.matmul(out=bk[:, :], lhsT=ka_fn(j, K, s), rhs=qa[0:K, G * 512:(G + 1) * 512],
                                                          start=True, stop=True),
                 reads=kreads + qreads, writes=[bkey])
            if mode == "W":
                P.op("dve", lambda e, bk=bk, r=r, woff=woff: e.scalar_tensor_tensor(
                    out=lg[r], in0=bk[:, :], scalar=0.125, in1=W[:, woff:woff + 512], op0=ALU.mult, op1=ALU.add),
                    reads=[bkey, ("W", s)], writes=[("lg", r)])
                P.op("act", lambda e, r=r: e.activation(out=pTb[r], in_=lg[r], func=AF.Exp),
                     reads=[("lg", r)], writes=[("pT", r)])
            elif mode == "plain":
                P.op("act", lambda e, r=r, bk=bk: e.activation(out=pTb[r], in_=bk[:, :], func=AF.Exp, scale=0.125),
                     reads=[bkey], writes=[("pT", r)])
            else:
                bap = mode[1]
                P.op("act", lambda e, r=r, bk=bk, bap=bap: e.activation(out=pTb[r], in_=bk[:, :], func=AF.Exp, bias=bap, scale=0.125),
                     reads=[bkey, "b31"], writes=[("pT", r)])
            if n >= LAG:
                emit_pv(n - LAG)
            tick()
            if n == hook_at and hook is not None:
                hook()
        for n in range(max(0, len(steps) - LAG), len(steps)):
            emit_pv(n)
        if run.get("post") is not None:
            deferred.append([4, run["post"], None])

    def execute(runs):
        def sset(i):
            return (rctr[0] + i) % 2

        def pf_dma(i):
            s = sset(i)
            runs[i]["loads"](s)
            if runs[i].get("strip") is not None:
                strip_dma(runs[i]["strip"])

        def pf_compute(i):
            s = sset(i)
            if runs[i].get("strip") is not None:
                strip_compute(runs[i]["strip"], s)
            if runs[i].get("pre") is not None:
                runs[i]["pre"](s)

        pf_dma(0)
        pf_compute(0)
        for i in range(len(runs)):
            hook = None
            if i + 1 < len(runs):
                pf_dma(i + 1)
                hook = (lambda i=i: pf_compute(i + 1))
            attn_core(runs[i], sset(i), hook)
        rctr[0] += len(runs)

    def std_steps(maxspan=None, plain_past=False, const_far=None):
        steps = []
        for G in range(8):
            j0 = 0 if maxspan is None else max(0, 4 * G - maxspan)
            for j in range(j0, 4 * G + 4):
                act = [i for i in range(4 * G, 4 * G + 4) if i >= j and (maxspan is None or i - j <= maxspan)]
                if act:
                    Dd = 512 * G - 128 * j
                    mode = "W"
                    if plain_past and Dd >= 128:
                        mode = "plain"
                    elif const_far is not None and Dd >= 1664:
                        mode = ("const", const_far)
                    steps.append((G, j, Dd + WOFF, act, mode))
        return steps

    def ep_common(G, tbv, okey, gate_col):
        P.op("dve", lambda e: e.tensor_scalar(out=rs4, in0=tbv[:, :, 64:65], scalar1=1e-30, scalar2=None, op0=ALU.max),
             reads=[okey], writes=["rs"])
        P.op("dve", lambda e: e.reciprocal(out=rs4, in_=rs4), reads=["rs"], writes=["rs"])
        if gate_col is not None:
            P.op("dve", lambda e: e.tensor_tensor(out=rs4, in0=rs4, in1=gates[:, 4 * G:4 * G + 4, gate_col:gate_col + 1], op=ALU.mult),
                 reads=["rs", "gates"], writes=["rs"])

    def ep_plain(gate_col=None, accumulate=False):
        def ep(G, tbv, okey):
            ep_common(G, tbv, okey, gate_col)
            okeys = [("ost", i) for i in range(4 * G, 4 * G + 4)]
            if accumulate:
                P.op("dve", lambda e: e.tensor_tensor(out=eptmp, in0=tbv[:, :, 0:64], in1=rs4.to_broadcast([128, 4, 64]), op=ALU.mult),
                     reads=[okey, "rs"], writes=["eptmp"])
                P.op("dve", lambda e: e.tensor_tensor(out=ost[:, 4 * G:4 * G + 4, :], in0=eptmp, in1=ost[:, 4 * G:4 * G + 4, :], op=ALU.add),
                     reads=["eptmp"] + okeys, writes=okeys)
            else:
                P.op("dve", lambda e: e.tensor_tensor(out=ost[:, 4 * G:4 * G + 4, :], in0=tbv[:, :, 0:64], in1=rs4.to_broadcast([128, 4, 64]), op=ALU.mult),
                     reads=[okey, "rs"], writes=okeys)
        return ep

    def ld_k(s, unit, extra=None):
        P.dma("sp", KA[s][0:64, :], t_QK.ap()[unit, 0:64, :], writes=[("ka", s)])
        if extra is not None:
            tg, r0, nr = extra
            P.dma("sp", KA[s][64:64 + nr, :], tg.ap()[r0:r0 + nr, :], writes=[("ka_hi", s)])

    def ld_q(s, unit):
        P.dma("sp", QA[s][0:64, :], t_QK.ap()[unit, 0:64, :], writes=[("qa", s)])

    def ld_v(s, unit):
        P.dma("sp", VA[s][:, :, 0:64], t_V.ap()[unit].rearrange("(t p) d -> p t d", p=128), writes=[("va", s)])

    def store_o(h):
        dst = bass.AP(t_MIX, h * DH, [[D, 128], [128 * D, NT], [1, DH]])
        P.dma("pool", dst, ost[:, :, :], reads=[("ost", i) for i in range(NT)])

    ka_std = lambda j, K, s: KA[s][0:K, j * 128:(j + 1) * 128]
    va_std = lambda j, s: VA[s][:, j, :]

    def phase2_common_init():
        for s in range(2):
            P.op("dve", lambda e, s=s: e.memset(VA[s][:, :, 64:66], 1.0), writes=[("va_ones", s)])

    def base_run(K, steps, ep, hi_k=False, hi_q=False):
        return dict(K=K, steps=steps, ka_fn=ka_std, va_fn=va_std, ep=ep,
                    kreads=(lambda s: [("ka", s)] + ([("ka_hi", s)] if hi_k else [])),
                    vreads=(lambda s: [("va", s), ("va_ones", s)]),
                    qreads=(lambda s: [("qa", s)] + ([("qa_hi", s)] if hi_q else [])))

    def phase2_fox():
        phase2_common_init()
        spec = (t_g0, 0, 1, WU)
        for s in range(2):
            strip_dma(spec)
            strip_compute(spec, s)
        steps = std_steps(plain_past=True)
        runs = []
        for h in range(H):
            r = base_run(70, steps, ep_plain(), hi_k=True, hi_q=True)

            def loads(s, h=h):
                ld_k(s, 16 + h)
                P.dma("sp", KA[s][64:70, :], bass.AP(t_AUG, (96 + h) * S, [[16 * S, 6], [1, S]]), writes=[("ka_hi", s)])
                ld_q(s, h)
                P.dma("sp", QA[s][64:70, :], bass.AP(t_AUG, h * S, [[16 * S, 6], [1, S]]), writes=[("qa_hi", s)])
                ld_v(s, h)
            r["loads"] = loads
            r["post"] = (lambda h=h: store_o(h))
            runs.append(r)
        execute(runs)
        flush_deferred()
        P.barrier()

    def phase2_dil():
        phase2_common_init()
        gsb = f32v(AA, 16384, GLEN)[0:16, :]
        lsb = f32v(AB, 45184, GLEN)[0:16, :]
        P.dma("sp", gsb, t_gBd.ap()[:, :], writes=["Hk"])
        P.dma("sp", lsb, bass.AP(t_lnc, 0, [[0, 16], [1, GLEN]]), writes=[("W", 1)])
        P.op("dve", lambda e: e.tensor_tensor(out=gsb, in0=gsb, in1=lsb, op=ALU.add), reads=["Hk", ("W", 1)], writes=["Hk"])
        P.dma("sp", t_gD.ap()[:, :], gsb, reads=["Hk"], writes=["gD"])
        steps = std_steps(maxspan=16)
        runs = []
        for h in range(H):
            r = base_run(64, steps, ep_plain())
            r["strip"] = (t_gD, h * GLEN, 1, 2048 + WOFF + 512 + 128, ["gD"])

            def loads(s, h=h):
                ld_k(s, 16 + h); ld_q(s, h); ld_v(s, h)
            r["loads"] = loads
            r["post"] = (lambda h=h: store_o(h))
            runs.append(r)
        execute(runs)
        flush_deferred()
        P.barrier()

    def phase2_moba():
        phase2_common_init()
        km32 = f32v(GA, 39936, 16)
        kmb = b16v(GA, 40064, 16)
        gm = f32v(GA, 40128, 16)
        mx8 = f32v(GA, 40192, 8)
        mb80 = f32v(GA, 40256, 80)
        P.op("dve", lambda e: e.memset(mb80, 0.0), writes=["mb80"])

        def pre(s):
            ka = KA[s]; qa = QA[s]
            P.op("dve", lambda e: e.tensor_reduce(out=km32[0:64, :], in_=ka[0:64, :].rearrange("p (n b) -> p n b", b=256),
                                                  axis=AX.X, op=ALU.add), reads=[("ka", s)], writes=["km32"])
            P.op("dve", lambda e: e.tensor_scalar(out=kmb[0:64, :], in0=km32[0:64, :], scalar1=1.0 / 256, scalar2=None, op0=ALU.mult),
                 reads=["km32"], writes=["kmb"])
            for i in range(NT):
                own = i // 2
                P.op("dve", lambda e: e.memset(gm, NEG), writes=["gm"])
                if own > 0:
                    P.op("pe", lambda e, i=i: e.matmul(out=b7[:, 0:16], lhsT=qa[0:64, i * 128:(i + 1) * 128], rhs=kmb[0:64, :],
                                                       start=True, stop=True), reads=[("qa", s), "kmb"], writes=[B7])
                    P.op("dve", lambda e, own=own: e.tensor_copy(out=gm[:, 0:own], in_=b7[:, 0:own]), reads=[B7, "gm"], writes=["gm"])
                if own > 3:
                    P.op("dve", lambda e: e.max(out=mx8, in_=gm), reads=["gm"], writes=["mx8"])
                    P.op("dve", lambda e: e.tensor_scalar(out=mb80[:, 64:80], in0=gm, scalar1=mx8[:, 2:3], scalar2=None, op0=ALU.is_ge),
                         reads=["gm", "mx8"], writes=["mb80"])
                else:
                    P.op("dve", lambda e: e.tensor_scalar(out=mb80[:, 64:80], in0=gm, scalar1=-1.0e29, scalar2=None, op0=ALU.is_ge),
                         reads=["gm"], writes=["mb80"])
                P.op("dve", lambda e: e.tensor_scalar(out=mb80[:, 64:80], in0=mb80[:, 64:80], scalar1=-1.0, scalar2=BIG,
                                                       op0=ALU.add, op1=ALU.mult), reads=["mb80"], writes=["mb80"])
                P.op("dve", lambda e, own=own: e.memset(mb80[:, 64 + own:65 + own], 0.0), reads=["mb80"], writes=["mb80"])
                P.op("pe", lambda e: e.transpose(out=b7[0:80, 128:256], in_=mb80, identity=cf[:, 0, :]), reads=["mb80", "cf"], writes=[B7])
                P.op("act", lambda e, i=i: e.copy(out=qa[64:80, i * 128:(i + 1) * 128], in_=b7[64:80, 128:256]),
                     reads=[B7], writes=[("qa_hi", s)])

        runs = []
        for h in range(H):
            r = base_run(80, std_steps(const_far=b31[:, h:h + 1]), ep_plain(), hi_k=True, hi_q=True)
            r["strip"] = (t_gB, h * GLEN, 1, 1664 + WOFF + 512 + 128)

            def loads(s, h=h):
                ld_k(s, 16 + h, (t_E16, 0, 16)); ld_q(s, h); ld_v(s, h)
            r["loads"] = loads
            r["pre"] = pre
            r["post"] = (lambda h=h: store_o(h))
            runs.append(r)
        execute(runs)
        flush_deferred()
        P.barrier()

    def phase2_nsa():
        phase2_common_init()
        kcR = b16v(AA, 16384, S)
        kcA = b16v(AA, 24576, S)
        kcB = b16v(AA, 34304, S)
        w1 = b16v(AA, 42496, 32 * 256).rearrange("p (t c) -> p t c", c=256)
        w2 = b16v(AA, 58880, 128).rearrange("p (m d) -> p m d", d=64)
        hid = b16v(AA, 59392, 512).rearrange("p (m n) -> p m n", n=256)
        kcn = b16v(AA, 60416, 256).rearrange("p (t d) -> p t d", d=128)
        kcmpT = b16v(AA, 60928, 256)
        vcmp = b16v(AA, 61440, 2 * 66).rearrange("p (t c) -> p t c", c=66)
        vcmpA = b16v(AA, 61952, 2 * 66).rearrange("p (t c) -> p t c", c=66)
        posT = f32v(AA, 62464, 32)
        impb = SBv["imp"]
        STt = SBv["ST"]
        OVt = SBv["OV"]
        impm = f32v(GA, 39936, 64)
        mx16 = f32v(GA, 40192, 16)
        mb128 = f32v(GA, 40256, 128)
        x2 = f32v(GA, 40768, 256)
        tg = f32v(GA, 41792, 256)
        P.dma("sp", STt[:], t_ST.ap()[:, :], writes=["ST"])
        P.dma("sp", OVt[:], t_OV.ap()[:, :, :], writes=["OV"])
        P.op("dve", lambda e: e.memset(mb128, 0.0), writes=["mb128"])
        stepwin = std_steps(maxspan=4)
        stepcmp = []
        for G in range(8):
            for ntl in range(2):
                if G >= 4 * ntl:
                    stepcmp.append((G, ntl, 512 * G - 2048 * ntl, list(range(4 * G, 4 * G + 4))))
        kc_fn = lambda j, K, s: kcmpT[0:64, j * 128:(j + 1) * 128]
        vc_fn = lambda j, s: vcmp[:, j, :]
        vcA_fn = lambda j, s: vcmpA[:, j, :]
        b7b = b7[:].bitcast(BF16)

        def compress(kh):
            P.op("dve", lambda e: e.memset(kcn, 0.0), writes=["kcn"])
            P.op("dve", lambda e: e.memset(vcmp, 0.0), writes=["vcmp"])
            P.op("dve", lambda e: e.memset(vcmp[:, :, 64:66], 1.0), reads=["vcmp"], writes=["vcmp"])
            P.op("dve", lambda e: e.memset(vcmpA[:, :, 64:66], 1.0), writes=["vcmpA"])
            P.op("dve", lambda e: e.tensor_copy(out=vcmpA[:, :, 0:64], in_=OVt[:]), reads=["vcmpA", "OV"], writes=["vcmpA"])
            for kv in range(2):
                P.dma("sp", kcR[0:64, :], t_QK.ap()[24 + 4 * kv + kh, 0:64, :], writes=["Hk"])
                P.dma("sp", posT[0:64, :], t_posT.ap()[kv], writes=["posT"])
                P.dma("pool", w1[0:64, :, :], t_w1.ap()[kv].rearrange("(t d) c -> d t c", d=64), writes=["w1"])
                P.dma("pool", w2[:, :, :], t_w2.ap()[kv].rearrange("(m p) d -> p m d", p=128), writes=["w2"])
                for ab, dstb in ((0, kcA), (1, kcB)):
                    P.op("dve", lambda e, ab=ab, dstb=dstb: e.tensor_tensor(
                        out=dstb[0:64, :].rearrange("p (n s) -> p n s", s=16), in0=kcR[0:64, :].rearrange("p (n s) -> p n s", s=16),
                        in1=posT[0:64, ab * 16:(ab + 1) * 16].unsqueeze(1).to_broadcast([64, 256, 16]), op=ALU.add),
                        reads=["Hk", "posT"], writes=[("kcAB", ab)] if ab == 1 else ["Hk"])
                kcAv = kcA[0:64, :].rearrange("p (n s) -> p n s", s=16)
                kcBv = kcB[0:64, :].rearrange("p (n s) -> p n s", s=16)
                for mc in range(2):
                    for t in range(32):
                        rhs = kcAv[:, 0:255, t] if t < 16 else kcBv[:, 1:256, t - 16]
                        P.op("pe", lambda e, t=t, mc=mc, rhs=rhs: e.matmul(out=b7[:, 0:255], lhsT=w1[0:64, t, mc * 128:(mc + 1) * 128],
                                                                       rhs=rhs, start=(t == 0), stop=(t == 31)),
                             reads=["Hk", ("kcAB", 1), "w1"], writes=[B7])
                    P.op("act", lambda e: e.activation(out=x2[:, 0:255], in_=b7[:, 0:255], func=AF.Square), reads=[B7], writes=["x2"])
                    P.op("dve", lambda e: e.tensor_scalar(out=x2[:, 0:255], in0=x2[:, 0:255], scalar1=0.044715, scalar2=1.0,
                                                           op0=ALU.mult, op1=ALU.add), reads=["x2"], writes=["x2"])
                    P.op("dve", lambda e: e.tensor_tensor(out=tg[:, 0:255], in0=b7[:, 0:255], in1=x2[:, 0:255], op=ALU.mult),
                         reads=[B7, "x2"], writes=["tg"])
                    P.op("act", lambda e: e.activation(out=tg[:, 0:255], in_=tg[:, 0:255], func=AF.Sigmoid, scale=1.5957691216),
                         reads=["tg"], writes=["tg"])
                    P.op("dve", lambda e, mc=mc: e.tensor_tensor(out=hid[:, mc, 0:255], in0=b7[:, 0:255], in1=tg[:, 0:255], op=ALU.mult),
                         reads=[B7, "tg"], writes=["hid"])
                for ntl in range(2):
                    nn = 128 if ntl == 0 else 127
                    for mc in range(2):
                        P.op("pe", lambda e, ntl=ntl, nn=nn, mc=mc: e.matmul(out=b7[0:nn, 256:320], lhsT=hid[:, mc, ntl * 128:ntl * 128 + nn],
                                                                         rhs=w2[:, mc, :], start=(mc == 0), stop=(mc == 1)),
                             reads=["hid", "w2"], writes=[B7])
                    if kv == 0:
                        ssq = small[:, 4:5]
                        P.op("act", lambda e, nn=nn: e.activation(out=x2[0:nn, 0:64], in_=b7[0:nn, 256:320], func=AF.Square, accum_out=ssq[0:nn, :]),
                             reads=[B7], writes=["x2", "ssq4"])
                        P.op("dve", lambda e: e.tensor_scalar(out=ssq, in0=ssq, scalar1=1.0 / DH, scalar2=1e-6, op0=ALU.mult, op1=ALU.add),
                             reads=["ssq4"], writes=["ssq4"])
                        P.op("act", lambda e: e.activation(out=ssq, in_=ssq, func=AF.Sqrt), reads=["ssq4"], writes=["ssq4"])
                        P.op("dve", lambda e: e.reciprocal(out=ssq, in_=ssq), reads=["ssq4"], writes=["ssq4"])
                        P.op("dve", lambda e, nn=nn, ntl=ntl: e.tensor_scalar(out=kcn[0:nn, ntl, 0:64], in0=b7[0:nn, 256:320], scalar1=ssq[0:nn, :],
                                                                           scalar2=None, op0=ALU.mult), reads=[B7, "ssq4"], writes=["kcn"])
                        P.op("pe", lambda e, ntl=ntl: e.transpose(out=b7b[:, 768:896], in_=kcn[:, ntl, :], identity=identb[:]),
                             reads=["kcn", "identb"], writes=[B7])
                        P.op("act", lambda e, ntl=ntl: e.copy(out=kcmpT[0:64, ntl * 128:(ntl + 1) * 128], in_=b7b[0:64, 768:896]),
                             reads=[B7], writes=["kcmpT"])
                    else:
                        P.op("act", lambda e, nn=nn, ntl=ntl: e.copy(out=vcmp[0:nn, ntl, 0:64], in_=b7[0:nn, 256:320]),
                             reads=[B7], writes=["vcmp"])
            P.op("dve", lambda e: e.memset(impb, 0.0), writes=["imp"])

        def epA(G, tbv, okey):
            ep_common(G, tbv, okey, None)
            P.op("dve", lambda e: e.tensor_tensor(out=eptmp, in0=tbv[:, :, 0:64], in1=rs4.to_broadcast([128, 4, 64]), op=ALU.mult),
                 reads=[okey, "rs"], writes=["eptmp"])
            P.op("dve", lambda e: e.tensor_tensor(out=impb[:, 4 * G:4 * G + 4, :], in0=eptmp, in1=impb[:, 4 * G:4 * G + 4, :], op=ALU.add),
                 reads=["eptmp", "imp"], writes=["imp"])

        def selection():
            for i in range(NT):
                P.op("dve", lambda e, i=i: e.tensor_tensor(out=impm, in0=impb[:, i, :], in1=STt[:, 64 - 2 * i:128 - 2 * i], op=ALU.add),
                     reads=["imp", "ST"], writes=["impm"])
                P.op("dve", lambda e: e.max(out=mx16[:, 0:8], in_=impm), reads=["impm"], writes=["mx16"])
                P.op("dve", lambda e: e.match_replace(out=mb128[:, 0:64], in_to_replace=mx16[:, 0:8], in_values=impm, imm_value=NEG),
                     reads=["impm", "mx16"], writes=["mb128"])
                P.op("dve", lambda e: e.max(out=mx16[:, 8:16], in_=mb128[:, 0:64]), reads=["mb128"], writes=["mx16"])
                P.op("dve", lambda e: e.tensor_scalar(out=mx16[:, 14:15], in0=mx16[:, 14:15], scalar1=-1.0e29, scalar2=None, op0=ALU.max),
                     reads=["mx16"], writes=["mx16"])
                P.op("dve", lambda e: e.tensor_scalar(out=mb128[:, 64:128], in0=impm, scalar1=mx16[:, 14:15], scalar2=None, op0=ALU.is_ge),
                     reads=["impm", "mx16", "mb128"], writes=["mb128"])
                P.op("dve", lambda e: e.tensor_scalar(out=mb128[:, 64:128], in0=mb128[:, 64:128], scalar1=-1.0, scalar2=BIG,
                                                       op0=ALU.add, op1=ALU.mult), reads=["mb128"], writes=["mb128"])
                P.op("dve", lambda e, i=i: e.memset(mb128[0:64, 64 + 2 * i:65 + 2 * i], 0.0), reads=["mb128"], writes=["mb128"])
                P.op("dve", lambda e, i=i: e.memset(mb128[64:128, 65 + 2 * i:66 + 2 * i], 0.0), reads=["mb128"], writes=["mb128"])
                P.op("pe", lambda e: e.transpose(out=b7[:, 0:128], in_=mb128, identity=cf[:, 0, :]), reads=["mb128", "cf"], writes=[B7])
                for s in range(2):
                    P.op("act", lambda e, i=i, s=s: e.copy(out=QA[s][64:128, i * 128:(i + 1) * 128], in_=b7[64:128, 0:128]),
                         reads=[B7], writes=[("qa_hi", s)])

        def cmp_run(u, ep, vfn, vkey):
            return dict(K=64, steps=stepcmp, ka_fn=kc_fn, va_fn=vfn, ep=ep,
                        kreads=(lambda s: ["kcmpT"]), vreads=(lambda s: [vkey]), qreads=(lambda s: [("qa", s)]),
                        strip=(t_gBc, u * GCLEN, 16, 4096), loads=(lambda s, u=u: ld_q(s, u)))

        for kh in range(4):
            compress(kh)
            runs = []
            for g in range(4):
                u = kh * 4 + g
                runs.append(cmp_run(u, epA, vcA_fn, "vcmpA"))
            runs[-1]["post"] = selection
            for g in range(4):
                u = kh * 4 + g
                runs.append(cmp_run(u, ep_plain(gate_col=u * 3 + 0), vc_fn, "vcmp"))
                r = base_run(128, std_steps(const_far=b31[:, u:u + 1]), ep_plain(gate_col=u * 3 + 1, accumulate=True), hi_k=True, hi_q=True)
                r["strip"] = (t_gB, u * GLEN, 1, 1664 + WOFF + 512 + 128)
                r["loads"] = (lambda s, u=u, kh=kh: (ld_k(s, 16 + kh, (t_E64, 0, 64)), ld_q(s, u), ld_v(s, kh)))
                runs.append(r)
                r = base_run(64, stepwin, ep_plain(gate_col=u * 3 + 2, accumulate=True))
                r["strip"] = (t_gBw, u * GLEN, 1, 512 + WOFF + 512 + 128)
                r["loads"] = (lambda s, u=u, kh=kh: (ld_k(s, 20 + kh), ld_q(s, u), ld_v(s, 4 + kh)))
                r["post"] = (lambda u=u: store_o(u))
                runs.append(r)
            execute(runs)
        flush_deferred()
        P.barrier()

    SBv = {}
    SBv["bfb"] = SB("bfb", [128, H], F32)
    SBv["caq"] = SB("caq", [128, 96], BF16)
    SBv["cak"] = SB("cak", [128, 96], BF16)
    SBv["carry"] = SB("carry", [128, H], F32)
    SBv["fz"] = SB("fz", [128, 48], F32)
    SBv["imp"] = AB[:, 4096:4096 + 2 * NT * 64].bitcast(F32).rearrange("p (t d) -> p t d", d=64)
    SBv["ST"] = SB("STt", [128, 128], F32)
    SBv["OV"] = SB("OVt", [128, 2, 64], BF16)

    cur = t_x
    for idx, L in enumerate(layers):
        kind = L % 4
        lastl = (idx == len(layers) - 1)
        dstt = t_y if lastl else t_X[idx % 2]
        phase1(L, kind, cur)
        if stop == "p1":
            break
        if kind == 0:
            phase2_dil()
        elif kind == 1:
            phase2_nsa()
        elif kind == 2:
            phase2_fox()
        else:
            phase2_moba()
        if stop == "p2":
            for t in range(NT):
                tmpb = b16v(GA, (t % 2) * 2048, 1024)
                P.dma("sp", tmpb, t_MIX.ap()[t * 128:(t + 1) * 128, :], writes=[("dbg", t % 2)])
                tmpf = f32v(GA, 8192 + (t % 2) * 4096, 1024)
                P.op("dve", lambda e, tmpb=tmpb, tmpf=tmpf: e.tensor_copy(out=tmpf, in_=tmpb), reads=[("dbg", t % 2)], writes=[("dbgf", t % 2)])
                P.dma("sp", t_y.ap()[t * 128:(t + 1) * 128, :], tmpf, reads=[("dbgf", t % 2)])
            break
        phase3(L, cur, dstt)
        cur = dstt
    P.barrier()
    P.emit()
    return nc


def _rel_bucket_np(d):
    d = np.maximum(d, 0)
    df = np.maximum(d.astype(np.float32), np.float32(1.0))
    large = 16 + (np.log(df / np.float32(16)) / np.float32(math.log(2048 / 16)) * np.float32(16)).astype(np.int32)
    large = np.minimum(large, 31)
    return np.where(d < 16, d, large)


def _host_tables(rel_table):
    rel_table = np.asarray(rel_table, dtype=np.float32)
    m = np.arange(GLEN)
    d = m - 511
    bk = _rel_bucket_np(d)
    gat = rel_table[bk, :].T.copy()
    valid = d >= 0
    gB = np.where(valid[None, :], gat, np.float32(NEG)).astype(np.float32)
    gBw = np.where((valid & (d <= 511))[None, :], gat, np.float32(NEG)).astype(np.float32)
    cnt = ((d <= 128) & valid).astype(np.int32) + ((d % 4 == 0) & (d <= 512) & valid) + ((d % 16 == 0) & (d <= 2048) & valid)
    gBd = np.where((cnt > 0)[None, :], gat, np.float32(NEG)).astype(np.float32)
    lnc = np.where(cnt > 0, np.log(np.maximum(cnt, 1)), 0.0).astype(np.float32)[None, :]
    g0 = np.where(valid, 0.0, NEG).astype(np.float32)[None, :]
    mc = np.arange(GCLEN)
    dc = mc - 2063
    gatc = rel_table[_rel_bucket_np(dc), :].T.copy()
    gBc = np.where((dc >= 0)[None, :], gatc, np.float32(NEG)).astype(np.float32)
    cf = np.zeros((128, 4, 128), np.float32)
    cf[:, 0, :] = np.eye(128)
    cf[:, 1, :] = np.eye(128)[::-1]
    cf[:, 2, :] = np.triu(np.ones((128, 128)))
    cf[:, 3, :] = 1.0
    identb = np.eye(128).astype(ml_dtypes.bfloat16)
    tok = np.arange(S)
    E16 = (tok[None, :] // 256 == np.arange(16)[:, None]).astype(ml_dtypes.bfloat16)
    E64 = (tok[None, :] // 64 == np.arange(64)[:, None]).astype(ml_dtypes.bfloat16)
    n = np.arange(256)
    j = np.arange(64)
    ov = ((16 * n[:, None] < 64 * j[None, :] + 64) & (16 * n[:, None] + 32 > 64 * j[None, :]) & (n[:, None] < 255))
    OV = ov.reshape(2, 128, 64).transpose(1, 0, 2).astype(ml_dtypes.bfloat16).copy()
    r = (np.arange(128) >= 64).astype(np.int32)
    c = np.arange(128)
    ST = np.where(c[None, :] < 64 + r[:, None], 0.0, NEG).astype(np.float32)
    assert (_rel_bucket_np(np.arange(1537, 8192)) == 31).all()
    return dict(rel31=np.ascontiguousarray(rel_table[31:32, :]), gB=gB, gBw=gBw, gBd=gBd, gBc=gBc, g0=g0, lnc=lnc, cf32=cf, identb=identb, E16=E16, E64=E64, OV=OV, ST=ST)


_PROG_CACHE = {}


def _get_prog(layers):
    key = tuple(layers)
    if key not in _PROG_CACHE:
        import os
        _PROG_CACHE[key] = build_program(list(layers), stop=os.environ.get("KSTOP"))
    return _PROG_CACHE[key]


def _common_inputs(rel_table, attn_norm, mlp_norm, q_gain, k_gain, w_out, mlp_w_up, mlp_w_down, dsa_w_in, nsa_w_in,
                   nsa_cmp_pos, nsa_cmp_w1, nsa_cmp_w2, fox_w_in, fox_b_f, moba_w_in):
    f = lambda a: np.ascontiguousarray(np.asarray(a, dtype=np.float32))
    m = dict(attn_norm=f(attn_norm), mlp_norm=f(mlp_norm), q_gain=f(q_gain), k_gain=f(k_gain), w_out=f(w_out),
             mlp_w_up=f(mlp_w_up), mlp_w_down=f(mlp_w_down), dsa_w_in=f(dsa_w_in)[0], nsa_w_in=f(nsa_w_in)[0],
             fox_w_in=f(fox_w_in)[0], moba_w_in=f(moba_w_in)[0],
             nsa_posT=np.ascontiguousarray(f(nsa_cmp_pos)[0].transpose(0, 2, 1)),
             nsa_cmp_w1=f(nsa_cmp_w1)[0], nsa_cmp_w2=f(nsa_cmp_w2)[0], fox_b_f=f(fox_b_f))
    m.update(_host_tables(rel_table))
    return m


def run_layers(layers, x, n_cores=8, **params):
    nc = _get_prog(layers)
    common = _common_inputs(**params)
    x = np.asarray(x, dtype=np.float32)
    in_maps = []
    for c in range(n_cores):
        mm = dict(common)
        mm["x"] = np.ascontiguousarray(x[c % x.shape[0]])
        in_maps.append(mm)
    res = run_bass_kernel_spmd(nc, in_maps, core_ids=list(range(n_cores)))
    return res


def kernel(x, rel_table, attn_norm, mlp_norm, q_gain, k_gain, w_out, mlp_w_up, mlp_w_down,
           dsa_w_in, nsa_w_in, nsa_cmp_pos, nsa_cmp_w1, nsa_cmp_w2, fox_w_in, fox_b_f, moba_w_in):
    params = dict(rel_table=rel_table, attn_norm=attn_norm, mlp_norm=mlp_norm, q_gain=q_gain, k_gain=k_gain,
                  w_out=w_out, mlp_w_up=mlp_w_up, mlp_w_down=mlp_w_down, dsa_w_in=dsa_w_in, nsa_w_in=nsa_w_in,
                  nsa_cmp_pos=nsa_cmp_pos, nsa_cmp_w1=nsa_cmp_w1, nsa_cmp_w2=nsa_cmp_w2, fox_w_in=fox_w_in,
                  fox_b_f=fox_b_f, moba_w_in=moba_w_in)
    res = run_layers([0, 1, 2, 3], x, n_cores=8, **params)
    out = np.stack([np.asarray(res.results[b]["y"], dtype=np.float32) for b in range(4)], axis=0)
    return out
```

```python
import math
import os
import contextlib
import numpy as np
import ml_dtypes
import concourse.bass as bass
import concourse.mybir as mybir
from concourse.bass_utils import run_bass_kernel_spmd

F32 = mybir.dt.float32
BF16 = mybir.dt.bfloat16
ALU = mybir.AluOpType
AF = mybir.ActivationFunctionType
AX = mybir.AxisListType

S = 4096
D = 1024
H = 16
DH = 64
NT = 32
DFF = 4096
NEG = -1.0e30
BIG = 30000.0
WU = 4480
WOFF = 384
GLEN = WU + 128
GCLEN = 4096 + 2032 + 16
NCOLS = {0: 3072, 1: 2608, 2: 3088, 3: 3072}

ENGS = ["pe", "act", "dve", "pool", "sp"]


class Prog:
    def __init__(self, nc, ndma=12):
        self.nc = nc
        self.streams = {e: [] for e in ENGS}
        self.cnt = {e: 0 for e in ENGS}
        self.sems = {}
        self.stack = contextlib.ExitStack()
        for e in ENGS:
            self.sems[e] = self.stack.enter_context(nc.semaphore("s_" + e))
        self.dsem = []
        for i in range(2 * ndma):
            self.dsem.append([self.stack.enter_context(nc.semaphore("d_%d" % i)), 0])
        self.ndma = ndma
        self.dnext = {"sp": 0, "pool": 0, "act": 0}
        self.waited = {e: {} for e in ENGS}
        self.lastw = {}
        self.readers = {}

    def _sem(self, key):
        return self.sems[key] if isinstance(key, str) else self.dsem[key[1]][0]

    def _wait(self, eng, tok):
        key, val, src = tok
        if src == "pe" and eng == "pe":
            return
        w = self.waited[eng]
        if w.get(key, 0) >= val:
            return
        w[key] = val
        self.streams[eng].append(("w", self._sem(key), val))

    def _deps(self, eng, reads, writes):
        for r in reads:
            t = self.lastw.get(r)
            if t is not None:
                self._wait(eng, t)
        for w in writes:
            t = self.lastw.get(w)
            if t is not None:
                self._wait(eng, t)
            for t in self.readers.get(w, ()):
                self._wait(eng, t)

    def _commit(self, tok, reads, writes):
        for w in writes:
            self.lastw[w] = tok
            self.readers[w] = []
        for r in reads:
            lst = self.readers.setdefault(r, [])
            lst.append(tok)
            if len(lst) > 24:
                best = {}
                for t in lst:
                    k = t[0]
                    if k not in best or best[k][1] < t[1]:
                        best[k] = t
                self.readers[r] = list(best.values())

    def op(self, eng, fn, reads=(), writes=(), crit=None):
        self._deps(eng, reads, writes)
        self.cnt[eng] += 1
        tok = (eng, self.cnt[eng], eng)
        cw = None
        if crit is not None:
            ct = self.lastw.get(crit)
            if ct is not None and ct[2] != eng:
                cw = (self._sem(ct[0]), ct[1])
        self.streams[eng].append(("o", fn, self.sems[eng], 1, cw))
        self._commit(tok, reads, writes)
        return tok

    def dma(self, q, out, in_, reads=(), writes=()):
        self._deps(q, reads, writes)
        base = self.ndma if q == "pool" else 0
        i = base + self.dnext[q]
        self.dnext[q] = (self.dnext[q] + 1) % self.ndma
        if self.dsem[i][1] > 0:
            self._wait(q, (("d", i), self.dsem[i][1], "dma"))
        self.dsem[i][1] += 16
        tok = (("d", i), self.dsem[i][1], "dma")
        self.streams[q].append(("o", lambda e, o=out, s=in_: e.dma_start(out=o, in_=s), self.dsem[i][0], 16))
        self._commit(tok, reads, writes)
        return tok

    def barrier(self):
        for e in ENGS:
            for e2 in ENGS:
                if e2 != e and self.cnt[e2] > 0:
                    self._wait(e, (e2, self.cnt[e2], e2))
            for i, (s, v) in enumerate(self.dsem):
                if v > 0:
                    self._wait(e, (("d", i), v, "dma"))
        self.lastw = {}
        self.readers = {}

    def emit(self):
        nc = self.nc
        streams = self.streams

        def replay(eng, items):
            for it in items:
                if it[0] == "w":
                    eng.wait_ge(it[1], it[2])
                else:
                    ins = it[1](eng)
                    if len(it) > 4 and it[4] is not None:
                        ins = ins._wait_ge(it[4][0], it[4][1])
                    ins.then_inc(it[2], it[3])

        with nc.Block() as block:
            @block.tensor
            def _(e):
                replay(e, streams["pe"])

            @block.scalar
            def _(e):
                replay(e, streams["act"])

            @block.vector
            def _(e):
                replay(e, streams["dve"])

            @block.gpsimd
            def _(e):
                replay(e, streams["pool"])

            @block.sync
            def _(e):
                replay(e, streams["sp"])


def bcast_rows(t, off, n):
    return bass.AP(t, off, [[0, 128], [1, n]])


def build_program(layers, stop=None):
    nc = bass.Bass("TRN2", target_bir_lowering=False)
    es = contextlib.ExitStack()

    def din(name, shape, dt=F32):
        return nc.dram_tensor(name, list(shape), dt, kind="ExternalInput")

    t_x = din("x", [S, D])
    t_an = din("attn_norm", [4, D]); t_mn = din("mlp_norm", [4, D])
    t_qg = din("q_gain", [4, DH]); t_kg = din("k_gain", [4, DH])
    t_wout = din("w_out", [4, D, D]); t_wup = din("mlp_w_up", [4, D, DFF]); t_wdn = din("mlp_w_down", [4, DFF, D])
    t_win = {0: din("dsa_w_in", [D, 3072]), 1: din("nsa_w_in", [D, 2608]),
             2: din("fox_w_in", [D, 3088]), 3: din("moba_w_in", [D, 3072])}
    t_posT = din("nsa_posT", [2, DH, 32])
    t_w1 = din("nsa_cmp_w1", [2, 2048, 256]); t_w2 = din("nsa_cmp_w2", [2, 256, DH])
    t_bf = din("fox_b_f", [1, H])
    t_r31 = din("rel31", [1, H])
    t_gB = din("gB", [H, GLEN]); t_gBw = din("gBw", [H, GLEN]); t_gBd = din("gBd", [H, GLEN])
    t_gBc = din("gBc", [H, GCLEN]); t_g0 = din("g0", [1, GLEN]); t_lnc = din("lnc", [1, GLEN])
    t_cf = din("cf32", [128, 4, 128]); t_idb = din("identb", [128, 128], BF16)
    t_E16 = din("E16", [16, S], BF16); t_E64 = din("E64", [64, S], BF16)
    t_OV = din("OV", [128, 2, 64], BF16); t_ST = din("ST", [128, 128])
    t_y = nc.dram_tensor("y", [S, D], F32, kind="ExternalOutput")
    t_X = [nc.dram_tensor("xs%d" % i, [S, D], F32, kind="Internal") for i in range(2)]
    t_QK = nc.dram_tensor("qk", [32, 128, S], BF16, kind="Internal")
    t_V = nc.dram_tensor("vv", [16, S, DH], BF16, kind="Internal")
    t_MIX = nc.dram_tensor("mix", [S, D], BF16, kind="Internal")
    t_AUG = nc.dram_tensor("aug", [192, S], BF16, kind="Internal")
    t_gD = nc.dram_tensor("gD", [H, GLEN], F32, kind="Internal")

    def SB(name, shape, dt):
        return es.enter_context(nc.sbuf_tensor("sb_" + name, list(shape), dt))

    banks = [es.enter_context(nc.psum_tensor("bank%d" % i, [128, 512], F32)) for i in range(8)]
    AA = SB("arenaA", [128, 32768], BF16)
    AB = SB("arenaB", [128, 32768], BF16)
    WO = SB("wout", [128, 8, 1024], BF16)
    identb = SB("identb", [128, 128], BF16)
    cf = SB("cf", [128, 4, 128], F32)
    ones_b = SB("ones_b", [128, 128], BF16)
    gbc_a = SB("gbc_a", [128, D], F32); gbc_m = SB("gbc_m", [128, D], F32)
    gq = SB("gq", [128, DH], F32); gk = SB("gk", [128, DH], F32); gqk = SB("gqk", [128, DH], F32)
    small = SB("small", [128, 64], F32)
    b31 = SB("b31", [128, H], F32)
    GA = SB("genA", [128, 21760], BF16)

    P = Prog(nc)
    gates = AB[:, 0:2 * NT * 48].bitcast(F32).rearrange("p (t c) -> p t c", c=48)

    def bf(ap):
        return ap

    def f32v(arena, off_b, n):
        return arena[:, off_b // 2: off_b // 2 + 2 * n].bitcast(F32)

    def b16v(arena, off_b, n):
        return arena[:, off_b // 2: off_b // 2 + n]

    P.dma("sp", identb[:], t_idb.ap()[:, :], writes=["identb"])
    P.dma("sp", cf[:], t_cf.ap()[:, :, :], writes=["cf"])
    P.op("dve", lambda e: e.memset(ones_b[:], 1.0), writes=["ones_b"])
    P.dma("sp", b31[:], bcast_rows(t_r31, 0, H), writes=["b31"])
    P.op("dve", lambda e: e.memset(small[:], 0.0), writes=["ssq", "rstd", "ssq8", "rs", "ssq4"])

    def phase1(L, kind, t_src):
        ncol = NCOLS[kind]
        wv = AA[:, 0:8 * ncol].rearrange("p (k n) -> p k n", n=ncol)
        win = t_win[kind].ap()
        for kc in range(8):
            P.dma("pool", wv[:, kc, :], win[kc * 128:(kc + 1) * 128, :], writes=[("win", kc)])
        P.dma("pool", WO[:, :, :], t_wout.ap()[L].rearrange("(k p) n -> p k n", p=128), writes=["wo"])
        P.dma("sp", gbc_a[:], bcast_rows(t_an, L * D, D), writes=["gbc_a"])
        P.dma("sp", gbc_m[:], bcast_rows(t_mn, L * D, D), writes=["gbc_m"])
        P.dma("sp", gq[:], bcast_rows(t_qg, L * DH, DH), writes=["gq"])
        P.dma("sp", gk[:], bcast_rows(t_kg, L * DH, DH), writes=["gk"])
        P.op("dve", lambda e: e.tensor_tensor(out=gqk[:], in0=gq[:], in1=gk[:], op=ALU.mult),
             reads=["gq", "gk"], writes=["gqk"])

        xt = [f32v(GA, 0, 1024), f32v(GA, 4096, 1024)]
        junk = b16v(GA, 8192, 1024)
        hb = b16v(GA, 10240, 1024)
        hT = [b16v(GA, 12288, 1024), b16v(GA, 14336, 1024)]
        sq = f32v(GA, 16384, 512)
        tm = b16v(GA, 18432, 18 * 128)
        vm = b16v(GA, 23040, 1024)
        tst = b16v(GA, 25088, 18 * 512).rearrange("p (b n) -> p b n", n=512)
        ssq = small[:, 0:1]; rstd = small[:, 1:2]; ssq8 = small[:, 8:16]
        if kind == 2:
            bfb = SBv["bfb"]
            caq = SBv["caq"]; cak = SBv["cak"]; carry = SBv["carry"]; fz = SBv["fz"]
            P.dma("sp", bfb[:], bcast_rows(t_bf, 0, H), writes=["bfb"])
            P.op("dve", lambda e: e.memset(carry[:], 0.0), writes=["carry"])
            P.op("dve", lambda e: e.memset(caq[:], 1.0), writes=["caq"])
            P.op("dve", lambda e: e.memset(cak[:], 1.0), writes=["cak"])
            P.op("dve", lambda e: e.memset(tm[:, 2048:2304], 0.0), writes=["tm"])

        if kind == 1:
            chunks = [(0, 512, [(0, 512, "q", 0)]), (512, 512, [(0, 512, "q", 512)]),
                      (1024, 512, [(0, 256, "c", 1536), (256, 256, "c", 1792)]),
                      (1536, 512, [(0, 256, "k", 1024), (256, 256, "v", 0)]),
                      (2048, 512, [(0, 256, "k", 1280), (256, 256, "v", 256)]),
                      (2560, 48, [(0, 48, "g", 0)])]
            nv = 8
        else:
            chunks = [(0, 512, [(0, 512, "q", 0)]), (512, 512, [(0, 512, "q", 512)]),
                      (1024, 512, [(0, 512, "k", 1024)]), (1536, 512, [(0, 512, "k", 1536)]),
                      (2048, 512, [(0, 512, "v", 0)]), (2560, 512, [(0, 512, "v", 512)])]
            if kind == 2:
                chunks.append((3072, 16, [(0, 16, "f", 0)]))
            nv = 16
        nblk = 18 if kind == 2 else 16
        xsrc = t_src.ap()
        cb = 0

        def front(t):
            s = t % 2
            P.dma("sp", xt[s], xsrc[t * 128:(t + 1) * 128, :], writes=[("xt", s)])
            P.op("act", lambda e, s=s: e.activation(out=junk, in_=xt[s], func=AF.Square, accum_out=ssq),
                 reads=[("xt", s)], writes=["junk", "ssq"])
            P.op("dve", lambda e: e.tensor_scalar(out=rstd, in0=ssq, scalar1=1.0 / D, scalar2=1e-6,
                                                   op0=ALU.mult, op1=ALU.add), reads=["ssq"], writes=["rstd"])
            P.op("act", lambda e: e.activation(out=rstd, in_=rstd, func=AF.Sqrt), reads=["rstd"], writes=["rstd"])
            P.op("dve", lambda e: e.reciprocal(out=rstd, in_=rstd), reads=["rstd"], writes=["rstd"])
            P.op("dve", lambda e, s=s: e.scalar_tensor_tensor(out=hb, in0=xt[s], scalar=rstd, in1=gbc_a[:],
                                                              op0=ALU.mult, op1=ALU.mult),
                 reads=[("xt", s), "rstd", "gbc_a"], writes=["hb"])

        def back(t):
            s = t % 2
            pT = banks[3][:].bitcast(BF16)
            for kc in range(8):
                P.op("pe", lambda e, kc=kc: e.transpose(out=pT[:, kc * 128:(kc + 1) * 128],
                                                         in_=hb[:, kc * 128:(kc + 1) * 128], identity=identb[:]),
                     reads=["hb", "identb"], writes=["b3"])
            P.op("act", lambda e, s=s: e.copy(out=hT[s], in_=pT), reads=["b3"], writes=[("hT", s)])

        front(0); back(0)
        for t in range(NT):
            tt = t % 4
            g = t // 4
            s = t % 2
            if t + 1 < NT:
                front(t + 1)
            hTs = hT[s].rearrange("p (k n) -> p k n", n=128)
            for (c0, n, segs) in chunks:
                bk = banks[cb % 3]; bkey = ("pb", cb % 3); cb += 1
                for kc in range(8):
                    P.op("pe", lambda e, kc=kc, bk=bk, c0=c0, n=n, hTs=hTs: e.matmul(
                        out=bk[:, 0:n], lhsT=hTs[:, kc, :], rhs=wv[:, kc, c0:c0 + n], start=(kc == 0), stop=(kc == 7)),
                        reads=[("hT", s), ("win", kc)], writes=[bkey], crit=("hT", s))
                need_norm = any(sg[2] in ("q", "k") for sg in segs)
                if need_norm:
                    nh = n // 64
                    P.op("act", lambda e, bk=bk, n=n: e.activation(out=sq[:, 0:n], in_=bk[:, 0:n], func=AF.Square),
                         reads=[bkey], writes=["sq"])
                    P.op("dve", lambda e, n=n, nh=nh: e.tensor_reduce(
                        out=ssq8[:, 0:nh], in_=sq[:, 0:n].rearrange("p (a b) -> p a b", b=64), axis=AX.X, op=ALU.add),
                        reads=["sq"], writes=["ssq8"])
                    P.op("dve", lambda e, nh=nh: e.tensor_scalar(out=ssq8[:, 0:nh], in0=ssq8[:, 0:nh], scalar1=1.0 / DH,
                                                                  scalar2=1e-6, op0=ALU.mult, op1=ALU.add),
                         reads=["ssq8"], writes=["ssq8"])
                    P.op("act", lambda e, nh=nh: e.activation(out=ssq8[:, 0:nh], in_=ssq8[:, 0:nh], func=AF.Sqrt),
                         reads=["ssq8"], writes=["ssq8"])
                    P.op("dve", lambda e, nh=nh: e.reciprocal(out=ssq8[:, 0:nh], in_=ssq8[:, 0:nh]),
                         reads=["ssq8"], writes=["ssq8"])
                for (so, sn, typ, dc) in segs:
                    if typ in ("q", "k"):
                        h0 = so // 64; nh = sn // 64
                        dst = tm[:, dc:dc + sn].rearrange("p (a b) -> p a b", b=64)
                        if typ == "k":
                            P.op("dve", lambda e, bk=bk, so=so, sn=sn, h0=h0, nh=nh, dst=dst: e.tensor_tensor(
                                out=dst, in0=bk[:, so:so + sn].rearrange("p (a b) -> p a b", b=64),
                                in1=ssq8[:, h0:h0 + nh].unsqueeze(2).to_broadcast([128, nh, 64]), op=ALU.mult),
                                reads=[bkey, "ssq8"], writes=["tm"])
                        else:
                            sqv = sq[:, so:so + sn].rearrange("p (a b) -> p a b", b=64)
                            P.op("dve", lambda e, bk=bk, so=so, sn=sn, h0=h0, nh=nh, sqv=sqv: e.tensor_tensor(
                                out=sqv, in0=bk[:, so:so + sn].rearrange("p (a b) -> p a b", b=64),
                                in1=ssq8[:, h0:h0 + nh].unsqueeze(2).to_broadcast([128, nh, 64]), op=ALU.mult),
                                reads=[bkey, "ssq8", "sq"], writes=["sq"])
                            P.op("dve", lambda e, nh=nh, sqv=sqv, dst=dst: e.tensor_tensor(
                                out=dst, in0=sqv, in1=gqk[:].unsqueeze(1).to_broadcast([128, nh, 64]), op=ALU.mult),
                                reads=["sq", "gqk"], writes=["tm"])
                    elif typ == "c":
                        P.op("act", lambda e, bk=bk, so=so, sn=sn, dc=dc: e.copy(out=tm[:, dc:dc + sn], in_=bk[:, so:so + sn]),
                             reads=[bkey], writes=["tm"])
                    elif typ == "v":
                        P.op("act", lambda e, bk=bk, so=so, sn=sn, dc=dc: e.copy(out=vm[:, dc:dc + sn], in_=bk[:, so:so + sn]),
                             reads=[bkey], writes=["vm"])
                    elif typ == "g":
                        P.op("act", lambda e, bk=bk, t=t: e.activation(out=gates[:, t, :], in_=bk[:, 0:48], func=AF.Sigmoid),
                             reads=[bkey], writes=["gates"])
                    elif typ == "f" and os.environ.get("KF_SKIP") == "1":
                        pass
                    elif typ == "f":
                        P.op("dve", lambda e, bk=bk: e.tensor_tensor(out=fz[:, 0:16], in0=bk[:, 0:16], in1=bfb[:], op=ALU.add),
                             reads=[bkey, "bfb"], writes=["fz"])
                        P.op("act", lambda e: e.activation(out=fz[:, 0:16], in_=fz[:, 0:16], func=AF.Exp, scale=-1.0),
                             reads=["fz"], writes=["fz"])
                        P.op("act", lambda e: e.activation(out=fz[:, 0:16], in_=fz[:, 0:16], func=AF.Ln, bias=1.0),
                             reads=["fz"], writes=["fz"])
                        b7 = banks[7]
                        P.op("pe", lambda e: e.matmul(out=b7[:, 0:16], lhsT=cf[:, 2, :], rhs=fz[:, 0:16], start=True, stop=True),
                             reads=["fz", "cf"], writes=["b7"])
                        P.op("pe", lambda e: e.matmul(out=b7[:, 16:32], lhsT=cf[:, 3, :], rhs=fz[:, 0:16], start=True, stop=True),
                             reads=["fz", "cf"], writes=["b7"])
                        P.op("dve", lambda e: e.tensor_tensor(out=fz[:, 16:32], in0=b7[:, 0:16], in1=carry[:], op=ALU.add),
                             reads=["b7", "carry", "fz"], writes=["fz2"])
                        P.op("dve", lambda e: e.tensor_tensor(out=carry[:], in0=b7[:, 16:32], in1=carry[:], op=ALU.add),
                             reads=["b7", "carry", "fz2"], writes=["carry"])
                        P.op("dve", lambda e: e.tensor_scalar(out=fz[:, 16:32], in0=fz[:, 16:32], scalar1=-8.0, scalar2=None,
                                                               op0=ALU.mult), reads=["fz2"], writes=["fz2"])
                        caqv = caq[:].rearrange("p (r h) -> p r h", h=16)
                        cakv = cak[:].rearrange("p (r h) -> p r h", h=16)
                        cur = fz[:, 16:32]; nxt = fz[:, 32:48]
                        for r in range(3):
                            P.op("dve", lambda e, r=r, cur=cur: e.tensor_copy(out=caqv[:, r, :], in_=cur),
                                 reads=["fz2"], writes=["caq"])
                            P.op("dve", lambda e, r=r: e.tensor_scalar(out=cakv[:, 3 + r, :], in0=caqv[:, r, :], scalar1=-1.0,
                                                                        scalar2=None, op0=ALU.mult),
                                 reads=["caq"], writes=["cak"])
                            if r < 2:
                                P.op("dve", lambda e, r=r, cur=cur, nxt=nxt: e.tensor_tensor(out=nxt, in0=cur, in1=caqv[:, r, :],
                                                                                             op=ALU.subtract),
                                     reads=["fz2", "caq"], writes=["fz2"])
                                cur, nxt = nxt, cur
                        P.op("dve", lambda e: e.tensor_copy(out=tm[:, 2048:2144], in_=caq[:]), reads=["caq"], writes=["tm"])
                        P.op("dve", lambda e: e.tensor_copy(out=tm[:, 2176:2272], in_=cak[:]), reads=["cak"], writes=["tm"])
            if t + 1 < NT:
                back(t + 1)
            for half in range((nblk + 7) // 8):
                pb = banks[4 + half][:].bitcast(BF16)
                nb = min(8, nblk - half * 8)
                for b in range(nb):
                    blk = half * 8 + b
                    P.op("pe", lambda e, pb=pb, b=b, blk=blk: e.transpose(out=pb[:, b * 128:(b + 1) * 128],
                                                                           in_=tm[:, blk * 128:(blk + 1) * 128], identity=identb[:]),
                         reads=["tm", "identb"], writes=[("b4", half)])
                dstv = tst[:, half * 8:half * 8 + nb, tt * 128:(tt + 1) * 128]
                srcv = pb[:, 0:nb * 128].rearrange("p (b n) -> p b n", n=128)
                eng = "act" if half == 0 else "dve"
                if eng == "act":
                    P.op("act", lambda e, dstv=dstv, srcv=srcv: e.copy(out=dstv, in_=srcv),
                         reads=[("b4", half)], writes=[("tst", half)])
                else:
                    P.op("dve", lambda e, dstv=dstv, srcv=srcv: e.tensor_copy(out=dstv, in_=srcv),
                         reads=[("b4", half)], writes=[("tst", half)])
            vdst = bass.AP(t_V, t * 128 * DH, [[DH, 128], [S * DH, nv], [1, DH]])
            P.dma("pool", vdst, vm[:, 0:nv * 64].rearrange("p (u d) -> p u d", d=64), reads=["vm"])
            if tt == 3:
                for half in range(2):
                    qdst = bass.AP(t_QK, half * 128 * S + g * 512, [[S, 64], [2 * 128 * S, 16], [1, 512]])
                    P.dma("pool", qdst, tst[half * 64:(half + 1) * 64, 0:16, :], reads=[("tst", 0), ("tst", 1)])
                if kind == 2:
                    for qk in range(2):
                        adst = bass.AP(t_AUG, qk * 96 * S + g * 512, [[S, 96], [1, 512]])
                        P.dma("pool", adst, tst[0:96, 16 + qk, :], reads=[("tst", 2)])
        P.barrier()

    def phase3(L, t_src, t_dst):
        wu = AA[:, :].rearrange("p (k n) -> p k n", n=DFF)
        wup = t_wup.ap()[L].rearrange("(k p) n -> p k n", p=128)
        for kc in range(8):
            P.dma("pool", wu[:, kc, :], wup[:, kc, :], writes=[("wup", kc)])
        wd = AB[:, :].rearrange("p (k n) -> p k n", n=1024)
        wdn = t_wdn.ap()[L].rearrange("(k p) n -> p k n", p=128)
        for q4 in range(4):
            P.dma("pool", wd[:, q4 * 8:(q4 + 1) * 8, :], wdn[:, q4 * 8:(q4 + 1) * 8, :], writes=[("wdn", q4)])
        ot = [b16v(GA, 0, 1024), b16v(GA, 2048, 1024)]
        xr = [f32v(GA, 4096, 1024), f32v(GA, 8192, 1024)]
        oT = b16v(GA, 12288, 1024)
        h2 = b16v(GA, 14336, 1024)
        h2T = b16v(GA, 16384, 1024)
        sq = f32v(GA, 18432, 512)
        uT = b16v(GA, 20480, 32 * 128).rearrange("p (f n) -> p f n", n=128)
        xo = [f32v(GA, 28672, 1024), f32v(GA, 32768, 1024)]
        junk = b16v(GA, 36864, 1024)
        ssq = small[:, 0:1]; rstd = small[:, 1:2]
        src = t_src.ap(); dst = t_dst.ap(); mix = t_MIX.ap()
        cbc = [0]

        def nextbank():
            bk = banks[cbc[0] % 3]; bkey = ("pb", cbc[0] % 3); cbc[0] += 1
            return bk, bkey

        pT = banks[3][:].bitcast(BF16)
        pT2 = banks[4][:].bitcast(BF16)
        oTv = oT.rearrange("p (k n) -> p k n", n=128)
        h2Tv = h2T.rearrange("p (k n) -> p k n", n=128)

        def A1(t):
            s = t % 2
            P.dma("sp", ot[s], mix[t * 128:(t + 1) * 128, :], writes=[("ot", s)])
            P.dma("sp", xr[s], src[t * 128:(t + 1) * 128, :], writes=[("xr", s)])
            for kc in range(8):
                P.op("pe", lambda e, kc=kc, s=s: e.transpose(out=pT[:, kc * 128:(kc + 1) * 128],
                                                              in_=ot[s][:, kc * 128:(kc + 1) * 128], identity=identb[:]),
                     reads=[("ot", s), "identb"], writes=["b3"])
            P.op("act", lambda e: e.copy(out=oT, in_=pT), reads=["b3"], writes=["oT"])
            for c in range(2):
                bk, bkey = nextbank()
                for kc in range(8):
                    P.op("pe", lambda e, kc=kc, bk=bk, c=c: e.matmul(out=bk[:, :], lhsT=oTv[:, kc, :], rhs=WO[:, kc, c * 512:(c + 1) * 512],
                                                                    start=(kc == 0), stop=(kc == 7)),
                         reads=["oT", "wo"], writes=[bkey], crit="oT")
                P.op("dve", lambda e, bk=bk, c=c, s=s: e.tensor_tensor(out=xr[s][:, c * 512:(c + 1) * 512], in0=bk[:, :],
                                                                      in1=xr[s][:, c * 512:(c + 1) * 512], op=ALU.add),
                     reads=[bkey, ("xr", s)], writes=[("xr", s)])
            P.op("act", lambda e, s=s: e.activation(out=junk, in_=xr[s], func=AF.Square, accum_out=ssq),
                 reads=[("xr", s)], writes=["junk", "ssq"])
            P.op("dve", lambda e: e.tensor_scalar(out=rstd, in0=ssq, scalar1=1.0 / D, scalar2=1e-6,
                                                   op0=ALU.mult, op1=ALU.add), reads=["ssq"], writes=["rstd"])
            P.op("act", lambda e: e.activation(out=rstd, in_=rstd, func=AF.Sqrt), reads=["rstd"], writes=["rstd"])
            P.op("dve", lambda e: e.reciprocal(out=rstd, in_=rstd), reads=["rstd"], writes=["rstd"])
            P.op("dve", lambda e, s=s: e.scalar_tensor_tensor(out=h2, in0=xr[s], scalar=rstd, in1=gbc_m[:],
                                                              op0=ALU.mult, op1=ALU.mult),
                 reads=[("xr", s), "rstd", "gbc_m"], writes=["h2"])

        def A2(t):
            for kc in range(8):
                P.op("pe", lambda e, kc=kc: e.transpose(out=pT2[:, kc * 128:(kc + 1) * 128],
                                                         in_=h2[:, kc * 128:(kc + 1) * 128], identity=identb[:]),
                     reads=["h2", "identb"], writes=["b4"])
            P.op("act", lambda e: e.copy(out=h2T, in_=pT2), reads=["b4"], writes=["h2T"])

        def Bup(t):
            for fq in range(8):
                bk, bkey = nextbank()
                for fi in range(4):
                    fc = fq * 4 + fi
                    for kc in range(8):
                        P.op("pe", lambda e, kc=kc, bk=bk, fc=fc, fi=fi: e.matmul(
                            out=bk[:, fi * 128:(fi + 1) * 128], lhsT=wu[:, kc, fc * 128:(fc + 1) * 128], rhs=h2Tv[:, kc, :],
                            start=(kc == 0), stop=(kc == 7)), reads=["h2T", ("wup", kc)], writes=[bkey])
                P.op("act", lambda e, bk=bk: e.activation(out=sq, in_=bk[:, :], func=AF.Square), reads=[bkey], writes=["sq"])
                P.op("dve", lambda e, bk=bk, fq=fq: e.scalar_tensor_tensor(
                    out=uT[:, fq * 4:(fq + 1) * 4, :], in0=bk[:, :].rearrange("p (f n) -> p f n", n=128), scalar=0.0,
                    in1=sq.rearrange("p (f n) -> p f n", n=128), op0=ALU.is_gt, op1=ALU.mult),
                    reads=[bkey, "sq"], writes=[("uT", fq)])

        def Cdown(t):
            s = t % 2
            for c in range(2):
                bk, bkey = nextbank()
                for fc in range(32):
                    P.op("pe", lambda e, fc=fc, bk=bk, c=c: e.matmul(out=bk[:, :], lhsT=uT[:, fc, :], rhs=wd[:, fc, c * 512:(c + 1) * 512],
                                                                    start=(fc == 0), stop=(fc == 31)),
                         reads=[("uT", fc // 4), ("wdn", fc // 8)], writes=[bkey], crit=("uT", fc // 4))
                P.op("dve", lambda e, bk=bk, c=c, s=s: e.tensor_tensor(out=xo[s][:, c * 512:(c + 1) * 512], in0=bk[:, :],
                                                                      in1=xr[s][:, c * 512:(c + 1) * 512], op=ALU.add),
                     reads=[bkey, ("xr", s)], writes=[("xo", s)])
            P.dma("sp", dst[t * 128:(t + 1) * 128, :], xo[s], reads=[("xo", s)])

        A1(0); A2(0); Bup(0)
        for t in range(NT):
            if t + 1 < NT:
                A1(t + 1)
            Cdown(t)
            if t + 1 < NT:
                A2(t + 1); Bup(t + 1)
        P.barrier()

    KA = [b16v(AA, 0, S), b16v(AB, 24576, S)]
    QA = [b16v(AA, 8192, S), b16v(AB, 32768, S)]
    VA = [b16v(GA, 0, NT * 66).rearrange("p (t c) -> p t c", c=66),
          b16v(AB, 40960, NT * 66).rearrange("p (t c) -> p t c", c=66)]
    WS = [f32v(GA, 4352, WU), f32v(AB, 45184, WU)]
    Hk = f32v(AA, 16384, WU)
    lg = [f32v(GA, 22528 + i * 2048, 512) for i in range(3)] + [f32v(AB, 20480, 512)]
    pTb = [b16v(GA, 28672 + i * 1024, 512) for i in range(3)] + [b16v(AB, 22528, 512)]
    NR = 4
    LAG = 3
    lbank_idx = [0, 1, 2, 6]
    lbanks = [banks[i] for i in lbank_idx]
    ost = f32v(GA, 31744, NT * 64).rearrange("p (t d) -> p t d", d=64)
    rs_t = small[:, 2:3]
    oTs = [f32v(AB, 16384, 512), f32v(AB, 18432, 512)]
    gctr = [0]
    rctr = [0]
    b7 = banks[7]
    B7 = ("bk", 7)

    deferred = []

    def tick():
        fire = []
        for it in deferred:
            it[0] -= 1
        while deferred and deferred[0][0] <= 0:
            fire.append(deferred.pop(0)[1])
        for fn in fire:
            fn()

    def force_until(tag):
        idx = -1
        for k, it in enumerate(deferred):
            if it[2] == tag:
                idx = k
        for _ in range(idx + 1):
            deferred.pop(0)[1]()

    def flush_deferred():
        while deferred:
            deferred.pop(0)[1]()

    rs4 = small[:, 16:20].unsqueeze(2)
    eptmp = f32v(AB, 63488, 256).rearrange("p (t d) -> p t d", d=64)

    def strip_dma(spec):
        tg, off, step, width = spec[0:4]
        rd = list(spec[4]) if len(spec) > 4 else []
        P.dma("sp", Hk[:, 0:width], bass.AP(tg, off, [[step, 128], [1, width]]), reads=rd, writes=["Hk"])

    def strip_compute(spec, s):
        width = spec[3]
        W = WS[s]
        for c in range((width + 511) // 512):
            n = min(512, width - c * 512)
            P.op("pe", lambda e, c=c, n=n: e.matmul(out=b7[:, 0:n], lhsT=cf[:, 1, :], rhs=Hk[:, c * 512:c * 512 + n],
                                                   start=True, stop=True), reads=["Hk", "cf"], writes=[B7])
            P.op("act", lambda e, c=c, n=n, W=W: e.copy(out=W[:, c * 512:c * 512 + n], in_=b7[:, 0:n]),
                 reads=[B7], writes=[("W", s)])

    def attn_core(run, s, hook=None):
        K = run["K"]; steps = run["steps"]; nvc = 66
        ka_fn = run["ka_fn"]; va_fn = run["va_fn"]; epilogue = run["ep"]
        kreads = run["kreads"](s); vreads = run["vreads"](s); qreads = run["qreads"](s)
        qa = QA[s]; W = WS[s]
        gfirst = {}; glast = {}; gpar = {}
        for n, st in enumerate(steps):
            G = st[0]
            if G not in gfirst:
                gfirst[G] = n
                gpar[G] = gctr[0] % 2
                gctr[0] += 1
            glast[G] = n

        def emit_pv(n):
            G, j = steps[n][0:2]
            r = n % NR
            par = gpar[G]
            otb = banks[3 + par]; okey = ("bk", 3 + par)
            P.op("pe", lambda e, otb=otb, r=r, j=j, n=n, G=G: e.matmul(
                out=otb[0:nvc, :], lhsT=va_fn(j, s), rhs=pTb[r][:, :], start=(gfirst[G] == n), stop=(glast[G] == n)),
                reads=[("pT", r)] + vreads, writes=[okey])
            if glast[G] == n:
                ots = oTs[par]
                force_until(("fin", par))
                P.op("act", lambda e, ots=ots, otb=otb: e.copy(out=ots[0:nvc, :], in_=otb[0:nvc, :]),
                     reads=[okey], writes=[("oTs", par)])

                def finish(G=G, ots=ots, par=par):
                    tb = banks[5]; tkey = ("bk", 5)
                    for t in range(4):
                        P.op("pe", lambda e, tb=tb, ots=ots, t=t: e.transpose(
                            out=tb[:, t * nvc:(t + 1) * nvc], in_=ots[0:nvc, t * 128:(t + 1) * 128], identity=cf[0:nvc, 0, 0:nvc]),
                            reads=[("oTs", par), "cf"], writes=[tkey], crit=("oTs", par))
                    epilogue(G, tb[:, 0:4 * nvc].rearrange("p (t c) -> p t c", c=nvc), tkey)
                deferred.append([4, finish, ("fin", par)])

        hook_at = min(5, len(steps) - 1)
        for n, st in enumerate(steps):
            G, j, woff = st[0:3]
            mode = st[4] if len(st) > 4 else "W"
            r = n % NR
            bk = lbanks[r]; bkey = ("bk", lbank_idx[r])
            P.op("pe", lambda e, bk=bk, j=j, G=G: e.matmul(out=bk[:, :], lhsT=ka_fn(j, K, s), rhs=qa[0:K, G * 512:(G + 1) * 512],
                                                          start=True, stop=True),
                 reads=kreads + qreads, writes=[bkey])
            if mode == "W":
                P.op("dve", lambda e, bk=bk, r=r, woff=woff: e.scalar_tensor_tensor(
                    out=lg[r], in0=bk[:, :], scalar=0.125, in1=W[:, woff:woff + 512], op0=ALU.mult, op1=ALU.add),
                    reads=[bkey, ("W", s)], writes=[("lg", r)])
                P.op("act", lambda e, r=r: e.activation(out=pTb[r], in_=lg[r], func=AF.Exp),
                     reads=[("lg", r)], writes=[("pT", r)])
            elif mode == "plain":
                P.op("act", lambda e, r=r, bk=bk: e.activation(out=pTb[r], in_=bk[:, :], func=AF.Exp, scale=0.125),
                     reads=[bkey], writes=[("pT", r)])
            else:
                bap = mode[1]
                P.op("act", lambda e, r=r, bk=bk, bap=bap: e.activation(out=pTb[r], in_=bk[:, :], func=AF.Exp, bias=bap, scale=0.125),
                     reads=[bkey, "b31"], writes=[("pT", r)])
            if n >= LAG:
                emit_pv(n - LAG)
            tick()
            if n == hook_at and hook is not None:
                hook()
        for n in range(max(0, len(steps) - LAG), len(steps)):
            emit_pv(n)
        if run.get("post") is not None:
            deferred.append([4, run["post"], None])

    def execute(runs):
        def sset(i):
            return (rctr[0] + i) % 2

        def pf_dma(i):
            s = sset(i)
            runs[i]["loads"](s)
            if runs[i].get("strip") is not None:
                strip_dma(runs[i]["strip"])

        def pf_compute(i):
            s = sset(i)
            if runs[i].get("strip") is not None:
                strip_compute(runs[i]["strip"], s)
            if runs[i].get("pre") is not None:
                runs[i]["pre"](s)

        pf_dma(0)
        pf_compute(0)
        for i in range(len(runs)):
            hook = None
            if i + 1 < len(runs):
                pf_dma(i + 1)
                hook = (lambda i=i: pf_compute(i + 1))
            attn_core(runs[i], sset(i), hook)
        rctr[0] += len(runs)

    def std_steps(maxspan=None, plain_past=False, const_far=None):
        steps = []
        for G in range(8):
            j0 = 0 if maxspan is None else max(0, 4 * G - maxspan)
            for j in range(j0, 4 * G + 4):
                act = [i for i in range(4 * G, 4 * G + 4) if i >= j and (maxspan is None or i - j <= maxspan)]
                if act:
                    Dd = 512 * G - 128 * j
                    mode = "W"
                    if plain_past and Dd >= 128:
                        mode = "plain"
                    elif const_far is not None and Dd >= 1664:
                        mode = ("const", const_far)
                    steps.append((G, j, Dd + WOFF, act, mode))
        return steps

    def ep_common(G, tbv, okey, gate_col):
        P.op("dve", lambda e: e.tensor_scalar(out=rs4, in0=tbv[:, :, 64:65], scalar1=1e-30, scalar2=None, op0=ALU.max),
             reads=[okey], writes=["rs"])
        P.op("dve", lambda e: e.reciprocal(out=rs4, in_=rs4), reads=["rs"], writes=["rs"])
        if gate_col is not None:
            P.op("dve", lambda e: e.tensor_tensor(out=rs4, in0=rs4, in1=gates[:, 4 * G:4 * G + 4, gate_col:gate_col + 1], op=ALU.mult),
                 reads=["rs", "gates"], writes=["rs"])

    def ep_plain(gate_col=None, accumulate=False):
        def ep(G, tbv, okey):
            ep_common(G, tbv, okey, gate_col)
            okeys = [("ost", i) for i in range(4 * G, 4 * G + 4)]
            if accumulate:
                P.op("dve", lambda e: e.tensor_tensor(out=eptmp, in0=tbv[:, :, 0:64], in1=rs4.to_broadcast([128, 4, 64]), op=ALU.mult),
                     reads=[okey, "rs"], writes=["eptmp"])
                P.op("dve", lambda e: e.tensor_tensor(out=ost[:, 4 * G:4 * G + 4, :], in0=eptmp, in1=ost[:, 4 * G:4 * G + 4, :], op=ALU.add),
                     reads=["eptmp"] + okeys, writes=okeys)
            else:
                P.op("dve", lambda e: e.tensor_tensor(out=ost[:, 4 * G:4 * G + 4, :], in0=tbv[:, :, 0:64], in1=rs4.to_broadcast([128, 4, 64]), op=ALU.mult),
                     reads=[okey, "rs"], writes=okeys)
        return ep

    def ld_k(s, unit, extra=None):
        P.dma("sp", KA[s][0:64, :], t_QK.ap()[unit, 0:64, :], writes=[("ka", s)])
        if extra is not None:
            tg, r0, nr = extra
            P.dma("sp", KA[s][64:64 + nr, :], tg.ap()[r0:r0 + nr, :], writes=[("ka_hi", s)])

    def ld_q(s, unit):
        P.dma("sp", QA[s][0:64, :], t_QK.ap()[unit, 0:64, :], writes=[("qa", s)])

    def ld_v(s, unit):
        P.dma("sp", VA[s][:, :, 0:64], t_V.ap()[unit].rearrange("(t p) d -> p t d", p=128), writes=[("va", s)])

    def store_o(h):
        dst = bass.AP(t_MIX, h * DH, [[D, 128], [128 * D, NT], [1, DH]])
        P.dma("pool", dst, ost[:, :, :], reads=[("ost", i) for i in range(NT)])

    ka_std = lambda j, K, s: KA[s][0:K, j * 128:(j + 1) * 128]
    va_std = lambda j, s: VA[s][:, j, :]

    def phase2_common_init():
        for s in range(2):
            P.op("dve", lambda e, s=s: e.memset(VA[s][:, :, 64:66], 1.0), writes=[("va_ones", s)])

    def base_run(K, steps, ep, hi_k=False, hi_q=False):
        return dict(K=K, steps=steps, ka_fn=ka_std, va_fn=va_std, ep=ep,
                    kreads=(lambda s: [("ka", s)] + ([("ka_hi", s)] if hi_k else [])),
                    vreads=(lambda s: [("va", s), ("va_ones", s)]),
                    qreads=(lambda s: [("qa", s)] + ([("qa_hi", s)] if hi_q else [])))

    def phase2_fox():
        phase2_common_init()
        spec = (t_g0, 0, 1, WU)
        for s in range(2):
            strip_dma(spec)
            strip_compute(spec, s)
        steps = std_steps(plain_past=True)
        runs = []
        for h in range(H):
            r = base_run(70, steps, ep_plain(), hi_k=True, hi_q=True)

            def loads(s, h=h):
                ld_k(s, 16 + h)
                P.dma("sp", KA[s][64:70, :], bass.AP(t_AUG, (96 + h) * S, [[16 * S, 6], [1, S]]), writes=[("ka_hi", s)])
                ld_q(s, h)
                P.dma("sp", QA[s][64:70, :], bass.AP(t_AUG, h * S, [[16 * S, 6], [1, S]]), writes=[("qa_hi", s)])
                ld_v(s, h)
            r["loads"] = loads
            r["post"] = (lambda h=h: store_o(h))
            runs.append(r)
        execute(runs)
        flush_deferred()
        P.barrier()

    def phase2_dil():
        phase2_common_init()
        gsb = f32v(AA, 16384, GLEN)[0:16, :]
        lsb = f32v(AB, 45184, GLEN)[0:16, :]
        P.dma("sp", gsb, t_gBd.ap()[:, :], writes=["Hk"])
        P.dma("sp", lsb, bass.AP(t_lnc, 0, [[0, 16], [1, GLEN]]), writes=[("W", 1)])
        P.op("dve", lambda e: e.tensor_tensor(out=gsb, in0=gsb, in1=lsb, op=ALU.add), reads=["Hk", ("W", 1)], writes=["Hk"])
        P.dma("sp", t_gD.ap()[:, :], gsb, reads=["Hk"], writes=["gD"])
        steps = std_steps(maxspan=16)
        runs = []
        for h in range(H):
            r = base_run(64, steps, ep_plain())
            r["strip"] = (t_gD, h * GLEN, 1, 2048 + WOFF + 512 + 128, ["gD"])

            def loads(s, h=h):
                ld_k(s, 16 + h); ld_q(s, h); ld_v(s, h)
            r["loads"] = loads
            r["post"] = (lambda h=h: store_o(h))
            runs.append(r)
        execute(runs)
        flush_deferred()
        P.barrier()

    def phase2_moba():
        phase2_common_init()
        km32 = f32v(GA, 39936, 16)
        kmb = b16v(GA, 40064, 16)
        gm = f32v(GA, 40128, 16)
        mx8 = f32v(GA, 40192, 8)
        mb80 = f32v(GA, 40256, 80)
        P.op("dve", lambda e: e.memset(mb80, 0.0), writes=["mb80"])

        def pre(s):
            ka = KA[s]; qa = QA[s]
            P.op("dve", lambda e: e.tensor_reduce(out=km32[0:64, :], in_=ka[0:64, :].rearrange("p (n b) -> p n b", b=256),
                                                  axis=AX.X, op=ALU.add), reads=[("ka", s)], writes=["km32"])
            P.op("dve", lambda e: e.tensor_scalar(out=kmb[0:64, :], in0=km32[0:64, :], scalar1=1.0 / 256, scalar2=None, op0=ALU.mult),
                 reads=["km32"], writes=["kmb"])
            for i in range(NT):
                own = i // 2
                P.op("dve", lambda e: e.memset(gm, NEG), writes=["gm"])
                if own > 0:
                    P.op("pe", lambda e, i=i: e.matmul(out=b7[:, 0:16], lhsT=qa[0:64, i * 128:(i + 1) * 128], rhs=kmb[0:64, :],
                                                       start=True, stop=True), reads=[("qa", s), "kmb"], writes=[B7])
                    P.op("dve", lambda e, own=own: e.tensor_copy(out=gm[:, 0:own], in_=b7[:, 0:own]), reads=[B7, "gm"], writes=["gm"])
                if own > 3:
                    P.op("dve", lambda e: e.max(out=mx8, in_=gm), reads=["gm"], writes=["mx8"])
                    P.op("dve", lambda e: e.tensor_scalar(out=mb80[:, 64:80], in0=gm, scalar1=mx8[:, 2:3], scalar2=None, op0=ALU.is_ge),
                         reads=["gm", "mx8"], writes=["mb80"])
                else:
                    P.op("dve", lambda e: e.tensor_scalar(out=mb80[:, 64:80], in0=gm, scalar1=-1.0e29, scalar2=None, op0=ALU.is_ge),
                         reads=["gm"], writes=["mb80"])
                P.op("dve", lambda e: e.tensor_scalar(out=mb80[:, 64:80], in0=mb80[:, 64:80], scalar1=-1.0, scalar2=BIG,
                                                       op0=ALU.add, op1=ALU.mult), reads=["mb80"], writes=["mb80"])
                P.op("dve", lambda e, own=own: e.memset(mb80[:, 64 + own:65 + own], 0.0), reads=["mb80"], writes=["mb80"])
                P.op("pe", lambda e: e.transpose(out=b7[0:80, 128:256], in_=mb80, identity=cf[:, 0, :]), reads=["mb80", "cf"], writes=[B7])
                P.op("act", lambda e, i=i: e.copy(out=qa[64:80, i * 128:(i + 1) * 128], in_=b7[64:80, 128:256]),
                     reads=[B7], writes=[("qa_hi", s)])

        runs = []
        for h in range(H):
            r = base_run(80, std_steps(const_far=b31[:, h:h + 1]), ep_plain(), hi_k=True, hi_q=True)
            r["strip"] = (t_gB, h * GLEN, 1, 1664 + WOFF + 512 + 128)

            def loads(s, h=h):
                ld_k(s, 16 + h, (t_E16, 0, 16)); ld_q(s, h); ld_v(s, h)
            r["loads"] = loads
            r["pre"] = pre
            r["post"] = (lambda h=h: store_o(h))
            runs.append(r)
        execute(runs)
        flush_deferred()
        P.barrier()

    def phase2_nsa():
        phase2_common_init()
        kcR = b16v(AA, 16384, S)
        kcA = b16v(AA, 24576, S)
        kcB = b16v(AA, 34304, S)
        w1 = b16v(AA, 42496, 32 * 256).rearrange("p (t c) -> p t c", c=256)
        w2 = b16v(AA, 58880, 128).rearrange("p (m d) -> p m d", d=64)
        hid = b16v(AA, 59392, 512).rearrange("p (m n) -> p m n", n=256)
        kcn = b16v(AA, 60416, 256).rearrange("p (t d) -> p t d", d=128)
        kcmpT = b16v(AA, 60928, 256)
        vcmp = b16v(AA, 61440, 2 * 66).rearrange("p (t c) -> p t c", c=66)
        vcmpA = b16v(AA, 61952, 2 * 66).rearrange("p (t c) -> p t c", c=66)
        posT = f32v(AA, 62464, 32)
        impb = SBv["imp"]
        STt = SBv["ST"]
        OVt = SBv["OV"]
        impm = f32v(GA, 39936, 64)
        mx16 = f32v(GA, 40192, 16)
        mb128 = f32v(GA, 40256, 128)
        x2 = f32v(GA, 40768, 256)
        tg = f32v(GA, 41792, 256)
        P.dma("sp", STt[:], t_ST.ap()[:, :], writes=["ST"])
        P.dma("sp", OVt[:], t_OV.ap()[:, :, :], writes=["OV"])
        P.op("dve", lambda e: e.memset(mb128, 0.0), writes=["mb128"])
        stepwin = std_steps(maxspan=4)
        stepcmp = []
        for G in range(8):
            for ntl in range(2):
                if G >= 4 * ntl:
                    stepcmp.append((G, ntl, 512 * G - 2048 * ntl, list(range(4 * G, 4 * G + 4))))
        kc_fn = lambda j, K, s: kcmpT[0:64, j * 128:(j + 1) * 128]
        vc_fn = lambda j, s: vcmp[:, j, :]
        vcA_fn = lambda j, s: vcmpA[:, j, :]
        b7b = b7[:].bitcast(BF16)

        def compress(kh):
            P.op("dve", lambda e: e.memset(kcn, 0.0), writes=["kcn"])
            P.op("dve", lambda e: e.memset(vcmp, 0.0), writes=["vcmp"])
            P.op("dve", lambda e: e.memset(vcmp[:, :, 64:66], 1.0), reads=["vcmp"], writes=["vcmp"])
            P.op("dve", lambda e: e.memset(vcmpA[:, :, 64:66], 1.0), writes=["vcmpA"])
            P.op("dve", lambda e: e.tensor_copy(out=vcmpA[:, :, 0:64], in_=OVt[:]), reads=["vcmpA", "OV"], writes=["vcmpA"])
            for kv in range(2):
                P.dma("sp", kcR[0:64, :], t_QK.ap()[24 + 4 * kv + kh, 0:64, :], writes=["Hk"])
                P.dma("sp", posT[0:64, :], t_posT.ap()[kv], writes=["posT"])
                P.dma("pool", w1[0:64, :, :], t_w1.ap()[kv].rearrange("(t d) c -> d t c", d=64), writes=["w1"])
                P.dma("pool", w2[:, :, :], t_w2.ap()[kv].rearrange("(m p) d -> p m d", p=128), writes=["w2"])
                for ab, dstb in ((0, kcA), (1, kcB)):
                    P.op("dve", lambda e, ab=ab, dstb=dstb: e.tensor_tensor(
                        out=dstb[0:64, :].rearrange("p (n s) -> p n s", s=16), in0=kcR[0:64, :].rearrange("p (n s) -> p n s", s=16),
                        in1=posT[0:64, ab * 16:(ab + 1) * 16].unsqueeze(1).to_broadcast([64, 256, 16]), op=ALU.add),
                        reads=["Hk", "posT"], writes=[("kcAB", ab)] if ab == 1 else ["Hk"])
                kcAv = kcA[0:64, :].rearrange("p (n s) -> p n s", s=16)
                kcBv = kcB[0:64, :].rearrange("p (n s) -> p n s", s=16)
                for mc in range(2):
                    for t in range(32):
                        rhs = kcAv[:, 0:255, t] if t < 16 else kcBv[:, 1:256, t - 16]
                        P.op("pe", lambda e, t=t, mc=mc, rhs=rhs: e.matmul(out=b7[:, 0:255], lhsT=w1[0:64, t, mc * 128:(mc + 1) * 128],
                                                                       rhs=rhs, start=(t == 0), stop=(t == 31)),
                             reads=["Hk", ("kcAB", 1), "w1"], writes=[B7])
                    P.op("act", lambda e: e.activation(out=x2[:, 0:255], in_=b7[:, 0:255], func=AF.Square), reads=[B7], writes=["x2"])
                    P.op("dve", lambda e: e.tensor_scalar(out=x2[:, 0:255], in0=x2[:, 0:255], scalar1=0.044715, scalar2=1.0,
                                                           op0=ALU.mult, op1=ALU.add), reads=["x2"], writes=["x2"])
                    P.op("dve", lambda e: e.tensor_tensor(out=tg[:, 0:255], in0=b7[:, 0:255], in1=x2[:, 0:255], op=ALU.mult),
                         reads=[B7, "x2"], writes=["tg"])
                    P.op("act", lambda e: e.activation(out=tg[:, 0:255], in_=tg[:, 0:255], func=AF.Sigmoid, scale=1.5957691216),
                         reads=["tg"], writes=["tg"])
                    P.op("dve", lambda e, mc=mc: e.tensor_tensor(out=hid[:, mc, 0:255], in0=b7[:, 0:255], in1=tg[:, 0:255], op=ALU.mult),
                         reads=[B7, "tg"], writes=["hid"])
                for ntl in range(2):
                    nn = 128 if ntl == 0 else 127
                    for mc in range(2):
                        P.op("pe", lambda e, ntl=ntl, nn=nn, mc=mc: e.matmul(out=b7[0:nn, 256:320], lhsT=hid[:, mc, ntl * 128:ntl * 128 + nn],
                                                                         rhs=w2[:, mc, :], start=(mc == 0), stop=(mc == 1)),
                             reads=["hid", "w2"], writes=[B7])
                    if kv == 0:
                        ssq = small[:, 4:5]
                        P.op("act", lambda e, nn=nn: e.activation(out=x2[0:nn, 0:64], in_=b7[0:nn, 256:320], func=AF.Square, accum_out=ssq[0:nn, :]),
                             reads=[B7], writes=["x2", "ssq4"])
                        P.op("dve", lambda e: e.tensor_scalar(out=ssq, in0=ssq, scalar1=1.0 / DH, scalar2=1e-6, op0=ALU.mult, op1=ALU.add),
                             reads=["ssq4"], writes=["ssq4"])
                        P.op("act", lambda e: e.activation(out=ssq, in_=ssq, func=AF.Sqrt), reads=["ssq4"], writes=["ssq4"])
                        P.op("dve", lambda e: e.reciprocal(out=ssq, in_=ssq), reads=["ssq4"], writes=["ssq4"])
                        P.op("dve", lambda e, nn=nn, ntl=ntl: e.tensor_scalar(out=kcn[0:nn, ntl, 0:64], in0=b7[0:nn, 256:320], scalar1=ssq[0:nn, :],
                                                                           scalar2=None, op0=ALU.mult), reads=[B7, "ssq4"], writes=["kcn"])
                        P.op("pe", lambda e, ntl=ntl: e.transpose(out=b7b[:, 768:896], in_=kcn[:, ntl, :], identity=identb[:]),
                             reads=["kcn", "identb"], writes=[B7])
                        P.op("act", lambda e, ntl=ntl: e.copy(out=kcmpT[0:64, ntl * 128:(ntl + 1) * 128], in_=b7b[0:64, 768:896]),
                             reads=[B7], writes=["kcmpT"])
                    else:
                        P.op("act", lambda e, nn=nn, ntl=ntl: e.copy(out=vcmp[0:nn, ntl, 0:64], in_=b7[0:nn, 256:320]),
                             reads=[B7], writes=["vcmp"])
            P.op("dve", lambda e: e.memset(impb, 0.0), writes=["imp"])

        def epA(G, tbv, okey):
            ep_common(G, tbv, okey, None)
            P.op("dve", lambda e: e.tensor_tensor(out=eptmp, in0=tbv[:, :, 0:64], in1=rs4.to_broadcast([128, 4, 64]), op=ALU.mult),
                 reads=[okey, "rs"], writes=["eptmp"])
            P.op("dve", lambda e: e.tensor_tensor(out=impb[:, 4 * G:4 * G + 4, :], in0=eptmp, in1=impb[:, 4 * G:4 * G + 4, :], op=ALU.add),
                 reads=["eptmp", "imp"], writes=["imp"])

        def selection():
            for i in range(NT):
                P.op("dve", lambda e, i=i: e.tensor_tensor(out=impm, in0=impb[:, i, :], in1=STt[:, 64 - 2 * i:128 - 2 * i], op=ALU.add),
                     reads=["imp", "ST"], writes=["impm"])
                P.op("dve", lambda e: e.max(out=mx16[:, 0:8], in_=impm), reads=["impm"], writes=["mx16"])
                P.op("dve", lambda e: e.match_replace(out=mb128[:, 0:64], in_to_replace=mx16[:, 0:8], in_values=impm, imm_value=NEG),
                     reads=["impm", "mx16"], writes=["mb128"])
                P.op("dve", lambda e: e.max(out=mx16[:, 8:16], in_=mb128[:, 0:64]), reads=["mb128"], writes=["mx16"])
                P.op("dve", lambda e: e.tensor_scalar(out=mx16[:, 14:15], in0=mx16[:, 14:15], scalar1=-1.0e29, scalar2=None, op0=ALU.max),
                     reads=["mx16"], writes=["mx16"])
                P.op("dve", lambda e: e.tensor_scalar(out=mb128[:, 64:128], in0=impm, scalar1=mx16[:, 14:15], scalar2=None, op0=ALU.is_ge),
                     reads=["impm", "mx16", "mb128"], writes=["mb128"])
                P.op("dve", lambda e: e.tensor_scalar(out=mb128[:, 64:128], in0=mb128[:, 64:128], scalar1=-1.0, scalar2=BIG,
                                                       op0=ALU.add, op1=ALU.mult), reads=["mb128"], writes=["mb128"])
                P.op("dve", lambda e, i=i: e.memset(mb128[0:64, 64 + 2 * i:65 + 2 * i], 0.0), reads=["mb128"], writes=["mb128"])
                P.op("dve", lambda e, i=i: e.memset(mb128[64:128, 65 + 2 * i:66 + 2 * i], 0.0), reads=["mb128"], writes=["mb128"])
                P.op("pe", lambda e: e.transpose(out=b7[:, 0:128], in_=mb128, identity=cf[:, 0, :]), reads=["mb128", "cf"], writes=[B7])
                for s in range(2):
                    P.op("act", lambda e, i=i, s=s: e.copy(out=QA[s][64:128, i * 128:(i + 1) * 128], in_=b7[64:128, 0:128]),
                         reads=[B7], writes=[("qa_hi", s)])

        def cmp_run(u, ep, vfn, vkey):
            return dict(K=64, steps=stepcmp, ka_fn=kc_fn, va_fn=vfn, ep=ep,
                        kreads=(lambda s: ["kcmpT"]), vreads=(lambda s: [vkey]), qreads=(lambda s: [("qa", s)]),
                        strip=(t_gBc, u * GCLEN, 16, 4096), loads=(lambda s, u=u: ld_q(s, u)))

        for kh in range(4):
            compress(kh)
            runs = []
            for g in range(4):
                u = kh * 4 + g
                runs.append(cmp_run(u, epA, vcA_fn, "vcmpA"))
            runs[-1]["post"] = selection
            for g in range(4):
                u = kh * 4 + g
                runs.append(cmp_run(u, ep_plain(gate_col=u * 3 + 0), vc_fn, "vcmp"))
                r = base_run(128, std_steps(const_far=b31[:, u:u + 1]), ep_plain(gate_col=u * 3 + 1, accumulate=True), hi_k=True, hi_q=True)
                r["strip"] = (t_gB, u * GLEN, 1, 1664 + WOFF + 512 + 128)
                r["loads"] = (lambda s, u=u, kh=kh: (ld_k(s, 16 + kh, (t_E64, 0, 64)), ld_q(s, u), ld_v(s, kh)))
                runs.append(r)
                r = base_run(64, stepwin, ep_plain(gate_col=u * 3 + 2, accumulate=True))
                r["strip"] = (t_gBw, u * GLEN, 1, 512 + WOFF + 512 + 128)
                r["loads"] = (lambda s, u=u, kh=kh: (ld_k(s, 20 + kh), ld_q(s, u), ld_v(s, 4 + kh)))
                r["post"] = (lambda u=u: store_o(u))
                runs.append(r)
            execute(runs)
        flush_deferred()
        P.barrier()

    SBv = {}
    SBv["bfb"] = SB("bfb", [128, H], F32)
    SBv["caq"] = SB("caq", [128, 96], BF16)
    SBv["cak"] = SB("cak", [128, 96], BF16)
    SBv["carry"] = SB("carry", [128, H], F32)
    SBv["fz"] = SB("fz", [128, 48], F32)
    SBv["imp"] = AB[:, 4096:4096 + 2 * NT * 64].bitcast(F32).rearrange("p (t d) -> p t d", d=64)
    SBv["ST"] = SB("STt", [128, 128], F32)
    SBv["OV"] = SB("OVt", [128, 2, 64], BF16)

    cur = t_x
    for idx, L in enumerate(layers):
        kind = L % 4
        lastl = (idx == len(layers) - 1)
        dstt = t_y if lastl else t_X[idx % 2]
        phase1(L, kind, cur)
        if stop == "p1":
            break
        if kind == 0:
            phase2_dil()
        elif kind == 1:
            phase2_nsa()
        elif kind == 2:
            phase2_fox()
        else:
            phase2_moba()
        if stop == "p2":
            for t in range(NT):
                tmpb = b16v(GA, (t % 2) * 2048, 1024)
                P.dma("sp", tmpb, t_MIX.ap()[t * 128:(t + 1) * 128, :], writes=[("dbg", t % 2)])
                tmpf = f32v(GA, 8192 + (t % 2) * 4096, 1024)
                P.op("dve", lambda e, tmpb=tmpb, tmpf=tmpf: e.tensor_copy(out=tmpf, in_=tmpb), reads=[("dbg", t % 2)], writes=[("dbgf", t % 2)])
                P.dma("sp", t_y.ap()[t * 128:(t + 1) * 128, :], tmpf, reads=[("dbgf", t % 2)])
            break
        phase3(L, cur, dstt)
        cur = dstt
    P.barrier()
    P.emit()
    return nc


def _rel_bucket_np(d):
    d = np.maximum(d, 0)
    df = np.maximum(d.astype(np.float32), np.float32(1.0))
    large = 16 + (np.log(df / np.float32(16)) / np.float32(math.log(2048 / 16)) * np.float32(16)).astype(np.int32)
    large = np.minimum(large, 31)
    return np.where(d < 16, d, large)


def _host_tables(rel_table):
    rel_table = np.asarray(rel_table, dtype=np.float32)
    m = np.arange(GLEN)
    d = m - 511
    bk = _rel_bucket_np(d)
    gat = rel_table[bk, :].T.copy()
    valid = d >= 0
    gB = np.where(valid[None, :], gat, np.float32(NEG)).astype(np.float32)
    gBw = np.where((valid & (d <= 511))[None, :], gat, np.float32(NEG)).astype(np.float32)
    cnt = ((d <= 128) & valid).astype(np.int32) + ((d % 4 == 0) & (d <= 512) & valid) + ((d % 16 == 0) & (d <= 2048) & valid)
    gBd = np.where((cnt > 0)[None, :], gat, np.float32(NEG)).astype(np.float32)
    lnc = np.where(cnt > 0, np.log(np.maximum(cnt, 1)), 0.0).astype(np.float32)[None, :]
    g0 = np.where(valid, 0.0, NEG).astype(np.float32)[None, :]
    mc = np.arange(GCLEN)
    dc = mc - 2063
    gatc = rel_table[_rel_bucket_np(dc), :].T.copy()
    gBc = np.where((dc >= 0)[None, :], gatc, np.float32(NEG)).astype(np.float32)
    cf = np.zeros((128, 4, 128), np.float32)
    cf[:, 0, :] = np.eye(128)
    cf[:, 1, :] = np.eye(128)[::-1]
    cf[:, 2, :] = np.triu(np.ones((128, 128)))
    cf[:, 3, :] = 1.0
    identb = np.eye(128).astype(ml_dtypes.bfloat16)
    tok = np.arange(S)
    E16 = (tok[None, :] // 256 == np.arange(16)[:, None]).astype(ml_dtypes.bfloat16)
    E64 = (tok[None, :] // 64 == np.arange(64)[:, None]).astype(ml_dtypes.bfloat16)
    n = np.arange(256)
    j = np.arange(64)
    ov = ((16 * n[:, None] < 64 * j[None, :] + 64) & (16 * n[:, None] + 32 > 64 * j[None, :]) & (n[:, None] < 255))
    OV = ov.reshape(2, 128, 64).transpose(1, 0, 2).astype(ml_dtypes.bfloat16).copy()
    r = (np.arange(128) >= 64).astype(np.int32)
    c = np.arange(128)
    ST = np.where(c[None, :] < 64 + r[:, None], 0.0, NEG).astype(np.float32)
    assert (_rel_bucket_np(np.arange(1537, 8192)) == 31).all()
    return dict(rel31=np.ascontiguousarray(rel_table[31:32, :]), gB=gB, gBw=gBw, gBd=gBd, gBc=gBc, g0=g0, lnc=lnc, cf32=cf, identb=identb, E16=E16, E64=E64, OV=OV, ST=ST)


_PROG_CACHE = {}


def _get_prog(layers):
    key = tuple(layers)
    if key not in _PROG_CACHE:
        import os
        _PROG_CACHE[key] = build_program(list(layers), stop=os.environ.get("KSTOP"))
    return _PROG_CACHE[key]


def _common_inputs(rel_table, attn_norm, mlp_norm, q_gain, k_gain, w_out, mlp_w_up, mlp_w_down, dsa_w_in, nsa_w_in,
                   nsa_cmp_pos, nsa_cmp_w1, nsa_cmp_w2, fox_w_in, fox_b_f, moba_w_in):
    f = lambda a: np.ascontiguousarray(np.asarray(a, dtype=np.float32))
    m = dict(attn_norm=f(attn_norm), mlp_norm=f(mlp_norm), q_gain=f(q_gain), k_gain=f(k_gain), w_out=f(w_out),
             mlp_w_up=f(mlp_w_up), mlp_w_down=f(mlp_w_down), dsa_w_in=f(dsa_w_in)[0], nsa_w_in=f(nsa_w_in)[0],
             fox_w_in=f(fox_w_in)[0], moba_w_in=f(moba_w_in)[0],
             nsa_posT=np.ascontiguousarray(f(nsa_cmp_pos)[0].transpose(0, 2, 1)),
             nsa_cmp_w1=f(nsa_cmp_w1)[0], nsa_cmp_w2=f(nsa_cmp_w2)[0], fox_b_f=f(fox_b_f))
    m.update(_host_tables(rel_table))
    return m


def run_layers(layers, x, n_cores=8, **params):
    nc = _get_prog(layers)
    common = _common_inputs(**params)
    x = np.asarray(x, dtype=np.float32)
    in_maps = []
    for c in range(n_cores):
        mm = dict(common)
        mm["x"] = np.ascontiguousarray(x[c % x.shape[0]])
        in_maps.append(mm)
    res = run_bass_kernel_spmd(nc, in_maps, core_ids=list(range(n_cores)))
    return res


def kernel(x, rel_table, attn_norm, mlp_norm, q_gain, k_gain, w_out, mlp_w_up, mlp_w_down,
           dsa_w_in, nsa_w_in, nsa_cmp_pos, nsa_cmp_w1, nsa_cmp_w2, fox_w_in, fox_b_f, moba_w_in):
    params = dict(rel_table=rel_table, attn_norm=attn_norm, mlp_norm=mlp_norm, q_gain=q_gain, k_gain=k_gain,
                  w_out=w_out, mlp_w_up=mlp_w_up, mlp_w_down=mlp_w_down, dsa_w_in=dsa_w_in, nsa_w_in=nsa_w_in,
                  nsa_cmp_pos=nsa_cmp_pos, nsa_cmp_w1=nsa_cmp_w1, nsa_cmp_w2=nsa_cmp_w2, fox_w_in=fox_w_in,
                  fox_b_f=fox_b_f, moba_w_in=moba_w_in)
    res = run_layers([0, 1, 2, 3], x, n_cores=8, **params)
    out = np.stack([np.asarray(res.results[b]["y"], dtype=np.float32) for b in range(4)], axis=0)
    return out
```

```python
import math
import os
import contextlib
import numpy as np
import ml_dtypes
import concourse.bass as bass
import concourse.mybir as mybir
from concourse.bass_utils import run_bass_kernel_spmd

F32 = mybir.dt.float32
BF16 = mybir.dt.bfloat16
ALU = mybir.AluOpType
AF = mybir.ActivationFunctionType
AX = mybir.AxisListType

S = 4096
D = 1024
H = 16
DH = 64
NT = 32
DFF = 4096
NEG = -1.0e30
BIG = 30000.0
WU = 4480
WOFF = 384
GLEN = WU + 128
GCLEN = 4096 + 2032 + 16
NCOLS = {0: 3072, 1: 2608, 2: 3088, 3: 3072}

ENGS = ["pe", "act", "dve", "pool", "sp"]


class Prog:
    def __init__(self, nc, ndma=12):
        self.nc = nc
        self.streams = {e: [] for e in ENGS}
        self.cnt = {e: 0 for e in ENGS}
        self.sems = {}
        self.stack = contextlib.ExitStack()
        for e in ENGS:
            self.sems[e] = self.stack.enter_context(nc.semaphore("s_" + e))
        self.dsem = []
        for i in range(2 * ndma):
            self.dsem.append([self.stack.enter_context(nc.semaphore("d_%d" % i)), 0])
        self.ndma = ndma
        self.dnext = {"sp": 0, "pool": 0, "act": 0}
        self.waited = {e: {} for e in ENGS}
        self.lastw = {}
        self.readers = {}

    def _sem(self, key):
        return self.sems[key] if isinstance(key, str) else self.dsem[key[1]][0]

    def _wait(self, eng, tok):
        key, val, src = tok
        if src == "pe" and eng == "pe":
            return
        w = self.waited[eng]
        if w.get(key, 0) >= val:
            return
        w[key] = val
        self.streams[eng].append(("w", self._sem(key), val))

    def _deps(self, eng, reads, writes):
        for r in reads:
            t = self.lastw.get(r)
            if t is not None:
                self._wait(eng, t)
        for w in writes:
            t = self.lastw.get(w)
            if t is not None:
                self._wait(eng, t)
            for t in self.readers.get(w, ()):
                self._wait(eng, t)

    def _commit(self, tok, reads, writes):
        for w in writes:
            self.lastw[w] = tok
            self.readers[w] = []
        for r in reads:
            lst = self.readers.setdefault(r, [])
            lst.append(tok)
            if len(lst) > 24:
                best = {}
                for t in lst:
                    k = t[0]
                    if k not in best or best[k][1] < t[1]:
                        best[k] = t
                self.readers[r] = list(best.values())

    def op(self, eng, fn, reads=(), writes=(), crit=None):
        self._deps(eng, reads, writes)
        self.cnt[eng] += 1
        tok = (eng, self.cnt[eng], eng)
        cw = None
        if crit is not None:
            ct = self.lastw.get(crit)
            if ct is not None and ct[2] != eng:
                cw = (self._sem(ct[0]), ct[1])
        self.streams[eng].append(("o", fn, self.sems[eng], 1, cw))
        self._commit(tok, reads, writes)
        return tok

    def dma(self, q, out, in_, reads=(), writes=()):
        self._deps(q, reads, writes)
        base = self.ndma if q == "pool" else 0
        i = base + self.dnext[q]
        self.dnext[q] = (self.dnext[q] + 1) % self.ndma
        if self.dsem[i][1] > 0:
            self._wait(q, (("d", i), self.dsem[i][1], "dma"))
        self.dsem[i][1] += 16
        tok = (("d", i), self.dsem[i][1], "dma")
        self.streams[q].append(("o", lambda e, o=out, s=in_: e.dma_start(out=o, in_=s), self.dsem[i][0], 16))
        self._commit(tok, reads, writes)
        return tok

    def barrier(self):
        for e in ENGS:
            for e2 in ENGS:
                if e2 != e and self.cnt[e2] > 0:
                    self._wait(e, (e2, self.cnt[e2], e2))
            for i, (s, v) in enumerate(self.dsem):
                if v > 0:
                    self._wait(e, (("d", i), v, "dma"))
        self.lastw = {}
        self.readers = {}

    def emit(self):
        nc = self.nc
        streams = self.streams

        def replay(eng, items):
            for it in items:
                if it[0] == "w":
                    eng.wait_ge(it[1], it[2])
                else:
                    ins = it[1](eng)
                    if len(it) > 4 and it[4] is not None:
                        ins = ins._wait_ge(it[4][0], it[4][1])
                    ins.then_inc(it[2], it[3])

        with nc.Block() as block:
            @block.tensor
            def _(e):
                replay(e, streams["pe"])

            @block.scalar
            def _(e):
                replay(e, streams["act"])

            @block.vector
            def _(e):
                replay(e, streams["dve"])

            @block.gpsimd
            def _(e):
                replay(e, streams["pool"])

            @block.sync
            def _(e):
                replay(e, streams["sp"])


def bcast_rows(t, off, n):
    return bass.AP(t, off, [[0, 128], [1, n]])


def build_program(layers, stop=None):
    nc = bass.Bass("TRN2", target_bir_lowering=False)
    es = contextlib.ExitStack()

    def din(name, shape, dt=F32):
        return nc.dram_tensor(name, list(shape), dt, kind="ExternalInput")

    t_x = din("x", [S, D])
    t_an = din("attn_norm", [4, D]); t_mn = din("mlp_norm", [4, D])
    t_qg = din("q_gain", [4, DH]); t_kg = din("k_gain", [4, DH])
    t_wout = din("w_out", [4, D, D]); t_wup = din("mlp_w_up", [4, D, DFF]); t_wdn = din("mlp_w_down", [4, DFF, D])
    t_win = {0: din("dsa_w_in", [D, 3072]), 1: din("nsa_w_in", [D, 2608]),
             2: din("fox_w_in", [D, 3088]), 3: din("moba_w_in", [D, 3072])}
    t_posT = din("nsa_posT", [2, DH, 32])
    t_w1 = din("nsa_cmp_w1", [2, 2048, 256]); t_w2 = din("nsa_cmp_w2", [2, 256, DH])
    t_bf = din("fox_b_f", [1, H])
    t_r31 = din("rel31", [1, H])
    t_gB = din("gB", [H, GLEN]); t_gBw = din("gBw", [H, GLEN]); t_gBd = din("gBd", [H, GLEN])
    t_gBc = din("gBc", [H, GCLEN]); t_g0 = din("g0", [1, GLEN]); t_lnc = din("lnc", [1, GLEN])
    t_cf = din("cf32", [128, 4, 128]); t_idb = din("identb", [128, 128], BF16)
    t_E16 = din("E16", [16, S], BF16); t_E64 = din("E64", [64, S], BF16)
    t_OV = din("OV", [128, 2, 64], BF16); t_ST = din("ST", [128, 128])
    t_y = nc.dram_tensor("y", [S, D], F32, kind="ExternalOutput")
    t_X = [nc.dram_tensor("xs%d" % i, [S, D], F32, kind="Internal") for i in range(2)]
    t_QK = nc.dram_tensor("qk", [32, 128, S], BF16, kind="Internal")
    t_V = nc.dram_tensor("vv", [16, S, DH], BF16, kind="Internal")
    t_MIX = nc.dram_tensor("mix", [S, D], BF16, kind="Internal")
    t_AUG = nc.dram_tensor("aug", [192, S], BF16, kind="Internal")
    t_gD = nc.dram_tensor("gD", [H, GLEN], F32, kind="Internal")

    def SB(name, shape, dt):
        return es.enter_context(nc.sbuf_tensor("sb_" + name, list(shape), dt))

    banks = [es.enter_context(nc.psum_tensor("bank%d" % i, [128, 512], F32)) for i in range(8)]
    AA = SB("arenaA", [128, 32768], BF16)
    AB = SB("arenaB", [128, 32768], BF16)
    WO = SB("wout", [128, 8, 1024], BF16)
    identb = SB("identb", [128, 128], BF16)
    cf = SB("cf", [128, 4, 128], F32)
    ones_b = SB("ones_b", [128, 128], BF16)
    gbc_a = SB("gbc_a", [128, D], F32); gbc_m = SB("gbc_m", [128, D], F32)
    gq = SB("gq", [128, DH], F32); gk = SB("gk", [128, DH], F32); gqk = SB("gqk", [128, DH], F32)
    small = SB("small", [128, 64], F32)
    b31 = SB("b31", [128, H], F32)
    GA = SB("genA", [128, 21760], BF16)

    P = Prog(nc)
    gates = AB[:, 0:2 * NT * 48].bitcast(F32).rearrange("p (t c) -> p t c", c=48)

    def bf(ap):
        return ap

    def f32v(arena, off_b, n):
        return arena[:, off_b // 2: off_b // 2 + 2 * n].bitcast(F32)

    def b16v(arena, off_b, n):
        return arena[:, off_b // 2: off_b // 2 + n]

    P.dma("sp", identb[:], t_idb.ap()[:, :], writes=["identb"])
    P.dma("sp", cf[:], t_cf.ap()[:, :, :], writes=["cf"])
    P.op("dve", lambda e: e.memset(ones_b[:], 1.0), writes=["ones_b"])
    P.dma("sp", b31[:], bcast_rows(t_r31, 0, H), writes=["b31"])
    P.op("dve", lambda e: e.memset(small[:], 0.0), writes=["ssq", "rstd", "ssq8", "rs", "ssq4"])

    def phase1(L, kind, t_src):
        ncol = NCOLS[kind]
        wv = AA[:, 0:8 * ncol].rearrange("p (k n) -> p k n", n=ncol)
        win = t_win[kind].ap()
        for kc in range(8):
            P.dma("pool", wv[:, kc, :], win[kc * 128:(kc + 1) * 128, :], writes=[("win", kc)])
        P.dma("pool", WO[:, :, :], t_wout.ap()[L].rearrange("(k p) n -> p k n", p=128), writes=["wo"])
        P.dma("sp", gbc_a[:], bcast_rows(t_an, L * D, D), writes=["gbc_a"])
        P.dma("sp", gbc_m[:], bcast_rows(t_mn, L * D, D), writes=["gbc_m"])
        P.dma("sp", gq[:], bcast_rows(t_qg, L * DH, DH), writes=["gq"])
        P.dma("sp", gk[:], bcast_rows(t_kg, L * DH, DH), writes=["gk"])
        P.op("dve", lambda e: e.tensor_tensor(out=gqk[:], in0=gq[:], in1=gk[:], op=ALU.mult),
             reads=["gq", "gk"], writes=["gqk"])

        xt = [f32v(GA, 0, 1024), f32v(GA, 4096, 1024)]
        junk = b16v(GA, 8192, 1024)
        hb = b16v(GA, 10240, 1024)
        hT = [b16v(GA, 12288, 1024), b16v(GA, 14336, 1024)]
        sq = f32v(GA, 16384, 512)
        tm = b16v(GA, 18432, 18 * 128)
        vm = b16v(GA, 23040, 1024)
        tst = b16v(GA, 25088, 18 * 512).rearrange("p (b n) -> p b n", n=512)
        ssq = small[:, 0:1]; rstd = small[:, 1:2]; ssq8 = small[:, 8:16]
        if kind == 2:
            bfb = SBv["bfb"]
            caq = SBv["caq"]; cak = SBv["cak"]; carry = SBv["carry"]; fz = SBv["fz"]
            P.dma("sp", bfb[:], bcast_rows(t_bf, 0, H), writes=["bfb"])
            P.op("dve", lambda e: e.memset(carry[:], 0.0), writes=["carry"])
            P.op("dve", lambda e: e.memset(caq[:], 1.0), writes=["caq"])
            P.op("dve", lambda e: e.memset(cak[:], 1.0), writes=["cak"])
            P.op("dve", lambda e: e.memset(tm[:, 2048:2304], 0.0), writes=["tm"])

        if kind == 1:
            chunks = [(0, 512, [(0, 512, "q", 0)]), (512, 512, [(0, 512, "q", 512)]),
                      (1024, 512, [(0, 256, "c", 1536), (256, 256, "c", 1792)]),
                      (1536, 512, [(0, 256, "k", 1024), (256, 256, "v", 0)]),
                      (2048, 512, [(0, 256, "k", 1280), (256, 256, "v", 256)]),
                      (2560, 48, [(0, 48, "g", 0)])]
            nv = 8
        else:
            chunks = [(0, 512, [(0, 512, "q", 0)]), (512, 512, [(0, 512, "q", 512)]),
                      (1024, 512, [(0, 512, "k", 1024)]), (1536, 512, [(0, 512, "k", 1536)]),
                      (2048, 512, [(0, 512, "v", 0)]), (2560, 512, [(0, 512, "v", 512)])]
            if kind == 2:
                chunks.append((3072, 16, [(0, 16, "f", 0)]))
            nv = 16
        nblk = 18 if kind == 2 else 16
        xsrc = t_src.ap()
        cb = 0

        def front(t):
            s = t % 2
            P.dma("sp", xt[s], xsrc[t * 128:(t + 1) * 128, :], writes=[("xt", s)])
            P.op("act", lambda e, s=s: e.activation(out=junk, in_=xt[s], func=AF.Square, accum_out=ssq),
                 reads=[("xt", s)], writes=["junk", "ssq"])
            P.op("dve", lambda e: e.tensor_scalar(out=rstd, in0=ssq, scalar1=1.0 / D, scalar2=1e-6,
                                                   op0=ALU.mult, op1=ALU.add), reads=["ssq"], writes=["rstd"])
            P.op("act", lambda e: e.activation(out=rstd, in_=rstd, func=AF.Sqrt), reads=["rstd"], writes=["rstd"])
            P.op("dve", lambda e: e.reciprocal(out=rstd, in_=rstd), reads=["rstd"], writes=["rstd"])
            P.op("dve", lambda e, s=s: e.scalar_tensor_tensor(out=hb, in0=xt[s], scalar=rstd, in1=gbc_a[:],
                                                              op0=ALU.mult, op1=ALU.mult),
                 reads=[("xt", s), "rstd", "gbc_a"], writes=["hb"])

        def back(t):
            s = t % 2
            pT = banks[3][:].bitcast(BF16)
            for kc in range(8):
                P.op("pe", lambda e, kc=kc: e.transpose(out=pT[:, kc * 128:(kc + 1) * 128],
                                                         in_=hb[:, kc * 128:(kc + 1) * 128], identity=identb[:]),
                     reads=["hb", "identb"], writes=["b3"])
            P.op("act", lambda e, s=s: e.copy(out=hT[s], in_=pT), reads=["b3"], writes=[("hT", s)])

        front(0); back(0)
        for t in range(NT):
            tt = t % 4
            g = t // 4
            s = t % 2
            if t + 1 < NT:
                front(t + 1)
            hTs = hT[s].rearrange("p (k n) -> p k n", n=128)
            for (c0, n, segs) in chunks:
                bk = banks[cb % 3]; bkey = ("pb", cb % 3); cb += 1
                for kc in range(8):
                    P.op("pe", lambda e, kc=kc, bk=bk, c0=c0, n=n, hTs=hTs: e.matmul(
                        out=bk[:, 0:n], lhsT=hTs[:, kc, :], rhs=wv[:, kc, c0:c0 + n], start=(kc == 0), stop=(kc == 7)),
                        reads=[("hT", s), ("win", kc)], writes=[bkey], crit=("hT", s))
                need_norm = any(sg[2] in ("q", "k") for sg in segs)
                if need_norm:
                    nh = n // 64
                    P.op("act", lambda e, bk=bk, n=n: e.activation(out=sq[:, 0:n], in_=bk[:, 0:n], func=AF.Square),
                         reads=[bkey], writes=["sq"])
                    P.op("dve", lambda e, n=n, nh=nh: e.tensor_reduce(
                        out=ssq8[:, 0:nh], in_=sq[:, 0:n].rearrange("p (a b) -> p a b", b=64), axis=AX.X, op=ALU.add),
                        reads=["sq"], writes=["ssq8"])
                    P.op("dve", lambda e, nh=nh: e.tensor_scalar(out=ssq8[:, 0:nh], in0=ssq8[:, 0:nh], scalar1=1.0 / DH,
                                                                  scalar2=1e-6, op0=ALU.mult, op1=ALU.add),
                         reads=["ssq8"], writes=["ssq8"])
                    P.op("act", lambda e, nh=nh: e.activation(out=ssq8[:, 0:nh], in_=ssq8[:, 0:nh], func=AF.Sqrt),
                         reads=["ssq8"], writes=["ssq8"])
                    P.op("dve", lambda e, nh=nh: e.reciprocal(out=ssq8[:, 0:nh], in_=ssq8[:, 0:nh]),
                         reads=["ssq8"], writes=["ssq8"])
                for (so, sn, typ, dc) in segs:
                    if typ in ("q", "k"):
                        h0 = so // 64; nh = sn // 64
                        dst = tm[:, dc:dc + sn].rearrange("p (a b) -> p a b", b=64)
                        if typ == "k":
                            P.op("dve", lambda e, bk=bk, so=so, sn=sn, h0=h0, nh=nh, dst=dst: e.tensor_tensor(
                                out=dst, in0=bk[:, so:so + sn].rearrange("p (a b) -> p a b", b=64),
                                in1=ssq8[:, h0:h0 + nh].unsqueeze(2).to_broadcast([128, nh, 64]), op=ALU.mult),
                                reads=[bkey, "ssq8"], writes=["tm"])
                        else:
                            sqv = sq[:, so:so + sn].rearrange("p (a b) -> p a b", b=64)
                            P.op("dve", lambda e, bk=bk, so=so, sn=sn, h0=h0, nh=nh, sqv=sqv: e.tensor_tensor(
                                out=sqv, in0=bk[:, so:so + sn].rearrange("p (a b) -> p a b", b=64),
                                in1=ssq8[:, h0:h0 + nh].unsqueeze(2).to_broadcast([128, nh, 64]), op=ALU.mult),
                                reads=[bkey, "ssq8", "sq"], writes=["sq"])
                            P.op("dve", lambda e, nh=nh, sqv=sqv, dst=dst: e.tensor_tensor(
                                out=dst, in0=sqv, in1=gqk[:].unsqueeze(1).to_broadcast([128, nh, 64]), op=ALU.mult),
                                reads=["sq", "gqk"], writes=["tm"])
                    elif typ == "c":
                        P.op("act", lambda e, bk=bk, so=so, sn=sn, dc=dc: e.copy(out=tm[:, dc:dc + sn], in_=bk[:, so:so + sn]),
                             reads=[bkey], writes=["tm"])
                    elif typ == "v":
                        P.op("act", lambda e, bk=bk, so=so, sn=sn, dc=dc: e.copy(out=vm[:, dc:dc + sn], in_=bk[:, so:so + sn]),
                             reads=[bkey], writes=["vm"])
                    elif typ == "g":
                        P.op("act", lambda e, bk=bk, t=t: e.activation(out=gates[:, t, :], in_=bk[:, 0:48], func=AF.Sigmoid),
                             reads=[bkey], writes=["gates"])
                    elif typ == "f" and os.environ.get("KF_SKIP") == "1":
                        pass
                    elif typ == "f":
                        P.op("dve", lambda e, bk=bk: e.tensor_tensor(out=fz[:, 0:16], in0=bk[:, 0:16], in1=bfb[:], op=ALU.add),
                             reads=[bkey, "bfb"], writes=["fz"])
                        P.op("act", lambda e: e.activation(out=fz[:, 0:16], in_=fz[:, 0:16], func=AF.Exp, scale=-1.0),
                             reads=["fz"], writes=["fz"])
                        P.op("act", lambda e: e.activation(out=fz[:, 0:16], in_=fz[:, 0:16], func=AF.Ln, bias=1.0),
                             reads=["fz"], writes=["fz"])
                        b7 = banks[7]
                        P.op("pe", lambda e: e.matmul(out=b7[:, 0:16], lhsT=cf[:, 2, :], rhs=fz[:, 0:16], start=True, stop=True),
                             reads=["fz", "cf"], writes=["b7"])
                        P.op("pe", lambda e: e.matmul(out=b7[:, 16:32], lhsT=cf[:, 3, :], rhs=fz[:, 0:16], start=True, stop=True),
                             reads=["fz", "cf"], writes=["b7"])
                        P.op("dve", lambda e: e.tensor_tensor(out=fz[:, 16:32], in0=b7[:, 0:16], in1=carry[:], op=ALU.add),
                             reads=["b7", "carry", "fz"], writes=["fz2"])
                        P.op("dve", lambda e: e.tensor_tensor(out=carry[:], in0=b7[:, 16:32], in1=carry[:], op=ALU.add),
                             reads=["b7", "carry", "fz2"], writes=["carry"])
                        P.op("dve", lambda e: e.tensor_scalar(out=fz[:, 16:32], in0=fz[:, 16:32], scalar1=-8.0, scalar2=None,
                                                               op0=ALU.mult), reads=["fz2"], writes=["fz2"])
                        caqv = caq[:].rearrange("p (r h) -> p r h", h=16)
                        cakv = cak[:].rearrange("p (r h) -> p r h", h=16)
                        cur = fz[:, 16:32]; nxt = fz[:, 32:48]
                        for r in range(3):
                            P.op("dve", lambda e, r=r, cur=cur: e.tensor_copy(out=caqv[:, r, :], in_=cur),
                                 reads=["fz2"], writes=["caq"])
                            P.op("dve", lambda e, r=r: e.tensor_scalar(out=cakv[:, 3 + r, :], in0=caqv[:, r, :], scalar1=-1.0,
                                                                        scalar2=None, op0=ALU.mult),
                                 reads=["caq"], writes=["cak"])
                            if r < 2:
                                P.op("dve", lambda e, r=r, cur=cur, nxt=nxt: e.tensor_tensor(out=nxt, in0=cur, in1=caqv[:, r, :],
                                                                                             op=ALU.subtract),
                                     reads=["fz2", "caq"], writes=["fz2"])
                                cur, nxt = nxt, cur
                        P.op("dve", lambda e: e.tensor_copy(out=tm[:, 2048:2144], in_=caq[:]), reads=["caq"], writes=["tm"])
                        P.op("dve", lambda e: e.tensor_copy(out=tm[:, 2176:2272], in_=cak[:]), reads=["cak"], writes=["tm"])
            if t + 1 < NT:
                back(t + 1)
            for half in range((nblk + 7) // 8):
                pb = banks[4 + half][:].bitcast(BF16)
                nb = min(8, nblk - half * 8)
                for b in range(nb):
                    blk = half * 8 + b
                    P.op("pe", lambda e, pb=pb, b=b, blk=blk: e.transpose(out=pb[:, b * 128:(b + 1) * 128],
                                                                           in_=tm[:, blk * 128:(blk + 1) * 128], identity=identb[:]),
                         reads=["tm", "identb"], writes=[("b4", half)])
                dstv = tst[:, half * 8:half * 8 + nb, tt * 128:(tt + 1) * 128]
                srcv = pb[:, 0:nb * 128].rearrange("p (b n) -> p b n", n=128)
                eng = "act" if half == 0 else "dve"
                if eng == "act":
                    P.op("act", lambda e, dstv=dstv, srcv=srcv: e.copy(out=dstv, in_=srcv),
                         reads=[("b4", half)], writes=[("tst", half)])
                else:
                    P.op("dve", lambda e, dstv=dstv, srcv=srcv: e.tensor_copy(out=dstv, in_=srcv),
                         reads=[("b4", half)], writes=[("tst", half)])
            vdst = bass.AP(t_V, t * 128 * DH, [[DH, 128], [S * DH, nv], [1, DH]])
            P.dma("pool", vdst, vm[:, 0:nv * 64].rearrange("p (u d) -> p u d", d=64), reads=["vm"])
            if tt == 3:
                for half in range(2):
                    qdst = bass.AP(t_QK, half * 128 * S + g * 512, [[S, 64], [2 * 128 * S, 16], [1, 512]])
                    P.dma("pool", qdst, tst[half * 64:(half + 1) * 64, 0:16, :], reads=[("tst", 0), ("tst", 1)])
                if kind == 2:
                    for qk in range(2):
                        adst = bass.AP(t_AUG, qk * 96 * S + g * 512, [[S, 96], [1, 512]])
                        P.dma("pool", adst, tst[0:96, 16 + qk, :], reads=[("tst", 2)])
        P.barrier()

    def phase3(L, t_src, t_dst):
        wu = AA[:, :].rearrange("p (k n) -> p k n", n=DFF)
        wup = t_wup.ap()[L].rearrange("(k p) n -> p k n", p=128)
        for kc in range(8):
            P.dma("pool", wu[:, kc, :], wup[:, kc, :], writes=[("wup", kc)])
        wd = AB[:, :].rearrange("p (k n) -> p k n", n=1024)
        wdn = t_wdn.ap()[L].rearrange("(k p) n -> p k n", p=128)
        for q4 in range(4):
            P.dma("pool", wd[:, q4 * 8:(q4 + 1) * 8, :], wdn[:, q4 * 8:(q4 + 1) * 8, :], writes=[("wdn", q4)])
        ot = [b16v(GA, 0, 1024), b16v(GA, 2048, 1024)]
        xr = [f32v(GA, 4096, 1024), f32v(GA, 8192, 1024)]
        oT = b16v(GA, 12288, 1024)
        h2 = b16v(GA, 14336, 1024)
        h2T = b16v(GA, 16384, 1024)
        sq = f32v(GA, 18432, 512)
        uT = b16v(GA, 20480, 32 * 128).rearrange("p (f n) -> p f n", n=128)
        xo = [f32v(GA, 28672, 1024), f32v(GA, 32768, 1024)]
        junk = b16v(GA, 36864, 1024)
        ssq = small[:, 0:1]; rstd = small[:, 1:2]
        src = t_src.ap(); dst = t_dst.ap(); mix = t_MIX.ap()
        cbc = [0]

        def nextbank():
            bk = banks[cbc[0] % 3]; bkey = ("pb", cbc[0] % 3); cbc[0] += 1
            return bk, bkey

        pT = banks[3][:].bitcast(BF16)
        pT2 = banks[4][:].bitcast(BF16)
        oTv = oT.rearrange("p (k n) -> p k n", n=128)
        h2Tv = h2T.rearrange("p (k n) -> p k n", n=128)

        def A1(t):
            s = t % 2
            P.dma("sp", ot[s], mix[t * 128:(t + 1) * 128, :], writes=[("ot", s)])
            P.dma("sp", xr[s], src[t * 128:(t + 1) * 128, :], writes=[("xr", s)])
            for kc in range(8):
                P.op("pe", lambda e, kc=kc, s=s: e.transpose(out=pT[:, kc * 128:(kc + 1) * 128],
                                                              in_=ot[s][:, kc * 128:(kc + 1) * 128], identity=identb[:]),
                     reads=[("ot", s), "identb"], writes=["b3"])
            P.op("act", lambda e: e.copy(out=oT, in_=pT), reads=["b3"], writes=["oT"])
            for c in range(2):
                bk, bkey = nextbank()
                for kc in range(8):
                    P.op("pe", lambda e, kc=kc, bk=bk, c=c: e.matmul(out=bk[:, :], lhsT=oTv[:, kc, :], rhs=WO[:, kc, c * 512:(c + 1) * 512],
                                                                    start=(kc == 0), stop=(kc == 7)),
                         reads=["oT", "wo"], writes=[bkey], crit="oT")
                P.op("dve", lambda e, bk=bk, c=c, s=s: e.tensor_tensor(out=xr[s][:, c * 512:(c + 1) * 512], in0=bk[:, :],
                                                                      in1=xr[s][:, c * 512:(c + 1) * 512], op=ALU.add),
                     reads=[bkey, ("xr", s)], writes=[("xr", s)])
            P.op("act", lambda e, s=s: e.activation(out=junk, in_=xr[s], func=AF.Square, accum_out=ssq),
                 reads=[("xr", s)], writes=["junk", "ssq"])
            P.op("dve", lambda e: e.tensor_scalar(out=rstd, in0=ssq, scalar1=1.0 / D, scalar2=1e-6,
                                                   op0=ALU.mult, op1=ALU.add), reads=["ssq"], writes=["rstd"])
            P.op("act", lambda e: e.activation(out=rstd, in_=rstd, func=AF.Sqrt), reads=["rstd"], writes=["rstd"])
            P.op("dve", lambda e: e.reciprocal(out=rstd, in_=rstd), reads=["rstd"], writes=["rstd"])
            P.op("dve", lambda e, s=s: e.scalar_tensor_tensor(out=h2, in0=xr[s], scalar=rstd, in1=gbc_m[:],
                                                              op0=ALU.mult, op1=ALU.mult),
                 reads=[("xr", s), "rstd", "gbc_m"], writes=["h2"])

        def A2(t):
            for kc in range(8):
                P.op("pe", lambda e, kc=kc: e.transpose(out=pT2[:, kc * 128:(kc + 1) * 128],
                                                         in_=h2[:, kc * 128:(kc + 1) * 128], identity=identb[:]),
                     reads=["h2", "identb"], writes=["b4"])
            P.op("act", lambda e: e.copy(out=h2T, in_=pT2), reads=["b4"], writes=["h2T"])

        def Bup(t):
            for fq in range(8):
                bk, bkey = nextbank()
                for fi in range(4):
                    fc = fq * 4 + fi
                    for kc in range(8):
                        P.op("pe", lambda e, kc=kc, bk=bk, fc=fc, fi=fi: e.matmul(
                            out=bk[:, fi * 128:(fi + 1) * 128], lhsT=wu[:, kc, fc * 128:(fc + 1) * 128], rhs=h2Tv[:, kc, :],
                            start=(kc == 0), stop=(kc == 7)), reads=["h2T", ("wup", kc)], writes=[bkey])
                P.op("act", lambda e, bk=bk: e.activation(out=sq, in_=bk[:, :], func=AF.Square), reads=[bkey], writes=["sq"])
                P.op("dve", lambda e, bk=bk, fq=fq: e.scalar_tensor_tensor(
                    out=uT[:, fq * 4:(fq + 1) * 4, :], in0=bk[:, :].rearrange("p (f n) -> p f n", n=128), scalar=0.0,
                    in1=sq.rearrange("p (f n) -> p f n", n=128), op0=ALU.is_gt, op1=ALU.mult),
                    reads=[bkey, "sq"], writes=[("uT", fq)])

        def Cdown(t):
            s = t % 2
            for c in range(2):
                bk, bkey = nextbank()
                for fc in range(32):
                    P.op("pe", lambda e, fc=fc, bk=bk, c=c: e.matmul(out=bk[:, :], lhsT=uT[:, fc, :], rhs=wd[:, fc, c * 512:(c + 1) * 512],
                                                                    start=(fc == 0), stop=(fc == 31)),
                         reads=[("uT", fc // 4), ("wdn", fc // 8)], writes=[bkey], crit=("uT", fc // 4))
                P.op("dve", lambda e, bk=bk, c=c, s=s: e.tensor_tensor(out=xo[s][:, c * 512:(c + 1) * 512], in0=bk[:, :],
                                                                      in1=xr[s][:, c * 512:(c + 1) * 512], op=ALU.add),
                     reads=[bkey, ("xr", s)], writes=[("xo", s)])
            P.dma("sp", dst[t * 128:(t + 1) * 128, :], xo[s], reads=[("xo", s)])

        A1(0); A2(0); Bup(0)
        for t in range(NT):
            if t + 1 < NT:
                A1(t + 1)
            Cdown(t)
            if t + 1 < NT:
                A2(t + 1); Bup(t + 1)
        P.barrier()

    KA = [b16v(AA, 0, S), b16v(AB, 24576, S)]
    QA = [b16v(AA, 8192, S), b16v(AB, 32768, S)]
    VA = [b16v(GA, 0, NT * 66).rearrange("p (t c) -> p t c", c=66),
          b16v(AB, 40960, NT * 66).rearrange("p (t c) -> p t c", c=66)]
    WS = [f32v(GA, 4352, WU), f32v(AB, 45184, WU)]
    Hk = f32v(AA, 16384, WU)
    lg = [f32v(GA, 22528 + i * 2048, 512) for i in range(3)] + [f32v(AB, 20480, 512)]
    pTb = [b16v(GA, 28672 + i * 1024, 512) for i in range(3)] + [b16v(AB, 22528, 512)]
    NR = 4
    LAG = 3
    lbank_idx = [0, 1, 2, 6]
    lbanks = [banks[i] for i in lbank_idx]
    ost = f32v(GA, 31744, NT * 64).rearrange("p (t d) -> p t d", d=64)
    rs_t = small[:, 2:3]
    oTs = [f32v(AB, 16384, 512), f32v(AB, 18432, 512)]
    gctr = [0]
    rctr = [0]
    b7 = banks[7]
    B7 = ("bk", 7)

    deferred = []

    def tick():
        fire = []
        for it in deferred:
            it[0] -= 1
        while deferred and deferred[0][0] <= 0:
            fire.append(deferred.pop(0)[1])
        for fn in fire:
            fn()

    def force_until(tag):
        idx = -1
        for k, it in enumerate(deferred):
            if it[2] == tag:
                idx = k
        for _ in range(idx + 1):
            deferred.pop(0)[1]()

    def flush_deferred():
        while deferred:
            deferred.pop(0)[1]()

    rs4 = small[:, 16:20].unsqueeze(2)
    eptmp = f32v(AB, 63488, 256).rearrange("p (t d) -> p t d", d=64)

    def strip_dma(spec):
        tg, off, step, width = spec[0:4]
        rd = list(spec[4]) if len(spec) > 4 else []
        P.dma("sp", Hk[:, 0:width], bass.AP(tg, off, [[step, 128], [1, width]]), reads=rd, writes=["Hk"])

    def strip_compute(spec, s):
        width = spec[3]
        W = WS[s]
        for c in range((width + 511) // 512):
            n = min(512, width - c * 512)
            P.op("pe", lambda e, c=c, n=n: e.matmul(out=b7[:, 0:n], lhsT=cf[:, 1, :], rhs=Hk[:, c * 512:c * 512 + n],
                                                   start=True, stop=True), reads=["Hk", "cf"], writes=[B7])
            P.op("act", lambda e, c=c, n=n, W=W: e.copy(out=W[:, c * 512:c * 512 + n], in_=b7[:, 0:n]),
                 reads=[B7], writes=[("W", s)])

    def attn_core(run, s, hook=None):
        K = run["K"]; steps = run["steps"]; nvc = 66
        ka_fn = run["ka_fn"]; va_fn = run["va_fn"]; epilogue = run["ep"]
        kreads = run["kreads"](s); vreads = run["vreads"](s); qreads = run["qreads"](s)
        qa = QA[s]; W = WS[s]
        gfirst = {}; glast = {}; gpar = {}
        for n, st in enumerate(steps):
            G = st[0]
            if G not in gfirst:
                gfirst[G] = n
                gpar[G] = gctr[0] % 2
                gctr[0] += 1
            glast[G] = n

        firststep = set(gfirst.values())

        def crange(st):
            G, act = st[0], st[3]
            if steps.index(st) in firststep:
                return 0, 512
            return (min(act) - 4 * G) * 128, (max(act) + 1 - 4 * G) * 128

        def emit_pv(n):
            G, j = steps[n][0:2]
            c0, c1 = crange(steps[n])
            r = n % NR
            par = gpar[G]
            otb = banks[3 + par]; okey = ("bk", 3 + par)
            P.op("pe", lambda e, otb=otb, r=r, j=j, n=n, G=G, c0=c0, c1=c1: e.matmul(
                out=otb[0:nvc, c0:c1], lhsT=va_fn(j, s), rhs=pTb[r][:, c0:c1], start=(gfirst[G] == n), stop=(glast[G] == n)),
                reads=[("pT", r)] + vreads, writes=[okey])
            if glast[G] == n:
                ots = oTs[par]
                force_until(("fin", par))
                P.op("act", lambda e, ots=ots, otb=otb: e.copy(out=ots[0:nvc, :], in_=otb[0:nvc, :]),
                     reads=[okey], writes=[("oTs", par)])

                def finish(G=G, ots=ots, par=par):
                    tb = banks[5]; tkey = ("bk", 5)
                    for t in range(4):
                        P.op("pe", lambda e, tb=tb, ots=ots, t=t: e.transpose(
                            out=tb[:, t * nvc:(t + 1) * nvc], in_=ots[0:nvc, t * 128:(t + 1) * 128], identity=cf[0:nvc, 0, 0:nvc]),
                            reads=[("oTs", par), "cf"], writes=[tkey], crit=("oTs", par))
                    epilogue(G, tb[:, 0:4 * nvc].rearrange("p (t c) -> p t c", c=nvc), tkey)
                deferred.append([4, finish, ("fin", par)])

        hook_at = min(5, len(steps) - 1)
        for n, st in enumerate(steps):
            G, j, woff = st[0:3]
            mode = st[4] if len(st) > 4 else "W"
            r = n % NR
            c0, c1 = crange(st)
            bk = lbanks[r]; bkey = ("bk", lbank_idx[r])
            P.op("pe", lambda e, bk=bk, j=j, G=G, c0=c0, c1=c1: e.matmul(out=bk[:, c0:c1], lhsT=ka_fn(j, K, s),
                                                                      rhs=qa[0:K, G * 512 + c0:G * 512 + c1], start=True, stop=True),
                 reads=kreads + qreads, writes=[bkey])
            if mode == "W":
                P.op("dve", lambda e, bk=bk, r=r, woff=woff, c0=c0, c1=c1: e.scalar_tensor_tensor(
                    out=lg[r][:, c0:c1], in0=bk[:, c0:c1], scalar=0.125, in1=W[:, woff + c0:woff + c1], op0=ALU.mult, op1=ALU.add),
                    reads=[bkey, ("W", s)], writes=[("lg", r)])
                P.op("act", lambda e, r=r, c0=c0, c1=c1: e.activation(out=pTb[r][:, c0:c1], in_=lg[r][:, c0:c1], func=AF.Exp),
                     reads=[("lg", r)], writes=[("pT", r)])
            elif mode == "plain":
                P.op("act", lambda e, r=r, bk=bk, c0=c0, c1=c1: e.activation(out=pTb[r][:, c0:c1], in_=bk[:, c0:c1], func=AF.Exp, scale=0.125),
                     reads=[bkey], writes=[("pT", r)])
            else:
                bap = mode[1]
                P.op("act", lambda e, r=r, bk=bk, bap=bap, c0=c0, c1=c1: e.activation(out=pTb[r][:, c0:c1], in_=bk[:, c0:c1], func=AF.Exp,
                                                                                 bias=bap, scale=0.125),
                     reads=[bkey, "b31"], writes=[("pT", r)])
            if n >= LAG:
                emit_pv(n - LAG)
            tick()
            if n == hook_at and hook is not None:
                hook()
        for n in range(max(0, len(steps) - LAG), len(steps)):
            emit_pv(n)
        if run.get("post") is not None:
            deferred.append([4, run["post"], None])

    def execute(runs):
        def sset(i):
            return (rctr[0] + i) % 2

        def pf_dma(i):
            s = sset(i)
            runs[i]["loads"](s)
            if runs[i].get("strip") is not None:
                strip_dma(runs[i]["strip"])

        def pf_compute(i):
            s = sset(i)
            if runs[i].get("strip") is not None:
                strip_compute(runs[i]["strip"], s)
            if runs[i].get("pre") is not None:
                runs[i]["pre"](s)

        pf_dma(0)
        pf_compute(0)
        for i in range(len(runs)):
            hook = None
            if i + 1 < len(runs):
                pf_dma(i + 1)
                hook = (lambda i=i: pf_compute(i + 1))
            attn_core(runs[i], sset(i), hook)
        rctr[0] += len(runs)

    def std_steps(maxspan=None, plain_past=False, const_far=None):
        steps = []
        for G in range(8):
            j0 = 0 if maxspan is None else max(0, 4 * G - maxspan)
            for j in range(j0, 4 * G + 4):
                act = [i for i in range(4 * G, 4 * G + 4) if i >= j and (maxspan is None or i - j <= maxspan)]
                if act:
                    Dd = 512 * G - 128 * j
                    mode = "W"
                    if plain_past and Dd >= 128:
                        mode = "plain"
                    elif const_far is not None and Dd >= 1664:
                        mode = ("const", const_far)
                    steps.append((G, j, Dd + WOFF, act, mode))
        return steps

    def ep_common(G, tbv, okey, gate_col):
        P.op("dve", lambda e: e.tensor_scalar(out=rs4, in0=tbv[:, :, 64:65], scalar1=1e-30, scalar2=None, op0=ALU.max),
             reads=[okey], writes=["rs"])
        P.op("dve", lambda e: e.reciprocal(out=rs4, in_=rs4), reads=["rs"], writes=["rs"])
        if gate_col is not None:
            P.op("dve", lambda e: e.tensor_tensor(out=rs4, in0=rs4, in1=gates[:, 4 * G:4 * G + 4, gate_col:gate_col + 1], op=ALU.mult),
                 reads=["rs", "gates"], writes=["rs"])

    def ep_plain(gate_col=None, accumulate=False):
        def ep(G, tbv, okey):
            ep_common(G, tbv, okey, gate_col)
            okeys = [("ost", i) for i in range(4 * G, 4 * G + 4)]
            if accumulate:
                P.op("dve", lambda e: e.tensor_tensor(out=eptmp, in0=tbv[:, :, 0:64], in1=rs4.to_broadcast([128, 4, 64]), op=ALU.mult),
                     reads=[okey, "rs"], writes=["eptmp"])
                P.op("dve", lambda e: e.tensor_tensor(out=ost[:, 4 * G:4 * G + 4, :], in0=eptmp, in1=ost[:, 4 * G:4 * G + 4, :], op=ALU.add),
                     reads=["eptmp"] + okeys, writes=okeys)
            else:
                P.op("dve", lambda e: e.tensor_tensor(out=ost[:, 4 * G:4 * G + 4, :], in0=tbv[:, :, 0:64], in1=rs4.to_broadcast([128, 4, 64]), op=ALU.mult),
                     reads=[okey, "rs"], writes=okeys)
        return ep

    def ld_k(s, unit, extra=None):
        P.dma("sp", KA[s][0:64, :], t_QK.ap()[unit, 0:64, :], writes=[("ka", s)])
        if extra is not None:
            tg, r0, nr = extra
            P.dma("sp", KA[s][64:64 + nr, :], tg.ap()[r0:r0 + nr, :], writes=[("ka_hi", s)])

    def ld_q(s, unit):
        P.dma("sp", QA[s][0:64, :], t_QK.ap()[unit, 0:64, :], writes=[("qa", s)])

    def ld_v(s, unit):
        P.dma("sp", VA[s][:, :, 0:64], t_V.ap()[unit].rearrange("(t p) d -> p t d", p=128), writes=[("va", s)])

    def store_o(h):
        dst = bass.AP(t_MIX, h * DH, [[D, 128], [128 * D, NT], [1, DH]])
        P.dma("pool", dst, ost[:, :, :], reads=[("ost", i) for i in range(NT)])

    ka_std = lambda j, K, s: KA[s][0:K, j * 128:(j + 1) * 128]
    va_std = lambda j, s: VA[s][:, j, :]

    def phase2_common_init():
        for s in range(2):
            P.op("dve", lambda e, s=s: e.memset(VA[s][:, :, 64:66], 1.0), writes=[("va_ones", s)])

    def base_run(K, steps, ep, hi_k=False, hi_q=False):
        return dict(K=K, steps=steps, ka_fn=ka_std, va_fn=va_std, ep=ep,
                    kreads=(lambda s: [("ka", s)] + ([("ka_hi", s)] if hi_k else [])),
                    vreads=(lambda s: [("va", s), ("va_ones", s)]),
                    qreads=(lambda s: [("qa", s)] + ([("qa_hi", s)] if hi_q else [])))

    def phase2_fox():
        phase2_common_init()
        spec = (t_g0, 0, 1, WU)
        for s in range(2):
            strip_dma(spec)
            strip_compute(spec, s)
        steps = std_steps(plain_past=True)
        runs = []
        for h in range(H):
            r = base_run(70, steps, ep_plain(), hi_k=True, hi_q=True)

            def loads(s, h=h):
                ld_k(s, 16 + h)
                P.dma("sp", KA[s][64:70, :], bass.AP(t_AUG, (96 + h) * S, [[16 * S, 6], [1, S]]), writes=[("ka_hi", s)])
                ld_q(s, h)
                P.dma("sp", QA[s][64:70, :], bass.AP(t_AUG, h * S, [[16 * S, 6], [1, S]]), writes=[("qa_hi", s)])
                ld_v(s, h)
            r["loads"] = loads
            r["post"] = (lambda h=h: store_o(h))
            runs.append(r)
        execute(runs)
        flush_deferred()
        P.barrier()

    def phase2_dil():
        phase2_common_init()
        gsb = f32v(AA, 16384, GLEN)[0:16, :]
        lsb = f32v(AB, 45184, GLEN)[0:16, :]
        P.dma("sp", gsb, t_gBd.ap()[:, :], writes=["Hk"])
        P.dma("sp", lsb, bass.AP(t_lnc, 0, [[0, 16], [1, GLEN]]), writes=[("W", 1)])
        P.op("dve", lambda e: e.tensor_tensor(out=gsb, in0=gsb, in1=lsb, op=ALU.add), reads=["Hk", ("W", 1)], writes=["Hk"])
        P.dma("sp", t_gD.ap()[:, :], gsb, reads=["Hk"], writes=["gD"])
        steps = std_steps(maxspan=16)
        runs = []
        for h in range(H):
            r = base_run(64, steps, ep_plain())
            r["strip"] = (t_gD, h * GLEN, 1, 2048 + WOFF + 512 + 128, ["gD"])

            def loads(s, h=h):
                ld_k(s, 16 + h); ld_q(s, h); ld_v(s, h)
            r["loads"] = loads
            r["post"] = (lambda h=h: store_o(h))
            runs.append(r)
        execute(runs)
        flush_deferred()
        P.barrier()

    def phase2_moba():
        phase2_common_init()
        km32 = f32v(GA, 39936, 16)
        kmb = b16v(GA, 40064, 16)
        gm = f32v(GA, 40128, 16)
        mx8 = f32v(GA, 40192, 8)
        mb80 = f32v(GA, 40256, 80)
        P.op("dve", lambda e: e.memset(mb80, 0.0), writes=["mb80"])

        def pre(s):
            ka = KA[s]; qa = QA[s]
            P.op("dve", lambda e: e.tensor_reduce(out=km32[0:64, :], in_=ka[0:64, :].rearrange("p (n b) -> p n b", b=256),
                                                  axis=AX.X, op=ALU.add), reads=[("ka", s)], writes=["km32"])
            P.op("dve", lambda e: e.tensor_scalar(out=kmb[0:64, :], in0=km32[0:64, :], scalar1=1.0 / 256, scalar2=None, op0=ALU.mult),
                 reads=["km32"], writes=["kmb"])
            for i in range(NT):
                own = i // 2
                P.op("dve", lambda e: e.memset(gm, NEG), writes=["gm"])
                if own > 0:
                    P.op("pe", lambda e, i=i: e.matmul(out=b7[:, 0:16], lhsT=qa[0:64, i * 128:(i + 1) * 128], rhs=kmb[0:64, :],
                                                       start=True, stop=True), reads=[("qa", s), "kmb"], writes=[B7])
                    P.op("dve", lambda e, own=own: e.tensor_copy(out=gm[:, 0:own], in_=b7[:, 0:own]), reads=[B7, "gm"], writes=["gm"])
                if own > 3:
                    P.op("dve", lambda e: e.max(out=mx8, in_=gm), reads=["gm"], writes=["mx8"])
                    P.op("dve", lambda e: e.tensor_scalar(out=mb80[:, 64:80], in0=gm, scalar1=mx8[:, 2:3], scalar2=None, op0=ALU.is_ge),
                         reads=["gm", "mx8"], writes=["mb80"])
                else:
                    P.op("dve", lambda e: e.tensor_scalar(out=mb80[:, 64:80], in0=gm, scalar1=-1.0e29, scalar2=None, op0=ALU.is_ge),
                         reads=["gm"], writes=["mb80"])
                P.op("dve", lambda e: e.tensor_scalar(out=mb80[:, 64:80], in0=mb80[:, 64:80], scalar1=-1.0, scalar2=BIG,
                                                       op0=ALU.add, op1=ALU.mult), reads=["mb80"], writes=["mb80"])
                P.op("dve", lambda e, own=own: e.memset(mb80[:, 64 + own:65 + own], 0.0), reads=["mb80"], writes=["mb80"])
                P.op("pe", lambda e: e.transpose(out=b7[0:80, 128:256], in_=mb80, identity=cf[:, 0, :]), reads=["mb80", "cf"], writes=[B7])
                P.op("act", lambda e, i=i: e.copy(out=qa[64:80, i * 128:(i + 1) * 128], in_=b7[64:80, 128:256]),
                     reads=[B7], writes=[("qa_hi", s)])

        runs = []
        for h in range(H):
            r = base_run(80, std_steps(const_far=b31[:, h:h + 1]), ep_plain(), hi_k=True, hi_q=True)
            r["strip"] = (t_gB, h * GLEN, 1, 1664 + WOFF + 512 + 128)

            def loads(s, h=h):
                ld_k(s, 16 + h, (t_E16, 0, 16)); ld_q(s, h); ld_v(s, h)
            r["loads"] = loads
            r["pre"] = pre
            r["post"] = (lambda h=h: store_o(h))
            runs.append(r)
        execute(runs)
        flush_deferred()
        P.barrier()

    def phase2_nsa():
        phase2_common_init()
        kcR = b16v(AA, 16384, S)
        kcA = b16v(AA, 24576, S)
        kcB = b16v(AA, 34304, S)
        w1 = b16v(AA, 42496, 32 * 256).rearrange("p (t c) -> p t c", c=256)
        w2 = b16v(AA, 58880, 128).rearrange("p (m d) -> p m d", d=64)
        hid = b16v(AA, 59392, 512).rearrange("p (m n) -> p m n", n=256)
        kcn = b16v(AA, 60416, 256).rearrange("p (t d) -> p t d", d=128)
        kcmpT = b16v(AA, 60928, 256)
        vcmp = b16v(AA, 61440, 2 * 66).rearrange("p (t c) -> p t c", c=66)
        vcmpA = b16v(AA, 61952, 2 * 66).rearrange("p (t c) -> p t c", c=66)
        posT = f32v(AA, 62464, 32)
        impb = SBv["imp"]
        STt = SBv["ST"]
        OVt = SBv["OV"]
        impm = f32v(GA, 39936, 64)
        mx16 = f32v(GA, 40192, 16)
        mb128 = f32v(GA, 40256, 128)
        x2 = f32v(GA, 40768, 256)
        tg = f32v(GA, 41792, 256)
        P.dma("sp", STt[:], t_ST.ap()[:, :], writes=["ST"])
        P.dma("sp", OVt[:], t_OV.ap()[:, :, :], writes=["OV"])
        P.op("dve", lambda e: e.memset(mb128, 0.0), writes=["mb128"])
        stepwin = std_steps(maxspan=4)
        stepcmp = []
        for G in range(8):
            for ntl in range(2):
                if G >= 4 * ntl:
                    stepcmp.append((G, ntl, 512 * G - 2048 * ntl, list(range(4 * G, 4 * G + 4))))
        kc_fn = lambda j, K, s: kcmpT[0:64, j * 128:(j + 1) * 128]
        vc_fn = lambda j, s: vcmp[:, j, :]
        vcA_fn = lambda j, s: vcmpA[:, j, :]
        b7b = b7[:].bitcast(BF16)

        def compress(kh):
            P.op("dve", lambda e: e.memset(kcn, 0.0), writes=["kcn"])
            P.op("dve", lambda e: e.memset(vcmp, 0.0), writes=["vcmp"])
            P.op("dve", lambda e: e.memset(vcmp[:, :, 64:66], 1.0), reads=["vcmp"], writes=["vcmp"])
            P.op("dve", lambda e: e.memset(vcmpA[:, :, 64:66], 1.0), writes=["vcmpA"])
            P.op("dve", lambda e: e.tensor_copy(out=vcmpA[:, :, 0:64], in_=OVt[:]), reads=["vcmpA", "OV"], writes=["vcmpA"])
            for kv in range(2):
                P.dma("sp", kcR[0:64, :], t_QK.ap()[24 + 4 * kv + kh, 0:64, :], writes=["Hk"])
                P.dma("sp", posT[0:64, :], t_posT.ap()[kv], writes=["posT"])
                P.dma("pool", w1[0:64, :, :], t_w1.ap()[kv].rearrange("(t d) c -> d t c", d=64), writes=["w1"])
                P.dma("pool", w2[:, :, :], t_w2.ap()[kv].rearrange("(m p) d -> p m d", p=128), writes=["w2"])
                for ab, dstb in ((0, kcA), (1, kcB)):
                    P.op("dve", lambda e, ab=ab, dstb=dstb: e.tensor_tensor(
                        out=dstb[0:64, :].rearrange("p (n s) -> p n s", s=16), in0=kcR[0:64, :].rearrange("p (n s) -> p n s", s=16),
                        in1=posT[0:64, ab * 16:(ab + 1) * 16].unsqueeze(1).to_broadcast([64, 256, 16]), op=ALU.add),
                        reads=["Hk", "posT"], writes=[("kcAB", ab)] if ab == 1 else ["Hk"])
                kcAv = kcA[0:64, :].rearrange("p (n s) -> p n s", s=16)
                kcBv = kcB[0:64, :].rearrange("p (n s) -> p n s", s=16)
                for mc in range(2):
                    for t in range(32):
                        rhs = kcAv[:, 0:255, t] if t < 16 else kcBv[:, 1:256, t - 16]
                        P.op("pe", lambda e, t=t, mc=mc, rhs=rhs: e.matmul(out=b7[:, 0:255], lhsT=w1[0:64, t, mc * 128:(mc + 1) * 128],
                                                                       rhs=rhs, start=(t == 0), stop=(t == 31)),
                             reads=["Hk", ("kcAB", 1), "w1"], writes=[B7])
                    P.op("act", lambda e: e.activation(out=x2[:, 0:255], in_=b7[:, 0:255], func=AF.Square), reads=[B7], writes=["x2"])
                    P.op("dve", lambda e: e.tensor_scalar(out=x2[:, 0:255], in0=x2[:, 0:255], scalar1=0.044715, scalar2=1.0,
                                                           op0=ALU.mult, op1=ALU.add), reads=["x2"], writes=["x2"])
                    P.op("dve", lambda e: e.tensor_tensor(out=tg[:, 0:255], in0=b7[:, 0:255], in1=x2[:, 0:255], op=ALU.mult),
                         reads=[B7, "x2"], writes=["tg"])
                    P.op("act", lambda e: e.activation(out=tg[:, 0:255], in_=tg[:, 0:255], func=AF.Sigmoid, scale=1.5957691216),
                         reads=["tg"], writes=["tg"])
                    P.op("dve", lambda e, mc=mc: e.tensor_tensor(out=hid[:, mc, 0:255], in0=b7[:, 0:255], in1=tg[:, 0:255], op=ALU.mult),
                         reads=[B7, "tg"], writes=["hid"])
                for ntl in range(2):
                    nn = 128 if ntl == 0 else 127
                    for mc in range(2):
                        P.op("pe", lambda e, ntl=ntl, nn=nn, mc=mc: e.matmul(out=b7[0:nn, 256:320], lhsT=hid[:, mc, ntl * 128:ntl * 128 + nn],
                                                                         rhs=w2[:, mc, :], start=(mc == 0), stop=(mc == 1)),
                             reads=["hid", "w2"], writes=[B7])
                    if kv == 0:
                        ssq = small[:, 4:5]
                        P.op("act", lambda e, nn=nn: e.activation(out=x2[0:nn, 0:64], in_=b7[0:nn, 256:320], func=AF.Square, accum_out=ssq[0:nn, :]),
                             reads=[B7], writes=["x2", "ssq4"])
                        P.op("dve", lambda e: e.tensor_scalar(out=ssq, in0=ssq, scalar1=1.0 / DH, scalar2=1e-6, op0=ALU.mult, op1=ALU.add),
                             reads=["ssq4"], writes=["ssq4"])
                        P.op("act", lambda e: e.activation(out=ssq, in_=ssq, func=AF.Sqrt), reads=["ssq4"], writes=["ssq4"])
                        P.op("dve", lambda e: e.reciprocal(out=ssq, in_=ssq), reads=["ssq4"], writes=["ssq4"])
                        P.op("dve", lambda e, nn=nn, ntl=ntl: e.tensor_scalar(out=kcn[0:nn, ntl, 0:64], in0=b7[0:nn, 256:320], scalar1=ssq[0:nn, :],
                                                                           scalar2=None, op0=ALU.mult), reads=[B7, "ssq4"], writes=["kcn"])
                        P.op("pe", lambda e, ntl=ntl: e.transpose(out=b7b[:, 768:896], in_=kcn[:, ntl, :], identity=identb[:]),
                             reads=["kcn", "identb"], writes=[B7])
                        P.op("act", lambda e, ntl=ntl: e.copy(out=kcmpT[0:64, ntl * 128:(ntl + 1) * 128], in_=b7b[0:64, 768:896]),
                             reads=[B7], writes=["kcmpT"])
                    else:
                        P.op("act", lambda e, nn=nn, ntl=ntl: e.copy(out=vcmp[0:nn, ntl, 0:64], in_=b7[0:nn, 256:320]),
                             reads=[B7], writes=["vcmp"])
            P.op("dve", lambda e: e.memset(impb, 0.0), writes=["imp"])

        def epA(G, tbv, okey):
            ep_common(G, tbv, okey, None)
            P.op("dve", lambda e: e.tensor_tensor(out=eptmp, in0=tbv[:, :, 0:64], in1=rs4.to_broadcast([128, 4, 64]), op=ALU.mult),
                 reads=[okey, "rs"], writes=["eptmp"])
            P.op("dve", lambda e: e.tensor_tensor(out=impb[:, 4 * G:4 * G + 4, :], in0=eptmp, in1=impb[:, 4 * G:4 * G + 4, :], op=ALU.add),
                 reads=["eptmp", "imp"], writes=["imp"])

        def selection():
            for i in range(NT):
                P.op("dve", lambda e, i=i: e.tensor_tensor(out=impm, in0=impb[:, i, :], in1=STt[:, 64 - 2 * i:128 - 2 * i], op=ALU.add),
                     reads=["imp", "ST"], writes=["impm"])
                P.op("dve", lambda e: e.max(out=mx16[:, 0:8], in_=impm), reads=["impm"], writes=["mx16"])
                P.op("dve", lambda e: e.match_replace(out=mb128[:, 0:64], in_to_replace=mx16[:, 0:8], in_values=impm, imm_value=NEG),
                     reads=["impm", "mx16"], writes=["mb128"])
                P.op("dve", lambda e: e.max(out=mx16[:, 8:16], in_=mb128[:, 0:64]), reads=["mb128"], writes=["mx16"])
                P.op("dve", lambda e: e.tensor_scalar(out=mx16[:, 14:15], in0=mx16[:, 14:15], scalar1=-1.0e29, scalar2=None, op0=ALU.max),
                     reads=["mx16"], writes=["mx16"])
                P.op("dve", lambda e: e.tensor_scalar(out=mb128[:, 64:128], in0=impm, scalar1=mx16[:, 14:15], scalar2=None, op0=ALU.is_ge),
                     reads=["impm", "mx16", "mb128"], writes=["mb128"])
                P.op("dve", lambda e: e.tensor_scalar(out=mb128[:, 64:128], in0=mb128[:, 64:128], scalar1=-1.0, scalar2=BIG,
                                                       op0=ALU.add, op1=ALU.mult), reads=["mb128"], writes=["mb128"])
                P.op("dve", lambda e, i=i: e.memset(mb128[0:64, 64 + 2 * i:65 + 2 * i], 0.0), reads=["mb128"], writes=["mb128"])
                P.op("dve", lambda e, i=i: e.memset(mb128[64:128, 65 + 2 * i:66 + 2 * i], 0.0), reads=["mb128"], writes=["mb128"])
                P.op("pe", lambda e: e.transpose(out=b7[:, 0:128], in_=mb128, identity=cf[:, 0, :]), reads=["mb128", "cf"], writes=[B7])
                for s in range(2):
                    P.op("act", lambda e, i=i, s=s: e.copy(out=QA[s][64:128, i * 128:(i + 1) * 128], in_=b7[64:128, 0:128]),
                         reads=[B7], writes=[("qa_hi", s)])

        def cmp_run(u, ep, vfn, vkey):
            return dict(K=64, steps=stepcmp, ka_fn=kc_fn, va_fn=vfn, ep=ep,
                        kreads=(lambda s: ["kcmpT"]), vreads=(lambda s: [vkey]), qreads=(lambda s: [("qa", s)]),
                        strip=(t_gBc, u * GCLEN, 16, 4096), loads=(lambda s, u=u: ld_q(s, u)))

        for kh in range(4):
            compress(kh)
            runs = []
            for g in range(4):
                u = kh * 4 + g
                runs.append(cmp_run(u, epA, vcA_fn, "vcmpA"))
            runs[-1]["post"] = selection
            for g in range(4):
                u = kh * 4 + g
                runs.append(cmp_run(u, ep_plain(gate_col=u * 3 + 0), vc_fn, "vcmp"))
                r = base_run(128, std_steps(const_far=b31[:, u:u + 1]), ep_plain(gate_col=u * 3 + 1, accumulate=True), hi_k=True, hi_q=True)
                r["strip"] = (t_gB, u * GLEN, 1, 1664 + WOFF + 512 + 128)
                r["loads"] = (lambda s, u=u, kh=kh: (ld_k(s, 16 + kh, (t_E64, 0, 64)), ld_q(s, u), ld_v(s, kh)))
                runs.append(r)
                r = base_run(64, stepwin, ep_plain(gate_col=u * 3 + 2, accumulate=True))
                r["strip"] = (t_gBw, u * GLEN, 1, 512 + WOFF + 512 + 128)
                r["loads"] = (lambda s, u=u, kh=kh: (ld_k(s, 20 + kh), ld_q(s, u), ld_v(s, 4 + kh)))
                r["post"] = (lambda u=u: store_o(u))
                runs.append(r)
            execute(runs)
        flush_deferred()
        P.barrier()

    SBv = {}
    SBv["bfb"] = SB("bfb", [128, H], F32)
    SBv["caq"] = SB("caq", [128, 96], BF16)
    SBv["cak"] = SB("cak", [128, 96], BF16)
    SBv["carry"] = SB("carry", [128, H], F32)
    SBv["fz"] = SB("fz", [128, 48], F32)
    SBv["imp"] = AB[:, 4096:4096 + 2 * NT * 64].bitcast(F32).rearrange("p (t d) -> p t d", d=64)
    SBv["ST"] = SB("STt", [128, 128], F32)
    SBv["OV"] = SB("OVt", [128, 2, 64], BF16)

    cur = t_x
    for idx, L in enumerate(layers):
        kind = L % 4
        lastl = (idx == len(layers) - 1)
        dstt = t_y if lastl else t_X[idx % 2]
        phase1(L, kind, cur)
        if stop == "p1":
            break
        if kind == 0:
            phase2_dil()
        elif kind == 1:
            phase2_nsa()
        elif kind == 2:
            phase2_fox()
        else:
            phase2_moba()
        if stop == "p2":
            for t in range(NT):
                tmpb = b16v(GA, (t % 2) * 2048, 1024)
                P.dma("sp", tmpb, t_MIX.ap()[t * 128:(t + 1) * 128, :], writes=[("dbg", t % 2)])
                tmpf = f32v(GA, 8192 + (t % 2) * 4096, 1024)
                P.op("dve", lambda e, tmpb=tmpb, tmpf=tmpf: e.tensor_copy(out=tmpf, in_=tmpb), reads=[("dbg", t % 2)], writes=[("dbgf", t % 2)])
                P.dma("sp", t_y.ap()[t * 128:(t + 1) * 128, :], tmpf, reads=[("dbgf", t % 2)])
            break
        phase3(L, cur, dstt)
        cur = dstt
    P.barrier()
    P.emit()
    return nc


def _rel_bucket_np(d):
    d = np.maximum(d, 0)
    df = np.maximum(d.astype(np.float32), np.float32(1.0))
    large = 16 + (np.log(df / np.float32(16)) / np.float32(math.log(2048 / 16)) * np.float32(16)).astype(np.int32)
    large = np.minimum(large, 31)
    return np.where(d < 16, d, large)


def _host_tables(rel_table):
    rel_table = np.asarray(rel_table, dtype=np.float32)
    m = np.arange(GLEN)
    d = m - 511
    bk = _rel_bucket_np(d)
    gat = rel_table[bk, :].T.copy()
    valid = d >= 0
    gB = np.where(valid[None, :], gat, np.float32(NEG)).astype(np.float32)
    gBw = np.where((valid & (d <= 511))[None, :], gat, np.float32(NEG)).astype(np.float32)
    cnt = ((d <= 128) & valid).astype(np.int32) + ((d % 4 == 0) & (d <= 512) & valid) + ((d % 16 == 0) & (d <= 2048) & valid)
    gBd = np.where((cnt > 0)[None, :], gat, np.float32(NEG)).astype(np.float32)
    lnc = np.where(cnt > 0, np.log(np.maximum(cnt, 1)), 0.0).astype(np.float32)[None, :]
    g0 = np.where(valid, 0.0, NEG).astype(np.float32)[None, :]
    mc = np.arange(GCLEN)
    dc = mc - 2063
    gatc = rel_table[_rel_bucket_np(dc), :].T.copy()
    gBc = np.where((dc >= 0)[None, :], gatc, np.float32(NEG)).astype(np.float32)
    cf = np.zeros((128, 4, 128), np.float32)
    cf[:, 0, :] = np.eye(128)
    cf[:, 1, :] = np.eye(128)[::-1]
    cf[:, 2, :] = np.triu(np.ones((128, 128)))
    cf[:, 3, :] = 1.0
    identb = np.eye(128).astype(ml_dtypes.bfloat16)
    tok = np.arange(S)
    E16 = (tok[None, :] // 256 == np.arange(16)[:, None]).astype(ml_dtypes.bfloat16)
    E64 = (tok[None, :] // 64 == np.arange(64)[:, None]).astype(ml_dtypes.bfloat16)
    n = np.arange(256)
    j = np.arange(64)
    ov = ((16 * n[:, None] < 64 * j[None, :] + 64) & (16 * n[:, None] + 32 > 64 * j[None, :]) & (n[:, None] < 255))
    OV = ov.reshape(2, 128, 64).transpose(1, 0, 2).astype(ml_dtypes.bfloat16).copy()
    r = (np.arange(128) >= 64).astype(np.int32)
    c = np.arange(128)
    ST = np.where(c[None, :] < 64 + r[:, None], 0.0, NEG).astype(np.float32)
    assert (_rel_bucket_np(np.arange(1537, 8192)) == 31).all()
    return dict(rel31=np.ascontiguousarray(rel_table[31:32, :]), gB=gB, gBw=gBw, gBd=gBd, gBc=gBc, g0=g0, lnc=lnc, cf32=cf, identb=identb, E16=E16, E64=E64, OV=OV, ST=ST)


_PROG_CACHE = {}


def _get_prog(layers):
    key = tuple(layers)
    if key not in _PROG_CACHE:
        import os
        _PROG_CACHE[key] = build_program(list(layers), stop=os.environ.get("KSTOP"))
    return _PROG_CACHE[key]


def _common_inputs(rel_table, attn_norm, mlp_norm, q_gain, k_gain, w_out, mlp_w_up, mlp_w_down, dsa_w_in, nsa_w_in,
                   nsa_cmp_pos, nsa_cmp_w1, nsa_cmp_w2, fox_w_in, fox_b_f, moba_w_in):
    f = lambda a: np.ascontiguousarray(np.asarray(a, dtype=np.float32))
    m = dict(attn_norm=f(attn_norm), mlp_norm=f(mlp_norm), q_gain=f(q_gain), k_gain=f(k_gain), w_out=f(w_out),
             mlp_w_up=f(mlp_w_up), mlp_w_down=f(mlp_w_down), dsa_w_in=f(dsa_w_in)[0], nsa_w_in=f(nsa_w_in)[0],
             fox_w_in=f(fox_w_in)[0], moba_w_in=f(moba_w_in)[0],
             nsa_posT=np.ascontiguousarray(f(nsa_cmp_pos)[0].transpose(0, 2, 1)),
             nsa_cmp_w1=f(nsa_cmp_w1)[0], nsa_cmp_w2=f(nsa_cmp_w2)[0], fox_b_f=f(fox_b_f))
    m.update(_host_tables(rel_table))
    return m


def run_layers(layers, x, n_cores=8, **params):
    nc = _get_prog(layers)
    common = _common_inputs(**params)
    x = np.asarray(x, dtype=np.float32)
    in_maps = []
    for c in range(n_cores):
        mm = dict(common)
        mm["x"] = np.ascontiguousarray(x[c % x.shape[0]])
        in_maps.append(mm)
    res = run_bass_kernel_spmd(nc, in_maps, core_ids=list(range(n_cores)))
    return res


def kernel(x, rel_table, attn_norm, mlp_norm, q_gain, k_gain, w_out, mlp_w_up, mlp_w_down,
           dsa_w_in, nsa_w_in, nsa_cmp_pos, nsa_cmp_w1, nsa_cmp_w2, fox_w_in, fox_b_f, moba_w_in):
    params = dict(rel_table=rel_table, attn_norm=attn_norm, mlp_norm=mlp_norm, q_gain=q_gain, k_gain=k_gain,
                  w_out=w_out, mlp_w_up=mlp_w_up, mlp_w_down=mlp_w_down, dsa_w_in=dsa_w_in, nsa_w_in=nsa_w_in,
                  nsa_cmp_pos=nsa_cmp_pos, nsa_cmp_w1=nsa_cmp_w1, nsa_cmp_w2=nsa_cmp_w2, fox_w_in=fox_w_in,
                  fox_b_f=fox_b_f, moba_w_in=moba_w_in)
    res = run_layers([0, 1, 2, 3], x, n_cores=8, **params)
    out = np.stack([np.asarray(res.results[b]["y"], dtype=np.float32) for b in range(4)], axis=0)
    return out
```

```python
import math
import os
import contextlib
import numpy as np
import ml_dtypes
import concourse.bass as bass
import concourse.mybir as mybir
from concourse.bass_utils import run_bass_kernel_spmd

F32 = mybir.dt.float32
BF16 = mybir.dt.bfloat16
ALU = mybir.AluOpType
AF = mybir.ActivationFunctionType
AX = mybir.AxisListType

S = 4096
D = 1024
H = 16
DH = 64
NT = 32
DFF = 4096
NEG = -1.0e30
BIG = 30000.0
WU = 4480
WOFF = 384
GLEN = WU + 128
GCLEN = 4096 + 2032 + 16
NCOLS = {0: 3072, 1: 2608, 2: 3088, 3: 3072}

ENGS = ["pe", "act", "dve", "pool", "sp"]


class Prog:
    def __init__(self, nc, ndma=12):
        self.nc = nc
        self.streams = {e: [] for e in ENGS}
        self.cnt = {e: 0 for e in ENGS}
        self.sems = {}
        self.stack = contextlib.ExitStack()
        for e in ENGS:
            self.sems[e] = self.stack.enter_context(nc.semaphore("s_" + e))
        self.dsem = []
        for i in range(2 * ndma):
            self.dsem.append([self.stack.enter_context(nc.semaphore("d_%d" % i)), 0])
        self.ndma = ndma
        self.dnext = {"sp": 0, "pool": 0, "act": 0}
        self.waited = {e: {} for e in ENGS}
        self.lastw = {}
        self.readers = {}

    def _sem(self, key):
        return self.sems[key] if isinstance(key, str) else self.dsem[key[1]][0]

    def _wait(self, eng, tok):
        key, val, src = tok
        if src == "pe" and eng == "pe":
            return
        w = self.waited[eng]
        if w.get(key, 0) >= val:
            return
        w[key] = val
        self.streams[eng].append(("w", self._sem(key), val))

    def _deps(self, eng, reads, writes):
        for r in reads:
            t = self.lastw.get(r)
            if t is not None:
                self._wait(eng, t)
        for w in writes:
            t = self.lastw.get(w)
            if t is not None:
                self._wait(eng, t)
            for t in self.readers.get(w, ()):
                self._wait(eng, t)

    def _commit(self, tok, reads, writes):
        for w in writes:
            self.lastw[w] = tok
            self.readers[w] = []
        for r in reads:
            lst = self.readers.setdefault(r, [])
            lst.append(tok)
            if len(lst) > 24:
                best = {}
                for t in lst:
                    k = t[0]
                    if k not in best or best[k][1] < t[1]:
                        best[k] = t
                self.readers[r] = list(best.values())

    def op(self, eng, fn, reads=(), writes=(), crit=None):
        self._deps(eng, reads, writes)
        self.cnt[eng] += 1
        tok = (eng, self.cnt[eng], eng)
        cw = None
        if crit is not None:
            ct = self.lastw.get(crit)
            if ct is not None and ct[2] != eng:
                cw = (self._sem(ct[0]), ct[1])
        self.streams[eng].append(("o", fn, self.sems[eng], 1, cw))
        self._commit(tok, reads, writes)
        return tok

    def dma(self, q, out, in_, reads=(), writes=()):
        self._deps(q, reads, writes)
        base = self.ndma if q == "pool" else 0
        i = base + self.dnext[q]
        self.dnext[q] = (self.dnext[q] + 1) % self.ndma
        if self.dsem[i][1] > 0:
            self._wait(q, (("d", i), self.dsem[i][1], "dma"))
        self.dsem[i][1] += 16
        tok = (("d", i), self.dsem[i][1], "dma")
        self.streams[q].append(("o", lambda e, o=out, s=in_: e.dma_start(out=o, in_=s), self.dsem[i][0], 16))
        self._commit(tok, reads, writes)
        return tok

    def barrier(self):
        for e in ENGS:
            for e2 in ENGS:
                if e2 != e and self.cnt[e2] > 0:
                    self._wait(e, (e2, self.cnt[e2], e2))
            for i, (s, v) in enumerate(self.dsem):
                if v > 0:
                    self._wait(e, (("d", i), v, "dma"))
        self.lastw = {}
        self.readers = {}

    def emit(self):
        nc = self.nc
        streams = self.streams

        def replay(eng, items):
            for it in items:
                if it[0] == "w":
                    eng.wait_ge(it[1], it[2])
                else:
                    ins = it[1](eng)
                    if len(it) > 4 and it[4] is not None:
                        ins = ins._wait_ge(it[4][0], it[4][1])
                    ins.then_inc(it[2], it[3])

        with nc.Block() as block:
            @block.tensor
            def _(e):
                replay(e, streams["pe"])

            @block.scalar
            def _(e):
                replay(e, streams["act"])

            @block.vector
            def _(e):
                replay(e, streams["dve"])

            @block.gpsimd
            def _(e):
                replay(e, streams["pool"])

            @block.sync
            def _(e):
                replay(e, streams["sp"])


def bcast_rows(t, off, n):
    return bass.AP(t, off, [[0, 128], [1, n]])


def build_program(layers, stop=None):
    nc = bass.Bass("TRN2", target_bir_lowering=False)
    es = contextlib.ExitStack()

    def din(name, shape, dt=F32):
        return nc.dram_tensor(name, list(shape), dt, kind="ExternalInput")

    t_x = din("x", [S, D])
    t_an = din("attn_norm", [4, D]); t_mn = din("mlp_norm", [4, D])
    t_qg = din("q_gain", [4, DH]); t_kg = din("k_gain", [4, DH])
    t_wout = din("w_out", [4, D, D]); t_wup = din("mlp_w_up", [4, D, DFF]); t_wdn = din("mlp_w_down", [4, DFF, D])
    t_win = {0: din("dsa_w_in", [D, 3072]), 1: din("nsa_w_in", [D, 2608]),
             2: din("fox_w_in", [D, 3088]), 3: din("moba_w_in", [D, 3072])}
    t_posT = din("nsa_posT", [2, DH, 32])
    t_w1 = din("nsa_cmp_w1", [2, 2048, 256]); t_w2 = din("nsa_cmp_w2", [2, 256, DH])
    t_bf = din("fox_b_f", [1, H])
    t_r31 = din("rel31", [1, H])
    t_mmask = din("moba_mask", [1, 512]); t_mown = din("moba_own", [1, 512])
    t_gB = din("gB", [H, GLEN]); t_gBw = din("gBw", [H, GLEN]); t_gBd = din("gBd", [H, GLEN])
    t_gBc = din("gBc", [H, GCLEN]); t_g0 = din("g0", [1, GLEN]); t_lnc = din("lnc", [1, GLEN])
    t_cf = din("cf32", [128, 4, 128]); t_idb = din("identb", [128, 128], BF16)
    t_E16 = din("E16", [16, S], BF16); t_E64 = din("E64", [64, S], BF16)
    t_OV = din("OV", [128, 2, 64], BF16); t_ST = din("ST", [128, 128])
    t_y = nc.dram_tensor("y", [S, D], F32, kind="ExternalOutput")
    t_X = [nc.dram_tensor("xs%d" % i, [S, D], F32, kind="Internal") for i in range(2)]
    t_QK = nc.dram_tensor("qk", [32, 128, S], BF16, kind="Internal")
    t_V = nc.dram_tensor("vv", [16, S, DH], BF16, kind="Internal")
    t_MIX = nc.dram_tensor("mix", [S, D], BF16, kind="Internal")
    t_AUG = nc.dram_tensor("aug", [192, S], BF16, kind="Internal")
    t_gD = nc.dram_tensor("gD", [H, GLEN], F32, kind="Internal")

    def SB(name, shape, dt):
        return es.enter_context(nc.sbuf_tensor("sb_" + name, list(shape), dt))

    banks = [es.enter_context(nc.psum_tensor("bank%d" % i, [128, 512], F32)) for i in range(8)]
    AA = SB("arenaA", [128, 32768], BF16)
    AB = SB("arenaB", [128, 32768], BF16)
    WO = SB("wout", [128, 8, 1024], BF16)
    identb = SB("identb", [128, 128], BF16)
    cf = SB("cf", [128, 4, 128], F32)
    ones_b = SB("ones_b", [128, 128], BF16)
    gbc_a = SB("gbc_a", [128, D], F32); gbc_m = SB("gbc_m", [128, D], F32)
    gq = SB("gq", [128, DH], F32); gk = SB("gk", [128, DH], F32); gqk = SB("gqk", [128, DH], F32)
    small = SB("small", [128, 64], F32)
    b31 = SB("b31", [128, H], F32)
    GA = SB("genA", [128, 21760], BF16)

    P = Prog(nc)
    gates = AB[:, 0:2 * NT * 48].bitcast(F32).rearrange("p (t c) -> p t c", c=48)

    def bf(ap):
        return ap

    def f32v(arena, off_b, n):
        return arena[:, off_b // 2: off_b // 2 + 2 * n].bitcast(F32)

    def b16v(arena, off_b, n):
        return arena[:, off_b // 2: off_b // 2 + n]

    P.dma("sp", identb[:], t_idb.ap()[:, :], writes=["identb"])
    P.dma("sp", cf[:], t_cf.ap()[:, :, :], writes=["cf"])
    P.op("dve", lambda e: e.memset(ones_b[:], 1.0), writes=["ones_b"])
    P.dma("sp", b31[:], bcast_rows(t_r31, 0, H), writes=["b31"])
    P.op("dve", lambda e: e.memset(small[:], 0.0), writes=["ssq", "rstd", "ssq8", "rs", "ssq4"])

    def phase1(L, kind, t_src):
        ncol = NCOLS[kind]
        wv = AA[:, 0:8 * ncol].rearrange("p (k n) -> p k n", n=ncol)
        win = t_win[kind].ap()
        for kc in range(8):
            P.dma("pool", wv[:, kc, :], win[kc * 128:(kc + 1) * 128, :], writes=[("win", kc)])
        P.dma("pool", WO[:, :, :], t_wout.ap()[L].rearrange("(k p) n -> p k n", p=128), writes=["wo"])
        P.dma("sp", gbc_a[:], bcast_rows(t_an, L * D, D), writes=["gbc_a"])
        P.dma("sp", gbc_m[:], bcast_rows(t_mn, L * D, D), writes=["gbc_m"])
        P.dma("sp", gq[:], bcast_rows(t_qg, L * DH, DH), writes=["gq"])
        P.dma("sp", gk[:], bcast_rows(t_kg, L * DH, DH), writes=["gk"])
        P.op("dve", lambda e: e.tensor_tensor(out=gqk[:], in0=gq[:], in1=gk[:], op=ALU.mult),
             reads=["gq", "gk"], writes=["gqk"])

        xt = [f32v(GA, 0, 1024), f32v(GA, 4096, 1024)]
        junk = b16v(GA, 8192, 1024)
        hb = b16v(GA, 10240, 1024)
        hT = [b16v(GA, 12288, 1024), b16v(GA, 14336, 1024)]
        sq = f32v(GA, 16384, 512)
        tm = b16v(GA, 18432, 18 * 128)
        vm = b16v(GA, 23040, 1024)
        tst = b16v(GA, 25088, 18 * 512).rearrange("p (b n) -> p b n", n=512)
        ssq = small[:, 0:1]; rstd = small[:, 1:2]; ssq8 = small[:, 8:16]
        if kind == 2:
            bfb = SBv["bfb"]
            caq = SBv["caq"]; cak = SBv["cak"]; carry = SBv["carry"]; fz = SBv["fz"]
            P.dma("sp", bfb[:], bcast_rows(t_bf, 0, H), writes=["bfb"])
            P.op("dve", lambda e: e.memset(carry[:], 0.0), writes=["carry"])
            P.op("dve", lambda e: e.memset(caq[:], 1.0), writes=["caq"])
            P.op("dve", lambda e: e.memset(cak[:], 1.0), writes=["cak"])
            P.op("dve", lambda e: e.memset(tm[:, 2048:2304], 0.0), writes=["tm"])

        if kind == 1:
            chunks = [(0, 512, [(0, 512, "q", 0)]), (512, 512, [(0, 512, "q", 512)]),
                      (1024, 512, [(0, 256, "c", 1536), (256, 256, "c", 1792)]),
                      (1536, 512, [(0, 256, "k", 1024), (256, 256, "v", 0)]),
                      (2048, 512, [(0, 256, "k", 1280), (256, 256, "v", 256)]),
                      (2560, 48, [(0, 48, "g", 0)])]
            nv = 8
        else:
            chunks = [(0, 512, [(0, 512, "q", 0)]), (512, 512, [(0, 512, "q", 512)]),
                      (1024, 512, [(0, 512, "k", 1024)]), (1536, 512, [(0, 512, "k", 1536)]),
                      (2048, 512, [(0, 512, "v", 0)]), (2560, 512, [(0, 512, "v", 512)])]
            if kind == 2:
                chunks.append((3072, 16, [(0, 16, "f", 0)]))
            nv = 16
        nblk = 18 if kind == 2 else 16
        xsrc = t_src.ap()
        cb = 0

        def front(t):
            s = t % 2
            P.dma("sp", xt[s], xsrc[t * 128:(t + 1) * 128, :], writes=[("xt", s)])
            P.op("act", lambda e, s=s: e.activation(out=junk, in_=xt[s], func=AF.Square, accum_out=ssq),
                 reads=[("xt", s)], writes=["junk", "ssq"])
            P.op("dve", lambda e: e.tensor_scalar(out=rstd, in0=ssq, scalar1=1.0 / D, scalar2=1e-6,
                                                   op0=ALU.mult, op1=ALU.add), reads=["ssq"], writes=["rstd"])
            P.op("act", lambda e: e.activation(out=rstd, in_=rstd, func=AF.Sqrt), reads=["rstd"], writes=["rstd"])
            P.op("dve", lambda e: e.reciprocal(out=rstd, in_=rstd), reads=["rstd"], writes=["rstd"])
            P.op("dve", lambda e, s=s: e.scalar_tensor_tensor(out=hb, in0=xt[s], scalar=rstd, in1=gbc_a[:],
                                                              op0=ALU.mult, op1=ALU.mult),
                 reads=[("xt", s), "rstd", "gbc_a"], writes=["hb"])

        def back(t):
            s = t % 2
            pT = banks[3][:].bitcast(BF16)
            for kc in range(8):
                P.op("pe", lambda e, kc=kc: e.transpose(out=pT[:, kc * 128:(kc + 1) * 128],
                                                         in_=hb[:, kc * 128:(kc + 1) * 128], identity=identb[:]),
                     reads=["hb", "identb"], writes=["b3"])
            P.op("act", lambda e, s=s: e.copy(out=hT[s], in_=pT), reads=["b3"], writes=[("hT", s)])

        front(0); back(0)
        for t in range(NT):
            tt = t % 4
            g = t // 4
            s = t % 2
            if t + 1 < NT:
                front(t + 1)
            hTs = hT[s].rearrange("p (k n) -> p k n", n=128)
            for (c0, n, segs) in chunks:
                bk = banks[cb % 3]; bkey = ("pb", cb % 3); cb += 1
                for kc in range(8):
                    P.op("pe", lambda e, kc=kc, bk=bk, c0=c0, n=n, hTs=hTs: e.matmul(
                        out=bk[:, 0:n], lhsT=hTs[:, kc, :], rhs=wv[:, kc, c0:c0 + n], start=(kc == 0), stop=(kc == 7)),
                        reads=[("hT", s), ("win", kc)], writes=[bkey], crit=("hT", s))
                need_norm = any(sg[2] in ("q", "k") for sg in segs)
                if need_norm:
                    nh = n // 64
                    P.op("act", lambda e, bk=bk, n=n: e.activation(out=sq[:, 0:n], in_=bk[:, 0:n], func=AF.Square),
                         reads=[bkey], writes=["sq"])
                    P.op("dve", lambda e, n=n, nh=nh: e.tensor_reduce(
                        out=ssq8[:, 0:nh], in_=sq[:, 0:n].rearrange("p (a b) -> p a b", b=64), axis=AX.X, op=ALU.add),
                        reads=["sq"], writes=["ssq8"])
                    P.op("dve", lambda e, nh=nh: e.tensor_scalar(out=ssq8[:, 0:nh], in0=ssq8[:, 0:nh], scalar1=1.0 / DH,
                                                                  scalar2=1e-6, op0=ALU.mult, op1=ALU.add),
                         reads=["ssq8"], writes=["ssq8"])
                    P.op("act", lambda e, nh=nh: e.activation(out=ssq8[:, 0:nh], in_=ssq8[:, 0:nh], func=AF.Sqrt),
                         reads=["ssq8"], writes=["ssq8"])
                    P.op("dve", lambda e, nh=nh: e.reciprocal(out=ssq8[:, 0:nh], in_=ssq8[:, 0:nh]),
                         reads=["ssq8"], writes=["ssq8"])
                for (so, sn, typ, dc) in segs:
                    if typ in ("q", "k"):
                        h0 = so // 64; nh = sn // 64
                        dst = tm[:, dc:dc + sn].rearrange("p (a b) -> p a b", b=64)
                        if typ == "k":
                            P.op("dve", lambda e, bk=bk, so=so, sn=sn, h0=h0, nh=nh, dst=dst: e.tensor_tensor(
                                out=dst, in0=bk[:, so:so + sn].rearrange("p (a b) -> p a b", b=64),
                                in1=ssq8[:, h0:h0 + nh].unsqueeze(2).to_broadcast([128, nh, 64]), op=ALU.mult),
                                reads=[bkey, "ssq8"], writes=["tm"])
                        else:
                            sqv = sq[:, so:so + sn].rearrange("p (a b) -> p a b", b=64)
                            P.op("dve", lambda e, bk=bk, so=so, sn=sn, h0=h0, nh=nh, sqv=sqv: e.tensor_tensor(
                                out=sqv, in0=bk[:, so:so + sn].rearrange("p (a b) -> p a b", b=64),
                                in1=ssq8[:, h0:h0 + nh].unsqueeze(2).to_broadcast([128, nh, 64]), op=ALU.mult),
                                reads=[bkey, "ssq8", "sq"], writes=["sq"])
                            P.op("dve", lambda e, nh=nh, sqv=sqv, dst=dst: e.tensor_tensor(
                                out=dst, in0=sqv, in1=gqk[:].unsqueeze(1).to_broadcast([128, nh, 64]), op=ALU.mult),
                                reads=["sq", "gqk"], writes=["tm"])
                    elif typ == "c":
                        P.op("act", lambda e, bk=bk, so=so, sn=sn, dc=dc: e.copy(out=tm[:, dc:dc + sn], in_=bk[:, so:so + sn]),
                             reads=[bkey], writes=["tm"])
                    elif typ == "v":
                        P.op("act", lambda e, bk=bk, so=so, sn=sn, dc=dc: e.copy(out=vm[:, dc:dc + sn], in_=bk[:, so:so + sn]),
                             reads=[bkey], writes=["vm"])
                    elif typ == "g":
                        P.op("act", lambda e, bk=bk, t=t: e.activation(out=gates[:, t, :], in_=bk[:, 0:48], func=AF.Sigmoid),
                             reads=[bkey], writes=["gates"])
                    elif typ == "f" and os.environ.get("KF_SKIP") == "1":
                        pass
                    elif typ == "f":
                        P.op("dve", lambda e, bk=bk: e.tensor_tensor(out=fz[:, 0:16], in0=bk[:, 0:16], in1=bfb[:], op=ALU.add),
                             reads=[bkey, "bfb"], writes=["fz"])
                        P.op("act", lambda e: e.activation(out=fz[:, 0:16], in_=fz[:, 0:16], func=AF.Exp, scale=-1.0),
                             reads=["fz"], writes=["fz"])
                        P.op("act", lambda e: e.activation(out=fz[:, 0:16], in_=fz[:, 0:16], func=AF.Ln, bias=1.0),
                             reads=["fz"], writes=["fz"])
                        b7 = banks[7]
                        P.op("pe", lambda e: e.matmul(out=b7[:, 0:16], lhsT=cf[:, 2, :], rhs=fz[:, 0:16], start=True, stop=True),
                             reads=["fz", "cf"], writes=["b7"])
                        P.op("pe", lambda e: e.matmul(out=b7[:, 16:32], lhsT=cf[:, 3, :], rhs=fz[:, 0:16], start=True, stop=True),
                             reads=["fz", "cf"], writes=["b7"])
                        P.op("dve", lambda e: e.tensor_tensor(out=fz[:, 16:32], in0=b7[:, 0:16], in1=carry[:], op=ALU.add),
                             reads=["b7", "carry", "fz"], writes=["fz2"])
                        P.op("dve", lambda e: e.tensor_tensor(out=carry[:], in0=b7[:, 16:32], in1=carry[:], op=ALU.add),
                             reads=["b7", "carry", "fz2"], writes=["carry"])
                        P.op("dve", lambda e: e.tensor_scalar(out=fz[:, 16:32], in0=fz[:, 16:32], scalar1=-8.0, scalar2=None,
                                                               op0=ALU.mult), reads=["fz2"], writes=["fz2"])
                        caqv = caq[:].rearrange("p (r h) -> p r h", h=16)
                        cakv = cak[:].rearrange("p (r h) -> p r h", h=16)
                        cur = fz[:, 16:32]; nxt = fz[:, 32:48]
                        for r in range(3):
                            P.op("dve", lambda e, r=r, cur=cur: e.tensor_copy(out=caqv[:, r, :], in_=cur),
                                 reads=["fz2"], writes=["caq"])
                            P.op("dve", lambda e, r=r: e.tensor_scalar(out=cakv[:, 3 + r, :], in0=caqv[:, r, :], scalar1=-1.0,
                                                                        scalar2=None, op0=ALU.mult),
                                 reads=["caq"], writes=["cak"])
                            if r < 2:
                                P.op("dve", lambda e, r=r, cur=cur, nxt=nxt: e.tensor_tensor(out=nxt, in0=cur, in1=caqv[:, r, :],
                                                                                             op=ALU.subtract),
                                     reads=["fz2", "caq"], writes=["fz2"])
                                cur, nxt = nxt, cur
                        P.op("dve", lambda e: e.tensor_copy(out=tm[:, 2048:2144], in_=caq[:]), reads=["caq"], writes=["tm"])
                        P.op("dve", lambda e: e.tensor_copy(out=tm[:, 2176:2272], in_=cak[:]), reads=["cak"], writes=["tm"])
            if t + 1 < NT:
                back(t + 1)
            for half in range((nblk + 7) // 8):
                pb = banks[4 + half][:].bitcast(BF16)
                nb = min(8, nblk - half * 8)
                for b in range(nb):
                    blk = half * 8 + b
                    P.op("pe", lambda e, pb=pb, b=b, blk=blk: e.transpose(out=pb[:, b * 128:(b + 1) * 128],
                                                                           in_=tm[:, blk * 128:(blk + 1) * 128], identity=identb[:]),
                         reads=["tm", "identb"], writes=[("b4", half)])
                dstv = tst[:, half * 8:half * 8 + nb, tt * 128:(tt + 1) * 128]
                srcv = pb[:, 0:nb * 128].rearrange("p (b n) -> p b n", n=128)
                eng = "act" if half == 0 else "dve"
                if eng == "act":
                    P.op("act", lambda e, dstv=dstv, srcv=srcv: e.copy(out=dstv, in_=srcv),
                         reads=[("b4", half)], writes=[("tst", half)])
                else:
                    P.op("dve", lambda e, dstv=dstv, srcv=srcv: e.tensor_copy(out=dstv, in_=srcv),
                         reads=[("b4", half)], writes=[("tst", half)])
            vdst = bass.AP(t_V, t * 128 * DH, [[DH, 128], [S * DH, nv], [1, DH]])
            P.dma("pool", vdst, vm[:, 0:nv * 64].rearrange("p (u d) -> p u d", d=64), reads=["vm"])
            if tt == 3:
                for half in range(2):
                    qdst = bass.AP(t_QK, half * 128 * S + g * 512, [[S, 64], [2 * 128 * S, 16], [1, 512]])
                    P.dma("pool", qdst, tst[half * 64:(half + 1) * 64, 0:16, :], reads=[("tst", 0), ("tst", 1)])
                if kind == 2:
                    for qk in range(2):
                        adst = bass.AP(t_AUG, qk * 96 * S + g * 512, [[S, 96], [1, 512]])
                        P.dma("pool", adst, tst[0:96, 16 + qk, :], reads=[("tst", 2)])
        P.barrier()

    def phase3(L, t_src, t_dst):
        wu = AA[:, :].rearrange("p (k n) -> p k n", n=DFF)
        wup = t_wup.ap()[L].rearrange("(k p) n -> p k n", p=128)
        for kc in range(8):
            P.dma("pool", wu[:, kc, :], wup[:, kc, :], writes=[("wup", kc)])
        wd = AB[:, :].rearrange("p (k n) -> p k n", n=1024)
        wdn = t_wdn.ap()[L].rearrange("(k p) n -> p k n", p=128)
        for q4 in range(4):
            P.dma("pool", wd[:, q4 * 8:(q4 + 1) * 8, :], wdn[:, q4 * 8:(q4 + 1) * 8, :], writes=[("wdn", q4)])
        ot = [b16v(GA, 0, 1024), b16v(GA, 2048, 1024)]
        xr = [f32v(GA, 4096, 1024), f32v(GA, 8192, 1024)]
        oT = b16v(GA, 12288, 1024)
        h2 = b16v(GA, 14336, 1024)
        h2T = b16v(GA, 16384, 1024)
        sq = f32v(GA, 18432, 512)
        uT = b16v(GA, 20480, 32 * 128).rearrange("p (f n) -> p f n", n=128)
        xo = [f32v(GA, 28672, 1024), f32v(GA, 32768, 1024)]
        junk = b16v(GA, 36864, 1024)
        ssq = small[:, 0:1]; rstd = small[:, 1:2]
        src = t_src.ap(); dst = t_dst.ap(); mix = t_MIX.ap()
        cbc = [0]

        def nextbank():
            bk = banks[cbc[0] % 3]; bkey = ("pb", cbc[0] % 3); cbc[0] += 1
            return bk, bkey

        pT = banks[3][:].bitcast(BF16)
        pT2 = banks[4][:].bitcast(BF16)
        oTv = oT.rearrange("p (k n) -> p k n", n=128)
        h2Tv = h2T.rearrange("p (k n) -> p k n", n=128)

        def A1(t):
            s = t % 2
            P.dma("sp", ot[s], mix[t * 128:(t + 1) * 128, :], writes=[("ot", s)])
            P.dma("sp", xr[s], src[t * 128:(t + 1) * 128, :], writes=[("xr", s)])
            for kc in range(8):
                P.op("pe", lambda e, kc=kc, s=s: e.transpose(out=pT[:, kc * 128:(kc + 1) * 128],
                                                              in_=ot[s][:, kc * 128:(kc + 1) * 128], identity=identb[:]),
                     reads=[("ot", s), "identb"], writes=["b3"])
            P.op("act", lambda e: e.copy(out=oT, in_=pT), reads=["b3"], writes=["oT"])
            for c in range(2):
                bk, bkey = nextbank()
                for kc in range(8):
                    P.op("pe", lambda e, kc=kc, bk=bk, c=c: e.matmul(out=bk[:, :], lhsT=oTv[:, kc, :], rhs=WO[:, kc, c * 512:(c + 1) * 512],
                                                                    start=(kc == 0), stop=(kc == 7)),
                         reads=["oT", "wo"], writes=[bkey], crit="oT")
                P.op("dve", lambda e, bk=bk, c=c, s=s: e.tensor_tensor(out=xr[s][:, c * 512:(c + 1) * 512], in0=bk[:, :],
                                                                      in1=xr[s][:, c * 512:(c + 1) * 512], op=ALU.add),
                     reads=[bkey, ("xr", s)], writes=[("xr", s)])
            P.op("act", lambda e, s=s: e.activation(out=junk, in_=xr[s], func=AF.Square, accum_out=ssq),
                 reads=[("xr", s)], writes=["junk", "ssq"])
            P.op("dve", lambda e: e.tensor_scalar(out=rstd, in0=ssq, scalar1=1.0 / D, scalar2=1e-6,
                                                   op0=ALU.mult, op1=ALU.add), reads=["ssq"], writes=["rstd"])
            P.op("act", lambda e: e.activation(out=rstd, in_=rstd, func=AF.Sqrt), reads=["rstd"], writes=["rstd"])
            P.op("dve", lambda e: e.reciprocal(out=rstd, in_=rstd), reads=["rstd"], writes=["rstd"])
            P.op("dve", lambda e, s=s: e.scalar_tensor_tensor(out=h2, in0=xr[s], scalar=rstd, in1=gbc_m[:],
                                                              op0=ALU.mult, op1=ALU.mult),
                 reads=[("xr", s), "rstd", "gbc_m"], writes=["h2"])

        def A2(t):
            for kc in range(8):
                P.op("pe", lambda e, kc=kc: e.transpose(out=pT2[:, kc * 128:(kc + 1) * 128],
                                                         in_=h2[:, kc * 128:(kc + 1) * 128], identity=identb[:]),
                     reads=["h2", "identb"], writes=["b4"])
            P.op("act", lambda e: e.copy(out=h2T, in_=pT2), reads=["b4"], writes=["h2T"])

        def Bup(t):
            for fq in range(8):
                bk, bkey = nextbank()
                for fi in range(4):
                    fc = fq * 4 + fi
                    for kc in range(8):
                        P.op("pe", lambda e, kc=kc, bk=bk, fc=fc, fi=fi: e.matmul(
                            out=bk[:, fi * 128:(fi + 1) * 128], lhsT=wu[:, kc, fc * 128:(fc + 1) * 128], rhs=h2Tv[:, kc, :],
                            start=(kc == 0), stop=(kc == 7)), reads=["h2T", ("wup", kc)], writes=[bkey])
                P.op("act", lambda e, bk=bk: e.activation(out=sq, in_=bk[:, :], func=AF.Square), reads=[bkey], writes=["sq"])
                P.op("dve", lambda e, bk=bk, fq=fq: e.scalar_tensor_tensor(
                    out=uT[:, fq * 4:(fq + 1) * 4, :], in0=bk[:, :].rearrange("p (f n) -> p f n", n=128), scalar=0.0,
                    in1=sq.rearrange("p (f n) -> p f n", n=128), op0=ALU.is_gt, op1=ALU.mult),
                    reads=[bkey, "sq"], writes=[("uT", fq)])

        def Cdown(t):
            s = t % 2
            for c in range(2):
                bk, bkey = nextbank()
                for fc in range(32):
                    P.op("pe", lambda e, fc=fc, bk=bk, c=c: e.matmul(out=bk[:, :], lhsT=uT[:, fc, :], rhs=wd[:, fc, c * 512:(c + 1) * 512],
                                                                    start=(fc == 0), stop=(fc == 31)),
                         reads=[("uT", fc // 4), ("wdn", fc // 8)], writes=[bkey], crit=("uT", fc // 4))
                P.op("dve", lambda e, bk=bk, c=c, s=s: e.tensor_tensor(out=xo[s][:, c * 512:(c + 1) * 512], in0=bk[:, :],
                                                                      in1=xr[s][:, c * 512:(c + 1) * 512], op=ALU.add),
                     reads=[bkey, ("xr", s)], writes=[("xo", s)])
            P.dma("sp", dst[t * 128:(t + 1) * 128, :], xo[s], reads=[("xo", s)])

        A1(0); A2(0); Bup(0)
        for t in range(NT):
            if t + 1 < NT:
                A1(t + 1)
            Cdown(t)
            if t + 1 < NT:
                A2(t + 1); Bup(t + 1)
        P.barrier()

    KA = [b16v(AA, 0, S), b16v(AB, 24576, S)]
    QA = [b16v(AA, 8192, S), b16v(AB, 32768, S)]
    VA = [b16v(GA, 0, NT * 66).rearrange("p (t c) -> p t c", c=66),
          b16v(AB, 40960, NT * 66).rearrange("p (t c) -> p t c", c=66)]
    WS = [f32v(GA, 4352, WU), f32v(AB, 45184, WU)]
    Hk = f32v(AA, 16384, WU)
    lg = [f32v(GA, 22528 + i * 2048, 512) for i in range(3)] + [f32v(AB, 20480, 512)]
    pTb = [b16v(GA, 28672 + i * 1024, 512) for i in range(3)] + [b16v(AB, 22528, 512)]
    NR = 4
    LAG = 3
    lbank_idx = [0, 1, 2, 6]
    lbanks = [banks[i] for i in lbank_idx]
    ost = f32v(GA, 31744, NT * 64).rearrange("p (t d) -> p t d", d=64)
    rs_t = small[:, 2:3]
    oTs = [f32v(AB, 16384, 512), f32v(AB, 18432, 512)]
    gctr = [0]
    rctr = [0]
    b7 = banks[7]
    B7 = ("bk", 7)

    deferred = []

    def tick():
        fire = []
        for it in deferred:
            it[0] -= 1
        while deferred and deferred[0][0] <= 0:
            fire.append(deferred.pop(0)[1])
        for fn in fire:
            fn()

    def force_until(tag):
        idx = -1
        for k, it in enumerate(deferred):
            if it[2] == tag:
                idx = k
        for _ in range(idx + 1):
            deferred.pop(0)[1]()

    def flush_deferred():
        while deferred:
            deferred.pop(0)[1]()

    rs4 = small[:, 16:20].unsqueeze(2)
    eptmp = f32v(AB, 63488, 256).rearrange("p (t d) -> p t d", d=64)

    def strip_dma(spec):
        tg, off, step, width = spec[0:4]
        rd = list(spec[4]) if len(spec) > 4 else []
        P.dma("sp", Hk[:, 0:width], bass.AP(tg, off, [[step, 128], [1, width]]), reads=rd, writes=["Hk"])

    def strip_compute(spec, s):
        width = spec[3]
        W = WS[s]
        for c in range((width + 511) // 512):
            n = min(512, width - c * 512)
            P.op("pe", lambda e, c=c, n=n: e.matmul(out=b7[:, 0:n], lhsT=cf[:, 1, :], rhs=Hk[:, c * 512:c * 512 + n],
                                                   start=True, stop=True), reads=["Hk", "cf"], writes=[B7])
            P.op("act", lambda e, c=c, n=n, W=W: e.copy(out=W[:, c * 512:c * 512 + n], in_=b7[:, 0:n]),
                 reads=[B7], writes=[("W", s)])

    def attn_core(run, s, hook=None):
        K = run["K"]; steps = run["steps"]; nvc = 66
        ka_fn = run["ka_fn"]; va_fn = run["va_fn"]; epilogue = run["ep"]
        kreads = run["kreads"](s); vreads = run["vreads"](s); qreads = run["qreads"](s)
        qa = QA[s]; W = WS[s]
        gfirst = {}; glast = {}; gpar = {}
        for n, st in enumerate(steps):
            G = st[0]
            if G not in gfirst:
                gfirst[G] = n
                gpar[G] = gctr[0] % 2
                gctr[0] += 1
            glast[G] = n

        firststep = set(gfirst.values())

        def crange(st):
            G, act = st[0], st[3]
            if steps.index(st) in firststep:
                return 0, 512
            return (min(act) - 4 * G) * 128, (max(act) + 1 - 4 * G) * 128

        def emit_pv(n):
            G, j = steps[n][0:2]
            c0, c1 = crange(steps[n])
            r = n % NR
            par = gpar[G]
            otb = banks[3 + par]; okey = ("bk", 3 + par)
            P.op("pe", lambda e, otb=otb, r=r, j=j, n=n, G=G, c0=c0, c1=c1: e.matmul(
                out=otb[0:nvc, c0:c1], lhsT=va_fn(j, s), rhs=pTb[r][:, c0:c1], start=(gfirst[G] == n), stop=(glast[G] == n)),
                reads=[("pT", r)] + vreads, writes=[okey])
            if glast[G] == n:
                ots = oTs[par]
                force_until(("fin", par))
                P.op("act", lambda e, ots=ots, otb=otb: e.copy(out=ots[0:nvc, :], in_=otb[0:nvc, :]),
                     reads=[okey], writes=[("oTs", par)])

                def finish(G=G, ots=ots, par=par):
                    tb = banks[5]; tkey = ("bk", 5)
                    for t in range(4):
                        P.op("pe", lambda e, tb=tb, ots=ots, t=t: e.transpose(
                            out=tb[:, t * nvc:(t + 1) * nvc], in_=ots[0:nvc, t * 128:(t + 1) * 128], identity=cf[0:nvc, 0, 0:nvc]),
                            reads=[("oTs", par), "cf"], writes=[tkey], crit=("oTs", par))
                    epilogue(G, tb[:, 0:4 * nvc].rearrange("p (t c) -> p t c", c=nvc), tkey)
                deferred.append([4, finish, ("fin", par)])

        hook_at = min(5, len(steps) - 1)
        for n, st in enumerate(steps):
            G, j, woff = st[0:3]
            mode = st[4] if len(st) > 4 else "W"
            r = n % NR
            c0, c1 = crange(st)
            bk = lbanks[r]; bkey = ("bk", lbank_idx[r])
            P.op("pe", lambda e, bk=bk, j=j, G=G, c0=c0, c1=c1: e.matmul(out=bk[:, c0:c1], lhsT=ka_fn(j, K, s),
                                                                      rhs=qa[0:K, G * 512 + c0:G * 512 + c1], start=True, stop=True),
                 reads=kreads + qreads, writes=[bkey])
            if mode == "W":
                P.op("dve", lambda e, bk=bk, r=r, woff=woff, c0=c0, c1=c1: e.scalar_tensor_tensor(
                    out=lg[r][:, c0:c1], in0=bk[:, c0:c1], scalar=0.125, in1=W[:, woff + c0:woff + c1], op0=ALU.mult, op1=ALU.add),
                    reads=[bkey, ("W", s)], writes=[("lg", r)])
                P.op("act", lambda e, r=r, c0=c0, c1=c1: e.activation(out=pTb[r][:, c0:c1], in_=lg[r][:, c0:c1], func=AF.Exp),
                     reads=[("lg", r)], writes=[("pT", r)])
            elif mode == "plain":
                P.op("act", lambda e, r=r, bk=bk, c0=c0, c1=c1: e.activation(out=pTb[r][:, c0:c1], in_=bk[:, c0:c1], func=AF.Exp, scale=0.125),
                     reads=[bkey], writes=[("pT", r)])
            else:
                bap = mode[1]
                P.op("act", lambda e, r=r, bk=bk, bap=bap, c0=c0, c1=c1: e.activation(out=pTb[r][:, c0:c1], in_=bk[:, c0:c1], func=AF.Exp,
                                                                                 bias=bap, scale=0.125),
                     reads=[bkey, "b31"], writes=[("pT", r)])
            if n >= LAG:
                emit_pv(n - LAG)
            tick()
            if n == hook_at and hook is not None:
                hook()
        for n in range(max(0, len(steps) - LAG), len(steps)):
            emit_pv(n)
        if run.get("post") is not None:
            deferred.append([4, run["post"], None])

    def execute(runs):
        def sset(i):
            return (rctr[0] + i) % 2

        def pf_dma(i):
            s = sset(i)
            runs[i]["loads"](s)
            if runs[i].get("strip") is not None:
                strip_dma(runs[i]["strip"])

        def pf_compute(i):
            s = sset(i)
            if runs[i].get("strip") is not None:
                strip_compute(runs[i]["strip"], s)
            if runs[i].get("pre") is not None:
                runs[i]["pre"](s)

        pf_dma(0)
        pf_compute(0)
        for i in range(len(runs)):
            hook = None
            if i + 1 < len(runs):
                pf_dma(i + 1)
                hook = (lambda i=i: pf_compute(i + 1))
            attn_core(runs[i], sset(i), hook)
        rctr[0] += len(runs)

    def std_steps(maxspan=None, plain_past=False, const_far=None):
        steps = []
        for G in range(8):
            j0 = 0 if maxspan is None else max(0, 4 * G - maxspan)
            for j in range(j0, 4 * G + 4):
                act = [i for i in range(4 * G, 4 * G + 4) if i >= j and (maxspan is None or i - j <= maxspan)]
                if act:
                    Dd = 512 * G - 128 * j
                    mode = "W"
                    if plain_past and Dd >= 128:
                        mode = "plain"
                    elif const_far is not None and Dd >= 1664:
                        mode = ("const", const_far)
                    steps.append((G, j, Dd + WOFF, act, mode))
        return steps

    def ep_common(G, tbv, okey, gate_col):
        P.op("dve", lambda e: e.tensor_scalar(out=rs4, in0=tbv[:, :, 64:65], scalar1=1e-30, scalar2=None, op0=ALU.max),
             reads=[okey], writes=["rs"])
        P.op("dve", lambda e: e.reciprocal(out=rs4, in_=rs4), reads=["rs"], writes=["rs"])
        if gate_col is not None:
            P.op("dve", lambda e: e.tensor_tensor(out=rs4, in0=rs4, in1=gates[:, 4 * G:4 * G + 4, gate_col:gate_col + 1], op=ALU.mult),
                 reads=["rs", "gates"], writes=["rs"])

    def ep_plain(gate_col=None, accumulate=False):
        def ep(G, tbv, okey):
            ep_common(G, tbv, okey, gate_col)
            okeys = [("ost", i) for i in range(4 * G, 4 * G + 4)]
            if accumulate:
                P.op("dve", lambda e: e.tensor_tensor(out=eptmp, in0=tbv[:, :, 0:64], in1=rs4.to_broadcast([128, 4, 64]), op=ALU.mult),
                     reads=[okey, "rs"], writes=["eptmp"])
                P.op("dve", lambda e: e.tensor_tensor(out=ost[:, 4 * G:4 * G + 4, :], in0=eptmp, in1=ost[:, 4 * G:4 * G + 4, :], op=ALU.add),
                     reads=["eptmp"] + okeys, writes=okeys)
            else:
                P.op("dve", lambda e: e.tensor_tensor(out=ost[:, 4 * G:4 * G + 4, :], in0=tbv[:, :, 0:64], in1=rs4.to_broadcast([128, 4, 64]), op=ALU.mult),
                     reads=[okey, "rs"], writes=okeys)
        return ep

    def ld_k(s, unit, extra=None):
        P.dma("sp", KA[s][0:64, :], t_QK.ap()[unit, 0:64, :], writes=[("ka", s)])
        if extra is not None:
            tg, r0, nr = extra
            P.dma("sp", KA[s][64:64 + nr, :], tg.ap()[r0:r0 + nr, :], writes=[("ka_hi", s)])

    def ld_q(s, unit):
        P.dma("sp", QA[s][0:64, :], t_QK.ap()[unit, 0:64, :], writes=[("qa", s)])

    def ld_v(s, unit):
        P.dma("sp", VA[s][:, :, 0:64], t_V.ap()[unit].rearrange("(t p) d -> p t d", p=128), writes=[("va", s)])

    def store_o(h):
        dst = bass.AP(t_MIX, h * DH, [[D, 128], [128 * D, NT], [1, DH]])
        P.dma("pool", dst, ost[:, :, :], reads=[("ost", i) for i in range(NT)])

    ka_std = lambda j, K, s: KA[s][0:K, j * 128:(j + 1) * 128]
    va_std = lambda j, s: VA[s][:, j, :]

    def phase2_common_init():
        for s in range(2):
            P.op("dve", lambda e, s=s: e.memset(VA[s][:, :, 64:66], 1.0), writes=[("va_ones", s)])

    def base_run(K, steps, ep, hi_k=False, hi_q=False):
        return dict(K=K, steps=steps, ka_fn=ka_std, va_fn=va_std, ep=ep,
                    kreads=(lambda s: [("ka", s)] + ([("ka_hi", s)] if hi_k else [])),
                    vreads=(lambda s: [("va", s), ("va_ones", s)]),
                    qreads=(lambda s: [("qa", s)] + ([("qa_hi", s)] if hi_q else [])))

    def phase2_fox():
        phase2_common_init()
        spec = (t_g0, 0, 1, WU)
        for s in range(2):
            strip_dma(spec)
            strip_compute(spec, s)
        steps = std_steps(plain_past=True)
        runs = []
        for h in range(H):
            r = base_run(70, steps, ep_plain(), hi_k=True, hi_q=True)

            def loads(s, h=h):
                ld_k(s, 16 + h)
                P.dma("sp", KA[s][64:70, :], bass.AP(t_AUG, (96 + h) * S, [[16 * S, 6], [1, S]]), writes=[("ka_hi", s)])
                ld_q(s, h)
                P.dma("sp", QA[s][64:70, :], bass.AP(t_AUG, h * S, [[16 * S, 6], [1, S]]), writes=[("qa_hi", s)])
                ld_v(s, h)
            r["loads"] = loads
            r["post"] = (lambda h=h: store_o(h))
            runs.append(r)
        execute(runs)
        flush_deferred()
        P.barrier()

    def phase2_dil():
        phase2_common_init()
        gsb = f32v(AA, 16384, GLEN)[0:16, :]
        lsb = f32v(AB, 45184, GLEN)[0:16, :]
        P.dma("sp", gsb, t_gBd.ap()[:, :], writes=["Hk"])
        P.dma("sp", lsb, bass.AP(t_lnc, 0, [[0, 16], [1, GLEN]]), writes=[("W", 1)])
        P.op("dve", lambda e: e.tensor_tensor(out=gsb, in0=gsb, in1=lsb, op=ALU.add), reads=["Hk", ("W", 1)], writes=["Hk"])
        P.dma("sp", t_gD.ap()[:, :], gsb, reads=["Hk"], writes=["gD"])
        steps = std_steps(maxspan=16)
        runs = []
        for h in range(H):
            r = base_run(64, steps, ep_plain())
            r["strip"] = (t_gD, h * GLEN, 1, 2048 + WOFF + 512 + 128, ["gD"])

            def loads(s, h=h):
                ld_k(s, 16 + h); ld_q(s, h); ld_v(s, h)
            r["loads"] = loads
            r["post"] = (lambda h=h: store_o(h))
            runs.append(r)
        execute(runs)
        flush_deferred()
        P.barrier()

    def phase2_moba():
        phase2_common_init()
        km32 = f32v(GA, 39936, 16)
        kmb = b16v(GA, 40064, 16)
        gm = f32v(GA, 40128, 16)
        mx8 = f32v(GA, 40192, 8)
        mb80 = f32v(GA, 40256, 80)

        gmall = f32v(AA, 34304, 512); gmv = gmall.rearrange("p (t n) -> p t n", n=16)
        tmpb = f32v(AA, 36352, 512); tmpv = tmpb.rearrange("p (t n) -> p t n", n=16)
        MBbuf = f32v(AA, 38400, 576); MBm = MBbuf[:, 64:576]; MBv = MBm.rearrange("p (t n) -> p t n", n=16)
        mall = f32v(AA, 40704, 96)
        maskall = f32v(AA, 41216, 512)
        own1h = f32v(AA, 43264, 512)
        P.dma("sp", maskall, bcast_rows(t_mmask, 0, 512), writes=["maskall"])
        P.dma("sp", own1h, bcast_rows(t_mown, 0, 512), writes=["own1h"])
        P.op("dve", lambda e: e.memset(MBbuf, 0.0), writes=["MBbuf"])

        def bc(i):
            return mall[:, 32 * i:32 * (i + 1)].unsqueeze(2).to_broadcast([128, 32, 16])

        def pre(s):
            ka = KA[s]; qa = QA[s]
            P.op("dve", lambda e: e.tensor_reduce(out=km32[0:64, :], in_=ka[0:64, :].rearrange("p (n b) -> p n b", b=256),
                                                  axis=AX.X, op=ALU.add), reads=[("ka", s)], writes=["km32"])
            P.op("dve", lambda e: e.tensor_scalar(out=kmb[0:64, :], in0=km32[0:64, :], scalar1=1.0 / 256, scalar2=None, op0=ALU.mult),
                 reads=["km32"], writes=["kmb"])
            for i in range(NT):
                P.op("pe", lambda e, i=i: e.matmul(out=b7[:, i * 16:(i + 1) * 16], lhsT=qa[0:64, i * 128:(i + 1) * 128], rhs=kmb[0:64, :],
                                                   start=True, stop=True), reads=[("qa", s), "kmb"], writes=[B7])
            P.op("dve", lambda e: e.tensor_tensor(out=gmall, in0=b7[:, 0:512], in1=maskall, op=ALU.add),
                 reads=[B7, "maskall"], writes=["gmall"])
            P.op("dve", lambda e: e.tensor_reduce(out=mall[:, 0:32], in_=gmv, axis=AX.X, op=ALU.max), reads=["gmall"], writes=["mall"])
            P.op("dve", lambda e: e.tensor_tensor(out=tmpv, in0=gmv, in1=bc(0), op=ALU.is_ge), reads=["gmall", "mall"], writes=["tmpb"])
            P.op("dve", lambda e: e.scalar_tensor_tensor(out=tmpb, in0=tmpb, scalar=NEG, in1=gmall, op0=ALU.mult, op1=ALU.add),
                 reads=["tmpb", "gmall"], writes=["tmpb"])
            P.op("dve", lambda e: e.tensor_reduce(out=mall[:, 32:64], in_=tmpv, axis=AX.X, op=ALU.max), reads=["tmpb"], writes=["mall"])
            P.op("dve", lambda e: e.tensor_tensor(out=MBv, in0=tmpv, in1=bc(1), op=ALU.is_ge), reads=["tmpb", "mall"], writes=["MBbuf"])
            P.op("dve", lambda e: e.scalar_tensor_tensor(out=tmpb, in0=MBm, scalar=NEG, in1=tmpb, op0=ALU.mult, op1=ALU.add),
                 reads=["MBbuf", "tmpb"], writes=["tmpb"])
            P.op("dve", lambda e: e.tensor_reduce(out=mall[:, 64:96], in_=tmpv, axis=AX.X, op=ALU.max), reads=["tmpb"], writes=["mall"])
            P.op("dve", lambda e: e.tensor_scalar(out=mall[:, 64:96], in0=mall[:, 64:96], scalar1=-1.0e29, scalar2=None, op0=ALU.max),
                 reads=["mall"], writes=["mall"])
            P.op("dve", lambda e: e.tensor_tensor(out=MBv, in0=gmv, in1=bc(2), op=ALU.is_ge), reads=["gmall", "mall"], writes=["MBbuf"])
            P.op("dve", lambda e: e.tensor_tensor(out=MBm, in0=MBm, in1=own1h, op=ALU.add), reads=["MBbuf", "own1h"], writes=["MBbuf"])
            P.op("dve", lambda e: e.tensor_scalar(out=MBm, in0=MBm, scalar1=-1.0, scalar2=BIG, op0=ALU.add, op1=ALU.mult),
                 reads=["MBbuf"], writes=["MBbuf"])
            for gi in range(8):
                for k in range(4):
                    i = 4 * gi + k
                    P.op("pe", lambda e, i=i, k=k: e.transpose(out=b7[0:80, k * 128:(k + 1) * 128], in_=MBbuf[:, i * 16:i * 16 + 80],
                                                                 identity=cf[:, 0, :]), reads=["MBbuf", "cf"], writes=[B7])
                P.op("act", lambda e, gi=gi: e.copy(out=qa[64:80, gi * 512:(gi + 1) * 512], in_=b7[64:80, 0:512]),
                     reads=[B7], writes=[("qa_hi", s)])

        runs = []
        for h in range(H):
            r = base_run(80, std_steps(const_far=b31[:, h:h + 1]), ep_plain(), hi_k=True, hi_q=True)
            r["strip"] = (t_gB, h * GLEN, 1, 1664 + WOFF + 512 + 128)

            def loads(s, h=h):
                ld_k(s, 16 + h, (t_E16, 0, 16)); ld_q(s, h); ld_v(s, h)
            r["loads"] = loads
            r["pre"] = pre
            r["post"] = (lambda h=h: store_o(h))
            runs.append(r)
        execute(runs)
        flush_deferred()
        P.barrier()

    def phase2_nsa():
        phase2_common_init()
        kcR = b16v(AA, 16384, S)
        kcA = b16v(AA, 24576, S)
        kcB = b16v(AA, 34304, S)
        w1 = b16v(AA, 42496, 32 * 256).rearrange("p (t c) -> p t c", c=256)
        w2 = b16v(AA, 58880, 128).rearrange("p (m d) -> p m d", d=64)
        hid = b16v(AA, 59392, 512).rearrange("p (m n) -> p m n", n=256)
        kcn = b16v(AA, 60416, 256).rearrange("p (t d) -> p t d", d=128)
        kcmpT = b16v(AA, 60928, 256)
        vcmp = b16v(AA, 61440, 2 * 66).rearrange("p (t c) -> p t c", c=66)
        vcmpA = b16v(AA, 61952, 2 * 66).rearrange("p (t c) -> p t c", c=66)
        posT = f32v(AA, 62464, 32)
        impb = SBv["imp"]
        STt = SBv["ST"]
        OVt = SBv["OV"]
        impm = f32v(GA, 39936, 64)
        mx16 = f32v(GA, 40192, 16)
        mb128 = f32v(GA, 40256, 128)
        x2 = f32v(GA, 40768, 256)
        tg = f32v(GA, 41792, 256)
        P.dma("sp", STt[:], t_ST.ap()[:, :], writes=["ST"])
        P.dma("sp", OVt[:], t_OV.ap()[:, :, :], writes=["OV"])
        P.op("dve", lambda e: e.memset(mb128, 0.0), writes=["mb128"])
        stepwin = std_steps(maxspan=4)
        stepcmp = []
        for G in range(8):
            for ntl in range(2):
                if G >= 4 * ntl:
                    stepcmp.append((G, ntl, 512 * G - 2048 * ntl, list(range(4 * G, 4 * G + 4))))
        kc_fn = lambda j, K, s: kcmpT[0:64, j * 128:(j + 1) * 128]
        vc_fn = lambda j, s: vcmp[:, j, :]
        vcA_fn = lambda j, s: vcmpA[:, j, :]
        b7b = b7[:].bitcast(BF16)

        def compress(kh):
            P.op("dve", lambda e: e.memset(kcn, 0.0), writes=["kcn"])
            P.op("dve", lambda e: e.memset(vcmp, 0.0), writes=["vcmp"])
            P.op("dve", lambda e: e.memset(vcmp[:, :, 64:66], 1.0), reads=["vcmp"], writes=["vcmp"])
            P.op("dve", lambda e: e.memset(vcmpA[:, :, 64:66], 1.0), writes=["vcmpA"])
            P.op("dve", lambda e: e.tensor_copy(out=vcmpA[:, :, 0:64], in_=OVt[:]), reads=["vcmpA", "OV"], writes=["vcmpA"])
            for kv in range(2):
                P.dma("sp", kcR[0:64, :], t_QK.ap()[24 + 4 * kv + kh, 0:64, :], writes=["Hk"])
                P.dma("sp", posT[0:64, :], t_posT.ap()[kv], writes=["posT"])
                P.dma("pool", w1[0:64, :, :], t_w1.ap()[kv].rearrange("(t d) c -> d t c", d=64), writes=["w1"])
                P.dma("pool", w2[:, :, :], t_w2.ap()[kv].rearrange("(m p) d -> p m d", p=128), writes=["w2"])
                for ab, dstb in ((0, kcA), (1, kcB)):
                    P.op("dve", lambda e, ab=ab, dstb=dstb: e.tensor_tensor(
                        out=dstb[0:64, :].rearrange("p (n s) -> p n s", s=16), in0=kcR[0:64, :].rearrange("p (n s) -> p n s", s=16),
                        in1=posT[0:64, ab * 16:(ab + 1) * 16].unsqueeze(1).to_broadcast([64, 256, 16]), op=ALU.add),
                        reads=["Hk", "posT"], writes=[("kcAB", ab)] if ab == 1 else ["Hk"])
                kcAv = kcA[0:64, :].rearrange("p (n s) -> p n s", s=16)
                kcBv = kcB[0:64, :].rearrange("p (n s) -> p n s", s=16)
                for mc in range(2):
                    for t in range(32):
                        rhs = kcAv[:, 0:255, t] if t < 16 else kcBv[:, 1:256, t - 16]
                        P.op("pe", lambda e, t=t, mc=mc, rhs=rhs: e.matmul(out=b7[:, 0:255], lhsT=w1[0:64, t, mc * 128:(mc + 1) * 128],
                                                                       rhs=rhs, start=(t == 0), stop=(t == 31)),
                             reads=["Hk", ("kcAB", 1), "w1"], writes=[B7])
                    P.op("act", lambda e: e.activation(out=x2[:, 0:255], in_=b7[:, 0:255], func=AF.Square), reads=[B7], writes=["x2"])
                    P.op("dve", lambda e: e.tensor_scalar(out=x2[:, 0:255], in0=x2[:, 0:255], scalar1=0.044715, scalar2=1.0,
                                                           op0=ALU.mult, op1=ALU.add), reads=["x2"], writes=["x2"])
                    P.op("dve", lambda e: e.tensor_tensor(out=tg[:, 0:255], in0=b7[:, 0:255], in1=x2[:, 0:255], op=ALU.mult),
                         reads=[B7, "x2"], writes=["tg"])
                    P.op("act", lambda e: e.activation(out=tg[:, 0:255], in_=tg[:, 0:255], func=AF.Sigmoid, scale=1.5957691216),
                         reads=["tg"], writes=["tg"])
                    P.op("dve", lambda e, mc=mc: e.tensor_tensor(out=hid[:, mc, 0:255], in0=b7[:, 0:255], in1=tg[:, 0:255], op=ALU.mult),
                         reads=[B7, "tg"], writes=["hid"])
                for ntl in range(2):
                    nn = 128 if ntl == 0 else 127
                    for mc in range(2):
                        P.op("pe", lambda e, ntl=ntl, nn=nn, mc=mc: e.matmul(out=b7[0:nn, 256:320], lhsT=hid[:, mc, ntl * 128:ntl * 128 + nn],
                                                                         rhs=w2[:, mc, :], start=(mc == 0), stop=(mc == 1)),
                             reads=["hid", "w2"], writes=[B7])
                    if kv == 0:
                        ssq = small[:, 4:5]
                        P.op("act", lambda e, nn=nn: e.activation(out=x2[0:nn, 0:64], in_=b7[0:nn, 256:320], func=AF.Square, accum_out=ssq[0:nn, :]),
                             reads=[B7], writes=["x2", "ssq4"])
                        P.op("dve", lambda e: e.tensor_scalar(out=ssq, in0=ssq, scalar1=1.0 / DH, scalar2=1e-6, op0=ALU.mult, op1=ALU.add),
                             reads=["ssq4"], writes=["ssq4"])
                        P.op("act", lambda e: e.activation(out=ssq, in_=ssq, func=AF.Sqrt), reads=["ssq4"], writes=["ssq4"])
                        P.op("dve", lambda e: e.reciprocal(out=ssq, in_=ssq), reads=["ssq4"], writes=["ssq4"])
                        P.op("dve", lambda e, nn=nn, ntl=ntl: e.tensor_scalar(out=kcn[0:nn, ntl, 0:64], in0=b7[0:nn, 256:320], scalar1=ssq[0:nn, :],
                                                                           scalar2=None, op0=ALU.mult), reads=[B7, "ssq4"], writes=["kcn"])
                        P.op("pe", lambda e, ntl=ntl: e.transpose(out=b7b[:, 768:896], in_=kcn[:, ntl, :], identity=identb[:]),
                             reads=["kcn", "identb"], writes=[B7])
                        P.op("act", lambda e, ntl=ntl: e.copy(out=kcmpT[0:64, ntl * 128:(ntl + 1) * 128], in_=b7b[0:64, 768:896]),
                             reads=[B7], writes=["kcmpT"])
                    else:
                        P.op("act", lambda e, nn=nn, ntl=ntl: e.copy(out=vcmp[0:nn, ntl, 0:64], in_=b7[0:nn, 256:320]),
                             reads=[B7], writes=["vcmp"])
            P.op("dve", lambda e: e.memset(impb, 0.0), writes=["imp"])

        def epA(G, tbv, okey):
            ep_common(G, tbv, okey, None)
            P.op("dve", lambda e: e.tensor_tensor(out=eptmp, in0=tbv[:, :, 0:64], in1=rs4.to_broadcast([128, 4, 64]), op=ALU.mult),
                 reads=[okey, "rs"], writes=["eptmp"])
            P.op("dve", lambda e: e.tensor_tensor(out=impb[:, 4 * G:4 * G + 4, :], in0=eptmp, in1=impb[:, 4 * G:4 * G + 4, :], op=ALU.add),
                 reads=["eptmp", "imp"], writes=["imp"])

        def selection():
            for i in range(NT):
                P.op("dve", lambda e, i=i: e.tensor_tensor(out=impm, in0=impb[:, i, :], in1=STt[:, 64 - 2 * i:128 - 2 * i], op=ALU.add),
                     reads=["imp", "ST"], writes=["impm"])
                P.op("dve", lambda e: e.max(out=mx16[:, 0:8], in_=impm), reads=["impm"], writes=["mx16"])
                P.op("dve", lambda e: e.match_replace(out=mb128[:, 0:64], in_to_replace=mx16[:, 0:8], in_values=impm, imm_value=NEG),
                     reads=["impm", "mx16"], writes=["mb128"])
                P.op("dve", lambda e: e.max(out=mx16[:, 8:16], in_=mb128[:, 0:64]), reads=["mb128"], writes=["mx16"])
                P.op("dve", lambda e: e.tensor_scalar(out=mx16[:, 14:15], in0=mx16[:, 14:15], scalar1=-1.0e29, scalar2=None, op0=ALU.max),
                     reads=["mx16"], writes=["mx16"])
                P.op("dve", lambda e: e.tensor_scalar(out=mb128[:, 64:128], in0=impm, scalar1=mx16[:, 14:15], scalar2=None, op0=ALU.is_ge),
                     reads=["impm", "mx16", "mb128"], writes=["mb128"])
                P.op("dve", lambda e: e.tensor_scalar(out=mb128[:, 64:128], in0=mb128[:, 64:128], scalar1=-1.0, scalar2=BIG,
                                                       op0=ALU.add, op1=ALU.mult), reads=["mb128"], writes=["mb128"])
                P.op("dve", lambda e, i=i: e.memset(mb128[0:64, 64 + 2 * i:65 + 2 * i], 0.0), reads=["mb128"], writes=["mb128"])
                P.op("dve", lambda e, i=i: e.memset(mb128[64:128, 65 + 2 * i:66 + 2 * i], 0.0), reads=["mb128"], writes=["mb128"])
                P.op("pe", lambda e: e.transpose(out=b7[:, 0:128], in_=mb128, identity=cf[:, 0, :]), reads=["mb128", "cf"], writes=[B7])
                for s in range(2):
                    P.op("act", lambda e, i=i, s=s: e.copy(out=QA[s][64:128, i * 128:(i + 1) * 128], in_=b7[64:128, 0:128]),
                         reads=[B7], writes=[("qa_hi", s)])

        def cmp_run(u, ep, vfn, vkey):
            return dict(K=64, steps=stepcmp, ka_fn=kc_fn, va_fn=vfn, ep=ep,
                        kreads=(lambda s: ["kcmpT"]), vreads=(lambda s: [vkey]), qreads=(lambda s: [("qa", s)]),
                        strip=(t_gBc, u * GCLEN, 16, 4096), loads=(lambda s, u=u: ld_q(s, u)))

        for kh in range(4):
            compress(kh)
            runs = []
            for g in range(4):
                u = kh * 4 + g
                runs.append(cmp_run(u, epA, vcA_fn, "vcmpA"))
            runs[-1]["post"] = selection
            for g in range(4):
                u = kh * 4 + g
                runs.append(cmp_run(u, ep_plain(gate_col=u * 3 + 0), vc_fn, "vcmp"))
                r = base_run(128, std_steps(const_far=b31[:, u:u + 1]), ep_plain(gate_col=u * 3 + 1, accumulate=True), hi_k=True, hi_q=True)
                r["strip"] = (t_gB, u * GLEN, 1, 1664 + WOFF + 512 + 128)
                r["loads"] = (lambda s, u=u, kh=kh: (ld_k(s, 16 + kh, (t_E64, 0, 64)), ld_q(s, u), ld_v(s, kh)))
                runs.append(r)
                r = base_run(64, stepwin, ep_plain(gate_col=u * 3 + 2, accumulate=True))
                r["strip"] = (t_gBw, u * GLEN, 1, 512 + WOFF + 512 + 128)
                r["loads"] = (lambda s, u=u, kh=kh: (ld_k(s, 20 + kh), ld_q(s, u), ld_v(s, 4 + kh)))
                r["post"] = (lambda u=u: store_o(u))
                runs.append(r)
            execute(runs)
        flush_deferred()
        P.barrier()

    SBv = {}
    SBv["bfb"] = SB("bfb", [128, H], F32)
    SBv["caq"] = SB("caq", [128, 96], BF16)
    SBv["cak"] = SB("cak", [128, 96], BF16)
    SBv["carry"] = SB("carry", [128, H], F32)
    SBv["fz"] = SB("fz", [128, 48], F32)
    SBv["imp"] = AB[:, 4096:4096 + 2 * NT * 64].bitcast(F32).rearrange("p (t d) -> p t d", d=64)
    SBv["ST"] = SB("STt", [128, 128], F32)
    SBv["OV"] = SB("OVt", [128, 2, 64], BF16)

    cur = t_x
    for idx, L in enumerate(layers):
        kind = L % 4
        lastl = (idx == len(layers) - 1)
        dstt = t_y if lastl else t_X[idx % 2]
        phase1(L, kind, cur)
        if stop == "p1":
            break
        if kind == 0:
            phase2_dil()
        elif kind == 1:
            phase2_nsa()
        elif kind == 2:
            phase2_fox()
        else:
            phase2_moba()
        if stop == "p2":
            for t in range(NT):
                tmpb = b16v(GA, (t % 2) * 2048, 1024)
                P.dma("sp", tmpb, t_MIX.ap()[t * 128:(t + 1) * 128, :], writes=[("dbg", t % 2)])
                tmpf = f32v(GA, 8192 + (t % 2) * 4096, 1024)
                P.op("dve", lambda e, tmpb=tmpb, tmpf=tmpf: e.tensor_copy(out=tmpf, in_=tmpb), reads=[("dbg", t % 2)], writes=[("dbgf", t % 2)])
                P.dma("sp", t_y.ap()[t * 128:(t + 1) * 128, :], tmpf, reads=[("dbgf", t % 2)])
            break
        phase3(L, cur, dstt)
        cur = dstt
    P.barrier()
    P.emit()
    return nc


def _rel_bucket_np(d):
    d = np.maximum(d, 0)
    df = np.maximum(d.astype(np.float32), np.float32(1.0))
    large = 16 + (np.log(df / np.float32(16)) / np.float32(math.log(2048 / 16)) * np.float32(16)).astype(np.int32)
    large = np.minimum(large, 31)
    return np.where(d < 16, d, large)


def _host_tables(rel_table):
    rel_table = np.asarray(rel_table, dtype=np.float32)
    m = np.arange(GLEN)
    d = m - 511
    bk = _rel_bucket_np(d)
    gat = rel_table[bk, :].T.copy()
    valid = d >= 0
    gB = np.where(valid[None, :], gat, np.float32(NEG)).astype(np.float32)
    gBw = np.where((valid & (d <= 511))[None, :], gat, np.float32(NEG)).astype(np.float32)
    cnt = ((d <= 128) & valid).astype(np.int32) + ((d % 4 == 0) & (d <= 512) & valid) + ((d % 16 == 0) & (d <= 2048) & valid)
    gBd = np.where((cnt > 0)[None, :], gat, np.float32(NEG)).astype(np.float32)
    lnc = np.where(cnt > 0, np.log(np.maximum(cnt, 1)), 0.0).astype(np.float32)[None, :]
    g0 = np.where(valid, 0.0, NEG).astype(np.float32)[None, :]
    mc = np.arange(GCLEN)
    dc = mc - 2063
    gatc = rel_table[_rel_bucket_np(dc), :].T.copy()
    gBc = np.where((dc >= 0)[None, :], gatc, np.float32(NEG)).astype(np.float32)
    cf = np.zeros((128, 4, 128), np.float32)
    cf[:, 0, :] = np.eye(128)
    cf[:, 1, :] = np.eye(128)[::-1]
    cf[:, 2, :] = np.triu(np.ones((128, 128)))
    cf[:, 3, :] = 1.0
    identb = np.eye(128).astype(ml_dtypes.bfloat16)
    tok = np.arange(S)
    E16 = (tok[None, :] // 256 == np.arange(16)[:, None]).astype(ml_dtypes.bfloat16)
    E64 = (tok[None, :] // 64 == np.arange(64)[:, None]).astype(ml_dtypes.bfloat16)
    n = np.arange(256)
    j = np.arange(64)
    ov = ((16 * n[:, None] < 64 * j[None, :] + 64) & (16 * n[:, None] + 32 > 64 * j[None, :]) & (n[:, None] < 255))
    OV = ov.reshape(2, 128, 64).transpose(1, 0, 2).astype(ml_dtypes.bfloat16).copy()
    r = (np.arange(128) >= 64).astype(np.int32)
    c = np.arange(128)
    ST = np.where(c[None, :] < 64 + r[:, None], 0.0, NEG).astype(np.float32)
    assert (_rel_bucket_np(np.arange(1537, 8192)) == 31).all()
    ti = np.arange(32)[:, None]; bn = np.arange(16)[None, :]
    moba_mask = np.where(bn < ti // 2, 0.0, NEG).astype(np.float32).reshape(1, 512)
    moba_own = (bn == ti // 2).astype(np.float32).reshape(1, 512)
    return dict(moba_mask=moba_mask, moba_own=moba_own, rel31=np.ascontiguousarray(rel_table[31:32, :]), gB=gB, gBw=gBw, gBd=gBd, gBc=gBc, g0=g0, lnc=lnc, cf32=cf, identb=identb, E16=E16, E64=E64, OV=OV, ST=ST)


_PROG_CACHE = {}


def _get_prog(layers):
    key = tuple(layers)
    if key not in _PROG_CACHE:
        import os
        _PROG_CACHE[key] = build_program(list(layers), stop=os.environ.get("KSTOP"))
    return _PROG_CACHE[key]


def _common_inputs(rel_table, attn_norm, mlp_norm, q_gain, k_gain, w_out, mlp_w_up, mlp_w_down, dsa_w_in, nsa_w_in,
                   nsa_cmp_pos, nsa_cmp_w1, nsa_cmp_w2, fox_w_in, fox_b_f, moba_w_in):
    f = lambda a: np.ascontiguousarray(np.asarray(a, dtype=np.float32))
    m = dict(attn_norm=f(attn_norm), mlp_norm=f(mlp_norm), q_gain=f(q_gain), k_gain=f(k_gain), w_out=f(w_out),
             mlp_w_up=f(mlp_w_up), mlp_w_down=f(mlp_w_down), dsa_w_in=f(dsa_w_in)[0], nsa_w_in=f(nsa_w_in)[0],
             fox_w_in=f(fox_w_in)[0], moba_w_in=f(moba_w_in)[0],
             nsa_posT=np.ascontiguousarray(f(nsa_cmp_pos)[0].transpose(0, 2, 1)),
             nsa_cmp_w1=f(nsa_cmp_w1)[0], nsa_cmp_w2=f(nsa_cmp_w2)[0], fox_b_f=f(fox_b_f))
    m.update(_host_tables(rel_table))
    return m


def run_layers(layers, x, n_cores=8, **params):
    nc = _get_prog(layers)
    common = _common_inputs(**params)
    x = np.asarray(x, dtype=np.float32)
    in_maps = []
    for c in range(n_cores):
        mm = dict(common)
        mm["x"] = np.ascontiguousarray(x[c % x.shape[0]])
        in_maps.append(mm)
    res = run_bass_kernel_spmd(nc, in_maps, core_ids=list(range(n_cores)))
    return res


def kernel(x, rel_table, attn_norm, mlp_norm, q_gain, k_gain, w_out, mlp_w_up, mlp_w_down,
           dsa_w_in, nsa_w_in, nsa_cmp_pos, nsa_cmp_w1, nsa_cmp_w2, fox_w_in, fox_b_f, moba_w_in):
    params = dict(rel_table=rel_table, attn_norm=attn_norm, mlp_norm=mlp_norm, q_gain=q_gain, k_gain=k_gain,
                  w_out=w_out, mlp_w_up=mlp_w_up, mlp_w_down=mlp_w_down, dsa_w_in=dsa_w_in, nsa_w_in=nsa_w_in,
                  nsa_cmp_pos=nsa_cmp_pos, nsa_cmp_w1=nsa_cmp_w1, nsa_cmp_w2=nsa_cmp_w2, fox_w_in=fox_w_in,
                  fox_b_f=fox_b_f, moba_w_in=moba_w_in)
    res = run_layers([0, 1, 2, 3], x, n_cores=8, **params)
    out = np.stack([np.asarray(res.results[b]["y"], dtype=np.float32) for b in range(4)], axis=0)
    return out
```
